# Optimizing a Trainium2 kernel written in Bass

```python
import math
import jax, jax.numpy as jnp
from jax import lax
import numpy as np

D_MODEL = 2048
BATCH = 2
SEQ = 8192
DEPTH = 1
DEC_BATCH = 2
DEC_SEQ = 4096
PAST_LEN = 128

N_Q_HEADS = 16
N_KV_HEADS = 4
HEAD_DIM = 64
Q_PER_KV = N_Q_HEADS // N_KV_HEADS
WINDOW = 128
BLOCK = 128
ATTN_W = N_Q_HEADS * HEAD_DIM
KV_W = N_KV_HEADS * HEAD_DIM
NEG_BIG = -1e30
D_INNER = D_MODEL
SSM_HEAD_DIM = 64
N_SSM_HEADS = D_INNER // SSM_HEAD_DIM
N_SSM_GROUPS = 4
HEADS_PER_GROUP = N_SSM_HEADS // N_SSM_GROUPS
D_STATE = 128
D_CONV = 5
CHUNK = 128
XBC_W = D_INNER + 2 * N_SSM_GROUPS * D_STATE
N_BRANCH = 2
Q_END = ATTN_W
K_END = Q_END + KV_W
V_END = K_END + KV_W
Z_END = V_END + D_INNER
XBC_END = Z_END + XBC_W
DT_START = XBC_END
DT_END = DT_START + 2 * N_SSM_HEADS
IN_W = DT_END + N_BRANCH * D_MODEL
SPLIT_POINTS = [Q_END, K_END, V_END, Z_END, XBC_END, DT_END]
N_KEYS = 128
N_EXPERTS = N_KEYS * N_KEYS
PEER_HEADS = 8
PEER_TOPK = 16
D_KEY = 256
HALF_KEY = D_KEY // 2
EXPERT_BLOCK = 128
EPS = 1e-6

kernel_name = "hybrid_swa_ssd_peer_encoder"


def rms_norm(x, g):
    x32 = x.astype(jnp.float32)
    y = x32 * lax.rsqrt(jnp.mean(x32 * x32, axis=-1, keepdims=True) + EPS)
    return y.astype(x.dtype) * g


def alibi_slopes():
    return jnp.exp2(-8.0 * jnp.arange(1, N_Q_HEADS + 1, dtype=jnp.float32) / N_Q_HEADS)


def banded_alibi_gqa(q, k, v, sink):
    b, s = q.shape[0], q.shape[1]
    nb = s // BLOCK
    q = q.reshape(b, nb, BLOCK, N_KV_HEADS, Q_PER_KV, HEAD_DIM)
    pad = ((0, 0), (BLOCK, BLOCK), (0, 0), (0, 0))
    kp = jnp.pad(k, pad).reshape(b, nb + 2, BLOCK, N_KV_HEADS, HEAD_DIM)
    vp = jnp.pad(v, pad).reshape(b, nb + 2, BLOCK, N_KV_HEADS, HEAD_DIM)
    kw = jnp.concatenate([kp[:, :-2], kp[:, 1:-1], kp[:, 2:]], axis=2)
    vw = jnp.concatenate([vp[:, :-2], vp[:, 1:-1], vp[:, 2:]], axis=2)
    scores = jnp.einsum('bjikgd,bjmkd->bkgjim', q, kw).astype(jnp.float32) * (HEAD_DIM ** -0.5)
    qi = jnp.arange(BLOCK)[:, None]
    km = jnp.arange(3 * BLOCK)[None, :]
    rel = qi - km + BLOCK
    dist = jnp.abs(rel).astype(jnp.float32)
    band = jnp.abs(rel) <= WINDOW
    key_pos = jnp.arange(nb)[:, None] * BLOCK - BLOCK + jnp.arange(3 * BLOCK)[None, :]
    in_range = (key_pos >= 0) & (key_pos < s)
    valid = band[None, :, :] & in_range[:, None, :]
    slopes = alibi_slopes().reshape(N_KV_HEADS, Q_PER_KV)
    scores = scores - slopes[None, :, :, None, None, None] * dist
    scores = jnp.where(valid, scores, NEG_BIG)
    sink32 = sink.astype(jnp.float32).reshape(N_KV_HEADS, Q_PER_KV)[None, :, :, None, None, None]
    mx = jnp.maximum(jnp.max(scores, axis=-1, keepdims=True), sink32)
    p = jnp.exp(scores - mx)
    denom = jnp.sum(p, axis=-1, keepdims=True) + jnp.exp(sink32 - mx)
    p = (p / denom).astype(v.dtype)
    out = jnp.einsum('bkgjim,bjmkd->bjikgd', p, vw)
    return out.reshape(b, s, ATTN_W)


def centred_depthwise_conv(x, w, bias):
    c = x.shape[-1]
    half = D_CONV // 2
    y = lax.conv_general_dilated(x, w[:, None, :].astype(x.dtype), window_strides=(1,),
                                 padding=[(half, half)], dimension_numbers=('NWC', 'WIO', 'NWC'),
                                 feature_group_count=c)
    return y + bias


def ssd_chunked(x, dt, a, bm, cm):
    b, s, g, r, p = x.shape
    n = bm.shape[-1]
    nc = s // CHUNK
    f32 = jnp.float32
    xc = (x.astype(f32) * dt[..., None]).reshape(b, nc, CHUNK, g, r, p)
    bc = bm.astype(f32).reshape(b, nc, CHUNK, g, n)
    cc = cm.astype(f32).reshape(b, nc, CHUNK, g, n)
    a_cum = jnp.cumsum((dt * a).reshape(b, nc, CHUNK, g, r), axis=2)
    lower = jnp.tril(jnp.ones((CHUNK, CHUNK), dtype=bool))[None, None, :, :, None, None]
    seg = a_cum[:, :, :, None] - a_cum[:, :, None, :]
    decay_in = jnp.exp(jnp.where(lower, seg, -jnp.inf))
    cb = jnp.einsum('bclgn,bcsgn->bclsg', cc, bc)
    y_diag = jnp.einsum('bclsg,bclsgr,bcsgrp->bclgrp', cb, decay_in, xc)
    decay_to_end = jnp.exp(a_cum[:, :, -1:] - a_cum)
    states = jnp.einsum('bclgn,bclgr,bclgrp->bcgrpn', bc, decay_to_end, xc)
    chunk_decay = jnp.exp(a_cum[:, :, -1])

    def step(h, inp):
        st, dec = inp
        return h * dec[..., None, None] + st, h

    h0 = jnp.zeros((b, g, r, p, n), f32)
    _, h_prev = lax.scan(step, h0, (jnp.moveaxis(states, 1, 0), jnp.moveaxis(chunk_decay, 1, 0)))
    h_prev = jnp.moveaxis(h_prev, 0, 1)
    y_off = jnp.einsum('bclgn,bcgrpn,bclgr->bclgrp', cc, h_prev, jnp.exp(a_cum))
    return (y_diag + y_off).reshape(b, s, g, r, p)


def ssd_branch(z, xbc, dt_raw, conv_w, conv_b, a_log_f, a_log_b, dt_bias_f, dt_bias_b, d_skip, g_norm):
    b, s, _ = z.shape
    xbc = jax.nn.silu(centred_depthwise_conv(xbc, conv_w, conv_b))
    xs = xbc[..., :D_INNER].reshape(b, s, N_SSM_GROUPS, HEADS_PER_GROUP, SSM_HEAD_DIM)
    bm = xbc[..., D_INNER:D_INNER + N_SSM_GROUPS * D_STATE].reshape(b, s, N_SSM_GROUPS, D_STATE)
    cm = xbc[..., D_INNER + N_SSM_GROUPS * D_STATE:].reshape(b, s, N_SSM_GROUPS, D_STATE)
    dt32 = dt_raw.astype(jnp.float32)
    dt_f = jax.nn.softplus(dt32[..., :N_SSM_HEADS] + dt_bias_f.astype(jnp.float32)).reshape(b, s, N_SSM_GROUPS, HEADS_PER_GROUP)
    dt_b = jax.nn.softplus(dt32[..., N_SSM_HEADS:] + dt_bias_b.astype(jnp.float32)).reshape(b, s, N_SSM_GROUPS, HEADS_PER_GROUP)
    a_f = -jnp.exp(a_log_f.astype(jnp.float32)).reshape(N_SSM_GROUPS, HEADS_PER_GROUP)
    a_b = -jnp.exp(a_log_b.astype(jnp.float32)).reshape(N_SSM_GROUPS, HEADS_PER_GROUP)
    y_f = ssd_chunked(xs, dt_f, a_f, bm, cm)
    flip = lambda t: jnp.flip(t, axis=1)
    y_b = flip(ssd_chunked(flip(xs), flip(dt_b), a_b, flip(bm), flip(cm)))
    skip = d_skip.astype(jnp.float32).reshape(N_SSM_GROUPS, HEADS_PER_GROUP)[:, :, None] * xs.astype(jnp.float32)
    y = (y_f + y_b + skip).astype(z.dtype).reshape(b, s, D_INNER)
    gated = (y * jax.nn.silu(z)).reshape(b, s, N_SSM_GROUPS, D_INNER // N_SSM_GROUPS)
    return rms_norm(gated, g_norm.reshape(N_SSM_GROUPS, D_INNER // N_SSM_GROUPS)).reshape(b, s, D_INNER)


def peer_ffn(x, w_query, sub_keys, expert_u, expert_v):
    b, s, d = x.shape
    t = b * s
    xf = x.reshape(t, d)
    q = (xf @ w_query).reshape(t, PEER_HEADS, 2, HALF_KEY)
    scores = jnp.einsum('thcd,hcnd->thcn', q, sub_keys).astype(jnp.float32)
    top_s, top_i = lax.top_k(scores, PEER_TOPK)
    cand_s = (top_s[:, :, 0, :, None] + top_s[:, :, 1, None, :]).reshape(t, PEER_HEADS, PEER_TOPK * PEER_TOPK)
    cand_i = (top_i[:, :, 0, :, None] * N_KEYS + top_i[:, :, 1, None, :]).reshape(t, PEER_HEADS, PEER_TOPK * PEER_TOPK)
    best_s, pos = lax.top_k(cand_s, PEER_TOPK)
    idx = jnp.take_along_axis(cand_i, pos, axis=-1)
    gate = jax.nn.softmax(best_s, axis=-1).astype(x.dtype)
    nblk = t // EXPERT_BLOCK

    def expert_block(args):
        xb, ib, gb = args
        u = jnp.take(expert_u, ib, axis=0)
        act = jax.nn.gelu(jnp.einsum('td,thkd->thk', xb, u), approximate=False)
        vv = jnp.take(expert_v, ib, axis=0)
        return jnp.einsum('thk,thkd->td', act * gb, vv)

    out = lax.map(expert_block, (xf.reshape(nblk, EXPERT_BLOCK, d),
                                 idx.reshape(nblk, EXPERT_BLOCK, PEER_HEADS, PEER_TOPK),
                                 gate.reshape(nblk, EXPERT_BLOCK, PEER_HEADS, PEER_TOPK)))
    return out.reshape(b, s, d)


def encoder_layer(x, g_mix, w_in, attn_sink, conv_w, conv_b, a_log_f, a_log_b, dt_bias_f, dt_bias_b,
                  d_skip, g_ssm_norm, w_attn_o, w_ssm_o, w_out, g_ffn, w_query, sub_keys, expert_u, expert_v):
    b, s, _ = x.shape
    h = rms_norm(x, g_mix)
    proj = h @ w_in
    q, k, v, z, xbc, dt_raw, gates = jnp.split(proj, SPLIT_POINTS, axis=-1)
    attn = banded_alibi_gqa(q.reshape(b, s, N_Q_HEADS, HEAD_DIM), k.reshape(b, s, N_KV_HEADS, HEAD_DIM),
                            v.reshape(b, s, N_KV_HEADS, HEAD_DIM), attn_sink)
    ssm = ssd_branch(z, xbc, dt_raw, conv_w, conv_b, a_log_f, a_log_b, dt_bias_f, dt_bias_b, d_skip, g_ssm_norm)
    gate_a = jax.nn.sigmoid(gates[..., :D_MODEL])
    gate_s = jax.nn.sigmoid(gates[..., D_MODEL:])
    merged = gate_a * (attn @ w_attn_o) + gate_s * (ssm @ w_ssm_o)
    x = x + merged @ w_out
    x = x + peer_ffn(rms_norm(x, g_ffn), w_query, sub_keys, expert_u, expert_v)
    return x


def trunk(x, g_mix, w_in, attn_sink, conv_w, conv_b, a_log_f, a_log_b, dt_bias_f, dt_bias_b, d_skip,
          g_ssm_norm, w_attn_o, w_ssm_o, w_out, g_ffn, w_query, sub_keys, expert_u, expert_v, g_final):
    for l in range(DEPTH):
        x = encoder_layer(x, g_mix[l], w_in[l], attn_sink[l], conv_w[l], conv_b[l], a_log_f[l], a_log_b[l],
                          dt_bias_f[l], dt_bias_b[l], d_skip[l], g_ssm_norm[l], w_attn_o[l], w_ssm_o[l],
                          w_out[l], g_ffn[l], w_query[l], sub_keys[l], expert_u[l], expert_v[l])
    return rms_norm(x, g_final)


def setup_inputs(seed: int = 0) -> dict:
    key = jax.random.key(seed)
    ks = jax.random.split(key, 24)
    f32 = jnp.float32
    nrm = lambda k, shape, scale: jax.random.normal(k, shape, f32) * scale

    def dt_bias(k):
        dt = jnp.exp(jax.random.uniform(k, (DEPTH, N_SSM_HEADS), f32, math.log(1e-3), math.log(1e-1)))
        return dt + jnp.log(-jnp.expm1(-dt))

    col_scale = jnp.ones((IN_W,), f32).at[DT_START:DT_END].set(0.1)
    return {
        "x_prompt": nrm(ks[0], (BATCH, SEQ, D_MODEL), 1.0),
        "x_sample": nrm(ks[1], (DEC_BATCH, DEC_SEQ, D_MODEL), 1.0),
        "g_mix": 1.0 + nrm(ks[2], (DEPTH, D_MODEL), 0.02),
        "w_in": nrm(ks[3], (DEPTH, D_MODEL, IN_W), D_MODEL ** -0.5) * col_scale,
        "attn_sink": nrm(ks[4], (DEPTH, N_Q_HEADS), 0.5),
        "conv_w": nrm(ks[5], (DEPTH, D_CONV, XBC_W), D_CONV ** -0.5),
        "conv_b": nrm(ks[6], (DEPTH, XBC_W), 0.02),
        "a_log_f": jnp.log(jax.random.uniform(ks[7], (DEPTH, N_SSM_HEADS), f32, 1.0, 16.0)),
        "a_log_b": jnp.log(jax.random.uniform(ks[8], (DEPTH, N_SSM_HEADS), f32, 1.0, 16.0)),
        "dt_bias_f": dt_bias(ks[9]),
        "dt_bias_b": dt_bias(ks[10]),
        "d_skip": 1.0 + nrm(ks[11], (DEPTH, N_SSM_HEADS), 0.1),
        "g_ssm_norm": 1.0 + nrm(ks[12], (DEPTH, D_INNER), 0.02),
        "w_attn_o": nrm(ks[13], (DEPTH, ATTN_W, D_MODEL), ATTN_W ** -0.5),
        "w_ssm_o": nrm(ks[14], (DEPTH, D_INNER, D_MODEL), D_INNER ** -0.5),
        "w_out": nrm(ks[15], (DEPTH, D_MODEL, D_MODEL), D_MODEL ** -0.5),
        "g_ffn": 1.0 + nrm(ks[16], (DEPTH, D_MODEL), 0.02),
        "w_query": nrm(ks[17], (DEPTH, D_MODEL, PEER_HEADS * D_KEY), D_MODEL ** -0.5),
        "sub_keys": nrm(ks[18], (DEPTH, PEER_HEADS, 2, N_KEYS, HALF_KEY), HALF_KEY ** -0.5),
        "expert_u": nrm(ks[19], (DEPTH, N_EXPERTS, D_MODEL), D_MODEL ** -0.5),
        "expert_v": nrm(ks[20], (DEPTH, N_EXPERTS, D_MODEL), 0.25),
        "g_final": 1.0 + nrm(ks[21], (D_MODEL,), 0.02),
    }


def reference(x_prompt, x_sample, g_mix, w_in, attn_sink, conv_w, conv_b, a_log_f, a_log_b, dt_bias_f,
              dt_bias_b, d_skip, g_ssm_norm, w_attn_o, w_ssm_o, w_out, g_ffn, w_query, sub_keys,
              expert_u, expert_v, g_final):
    y_prompt = trunk(x_prompt, g_mix, w_in, attn_sink, conv_w, conv_b, a_log_f, a_log_b, dt_bias_f, dt_bias_b,
                     d_skip, g_ssm_norm, w_attn_o, w_ssm_o, w_out, g_ffn, w_query, sub_keys, expert_u,
                     expert_v, g_final)
    y_sample = trunk(x_sample, g_mix, w_in, attn_sink, conv_w, conv_b, a_log_f, a_log_b, dt_bias_f, dt_bias_b,
                     d_skip, g_ssm_norm, w_attn_o, w_ssm_o, w_out, g_ffn, w_query, sub_keys, expert_u,
                     expert_v, g_final)
    return (y_prompt, y_sample)
```

```python
import contextlib
import numpy as np
import concourse.bass as bass
import concourse.mybir as mybir
from concourse.bass_utils import run_bass_kernel_spmd

F32 = mybir.dt.float32
BF16 = mybir.dt.bfloat16
U32 = mybir.dt.uint32
AF = mybir.ActivationFunctionType
ALU = mybir.AluOpType
AX = mybir.AxisListType
ENGS = ("pe", "act", "dve", "pool", "sp")

D = 2048
INW = 10816
NCORES = 8
Q_END, K_END, V_END, Z_END, XBC_END, DT_END = 1024, 1280, 1536, 3584, 6656, 6720
NEG = -30000.0
EPS = 1e-6
OWN_GROUPS = [(0, 4), (1, 2)]
NG_OWN = 6
NG_OTH = 18
T_OWN = 3072


class Buf:
    __slots__ = ("w", "r")

    def __init__(self):
        self.w = None
        self.r = []


class Prog:
    def __init__(self, nc):
        self.nc = nc
        self.ops = {e: [] for e in ENGS}
        self.dma_sems = {}
        self.waited = {e: {} for e in ENGS}

    def _need(self, eng, dep, waits):
        kind, key, val = dep
        k = (kind, key)
        if self.waited[eng].get(k, -1) >= val:
            return
        self.waited[eng][k] = val
        waits.append(dep)
        if kind == "e":
            self.ops[key][val]["inc"] = True

    def _deps(self, eng, reads, writes, pe_accum):
        waits = []
        for b in reads:
            if b.w is not None:
                self._need(eng, b.w, waits)
        for b in writes:
            if b.w is not None:
                if not (pe_accum and eng == "pe" and b.w[0] == "e" and b.w[1] == "pe"):
                    self._need(eng, b.w, waits)
            for d in b.r:
                self._need(eng, d, waits)
        return waits

    def op(self, eng, fn, reads=(), writes=(), pe_accum=False):
        waits = self._deps(eng, reads, writes, pe_accum)
        idx = len(self.ops[eng])
        self.ops[eng].append(dict(fn=fn, waits=waits, inc=False, dma=None))
        me = ("e", eng, idx)
        for b in reads:
            b.r.append(me)
        for b in writes:
            b.w = me
            b.r = []
        return me

    def dma(self, eng, fn, sem, reads=(), writes=()):
        waits = self._deps(eng, reads, writes, False)
        self.dma_sems[sem] = self.dma_sems.get(sem, 0) + 16
        val = self.dma_sems[sem]
        self.ops[eng].append(dict(fn=fn, waits=waits, inc=False, dma=(sem, 16)))
        me = ("d", sem, val)
        for b in reads:
            b.r.append(me)
        for b in writes:
            b.w = me
            b.r = []
        return me

    def barrier(self, skip=tuple(["cast_w", "cast_uv"] + [f"ci{i}" for i in range(32)])):
        deps = []
        for e in ENGS:
            for i in range(len(self.ops[e]) - 1, -1, -1):
                o = self.ops[e][i]
                if o["fn"] is not None and o["dma"] is None:
                    deps.append(("e", e, i))
                    break
        for s, v in self.dma_sems.items():
            if s not in skip:
                deps.append(("d", s, v))
        for e in ENGS:
            waits = []
            for d in deps:
                self._need(e, d, waits)
            self.ops[e].append(dict(fn=None, waits=waits, inc=False, dma=None))

    def emit(self):
        nc = self.nc
        with contextlib.ExitStack() as st:
            esem = {e: st.enter_context(nc.semaphore("s_" + e)) for e in ENGS}
            dsem = {n: st.enter_context(nc.semaphore("d_" + n)) for n in self.dma_sems}
            cum = {}
            for e in ENGS:
                c = 0
                arr = []
                for o in self.ops[e]:
                    if o["inc"]:
                        c += 1
                    arr.append(c)
                cum[e] = arr
            block = st.enter_context(nc.Block())

            def run(engname, engobj):
                for o in self.ops[engname]:
                    for (kind, key, val) in o["waits"]:
                        if kind == "e":
                            engobj.wait_ge(esem[key], cum[key][val])
                        else:
                            engobj.wait_ge(dsem[key], val)
                    if o["fn"] is None:
                        continue
                    ins = o["fn"](engobj)
                    if o["dma"] is not None:
                        ins.then_inc(dsem[o["dma"][0]], o["dma"][1])
                    elif o["inc"]:
                        ins.then_inc(esem[engname], 1)

            block.tensor(lambda e: run("pe", e))
            block.scalar(lambda e: run("act", e))
            block.vector(lambda e: run("dve", e))
            block.gpsimd(lambda e: run("pool", e))
            block.sync(lambda e: run("sp", e))


def bc(ap, shape):
    return ap.to_broadcast(list(shape))


class TilePool:
    def __init__(self, nc, stack):
        self.nc, self.stack, self.tiles, self.bufs, self.i, self.first = nc, stack, {}, [], 0, True

    def begin(self):
        self.first = (len(self.tiles) == 0)
        self.i = 0

    def sb(self, name, shape, dt):
        if name not in self.tiles:
            self.tiles[name] = self.stack.enter_context(self.nc.sbuf_tensor(name, list(shape), dt))
        return self.tiles[name]

    def buf(self):
        if self.i == len(self.bufs):
            self.bufs.append(Buf())
        b = self.bufs[self.i]
        self.i += 1
        return b


def build(stages=("W", "A", "S", "B", "C"), dbg=()):
    nc = bass.Bass("TRN2", target_bir_lowering=False)
    p = Prog(nc)

    def din(name, shape, dt=F32):
        return nc.dram_tensor(name, list(shape), dt, kind="ExternalInput").ap()

    def dscr(name, shape, dt):
        kind = "ExternalOutput" if name in dbg else "Internal"
        return nc.dram_tensor(name, list(shape), dt, kind=kind).ap()

    x_own = din("x_own", [NG_OWN, 768, D])
    x_oth = din("x_oth", [NG_OTH, 516, D])
    x_res = din("x_res", [T_OWN, D])
    w_in = din("w_in", [D, INW])
    w_ao = din("w_ao", [1024, D])
    w_so = din("w_so", [D, D])
    w_out = din("w_out", [D, D])
    w_q = din("w_q", [D, D])
    keys = din("keys", [16, 128, 128])
    exp_u = din("exp_u", [16384, D])
    exp_v = din("exp_v", [16384, D])
    rep_d = din("rep_d", [128, 4, D])
    rep_s = din("rep_s", [128, 160])
    attn_bias = din("attn_bias", [128, 384])
    emask = din("emask", [128, NG_OWN, 2])
    omask = din("omask", [128, NG_OTH, 4])
    convw = din("convw", [128, 24, 5])
    convb = din("convb", [128, 24])
    dtb = din("dtb", [64, 1])
    cst = din("cst", [128, 9, 128])
    negm = din("negm", [128, 2, 512], BF16)
    sel = din("sel", [32, 32, 128])
    y_out = nc.dram_tensor("y_out", [T_OWN, D], F32, kind="ExternalOutput").ap()

    w_in_b = dscr("w_in_b", [D, INW], BF16)
    w_ao_b = dscr("w_ao_b", [1024, D], BF16)
    w_so_b = dscr("w_so_b", [D, D], BF16)
    w_out_b = dscr("w_out_b", [D, D], BF16)
    w_q_b = dscr("w_q_b", [D, D], BF16)
    v_b = dscr("v_b", [16384, D], BF16)
    u_b = dscr("u_b", [16384, D], BF16)
    ut_b = dscr("ut_b", [128, 128, D], BF16)
    zs_d = dscr("zs_d", [T_OWN, D], BF16)
    gT_d = dscr("gT_d", [32, 128, T_OWN], BF16)
    Xs_d = dscr("Xs_d", [T_OWN, D], BF16)
    Bs_d = dscr("Bs_d", [T_OWN, 512], BF16)
    BT_d = dscr("BT_d", [4, 128, T_OWN], BF16)
    CT_d = dscr("CT_d", [4, 128, T_OWN], BF16)
    dt_d = dscr("dt_d", [T_OWN, 64], F32)
    aT_d = dscr("aT_d", [8, 128, T_OWN], BF16)
    hb_d = dscr("hb_d", [24, 128, D], BF16)
    hin_d = dscr("hin_d", [4, 128, D], F32)
    x1_d = dscr("x1_d", [T_OWN, D], F32)

    with contextlib.ExitStack() as top:
        def SB(stack, name, shape, dt):
            return stack.enter_context(nc.sbuf_tensor(name, list(shape), dt))

        banks = [top.enter_context(nc.psum_tensor(f"bank{i}", [128, 512], F32)) for i in range(8)]
        bank_buf = [Buf() for _ in range(8)]
        bank_ctr = [0]

        def nextbank():
            i = bank_ctr[0] % 8
            bank_ctr[0] += 1
            return banks[i], bank_buf[i]

        c_cst = SB(top, "c_cst", [128, 9, 128], F32)
        c_idb = SB(top, "c_idb", [128, 128], BF16)
        c_reps = SB(top, "c_reps", [128, 160], F32)
        c_arep = SB(top, "c_arep", [128, 64], F32)
        B_const = Buf()
        p.dma("sp", lambda e: e.dma_start(out=c_cst[:], in_=cst), "c0", writes=[B_const])
        p.dma("sp", lambda e: e.dma_start(out=c_reps[:], in_=rep_s), "c0", writes=[B_const])
        p.op("dve", lambda e: e.tensor_copy(out=c_idb[:], in_=c_cst[:, 0, :]), reads=[B_const], writes=[B_const])
        p.op("act", lambda e: e.activation(out=c_arep[:], in_=c_reps[:, 0:64], func=AF.Exp), reads=[B_const], writes=[B_const])
        p.op("dve", lambda e: e.tensor_scalar(out=c_arep[:], in0=c_arep[:], scalar1=-1.0, scalar2=None, op0=ALU.mult),
             reads=[B_const], writes=[B_const])
        IDF = c_cst[:, 0, :]
        TRIF = c_cst[:, 1, :]
        TRIB = c_cst[:, 2, :]
        ONES = c_cst[:, 3, :]
        IOTA = c_cst[:, 4, :]

        B_win, B_w, B_wuv = {}, Buf(), Buf()
        lazy_casts = []

        def emit_casts(n):
            for _ in range(n):
                if lazy_casts:
                    lazy_casts.pop(0)()
        WBLOCKS = ([(Z_END + i * 512, 512) for i in range(6)] + [(XBC_END, 64)] + [(V_END + i * 512, 512) for i in range(4)]
                   + [(DT_END + i * 512, 512) for i in range(8)] + [(0, 512), (512, 512), (Q_END, 512)])
        if "W" in stages:
            def cast(dst, src, rows, cols, rblk, sem, buf):
                for r0 in range(0, rows, rblk):
                    lazy_casts.append(lambda r0=r0, dst=dst, src=src, rblk=rblk, sem=sem, buf=buf: p.dma(
                        "pool", lambda e: e.dma_start(out=dst[r0:r0 + rblk, :], in_=src[r0:r0 + rblk, :], max_dma_last_dim=4096), sem, writes=[buf]))
            for i, (c0, ncol) in enumerate(WBLOCKS):
                B_win[c0] = Buf()
                p.dma("pool", lambda e, c0=c0, ncol=ncol: e.dma_start(out=w_in_b[:, c0:c0 + ncol], in_=w_in[:, c0:c0 + ncol], max_dma_last_dim=4096),
                      f"ci{i}", writes=[B_win[c0]])
            cast(w_ao_b, w_ao, 1024, D, 512, "cast_w", B_w)
            cast(w_so_b, w_so, D, D, 512, "cast_w", B_w)
            cast(w_out_b, w_out, D, D, 512, "cast_w", B_w)
            cast(w_q_b, w_q, D, D, 512, "cast_w", B_w)
            if "C" in stages:
                cast(u_b, exp_u, 16384, D, 1024, "cast_uv", B_wuv)
                cast(v_b, exp_v, 16384, D, 1024, "cast_uv", B_wuv)

        def rms_to_hT(st, xt_tiles, ntiles, hT, g_idx, col0s, nrows=None):
            pass

        def prep_group(pool, xsrc, nwin, own, gidx, tok0):
            pool.begin()
            Buf = pool.buf
            tg = "A" if own else "S"
            lo = 128 if own else 2
            ntile = (nwin + 127) // 128
            hT = pool.sb(f"hT{tg}", [128, 16, nwin], BF16)
            B_hT = Buf()
            xin = [pool.sb(f"xin{tg}_{i}", [128, D], F32) for i in range(2)]
            xn = [pool.sb(f"xn{tg}_{i}", [128, D], BF16) for i in range(2)]
            c_g = pool.sb(f"cg{tg}", [128, D], F32)
            B_cg = Buf()
            if pool.first:
                p.dma("sp", lambda e: e.dma_start(out=c_g[:], in_=rep_d[:, 0, :]), "c1", writes=[B_cg])
            stat = pool.sb(f"stat{tg}", [128, 8, 4], F32)
            B_xin = [Buf(), Buf()]
            B_xn = [Buf(), Buf()]
            B_stat = Buf()
            for ti in range(ntile):
                r0 = ti * 128
                rows = min(128, nwin - r0)
                s = ti % 2
                p.dma("sp", lambda e, s=s, r0=r0, rows=rows: e.dma_start(out=xin[s][0:rows, :], in_=xsrc[r0:r0 + rows, :]),
                      f"xin{s}", writes=[B_xin[s]])
                p.op("act", lambda e, s=s, rows=rows, ti=ti: e.activation(out=xn[s][0:rows, :], in_=xin[s][0:rows, :], func=AF.Square,
                                                                         accum_out=stat[0:rows, ti, 0:1]),
                     reads=[B_xin[s]], writes=[B_xn[s], B_stat])
                p.op("act", lambda e, rows=rows, ti=ti: e.activation(out=stat[0:rows, ti, 1:2], in_=stat[0:rows, ti, 0:1], func=AF.Sqrt,
                                                                    scale=1.0 / D, bias=EPS), reads=[B_stat], writes=[B_stat])
                p.op("dve", lambda e, rows=rows, ti=ti: e.reciprocal(out=stat[0:rows, ti, 2:3], in_=stat[0:rows, ti, 1:2]),
                     reads=[B_stat], writes=[B_stat])
                p.op("dve", lambda e, s=s, rows=rows, ti=ti: e.scalar_tensor_tensor(
                    out=xn[s][0:rows, :], in0=xin[s][0:rows, :], scalar=stat[0:rows, ti, 2:3], in1=c_g[0:rows, :],
                    op0=ALU.mult, op1=ALU.mult), reads=[B_xin[s], B_stat, B_cg], writes=[B_xn[s]])
                for half in range(2):
                    bk, bb = nextbank()
                    bkb = bk[:].bitcast(BF16)
                    for kk in range(8):
                        k = half * 8 + kk
                        p.op("pe", lambda e, s=s, rows=rows, k=k, kk=kk, bkb=bkb: e.transpose(
                            out=bkb[:, kk * 128:kk * 128 + rows], in_=xn[s][0:rows, k * 128:(k + 1) * 128], identity=c_idb[0:rows, 0:rows]),
                            reads=[B_xn[s], B_const], writes=[bb], pe_accum=True)
                    eng = "act" if half == 0 else "dve"
                    src = bkb.rearrange("p (k t) -> p k t", t=128)[:, :, 0:rows]
                    dst = hT[:, half * 8:half * 8 + 8, r0:r0 + rows]
                    if eng == "act":
                        p.op("act", lambda e, src=src, dst=dst: e.copy(out=dst, in_=src), reads=[bb], writes=[B_hT])
                    else:
                        p.op("dve", lambda e, src=src, dst=dst: e.tensor_copy(out=dst, in_=src), reads=[bb], writes=[B_hT])

            wt = [pool.sb(f"wt{tg}_{i}", [128, 16, 512], BF16) for i in range(2)]
            B_wt = [Buf(), Buf()]
            wctr = [0]

            def load_w(c0, ncol):
                s = wctr[0] % 2
                wctr[0] += 1
                src = w_in_b[:, c0:c0 + ncol].rearrange("(k p) c -> p k c", p=128)
                p.dma("sp", lambda e, s=s, src=src, ncol=ncol: e.dma_start(out=wt[s][:, :, 0:ncol], in_=src), f"wt{s}",
                      reads=[B_win[c0]], writes=[B_wt[s]])
                return wt[s], B_wt[s]

            def fm_proj(wtile, wb, wc0, M, t0, N):
                bk, bb = nextbank()
                for k in range(16):
                    p.op("pe", lambda e, k=k, bk=bk: e.matmul(bk[0:M, 0:N], lhsT=wtile[:, k, wc0:wc0 + M], rhs=hT[:, k, t0:t0 + N],
                                                               start=(k == 0), stop=(k == 15)),
                         reads=[wb, B_hT], writes=[bb], pe_accum=True)
                return bk, bb

            def tm_proj(wtile, wb, wc0, Ncol, t0, rows=128):
                bk, bb = nextbank()
                for k in range(16):
                    p.op("pe", lambda e, k=k, bk=bk: e.matmul(bk[0:rows, 0:Ncol], lhsT=hT[:, k, t0:t0 + rows], rhs=wtile[:, k, wc0:wc0 + Ncol],
                                                               start=(k == 0), stop=(k == 15)),
                         reads=[wb, B_hT], writes=[bb], pe_accum=True)
                return bk, bb

            res = {}
            NT = 512
            nconv = 24 if own else 20
            xbc = [pool.sb(f"xbc{tg}_{i}", [128, 516], F32) for i in range(2)]
            B_xbc = [Buf(), Buf()]
            cacc = [pool.sb(f"cacc{tg}_{i}", [128, 512], F32) for i in range(2)]
            B_cacc = [Buf(), Buf()]
            ctmp = pool.sb(f"ctmp{tg}", [128, 512], F32)
            B_ctmp = Buf()
            csil = [pool.sb(f"csil{tg}_{i}", [128, 512], BF16) for i in range(2)]
            B_csil = [Buf(), Buf()]
            c_cw = pool.sb(f"cw{tg}", [128, 24, 5], F32)
            c_cb = pool.sb(f"cb{tg}", [128, 24], F32)
            c_dtb = pool.sb(f"dtb{tg}", [64, 1], F32)
            B_cw = Buf()
            if pool.first:
                p.dma("sp", lambda e: e.dma_start(out=c_cw[:], in_=convw), "c1", writes=[B_cw])
                p.dma("sp", lambda e: e.dma_start(out=c_cb[:], in_=convb), "c1", writes=[B_cw])
                p.dma("sp", lambda e: e.dma_start(out=c_dtb[:], in_=dtb), "c1", writes=[B_cw])
            Xtm = pool.sb(f"Xtm{tg}", [128, 4, D], BF16)
            Btm = pool.sb(f"Btm{tg}", [128, 4, 512], BF16)
            dttm = pool.sb(f"dttm{tg}", [128, 4, 64], F32)
            B_Xtm, B_Btm, B_dttm = Buf(), Buf(), Buf()
            w0 = lo - 2
            pending = [None]
            for cc4 in range(0, nconv, 4):
                wtile, wb = load_w(V_END + D + cc4 * 128, 512)
                for ci in range(4):
                    cc = cc4 + ci
                    s = cc % 2
                    for hf in range(2):
                        bk, bb = fm_proj(wtile, wb, ci * 128, 128, w0 + hf * 258, 258)
                        p.op("act", lambda e, s=s, hf=hf, bk=bk: e.copy(out=xbc[s][:, hf * 258:(hf + 1) * 258], in_=bk[:, 0:258]),
                             reads=[bb], writes=[B_xbc[s]])
                    if pending[0] is not None:
                        pending[0]()
                        pending[0] = None
                    eng = "dve"
                    for j in range(5):
                        if j == 0:
                            p.op(eng, lambda e, s=s, cc=cc: e.tensor_scalar(out=cacc[s][:], in0=xbc[s][:, 0:512], scalar1=c_cw[:, cc, 0:1],
                                                                            scalar2=None, op0=ALU.mult),
                                 reads=[B_xbc[s], B_cw], writes=[B_cacc[s]])
                        elif eng == "dve":
                            p.op(eng, lambda e, s=s, cc=cc, j=j: e.scalar_tensor_tensor(
                                out=cacc[s][:], in0=xbc[s][:, j:j + 512], scalar=c_cw[:, cc, j:j + 1], in1=cacc[s][:],
                                op0=ALU.mult, op1=ALU.add), reads=[B_xbc[s], B_cw, B_cacc[s]], writes=[B_cacc[s]])
                        else:
                            p.op(eng, lambda e, s=s, cc=cc, j=j: e.tensor_scalar(out=ctmp[:], in0=xbc[s][:, j:j + 512], scalar1=c_cw[:, cc, j:j + 1],
                                                                                 scalar2=None, op0=ALU.mult),
                                 reads=[B_xbc[s], B_cw], writes=[B_ctmp])
                            p.op(eng, lambda e, s=s: e.tensor_tensor(out=cacc[s][:], in0=cacc[s][:], in1=ctmp[:], op=ALU.add),
                                 reads=[B_ctmp, B_cacc[s]], writes=[B_cacc[s]])
                    p.op("act", lambda e, s=s, cc=cc: e.activation(out=csil[s][:], in_=cacc[s][:], func=AF.Silu, bias=c_cb[:, cc:cc + 1]),
                         reads=[B_cacc[s], B_cw], writes=[B_csil[s]])
                    if cc < 20:
                        def tr_chunk(s=s, cc=cc):
                            bk, bb = nextbank()
                            bkb = bk[:].bitcast(BF16)
                            for ti in range(4):
                                p.op("pe", lambda e, ti=ti: e.transpose(out=bkb[:, ti * 128:(ti + 1) * 128],
                                                                         in_=csil[s][:, ti * 128:(ti + 1) * 128], identity=c_idb[:]),
                                     reads=[B_csil[s], B_const], writes=[bb], pe_accum=True)
                            src = bkb[:, 0:512].rearrange("p (t c) -> p t c", c=128)
                            if cc < 16:
                                p.op("dve", lambda e: e.tensor_copy(out=Xtm[:, :, cc * 128:(cc + 1) * 128], in_=src), reads=[bb], writes=[B_Xtm])
                            else:
                                p.op("dve", lambda e: e.tensor_copy(out=Btm[:, :, (cc - 16) * 128:(cc - 15) * 128], in_=src), reads=[bb], writes=[B_Btm])
                        pending[0] = tr_chunk
                    if own and cc >= 16:
                        g = (cc - 16) % 4
                        dst = (BT_d if cc < 20 else CT_d)[g, :, tok0:tok0 + NT]
                        p.dma("pool", lambda e, s=s, dst=dst: e.dma_start(out=dst, in_=csil[s][:]), f"stc{s}", reads=[B_csil[s]])
            wtile, wb = load_w(XBC_END, 64)
            bk, bb = fm_proj(wtile, wb, 0, 64, lo, NT)
            if pending[0] is not None:
                pending[0]()
                pending[0] = None
            dtf = pool.sb(f"dtf{tg}", [64, 512], F32)
            B_dtf = Buf()
            p.op("act", lambda e, bk=bk: e.activation(out=dtf[:], in_=bk[0:64, 0:512], func=AF.Exp, bias=c_dtb[:, 0:1]),
                 reads=[bb, B_cw], writes=[B_dtf])
            p.op("act", lambda e: e.activation(out=dtf[:], in_=dtf[:], func=AF.Ln, bias=1.0), reads=[B_dtf], writes=[B_dtf])
            bk, bb = nextbank()
            for ti in range(4):
                p.op("pe", lambda e, ti=ti, bk=bk: e.transpose(out=bk[:, ti * 64:(ti + 1) * 64], in_=dtf[:, ti * 128:(ti + 1) * 128],
                                                                identity=c_cst[0:64, 0, 0:64]),
                     reads=[B_dtf, B_const], writes=[bb], pe_accum=True)
            p.op("dve", lambda e, bk=bk: e.tensor_copy(out=dttm[:], in_=bk[:, 0:256].rearrange("p (t c) -> p t c", c=64)),
                 reads=[bb], writes=[B_dttm])
            res.update(Xtm=Xtm, Btm=Btm, dttm=dttm, B_Xtm=B_Xtm, B_Btm=B_Btm, B_dttm=B_dttm)
            if not own:
                return res
            for ti in range(4):
                t = tok0 + ti * 128
                p.dma("pool", lambda e, ti=ti, t=t: e.dma_start(out=Xs_d[t:t + 128, :], in_=Xtm[:, ti, :]), "stX", reads=[B_Xtm])
                p.dma("pool", lambda e, ti=ti, t=t: e.dma_start(out=Bs_d[t:t + 128, :], in_=Btm[:, ti, :]), "stX", reads=[B_Btm])
                p.dma("pool", lambda e, ti=ti, t=t: e.dma_start(out=dt_d[t:t + 128, :], in_=dttm[:, ti, :]), "stX", reads=[B_dttm])

            zt = [pool.sb(f"zt{tg}_{i}", [128, 512], BF16) for i in range(2)]
            B_zt = [Buf(), Buf()]
            zc = 0
            for cb4 in range(4):
                wtile, wb = load_w(V_END + cb4 * 512, 512)
                for ti in range(4):
                    bk, bb = tm_proj(wtile, wb, 0, 512, lo + ti * 128)
                    s = zc % 2
                    zc += 1
                    p.op("act", lambda e, s=s, bk=bk: e.activation(out=zt[s][:], in_=bk[:], func=AF.Silu), reads=[bb], writes=[B_zt[s]])
                    t = tok0 + ti * 128
                    p.dma("pool", lambda e, s=s, t=t, cb4=cb4: e.dma_start(out=zs_d[t:t + 128, cb4 * 512:(cb4 + 1) * 512], in_=zt[s][:]),
                          f"stz{s}", reads=[B_zt[s]])
            gt = [pool.sb(f"gt{tg}_{i}", [128, 512], BF16) for i in range(2)]
            B_gt = [Buf(), Buf()]
            for cb4 in range(8):
                wtile, wb = load_w(DT_END + cb4 * 512, 512)
                for ci in range(4):
                    cc = cb4 * 4 + ci
                    bk, bb = fm_proj(wtile, wb, ci * 128, 128, lo, NT)
                    s = cc % 2
                    p.op("act", lambda e, s=s, bk=bk: e.activation(out=gt[s][:], in_=bk[:], func=AF.Sigmoid), reads=[bb], writes=[B_gt[s]])
                    p.dma("pool", lambda e, s=s, cc=cc: e.dma_start(out=gT_d[cc, :, tok0:tok0 + NT], in_=gt[s][:]), f"stg{s}", reads=[B_gt[s]])
            qT = pool.sb(f"qT{tg}", [64, 16, 512], BF16)
            kT = pool.sb(f"kT{tg}", [64, 4, 768], BF16)
            vt = pool.sb(f"vt{tg}", [128, 6, 256], BF16)
            B_qT, B_kT, B_vt = Buf(), Buf(), Buf()
            for cb4 in range(2):
                wtile, wb = load_w(cb4 * 512, 512)
                for hh in range(8):
                    h = cb4 * 8 + hh
                    bk, bb = fm_proj(wtile, wb, hh * 64, 64, lo, NT)
                    p.op("act", lambda e, h=h, bk=bk: e.activation(out=qT[:, h, :], in_=bk[0:64, :], func=AF.Copy, scale=0.125),
                         reads=[bb], writes=[B_qT])
            wtile, wb = load_w(Q_END, 512)
            for kv in range(4):
                for hf in range(2):
                    bk, bb = fm_proj(wtile, wb, kv * 64, 64, hf * 384, 384)
                    p.op("dve", lambda e, kv=kv, hf=hf, bk=bk: e.tensor_copy(out=kT[:, kv, hf * 384:(hf + 1) * 384], in_=bk[0:64, 0:384]),
                         reads=[bb], writes=[B_kT])
            for ti in range(6):
                bk, bb = tm_proj(wtile, wb, 256, 256, ti * 128)
                p.op("act", lambda e, ti=ti, bk=bk: e.copy(out=vt[:, ti, :], in_=bk[:, 0:256]), reads=[bb], writes=[B_vt])

            c_ab = pool.sb(f"ab{tg}", [128, 384], F32)
            c_em = pool.sb(f"em{tg}", [128, NG_OWN, 2], F32)
            B_ab = Buf()
            if pool.first:
                p.dma("sp", lambda e: e.dma_start(out=c_ab[:], in_=attn_bias), "c1", writes=[B_ab])
                p.dma("sp", lambda e: e.dma_start(out=c_em[:], in_=emask), "c1", writes=[B_ab])
            sc = [pool.sb(f"sc{tg}_{i}", [128, 4, 384], F32) for i in range(1)] * 2
            pr = [pool.sb(f"pr{tg}_{i}", [128, 4, 384], BF16) for i in range(1)] * 2
            prT = [pool.sb(f"prT{tg}_{i}", [128, 4, 3, 128], BF16) for i in range(1)] * 2
            ast = [pool.sb(f"ast{tg}_{i}", [128, 4, 8], F32) for i in range(2)]
            B_sc, B_pr, B_prT, B_ast = [Buf()] * 2, [Buf()] * 2, [Buf()] * 2, [Buf(), Buf()]
            atm = pool.sb(f"atm{tg}", [128, 1024], BF16)
            B_atm = Buf()
            aTt = pool.sb(f"aTt{tg}", [128, 8, 512], BF16)
            B_aTt = Buf()
            it = 0
            for j in range(4):
                for kv in range(4):
                    s = it % 2
                    it += 1
                    sbanks = []
                    for g in range(4):
                        h = kv * 4 + g
                        bk, bb = nextbank()
                        p.op("pe", lambda e, h=h, kv=kv, j=j, bk=bk: e.matmul(bk[:, 0:384], lhsT=qT[:, h, j * 128:(j + 1) * 128],
                                                                              rhs=kT[:, kv, j * 128:j * 128 + 384], start=True, stop=True),
                             reads=[B_qT, B_kT], writes=[bb])
                        p.op("dve", lambda e, s=s, g=g, h=h, bk=bk: e.scalar_tensor_tensor(out=sc[s][:, g, :], in0=c_ab[:], scalar=float(2.0 ** (-8.0 * (h + 1) / 16)),
                                                                                           in1=bk[:, 0:384], op0=ALU.mult, op1=ALU.add),
                             reads=[bb, B_ab], writes=[B_sc[s]])
                    if j == 0:
                        p.op("dve", lambda e, s=s: e.tensor_scalar(out=sc[s][:, :, 0:128], in0=sc[s][:, :, 0:128], scalar1=c_em[:, gidx, 0:1],
                                                                   scalar2=None, op0=ALU.add), reads=[B_sc[s], B_ab], writes=[B_sc[s]])
                    if j == 3:
                        p.op("dve", lambda e, s=s: e.tensor_scalar(out=sc[s][:, :, 256:384], in0=sc[s][:, :, 256:384], scalar1=c_em[:, gidx, 1:2],
                                                                   scalar2=None, op0=ALU.add), reads=[B_sc[s], B_ab], writes=[B_sc[s]])
                    p.op("dve", lambda e, s=s: e.tensor_reduce(out=ast[s][:, :, 0], in_=sc[s][:], axis=AX.X, op=ALU.max),
                         reads=[B_sc[s]], writes=[B_ast[s]])
                    p.op("dve", lambda e, s=s, kv=kv: e.tensor_tensor(out=ast[s][:, :, 1], in0=ast[s][:, :, 0], in1=c_reps[:, 96 + kv * 4:100 + kv * 4],
                                                                      op=ALU.max), reads=[B_ast[s], B_const], writes=[B_ast[s]])
                    p.op("dve", lambda e, s=s: e.tensor_scalar(out=ast[s][:, :, 2], in0=ast[s][:, :, 1], scalar1=-1.0, scalar2=None, op0=ALU.mult),
                         reads=[B_ast[s]], writes=[B_ast[s]])
                    for g in range(4):
                        p.op("act", lambda e, s=s, g=g: e.activation(out=pr[s][:, g, :], in_=sc[s][:, g, :], func=AF.Exp, bias=ast[s][:, g, 2:3],
                                                                     accum_out=ast[s][:, g, 3:4]), reads=[B_sc[s], B_ast[s]], writes=[B_pr[s], B_ast[s]])
                    p.op("dve", lambda e, s=s, kv=kv: e.tensor_tensor(out=ast[s][:, :, 4], in0=c_reps[:, 96 + kv * 4:100 + kv * 4], in1=ast[s][:, :, 1],
                                                                      op=ALU.subtract), reads=[B_ast[s], B_const], writes=[B_ast[s]])
                    p.op("act", lambda e, s=s: e.activation(out=ast[s][:, :, 5], in_=ast[s][:, :, 4], func=AF.Exp), reads=[B_ast[s]], writes=[B_ast[s]])
                    p.op("dve", lambda e, s=s: e.tensor_tensor(out=ast[s][:, :, 6], in0=ast[s][:, :, 5], in1=ast[s][:, :, 3], op=ALU.add),
                         reads=[B_ast[s]], writes=[B_ast[s]])
                    p.op("dve", lambda e, s=s: e.reciprocal(out=ast[s][:, :, 7], in_=ast[s][:, :, 6]), reads=[B_ast[s]], writes=[B_ast[s]])
                    for g in range(4):
                        bk, bb = nextbank()
                        bkb = bk[:].bitcast(BF16)
                        for m in range(3):
                            p.op("pe", lambda e, s=s, g=g, m=m, bkb=bkb: e.transpose(out=bkb[:, m * 128:(m + 1) * 128], in_=pr[s][:, g, m * 128:(m + 1) * 128],
                                                                                    identity=c_idb[:]), reads=[B_pr[s], B_const], writes=[bb], pe_accum=True)
                        eng = "act" if g % 2 == 0 else "dve"
                        if eng == "act":
                            p.op("act", lambda e, s=s, g=g, bkb=bkb: e.copy(out=prT[s][:, g, :, :], in_=bkb[:, 0:384].rearrange("p (m t) -> p m t", t=128)),
                                 reads=[bb], writes=[B_prT[s]])
                        else:
                            p.op("dve", lambda e, s=s, g=g, bkb=bkb: e.tensor_copy(out=prT[s][:, g, :, :], in_=bkb[:, 0:384].rearrange("p (m t) -> p m t", t=128)),
                                 reads=[bb], writes=[B_prT[s]])
                    bk, bb = nextbank()
                    for g in range(4):
                        for m in range(3):
                            p.op("pe", lambda e, s=s, g=g, m=m, kv=kv, j=j, bk=bk: e.matmul(bk[:, g * 64:(g + 1) * 64], lhsT=prT[s][:, g, m, :],
                                                                                         rhs=vt[:, j + m, kv * 64:(kv + 1) * 64], start=(m == 0), stop=(m == 2)),
                                 reads=[B_prT[s], B_vt], writes=[bb], pe_accum=True)
                    p.op("dve", lambda e, s=s, kv=kv, bk=bk: e.tensor_tensor(
                        out=atm[:, kv * 256:(kv + 1) * 256].rearrange("p (g d) -> p g d", d=64),
                        in0=bk[:, 0:256].rearrange("p (g d) -> p g d", d=64),
                        in1=bc(ast[s][:, :, 7:8], [128, 4, 64]), op=ALU.mult), reads=[bb, B_ast[s]], writes=[B_atm])
                bk, bb = nextbank()
                bkb = bk[:].bitcast(BF16)
                for k in range(8):
                    p.op("pe", lambda e, k=k, bkb=bkb: e.transpose(out=bkb[:, k * 128:(k + 1) * 128], in_=atm[:, k * 128:(k + 1) * 128], identity=c_idb[:]),
                         reads=[B_atm, B_const], writes=[bb], pe_accum=True)
                p.op("act", lambda e, j=j, bkb=bkb: e.copy(out=aTt[:, :, j * 128:(j + 1) * 128], in_=bkb.rearrange("p (k t) -> p k t", t=128)),
                     reads=[bb], writes=[B_aTt])
            for k in range(8):
                p.dma("pool", lambda e, k=k: e.dma_start(out=aT_d[k, :, tok0:tok0 + NT], in_=aTt[:, k, :]), "sta", reads=[B_aTt])
            return res

        if "A" in stages:
            gi = 0
            with contextlib.ExitStack() as stA:
                poolA = TilePool(nc, stA)
                for seg, ng in OWN_GROUPS:
                    for g in range(ng):
                        tok0 = (0 if seg == 0 else 2048) + g * 512
                        prep_group(poolA, x_own[gi], 768, True, gi, tok0)
                        emit_casts(2)
                        gi += 1
                p.barrier()


        ssd_stack = contextlib.ExitStack()
        Hst = SB(ssd_stack, "Hst", [128, 4, D], F32)
        B_H = [Buf() for _ in range(4)]

        def ssd_work(st, tag, two_xs=False):
            W = {}
            for nm, shp, dt in (("dA", [128, 64], F32), ("cumsb", [128, 128], F32), ("tmp", [128, 64], F32), ("dte", [128, 64], F32),
                                ("dec", [128, 64], F32), ("w", [128, 64], F32), ("xs", [128, D], BF16), ("xs2", [128, D], BF16), ("deff", [128, 64], F32),
                                ("tg", [128, 512], F32)):
                if nm == "xs2" and not two_xs:
                    continue
                W[nm] = SB(st, nm + tag, shp, dt)
                W["B_" + nm] = Buf()
            return W

        def chunk_pre(W, dt_ap, B_dt):
            p.op("dve", lambda e: e.tensor_tensor(out=W["dA"][:], in0=dt_ap, in1=c_arep[:], op=ALU.mult), reads=[B_dt, B_const], writes=[W["B_dA"]])
            bk, bb = nextbank()
            p.op("pe", lambda e, bk=bk: e.matmul(bk[:, 0:32], lhsT=TRIF, rhs=W["dA"][:, 0:32], start=True, stop=True), reads=[W["B_dA"], B_const], writes=[bb])
            p.op("pe", lambda e, bk=bk: e.matmul(bk[:, 32:64], lhsT=TRIB, rhs=W["dA"][:, 32:64], start=True, stop=True), reads=[W["B_dA"], B_const], writes=[bb], pe_accum=True)
            p.op("pe", lambda e, bk=bk: e.matmul(bk[:, 64:128], lhsT=ONES, rhs=W["dA"][:, 0:64], start=True, stop=True), reads=[W["B_dA"], B_const], writes=[bb], pe_accum=True)
            p.op("act", lambda e, bk=bk: e.copy(out=W["cumsb"][:], in_=bk[:, 0:128]), reads=[bb], writes=[W["B_cumsb"]])
            p.op("dve", lambda e: e.tensor_tensor(out=W["tmp"][:], in0=W["cumsb"][:, 64:128], in1=W["cumsb"][:, 0:64], op=ALU.subtract),
                 reads=[W["B_cumsb"]], writes=[W["B_tmp"]])
            p.op("act", lambda e: e.activation(out=W["dte"][:], in_=W["tmp"][:], func=AF.Exp), reads=[W["B_tmp"]], writes=[W["B_dte"]])
            p.op("act", lambda e: e.activation(out=W["dec"][:], in_=W["cumsb"][:, 64:128], func=AF.Exp), reads=[W["B_cumsb"]], writes=[W["B_dec"]])
            p.op("dve", lambda e: e.tensor_tensor(out=W["w"][:], in0=dt_ap, in1=W["dte"][:], op=ALU.mult), reads=[B_dt, W["B_dte"]], writes=[W["B_w"]])

        def chunk_states(W, X_ap, B_X, Bt_ap, B_Bt, d, xk="xs"):
            p.op("dve", lambda e: e.tensor_tensor(out=W[xk][:].rearrange("p (h d) -> p h d", d=64), in0=X_ap.rearrange("p (h d) -> p h d", d=64),
                                                  in1=bc(W["w"][:, d * 32:(d + 1) * 32].unsqueeze(2), [128, 32, 64]), op=ALU.mult),
                 reads=[B_X, W["B_w"]], writes=[W["B_" + xk]])
            out = []
            for g in range(4):
                bk, bb = nextbank()
                p.op("pe", lambda e, g=g, bk=bk: e.matmul(bk[:, 0:512], lhsT=Bt_ap[:, g * 128:(g + 1) * 128], rhs=W[xk][:, g * 512:(g + 1) * 512],
                                                         start=True, stop=True), reads=[B_Bt, W["B_" + xk]], writes=[bb])
                out.append((bk, bb))
            return out

        def h_update(W, hi, d, sbanks):
            Hv = Hst[:, hi, :]
            p.op("dve", lambda e: e.tensor_tensor(out=Hv.rearrange("p (h d) -> p h d", d=64), in0=Hv.rearrange("p (h d) -> p h d", d=64),
                                                  in1=bc(W["dec"][:, d * 32:(d + 1) * 32].unsqueeze(2), [128, 32, 64]), op=ALU.mult),
                 reads=[W["B_dec"], B_H[hi]], writes=[B_H[hi]])
            for g, (bk, bb) in enumerate(sbanks):
                p.op("dve", lambda e, g=g, bk=bk: e.tensor_tensor(out=Hst[:, hi, g * 512:(g + 1) * 512], in0=bk[:, 0:512], in1=Hst[:, hi, g * 512:(g + 1) * 512], op=ALU.add),
                     reads=[bb, B_H[hi]], writes=[B_H[hi]])

        if "S" in stages:
            with contextlib.ExitStack() as st0:
                Pb = SB(st0, "Pb", [128, 2, 32], F32)
                wpb = SB(st0, "wpb", [128, 32], F32)
                c_om = SB(st0, "c_om", [128, NG_OTH, 4], F32)
                B_Pb, B_wpb, B_om = Buf(), Buf(), Buf()
                p.dma("sp", lambda e: e.dma_start(out=c_om[:], in_=omask), "c1", writes=[B_om])
                p.op("pool", lambda e: e.memset(Pb[:], 1.0), writes=[B_Pb])
                for hi in range(4):
                    p.op("pool", lambda e, hi=hi: e.memset(Hst[:, hi, :], 0.0), writes=[B_H[hi]])
                poolS = TilePool(nc, st0)
                W_S = [ssd_work(st0, "S0", True), ssd_work(st0, "S1", True)]

                def other_group(gi):
                    seg = 0 if gi < 12 else 1
                    if True:
                        r = prep_group(poolS, x_oth[gi], 516, False, 100 + gi, 0)

                        def xs_scale(W, ti, d, xk):
                            p.op("dve", lambda e: e.tensor_tensor(out=W[xk][:].rearrange("p (h d) -> p h d", d=64), in0=r["Xtm"][:, ti, :].rearrange("p (h d) -> p h d", d=64),
                                                                  in1=bc(W["w"][:, d * 32:(d + 1) * 32].unsqueeze(2), [128, 32, 64]), op=ALU.mult),
                                 reads=[r["B_Xtm"], W["B_w"]], writes=[W["B_" + xk]])

                        def tile_pre(ti, W):
                            chunk_pre(W, r["dttm"][:, ti, :], r["B_dttm"])
                            xs_scale(W, ti, 0, "xs")
                            xs_scale(W, ti, 1, "xs2")

                        def st_mm(W, ti, g, xk):
                            bk, bb = nextbank()
                            p.op("pe", lambda e: e.matmul(bk[:, 0:512], lhsT=r["Btm"][:, ti, g * 128:(g + 1) * 128], rhs=W[xk][:, g * 512:(g + 1) * 512],
                                                          start=True, stop=True), reads=[r["B_Btm"], W["B_" + xk]], writes=[bb])
                            return bk, bb

                        def tile_main(ti, W):
                            hi = seg * 2
                            p.op("dve", lambda e: e.tensor_scalar(out=W["deff"][:, 0:32], in0=W["dec"][:, 0:32], scalar1=c_om[:, gi, 0:1], scalar2=c_om[:, gi, 1:2],
                                                                  op0=ALU.mult, op1=ALU.add), reads=[W["B_dec"], B_om], writes=[W["B_deff"]])
                            Hv = Hst[:, hi, :]
                            p.op("dve", lambda e: e.tensor_tensor(out=Hv.rearrange("p (h d) -> p h d", d=64), in0=Hv.rearrange("p (h d) -> p h d", d=64),
                                                                  in1=bc(W["deff"][:, 0:32].unsqueeze(2), [128, 32, 64]), op=ALU.mult),
                                 reads=[W["B_deff"], B_H[hi]], writes=[B_H[hi]])
                            for g in range(4):
                                bk, bb = st_mm(W, ti, g, "xs")
                                p.op("dve", lambda e, g=g, bk=bk, hi=hi: e.scalar_tensor_tensor(
                                    out=Hst[:, hi, g * 512:(g + 1) * 512], in0=bk[:, 0:512], scalar=c_om[:, gi, 0:1], in1=Hst[:, hi, g * 512:(g + 1) * 512],
                                    op0=ALU.mult, op1=ALU.add), reads=[bb, B_H[hi], B_om], writes=[B_H[hi]])
                            hi = seg * 2 + 1
                            p.op("dve", lambda e: e.tensor_scalar(out=wpb[:], in0=Pb[:, seg, :], scalar1=c_om[:, gi, 2:3], scalar2=None, op0=ALU.mult),
                                 reads=[B_Pb, B_om], writes=[B_wpb])
                            for g in range(4):
                                bk, bb = st_mm(W, ti, g, "xs2")
                                p.op("dve", lambda e, g=g, bk=bk: e.tensor_tensor(out=W["tg"][:].rearrange("p (h d) -> p h d", d=64),
                                                                                 in0=bk[:, 0:512].rearrange("p (h d) -> p h d", d=64),
                                                                                 in1=bc(wpb[:, g * 8:(g + 1) * 8].unsqueeze(2), [128, 8, 64]), op=ALU.mult),
                                     reads=[bb, B_wpb], writes=[W["B_tg"]])
                                p.op("dve", lambda e, g=g, hi=hi: e.tensor_tensor(out=Hst[:, hi, g * 512:(g + 1) * 512], in0=Hst[:, hi, g * 512:(g + 1) * 512],
                                                                                   in1=W["tg"][:], op=ALU.add), reads=[W["B_tg"], B_H[hi]], writes=[B_H[hi]])
                            p.op("dve", lambda e: e.tensor_scalar(out=W["deff"][:, 32:64], in0=W["dec"][:, 32:64], scalar1=c_om[:, gi, 2:3], scalar2=c_om[:, gi, 3:4],
                                                                  op0=ALU.mult, op1=ALU.add), reads=[W["B_dec"], B_om], writes=[W["B_deff"]])
                            p.op("dve", lambda e: e.tensor_tensor(out=Pb[:, seg, :], in0=Pb[:, seg, :], in1=W["deff"][:, 32:64], op=ALU.mult),
                                 reads=[W["B_deff"], B_Pb, B_wpb], writes=[B_Pb])

                        tile_pre(0, W_S[0])
                        for ti in range(4):
                            if ti + 1 < 4:
                                tile_pre(ti + 1, W_S[(ti + 1) % 2])
                            tile_main(ti, W_S[ti % 2])
                for gi in range(NG_OTH):
                    other_group(gi)
                    emit_casts(2)
                emit_casts(1000)
                if "hin_d" in dbg:
                    for hi in range(4):
                        p.dma("pool", lambda e, hi=hi: e.dma_start(out=hin_d[hi], in_=Hst[:, hi, :]), "sth", reads=[B_H[hi]])
                p.barrier()

        SEGS = [(0, 0, 16), (1, 2048, 8)]
        def stage_b1():
            with contextlib.ExitStack() as st:
                W2 = [ssd_work(st, "B1a"), ssd_work(st, "B1b")]
                Xc = [SB(st, f"b1X{i}", [128, D], BF16) for i in range(2)]
                Bc = [SB(st, f"b1B{i}", [128, 512], BF16) for i in range(2)]
                dc = [SB(st, f"b1d{i}", [128, 64], F32) for i in range(2)]
                hbt = [SB(st, f"b1h{i}", [128, D], BF16) for i in range(2)]
                B_Xc, B_Bc, B_dc, B_hbt = [Buf(), Buf()], [Buf(), Buf()], [Buf(), Buf()], [Buf(), Buf()]
                it = 0
                cbase = 0
                for seg, tok0, nch in SEGS:
                    hi = seg * 2 + 1
                    for c in range(nch - 1, -1, -1):
                        s = it % 2
                        W = W2[s]
                        it += 1
                        t = tok0 + c * 128
                        p.dma("sp", lambda e, s=s, t=t: e.dma_start(out=Xc[s][:], in_=Xs_d[t:t + 128, :]), f"b1l{s}", writes=[B_Xc[s]])
                        p.dma("sp", lambda e, s=s, t=t: e.dma_start(out=Bc[s][:], in_=Bs_d[t:t + 128, :]), f"b1l{s}", writes=[B_Bc[s]])
                        p.dma("sp", lambda e, s=s, t=t: e.dma_start(out=dc[s][:], in_=dt_d[t:t + 128, :]), f"b1l{s}", writes=[B_dc[s]])
                        p.op("act", lambda e, s=s, hi=hi: e.copy(out=hbt[s][:], in_=Hst[:, hi, :]), reads=[B_H[hi]], writes=[B_hbt[s]])
                        p.dma("pool", lambda e, s=s, cc=cbase + c: e.dma_start(out=hb_d[cc], in_=hbt[s][:]), f"b1s{s}", reads=[B_hbt[s]])
                        chunk_pre(W, dc[s][:], B_dc[s])
                        sb_ = chunk_states(W, Xc[s][:], B_Xc[s], Bc[s][:], B_Bc[s], 1)
                        h_update(W, hi, 1, sb_)
                    cbase += nch
                p.barrier()

        def stage_b2():
            with contextlib.ExitStack() as st:
                W2b = [ssd_work(st, "B2a"), ssd_work(st, "B2b")]
                Xc2 = [SB(st, f"b2X{i}", [128, D], BF16) for i in range(2)]
                Bc2 = [SB(st, f"b2B{i}", [128, 512], BF16) for i in range(2)]
                dc2 = [SB(st, f"b2d{i}", [128, 64], F32) for i in range(2)]
                zc2 = [SB(st, f"b2z{i}", [128, D], BF16) for i in range(2)]
                BTc2 = [SB(st, f"b2BT{i}", [128, 4, 128], BF16) for i in range(2)]
                CTc2 = [SB(st, f"b2CT{i}", [128, 4, 128], BF16) for i in range(2)]
                hbt2 = [SB(st, f"b2hb{i}", [128, D], BF16) for i in range(2)]
                hft = SB(st, "b2hf", [128, D], BF16)
                B_ld2, B_hbt2, B_hft = [Buf(), Buf()], [Buf(), Buf()], Buf()
                xdt = SB(st, "b2xdt", [128, 2, D], BF16)
                cumT = SB(st, "b2cumT", [32, 2, 128], F32)
                ecum = SB(st, "b2ecum", [128, 64], F32)
                ncum = SB(st, "b2ncum", [128, 64], F32)
                GTm = SB(st, "b2GTm", [128, 2, 4, 128], BF16)
                LT = SB(st, "b2LT", [128, 8, 128], BF16)
                MT = SB(st, "b2MT", [128, 8, 128], BF16)
                yv = SB(st, "b2y", [128, D], F32)
                t1 = SB(st, "b2t1", [128, 512], F32)
                gst = SB(st, "b2gst", [128, 4, 4], F32)
                ssm = SB(st, "b2ssm", [128, D], BF16)
                c_neg = SB(st, "b2neg", [128, 2, 512], BF16)
                c_gn = SB(st, "b2gn", [128, D], F32)
                ssmT = SB(st, "b2ssmT", [128, 16, 512], BF16)
                aTl = SB(st, "b2aT", [128, 8, 512], BF16)
                mT = SB(st, "b2mT", [128, 16, 512], BF16)
                B_xdt, B_cumT, B_ecum, B_GTm, B_LT, B_MT, B_y, B_t1, B_gst, B_ssm, B_c2, B_ssmT, B_aTl, B_mT = [Buf() for _ in range(14)]
                p.dma("sp", lambda e: e.dma_start(out=c_neg[:], in_=negm), "c1", writes=[B_c2])
                p.dma("sp", lambda e: e.dma_start(out=c_gn[:], in_=rep_d[:, 3, :]), "c1", writes=[B_c2])
                wo1 = [SB(st, f"b2wa{i}", [128, 8, 128], BF16) for i in range(2)]
                wo2 = [SB(st, f"b2ws{i}", [128, 16, 128], BF16) for i in range(2)]
                gl = [SB(st, f"b2gl{i}", [128, 2, 512], BF16) for i in range(2)]
                B_wo, B_gl = [Buf(), Buf()], [Buf(), Buf()]
                wo3 = SB(st, "b2wo", [128, 16, 512], BF16)
                B_wo3 = Buf()
                xr = SB(st, "b2xr", [128, 512], F32)
                B_xr = Buf()
                cbase = 0
                for seg, tok0, nch in SEGS:
                    hif, hib = seg * 2, seg * 2 + 1

                    def do_chunk(c, W, Xc, Bc, dc, zc, BTc, CTc, hbt, B_ld, B_hbt, seg=seg, tok0=tok0, hif=hif, hib=hib, cbase=cbase):
                        t = tok0 + c * 128
                        for dst, src in ((Xc[:], Xs_d[t:t + 128, :]), (Bc[:], Bs_d[t:t + 128, :]), (dc[:], dt_d[t:t + 128, :]), (zc[:], zs_d[t:t + 128, :]),
                                         (BTc[:], BT_d[:, :, t:t + 128].rearrange("g p t -> p g t")), (CTc[:], CT_d[:, :, t:t + 128].rearrange("g p t -> p g t"))):
                            p.dma("sp", lambda e, dst=dst, src=src: e.dma_start(out=dst, in_=src), f"b2l{(cbase + c) % 2}", writes=[B_ld])
                        p.dma("sp", lambda e, cc=cbase + c: e.dma_start(out=hbt[:], in_=hb_d[cc]), f"b2l{(cbase + c) % 2}", writes=[B_hbt])
                        p.op("act", lambda e, hif=hif: e.copy(out=hft[:], in_=Hst[:, hif, :]), reads=[B_H[hif]], writes=[B_hft])
                        chunk_pre(W, dc[:], B_ld)
                        bk, bb = nextbank()
                        p.op("pe", lambda e, bk=bk: e.matmul(bk[0:32, 0:128], lhsT=W["dA"][:, 0:32], rhs=TRIF, start=True, stop=True), reads=[W["B_dA"], B_const], writes=[bb])
                        p.op("pe", lambda e, bk=bk: e.matmul(bk[0:32, 128:256], lhsT=W["dA"][:, 32:64], rhs=TRIB, start=True, stop=True), reads=[W["B_dA"], B_const], writes=[bb], pe_accum=True)
                        p.op("act", lambda e, bk=bk: e.copy(out=cumT[:], in_=bk[0:32, 0:256].rearrange("p (d t) -> p d t", t=128)), reads=[bb], writes=[B_cumT])
                        p.op("act", lambda e: e.activation(out=ecum[:], in_=W["cumsb"][:, 0:64], func=AF.Exp), reads=[W["B_cumsb"]], writes=[B_ecum])
                        p.op("dve", lambda e: e.tensor_scalar(out=ncum[:], in0=W["cumsb"][:, 0:64], scalar1=-1.0, scalar2=None, op0=ALU.mult), reads=[W["B_cumsb"]], writes=[B_ecum])
                        for d in range(2):
                            p.op("dve" if d == 0 else "pool", lambda e, d=d: e.tensor_tensor(
                                out=xdt[:, d, :].rearrange("p (h d) -> p h d", d=64), in0=Xc[:].rearrange("p (h d) -> p h d", d=64),
                                in1=bc(dc[:, d * 32:(d + 1) * 32].unsqueeze(2), [128, 32, 64]), op=ALU.mult), reads=[B_ld], writes=[B_xdt])
                        bk, bb = nextbank()
                        for g in range(4):
                            p.op("pe", lambda e, g=g, bk=bk: e.matmul(bk[:, g * 128:(g + 1) * 128], lhsT=BTc[:, g, :], rhs=CTc[:, g, :], start=True, stop=True),
                                 reads=[B_ld], writes=[bb], pe_accum=True)
                        for d in range(2):
                            tri = TRIF if d == 0 else TRIB
                            p.op("dve", lambda e, d=d, tri=tri, bk=bk: e.tensor_tensor(out=GTm[:, d, :, :], in0=bk[:, 0:512].rearrange("p (g t) -> p g t", t=128),
                                                                                      in1=bc(tri.unsqueeze(1), [128, 4, 128]), op=ALU.mult),
                                 reads=[bb, B_const], writes=[B_GTm])
                        first = True
                        for d in range(2):
                            hsrc, B_hs = (hft, B_hft) if d == 0 else (hbt, B_hbt)
                            for g in range(4):
                                for half in range(2):
                                    bk, bb = nextbank()
                                    p.op("pe", lambda e, d=d, bk=bk: e.matmul(bk[:, 0:512], lhsT=c_idb[:], rhs=c_neg[:, d, :], start=True, stop=False),
                                         reads=[B_c2, B_const], writes=[bb])
                                    for j in range(4):
                                        h = g * 8 + half * 4 + j
                                        p.op("pe", lambda e, d=d, j=j, h=h, bk=bk: e.matmul(bk[:, j * 128:(j + 1) * 128], lhsT=bc(c_cst[0:32, 0, h:h + 1], [32, 128]),
                                                                                             rhs=cumT[:, d, :], start=False, stop=(j == 3)),
                                             reads=[B_cumT, B_const], writes=[bb], pe_accum=True)
                                    for j in range(4):
                                        h = g * 8 + half * 4 + j
                                        p.op("act", lambda e, d=d, j=j, h=h, half=half, bk=bk: e.activation(out=LT[:, half * 4 + j, :], in_=bk[:, j * 128:(j + 1) * 128], func=AF.Exp,
                                                                                                        bias=ncum[:, d * 32 + h:d * 32 + h + 1]),
                                             reads=[bb, B_ecum], writes=[B_LT])
                                p.op("dve", lambda e, d=d, g=g: e.tensor_tensor(out=MT[:], in0=LT[:], in1=bc(GTm[:, d, g, :].unsqueeze(1), [128, 8, 128]), op=ALU.mult),
                                     reads=[B_LT, B_GTm], writes=[B_MT])
                                bkd, bbd = nextbank()
                                for j in range(8):
                                    h = g * 8 + j
                                    p.op("pe", lambda e, d=d, j=j, h=h, bkd=bkd: e.matmul(bkd[:, j * 64:(j + 1) * 64], lhsT=MT[:, j, :], rhs=xdt[:, d, h * 64:(h + 1) * 64],
                                                                                         start=True, stop=True), reads=[B_MT, B_xdt], writes=[bbd], pe_accum=True)
                                bko, bbo = nextbank()
                                p.op("pe", lambda e, g=g, bko=bko, hsrc=hsrc: e.matmul(bko[:, 0:512], lhsT=CTc[:, g, :], rhs=hsrc[:, g * 512:(g + 1) * 512], start=True, stop=True),
                                     reads=[B_ld, B_hs], writes=[bbo])
                                p.op("dve", lambda e, d=d, g=g, bko=bko: e.tensor_tensor(out=t1[:].rearrange("p (h d) -> p h d", d=64),
                                                                                        in0=bko[:, 0:512].rearrange("p (h d) -> p h d", d=64),
                                                                                        in1=bc(ecum[:, d * 32 + g * 8:d * 32 + g * 8 + 8].unsqueeze(2), [128, 8, 64]), op=ALU.mult),
                                     reads=[bbo, B_ecum], writes=[B_t1])
                                if d == 0:
                                    p.op("dve", lambda e, g=g, bkd=bkd: e.tensor_tensor(out=yv[:, g * 512:(g + 1) * 512], in0=bkd[:, 0:512], in1=t1[:], op=ALU.add),
                                         reads=[bbd, B_t1], writes=[B_y])
                                else:
                                    p.op("dve", lambda e, g=g, bkd=bkd: e.tensor_tensor(out=t1[:], in0=bkd[:, 0:512], in1=t1[:], op=ALU.add),
                                         reads=[bbd, B_t1], writes=[B_t1])
                                    p.op("pool", lambda e, g=g: e.tensor_tensor(out=yv[:, g * 512:(g + 1) * 512], in0=yv[:, g * 512:(g + 1) * 512], in1=t1[:], op=ALU.add),
                                         reads=[B_t1, B_y], writes=[B_y])
                        p.op("dve", lambda e: e.tensor_tensor(out=xdt[:, 0, :].rearrange("p (h d) -> p h d", d=64), in0=Xc[:].rearrange("p (h d) -> p h d", d=64),
                                                              in1=bc(c_reps[:, 64:96].unsqueeze(2), [128, 32, 64]), op=ALU.mult),
                             reads=[B_ld, B_const, B_xdt], writes=[B_xdt])
                        p.op("dve", lambda e: e.tensor_tensor(out=yv[:], in0=yv[:], in1=xdt[:, 0, :], op=ALU.add), reads=[B_xdt, B_y], writes=[B_y])
                        p.op("dve", lambda e: e.tensor_tensor(out=yv[:], in0=yv[:], in1=zc[:], op=ALU.mult), reads=[B_ld, B_y], writes=[B_y])
                        for g in range(4):
                            p.op("act", lambda e, g=g: e.activation(out=ssm[:, g * 512:(g + 1) * 512], in_=yv[:, g * 512:(g + 1) * 512], func=AF.Square, accum_out=gst[:, g, 0:1]),
                                 reads=[B_y], writes=[B_ssm, B_gst])
                        p.op("act", lambda e: e.activation(out=gst[:, :, 1], in_=gst[:, :, 0], func=AF.Sqrt, scale=1.0 / 512, bias=EPS), reads=[B_gst], writes=[B_gst])
                        p.op("dve", lambda e: e.reciprocal(out=gst[:, :, 2], in_=gst[:, :, 1]), reads=[B_gst], writes=[B_gst])
                        p.op("dve", lambda e: e.tensor_tensor(out=yv[:].rearrange("p (g d) -> p g d", d=512), in0=yv[:].rearrange("p (g d) -> p g d", d=512),
                                                              in1=bc(gst[:, :, 2:3], [128, 4, 512]), op=ALU.mult), reads=[B_gst, B_y], writes=[B_y])
                        p.op("dve", lambda e: e.tensor_tensor(out=ssm[:], in0=yv[:], in1=c_gn[:], op=ALU.mult), reads=[B_y, B_c2, B_ssm], writes=[B_ssm])
                        ci = c % 4
                        for half in range(2):
                            bk, bb = nextbank()
                            bkb = bk[:].bitcast(BF16)
                            for kk in range(8):
                                k = half * 8 + kk
                                p.op("pe", lambda e, k=k, kk=kk, bkb=bkb: e.transpose(out=bkb[:, kk * 128:(kk + 1) * 128], in_=ssm[:, k * 128:(k + 1) * 128], identity=c_idb[:]),
                                     reads=[B_ssm, B_const], writes=[bb], pe_accum=True)
                            p.op("act", lambda e, half=half, ci=ci, bkb=bkb: e.copy(out=ssmT[:, half * 8:half * 8 + 8, ci * 128:(ci + 1) * 128],
                                                                                    in_=bkb.rearrange("p (k t) -> p k t", t=128)), reads=[bb], writes=[B_ssmT])
                        sb_ = chunk_states(W, Xc[:], B_ld, Bc[:], B_ld, 0)
                        h_update(W, hif, 0, sb_)
                        if ci == 3:
                            g0 = t - 384
                            p.dma("sp", lambda e, g0=g0: e.dma_start(out=aTl[:], in_=aT_d[:, :, g0:g0 + 512].rearrange("k p t -> p k t")), "b2a", writes=[B_aTl])
                            for cc in range(16):
                                s = cc % 2
                                p.dma("sp", lambda e, s=s, cc=cc: e.dma_start(out=wo1[s][:], in_=w_ao_b[:, cc * 128:(cc + 1) * 128].rearrange("(k p) c -> p k c", p=128)),
                                      f"b2w{s}", reads=[B_w], writes=[B_wo[s]])
                                p.dma("sp", lambda e, s=s, cc=cc: e.dma_start(out=wo2[s][:], in_=w_so_b[:, cc * 128:(cc + 1) * 128].rearrange("(k p) c -> p k c", p=128)),
                                      f"b2w{s}", reads=[B_w], writes=[B_wo[s]])
                                p.dma("sp", lambda e, s=s, cc=cc, g0=g0: e.dma_start(out=gl[s][:, 0, :], in_=gT_d[cc, :, g0:g0 + 512]), f"b2g{s}", writes=[B_gl[s]])
                                p.dma("sp", lambda e, s=s, cc=cc, g0=g0: e.dma_start(out=gl[s][:, 1, :], in_=gT_d[16 + cc, :, g0:g0 + 512]), f"b2g{s}", writes=[B_gl[s]])
                                bka, bba = nextbank()
                                for k in range(8):
                                    p.op("pe", lambda e, s=s, k=k, bka=bka: e.matmul(bka[:, 0:512], lhsT=wo1[s][:, k, :], rhs=aTl[:, k, :], start=(k == 0), stop=(k == 7)),
                                         reads=[B_wo[s], B_aTl], writes=[bba], pe_accum=True)
                                bks, bbs = nextbank()
                                for k in range(16):
                                    p.op("pe", lambda e, s=s, k=k, bks=bks: e.matmul(bks[:, 0:512], lhsT=wo2[s][:, k, :], rhs=ssmT[:, k, :], start=(k == 0), stop=(k == 15)),
                                         reads=[B_wo[s], B_ssmT], writes=[bbs], pe_accum=True)
                                p.op("dve", lambda e, s=s, bka=bka: e.tensor_tensor(out=t1[:], in0=bka[:, 0:512], in1=gl[s][:, 0, :], op=ALU.mult),
                                     reads=[bba, B_gl[s], B_t1], writes=[B_t1])
                                p.op("dve", lambda e, s=s, bks=bks: e.tensor_tensor(out=yv[:, 0:512], in0=bks[:, 0:512], in1=gl[s][:, 1, :], op=ALU.mult),
                                     reads=[bbs, B_gl[s], B_y], writes=[B_y])
                                p.op("dve", lambda e, cc=cc: e.tensor_tensor(out=mT[:, cc, :], in0=t1[:], in1=yv[:, 0:512], op=ALU.add),
                                     reads=[B_t1, B_y], writes=[B_mT])
                            for cb in range(4):
                                p.dma("sp", lambda e, cb=cb: e.dma_start(out=wo3[:], in_=w_out_b[:, cb * 512:(cb + 1) * 512].rearrange("(k p) c -> p k c", p=128)),
                                      "b2w3", reads=[B_w], writes=[B_wo3])
                                for ti in range(4):
                                    tt = g0 + ti * 128
                                    p.dma("sp", lambda e, tt=tt, cb=cb: e.dma_start(out=xr[:], in_=x_res[tt:tt + 128, cb * 512:(cb + 1) * 512]), "b2x", writes=[B_xr])
                                    bk, bb = nextbank()
                                    for k in range(16):
                                        p.op("pe", lambda e, k=k, ti=ti, bk=bk: e.matmul(bk[:, 0:512], lhsT=mT[:, k, ti * 128:(ti + 1) * 128], rhs=wo3[:, k, :],
                                                                                          start=(k == 0), stop=(k == 15)), reads=[B_mT, B_wo3], writes=[bb], pe_accum=True)
                                    p.op("dve", lambda e, bk=bk: e.tensor_tensor(out=xr[:], in0=bk[:, 0:512], in1=xr[:], op=ALU.add), reads=[bb, B_xr], writes=[B_xr])
                                    p.dma("pool", lambda e, tt=tt, cb=cb: e.dma_start(out=x1_d[tt:tt + 128, cb * 512:(cb + 1) * 512], in_=xr[:]), "b2xs", reads=[B_xr])
                    for c in range(nch):
                        s2 = (cbase + c) % 2
                        do_chunk(c, W2b[s2], Xc2[s2], Bc2[s2], dc2[s2], zc2[s2], BTc2[s2], CTc2[s2], hbt2[s2], B_ld2[s2], B_hbt2[s2])
                    cbase += nch
                p.barrier()
        if "B" in stages:
            stage_b1()
            stage_b2()
        p.barrier()
        ssd_stack.close()

        def stage_c():
            with contextlib.ExitStack() as st:
                keysT = SB(st, "keysT", [128, 16, 128], BF16)
                iob = SB(st, "iob", [128, 128], BF16)
                c_gf = SB(st, "c_gf", [128, 1, D], F32)
                B_kT, B_cc, B_gf = Buf(), Buf(), Buf()
                p.op("dve", lambda e: e.tensor_copy(out=iob[:], in_=IOTA), reads=[B_const], writes=[B_cc])
                with contextlib.ExitStack() as st2:
                    kf = SB(st2, "kf", [128, 16, 128], F32)
                    kb = SB(st2, "kb", [128, 16, 128], BF16)
                    B_kf = Buf()
                    p.dma("sp", lambda e: e.dma_start(out=kf[:], in_=keys.rearrange("a n d -> n a d")), "c1", writes=[B_kf])
                    p.op("dve", lambda e: e.tensor_copy(out=kb[:], in_=kf[:]), reads=[B_kf], writes=[B_kf])
                    for half in range(2):
                        bk, bb = nextbank()
                        bkb = bk[:].bitcast(BF16)
                        for kk in range(8):
                            p.op("pe", lambda e, a=half * 8 + kk, kk=kk, bkb=bkb: e.transpose(out=bkb[:, kk * 128:(kk + 1) * 128], in_=kb[:, a, :], identity=c_idb[:]),
                                 reads=[B_kf, B_const], writes=[bb], pe_accum=True)
                        p.op("act", lambda e, half=half, bkb=bkb: e.copy(out=keysT[:, half * 8:half * 8 + 8, :], in_=bkb.rearrange("p (k t) -> p k t", t=128)),
                             reads=[bb], writes=[B_kT])
                    ul = [SB(st2, f"ul{i}", [128, D], BF16) for i in range(2)]
                    ut = [SB(st2, f"ut{i}", [128, D], BF16) for i in range(2)]
                    B_ul, B_ut = [Buf(), Buf()], [Buf(), Buf()]
                    for c in range(128):
                        s_ = c % 2
                        p.dma("sp", lambda e, s_=s_, c=c: e.dma_start(out=ul[s_][:], in_=u_b[c * 128:(c + 1) * 128, :]), f"ul{s_}", reads=[B_wuv], writes=[B_ul[s_]])
                        for half in range(2):
                            bk, bb = nextbank()
                            bkb = bk[:].bitcast(BF16)
                            for kk in range(8):
                                k = half * 8 + kk
                                p.op("pe", lambda e, s_=s_, k=k, kk=kk, bkb=bkb: e.transpose(out=bkb[:, kk * 128:(kk + 1) * 128], in_=ul[s_][:, k * 128:(k + 1) * 128], identity=c_idb[:]),
                                     reads=[B_ul[s_], B_const], writes=[bb], pe_accum=True)
                            if half == 0:
                                p.op("act", lambda e, s_=s_, bkb=bkb: e.copy(out=ut[s_][:, 0:1024], in_=bkb), reads=[bb], writes=[B_ut[s_]])
                            else:
                                p.op("dve", lambda e, s_=s_, bkb=bkb: e.tensor_copy(out=ut[s_][:, 1024:2048], in_=bkb), reads=[bb], writes=[B_ut[s_]])
                        p.dma("pool", lambda e, s_=s_, c=c: e.dma_start(out=ut_b[c], in_=ut[s_][:]), f"us{s_}", reads=[B_ut[s_]])
                    p.barrier()

                x1t = SB(st, "x1t", [128, 1, D], F32)
                xn = SB(st, "cxn", [128, D], BF16)
                cst_ = SB(st, "cst_", [128, 2, 4], F32)
                cst2 = SB(st, "cst2", [128, 2, 4], F32)
                xnT2 = [SB(st, f"xnT{i}", [128, 16, 256], BF16) for i in range(2)]
                qT = SB(st, "cqT", [128, 16, 256], BF16)
                wq = [SB(st, f"wq{i}", [128, 16, 128], BF16) for i in range(2)]
                P2g = SB(st, "P2g", [128, 32, 128], BF16)
                OH1 = SB(st, "OH1", [128, 32, 128], BF16)
                scr = P2g[:].bitcast(F32).rearrange("p a b -> p (a b)").rearrange("p (h n) -> p h n", n=128)
                eq = OH1[:].bitcast(F32).rearrange("p a b -> p (a b)").rearrange("p (h k j) -> p h k j", k=16, j=16)
                wk = SB(st, "cwk", [128, 256], F32)
                topv2 = [SB(st, f"topv{i}", [128, 16, 16], F32) for i in range(2)]
                idxu2 = [SB(st, f"idxu{i}", [128, 16, 16], U32) for i in range(2)]
                idxf = SB(st, "idxf", [128, 16, 16], F32)
                cand = SB(st, "cand", [128, 8, 16, 16], F32)
                best = SB(st, "best", [128, 8, 16], F32)
                posu = SB(st, "posu", [128, 8, 16], U32)
                ku = SB(st, "ku", [128, 2, 8, 16], U32)
                kf_ = SB(st, "kf_", [128, 2, 8, 16], F32)
                gat = SB(st, "gat", [128, 8, 16], F32)
                gz = SB(st, "gz", [128, 8, 2], F32)
                I12_2 = [SB(st, f"I12_{i}", [128, 3, 128], F32) for i in range(2)]
                I12T = SB(st, "I12T", [128, 3, 128], BF16)
                Gs = SB(st, "Gs", [128, 128, 256], BF16)
                NSL = 4
                strm = [SB(st, f"strm{i}", [128, 2, D], BF16) for i in range(NSL)]
                ge = [SB(st, f"ge{i}", [128, 256], BF16) for i in range(2)]
                (B_x1t, B_xn, B_cst, B_cst2, B_qT, B_wk, B_idxf, B_cand, B_best, B_posu, B_ku, B_kf2, B_gat, B_gz,
                 B_I12T, B_OH1, B_P2g, B_Gs) = [Buf() for _ in range(18)]
                B_xnT2, B_topv2, B_idxu2, B_I12_2 = [Buf(), Buf()], [Buf(), Buf()], [Buf(), Buf()], [Buf(), Buf()]
                B_wq, B_ge = [Buf(), Buf()], [Buf(), Buf()]
                B_strm = [Buf() for _ in range(NSL)]
                B_scr, B_eq = B_P2g, B_OH1
                sctr = [0]

                def stage1_hc(ti, hc):
                    topv, idxu, B_topv, B_idxu = topv2[ti], idxu2[ti], B_topv2[ti], B_idxu2[ti]
                    p.op("dve", lambda e: e.max(out=topv[:, hc, 0:8], in_=scr[:, hc, :]), reads=[B_scr], writes=[B_topv])
                    p.op("dve", lambda e: e.match_replace(out=wk[:, 0:128], in_to_replace=topv[:, hc, 0:8], in_values=scr[:, hc, :], imm_value=-1e30),
                         reads=[B_scr, B_topv], writes=[B_wk])
                    p.op("dve", lambda e: e.max(out=topv[:, hc, 8:16], in_=wk[:, 0:128]), reads=[B_wk], writes=[B_topv])
                    p.op("dve", lambda e: e.max_index(out=idxu[:, hc, 0:8], in_max=topv[:, hc, 0:8], in_values=scr[:, hc, :]), reads=[B_scr, B_topv], writes=[B_idxu])
                    p.op("dve", lambda e: e.max_index(out=idxu[:, hc, 8:16], in_max=topv[:, hc, 8:16], in_values=scr[:, hc, :]), reads=[B_scr, B_topv], writes=[B_idxu])

                def scores_q(ti, qd, xsl):
                    bk, bb = nextbank()
                    for j in range(4):
                        hc = qd * 4 + j
                        p.op("pe", lambda e, hc=hc, j=j: e.matmul(bk[:, j * 128:(j + 1) * 128], lhsT=qT[:, hc, ti * 128:(ti + 1) * 128], rhs=keysT[:, hc, :],
                                                                    start=True, stop=True), reads=[B_qT, B_kT], writes=[bb], pe_accum=True)
                    p.op("act", lambda e: e.copy(out=scr[:, qd * 4:qd * 4 + 4, :], in_=bk[:, 0:512].rearrange("p (a n) -> p a n", n=128)),
                         reads=[bb], writes=[B_scr])

                def phaseA1(gi):
                    tok0 = gi * 256
                    xnT, B_xnT = xnT2[gi % 2], B_xnT2[gi % 2]
                    p.dma("sp", lambda e: e.dma_start(out=c_gf[:, 0, :], in_=rep_d[:, 1, :]), "cgf", writes=[B_gf])
                    for ti in range(2):
                        t = tok0 + ti * 128
                        p.dma("sp", lambda e, t=t: e.dma_start(out=x1t[:, 0, :], in_=x1_d[t:t + 128, :]), "cx", writes=[B_x1t])
                        p.op("act", lambda e, ti=ti: e.activation(out=xn[:], in_=x1t[:, 0, :], func=AF.Square, accum_out=cst2[:, ti, 0:1]), reads=[B_x1t], writes=[B_xn, B_cst2])
                        p.op("act", lambda e, ti=ti: e.activation(out=cst2[:, ti, 1:2], in_=cst2[:, ti, 0:1], func=AF.Sqrt, scale=1.0 / D, bias=EPS), reads=[B_cst2], writes=[B_cst2])
                        p.op("dve", lambda e, ti=ti: e.reciprocal(out=cst2[:, ti, 2:3], in_=cst2[:, ti, 1:2]), reads=[B_cst2], writes=[B_cst2])
                        p.op("dve", lambda e, ti=ti: e.scalar_tensor_tensor(out=xn[:], in0=x1t[:, 0, :], scalar=cst2[:, ti, 2:3], in1=c_gf[:, 0, :], op0=ALU.mult, op1=ALU.mult),
                             reads=[B_x1t, B_cst2, B_gf, B_xn], writes=[B_xn])
                        yield
                        for half in range(2):
                            bk, bb = nextbank()
                            bkb = bk[:].bitcast(BF16)
                            for kk in range(8):
                                k = half * 8 + kk
                                p.op("pe", lambda e, k=k, kk=kk, bkb=bkb: e.transpose(out=bkb[:, kk * 128:(kk + 1) * 128], in_=xn[:, k * 128:(k + 1) * 128], identity=c_idb[:]),
                                     reads=[B_xn, B_const], writes=[bb], pe_accum=True)
                            p.op("act", lambda e, half=half, ti=ti, bkb=bkb: e.copy(out=xnT[:, half * 8:half * 8 + 8, ti * 128:(ti + 1) * 128], in_=bkb.rearrange("p (k t) -> p k t", t=128)),
                                 reads=[bb], writes=[B_xnT])
                            yield
                    def ld_wq(cc):
                        s_ = cc % 2
                        p.dma("sp", lambda e: e.dma_start(out=wq[s_][:], in_=w_q_b[:, cc * 128:(cc + 1) * 128].rearrange("(k p) c -> p k c", p=128)),
                              f"cwq{s_}", reads=[B_w], writes=[B_wq[s_]])
                    ld_wq(0)
                    for cc in range(16):
                        s_ = cc % 2
                        bk, bb = nextbank()
                        for k in range(16):
                            p.op("pe", lambda e, s_=s_, k=k, bk=bk: e.matmul(bk[:, 0:256], lhsT=wq[s_][:, k, :], rhs=xnT[:, k, :], start=(k == 0), stop=(k == 15)),
                                 reads=[B_wq[s_], B_xnT], writes=[bb], pe_accum=True)
                        p.op("act", lambda e, cc=cc, bk=bk: e.copy(out=qT[:, cc, :], in_=bk[:, 0:256]), reads=[bb], writes=[B_qT])
                        if cc + 1 < 16:
                            ld_wq(cc + 1)
                        yield
                    for qd in range(4):
                        scores_q(0, qd, None)
                        yield
                    for hc in range(16):
                        stage1_hc(0, hc)
                        yield
                    for qd in range(4):
                        scores_q(1, qd, None)
                        yield

                def phaseA2(gi):
                    for hc in range(16):
                        stage1_hc(1, hc)
                        yield
                    for ti in range(2):
                        topv, idxu, B_topv, B_idxu = topv2[ti], idxu2[ti], B_topv2[ti], B_idxu2[ti]
                        I12, B_I12 = I12_2[ti], B_I12_2[ti]
                        p.op("dve", lambda e, idxu=idxu: e.tensor_copy(out=idxf[:], in_=idxu[:]), reads=[B_idxu], writes=[B_idxf])
                        tv = topv[:].rearrange("p (h c) k -> p h c k", c=2)
                        p.op("dve", lambda e, tv=tv: e.tensor_tensor(out=cand[:], in0=bc(tv[:, :, 0, :].unsqueeze(3), [128, 8, 16, 16]), in1=bc(tv[:, :, 1, :].unsqueeze(2), [128, 8, 16, 16]), op=ALU.add),
                             reads=[B_topv], writes=[B_cand])
                        yield
                        for h in range(8):
                            cv = cand[:, h, :, :].rearrange("p a b -> p (a b)")
                            p.op("dve", lambda e, h=h, cv=cv: e.max(out=best[:, h, 0:8], in_=cv), reads=[B_cand], writes=[B_best])
                            p.op("dve", lambda e, h=h, cv=cv: e.match_replace(out=wk[:], in_to_replace=best[:, h, 0:8], in_values=cv, imm_value=-1e30), reads=[B_cand, B_best], writes=[B_wk])
                            p.op("dve", lambda e, h=h: e.max(out=best[:, h, 8:16], in_=wk[:]), reads=[B_wk], writes=[B_best])
                            p.op("dve", lambda e, h=h, cv=cv: e.max_index(out=posu[:, h, 0:8], in_max=best[:, h, 0:8], in_values=cv), reads=[B_cand, B_best], writes=[B_posu])
                            p.op("dve", lambda e, h=h, cv=cv: e.max_index(out=posu[:, h, 8:16], in_max=best[:, h, 8:16], in_values=cv), reads=[B_cand, B_best], writes=[B_posu])
                            yield
                        p.op("dve", lambda e: e.tensor_tensor(out=gat[:], in0=best[:], in1=bc(best[:, :, 0:1], [128, 8, 16]), op=ALU.subtract), reads=[B_best], writes=[B_gat])
                        p.op("act", lambda e: e.activation(out=gat[:], in_=gat[:], func=AF.Exp), reads=[B_gat], writes=[B_gat])
                        p.op("dve", lambda e: e.tensor_reduce(out=gz[:, :, 0], in_=gat[:], axis=AX.X, op=ALU.add), reads=[B_gat], writes=[B_gz])
                        p.op("dve", lambda e: e.reciprocal(out=gz[:, :, 1], in_=gz[:, :, 0]), reads=[B_gz], writes=[B_gz])
                        p.op("dve", lambda e, I12=I12: e.tensor_tensor(out=I12[:, 2, :].rearrange("p (h k) -> p h k", k=16), in0=gat[:], in1=bc(gz[:, :, 1:2], [128, 8, 16]), op=ALU.mult),
                             reads=[B_gat, B_gz], writes=[B_I12])
                        p.op("dve", lambda e: e.tensor_single_scalar(out=ku[:, 0, :, :], in_=posu[:], scalar=4, op=ALU.logical_shift_right), reads=[B_posu], writes=[B_ku])
                        p.op("dve", lambda e: e.tensor_single_scalar(out=ku[:, 1, :, :], in_=posu[:], scalar=15, op=ALU.bitwise_and), reads=[B_posu], writes=[B_ku])
                        p.op("dve", lambda e: e.tensor_copy(out=kf_[:], in_=ku[:]), reads=[B_ku], writes=[B_kf2])
                        yield
                        iv = idxf[:].rearrange("p (h c) k -> p h c k", c=2)
                        for c_ in range(2):
                            p.op("dve", lambda e, c_=c_: e.tensor_tensor(out=eq, in0=bc(kf_[:, c_, :, :].unsqueeze(3), [128, 8, 16, 16]),
                                                                          in1=bc(IOTA[:, 0:16].unsqueeze(1).unsqueeze(1), [128, 8, 16, 16]), op=ALU.is_equal),
                                 reads=[B_kf2, B_const], writes=[B_eq])
                            p.op("dve", lambda e, c_=c_, iv=iv: e.tensor_tensor(out=eq, in0=eq, in1=bc(iv[:, :, c_, :].unsqueeze(2), [128, 8, 16, 16]), op=ALU.mult),
                                 reads=[B_idxf, B_eq], writes=[B_eq])
                            p.op("dve", lambda e, c_=c_, I12=I12: e.tensor_reduce(out=I12[:, c_, :], in_=eq.rearrange("p h k j -> p (h k) j"), axis=AX.X, op=ALU.add),
                                 reads=[B_eq], writes=[B_I12])
                            yield

                def gbuild(gi):
                    for ti in range(2):
                        I12, B_I12 = I12_2[ti], B_I12_2[ti]
                        bk, bb = nextbank()
                        for w_ in range(3):
                            p.op("pe", lambda e, w_=w_, bk=bk, I12=I12: e.transpose(out=bk[:, w_ * 128:(w_ + 1) * 128], in_=I12[:, w_, :], identity=IDF), reads=[B_I12, B_const], writes=[bb], pe_accum=True)
                        p.op("act", lambda e, bk=bk: e.copy(out=I12T[:], in_=bk[:, 0:384].rearrange("p (w t) -> p w t", t=128)), reads=[bb], writes=[B_I12T])
                        for hf in range(4):
                            tsl = slice(hf * 32, (hf + 1) * 32)
                            p.op("dve", lambda e, tsl=tsl: e.tensor_tensor(out=OH1[:], in0=bc(iob[:].unsqueeze(1), [128, 32, 128]), in1=bc(I12T[:, 0, tsl].unsqueeze(2), [128, 32, 128]), op=ALU.is_equal),
                                 reads=[B_I12T, B_cc], writes=[B_OH1])
                            p.op("dve", lambda e, tsl=tsl: e.tensor_tensor(out=P2g[:], in0=bc(iob[:].unsqueeze(1), [128, 32, 128]), in1=bc(I12T[:, 1, tsl].unsqueeze(2), [128, 32, 128]), op=ALU.is_equal),
                                 reads=[B_I12T, B_cc], writes=[B_P2g])
                            p.op("dve", lambda e, tsl=tsl: e.tensor_tensor(out=P2g[:], in0=P2g[:], in1=bc(I12T[:, 2, tsl].unsqueeze(2), [128, 32, 128]), op=ALU.mult),
                                 reads=[B_I12T, B_P2g], writes=[B_P2g])
                            for q4 in range(8):
                                bk, bb = nextbank()
                                for j in range(4):
                                    tl = q4 * 4 + j
                                    p.op("pe", lambda e, tl=tl, j=j, bk=bk: e.matmul(bk[:, j * 128:(j + 1) * 128], lhsT=P2g[:, tl, :], rhs=OH1[:, tl, :], start=True, stop=True),
                                         reads=[B_P2g, B_OH1], writes=[bb], pe_accum=True)
                                tg0 = ti * 128 + hf * 32 + q4 * 4
                                src = bk[:, 0:512].rearrange("p (t i) -> p i t", i=128)
                                p.op("act", lambda e, tg0=tg0, src=src: e.copy(out=Gs[:, :, tg0:tg0 + 4], in_=src), reads=[bb], writes=[B_Gs])

                def drain(gen, n):
                    if gen is None:
                        return None
                    for _ in range(n):
                        try:
                            next(gen)
                        except StopIteration:
                            return None
                    return gen

                def passes_and_epilogue(gi, ga, gb):
                    tok0 = gi * 256
                    xnT, B_xnT = xnT2[gi % 2], B_xnT2[gi % 2]
                    for c2 in range(64):
                        sl = sctr[0] % NSL
                        sctr[0] += 1
                        p.dma("sp", lambda e, sl=sl, c2=c2: e.dma_start(out=strm[sl][:], in_=ut_b[2 * c2:2 * c2 + 2].rearrange("c p f -> p c f")), f"cs{sl}", writes=[B_strm[sl]])
                        for cj in range(2):
                            c = 2 * c2 + cj
                            s_ = c % 2
                            bk, bb = nextbank()
                            for k in range(16):
                                p.op("pe", lambda e, sl=sl, cj=cj, k=k, bk=bk: e.matmul(bk[:, 0:256], lhsT=strm[sl][:, cj, k * 128:(k + 1) * 128], rhs=xnT[:, k, :], start=(k == 0), stop=(k == 15)),
                                     reads=[B_strm[sl], B_xnT], writes=[bb], pe_accum=True)
                            p.op("act", lambda e, s_=s_, bk=bk: e.activation(out=ge[s_][:], in_=bk[:, 0:256], func=AF.Gelu), reads=[bb], writes=[B_ge[s_]])
                            p.op("dve", lambda e, s_=s_, c=c: e.tensor_tensor(out=Gs[:, c, :], in0=Gs[:, c, :], in1=ge[s_][:], op=ALU.mult),
                                 reads=[B_ge[s_], B_Gs], writes=[B_Gs])
                        if ga is not None:
                            ga = drain(ga, 1)
                        else:
                            gb = drain(gb, 1)
                    ga = drain(ga, 10000)
                    for c2 in range(64):
                        sl = sctr[0] % NSL
                        sctr[0] += 1
                        p.dma("sp", lambda e, sl=sl, c2=c2: e.dma_start(out=strm[sl][:], in_=v_b[c2 * 256:(c2 + 1) * 256, :].rearrange("(c p) f -> p c f", p=128)), f"cs{sl}",
                              reads=[B_wuv], writes=[B_strm[sl]])
                        for cj in range(2):
                            c = 2 * c2 + cj
                            for ti in range(2):
                                for db in range(4):
                                    bi = ti * 4 + db
                                    p.op("pe", lambda e, sl=sl, cj=cj, c=c, ti=ti, db=db, bi=bi: e.matmul(banks[bi][:, 0:512], lhsT=Gs[:, c, ti * 128:(ti + 1) * 128], rhs=strm[sl][:, cj, db * 512:(db + 1) * 512],
                                                                                                   start=(c == 0), stop=(c == 127)), reads=[B_strm[sl], B_Gs], writes=[bank_buf[bi]], pe_accum=True)
                        gb = drain(gb, 1)
                    gb = drain(gb, 10000)
                    p.dma("sp", lambda e: e.dma_start(out=c_gf[:, 0, :], in_=rep_d[:, 2, :]), "cgf", writes=[B_gf])
                    for ti in range(2):
                        t = tok0 + ti * 128
                        p.dma("sp", lambda e, t=t: e.dma_start(out=x1t[:, 0, :], in_=x1_d[t:t + 128, :]), "cx", writes=[B_x1t])
                        for db in range(4):
                            bi = ti * 4 + db
                            p.op("dve", lambda e, db=db, bi=bi: e.tensor_tensor(out=x1t[:, 0, db * 512:(db + 1) * 512], in0=banks[bi][:, 0:512], in1=x1t[:, 0, db * 512:(db + 1) * 512], op=ALU.add),
                                 reads=[bank_buf[bi], B_x1t], writes=[B_x1t])
                        p.op("act", lambda e, ti=ti: e.activation(out=xn[:], in_=x1t[:, 0, :], func=AF.Square, accum_out=cst_[:, ti, 0:1]), reads=[B_x1t, B_xn], writes=[B_xn, B_cst])
                        p.op("act", lambda e, ti=ti: e.activation(out=cst_[:, ti, 1:2], in_=cst_[:, ti, 0:1], func=AF.Sqrt, scale=1.0 / D, bias=EPS), reads=[B_cst], writes=[B_cst])
                        p.op("dve", lambda e, ti=ti: e.reciprocal(out=cst_[:, ti, 2:3], in_=cst_[:, ti, 1:2]), reads=[B_cst], writes=[B_cst])
                        p.op("dve", lambda e, ti=ti: e.scalar_tensor_tensor(out=x1t[:, 0, :], in0=x1t[:, 0, :], scalar=cst_[:, ti, 2:3], in1=c_gf[:, 0, :], op0=ALU.mult, op1=ALU.mult),
                             reads=[B_cst, B_gf, B_x1t], writes=[B_x1t])
                        p.dma("pool", lambda e, t=t: e.dma_start(out=y_out[t:t + 128, :], in_=x1t[:, 0, :]), "cy", reads=[B_x1t])

                NGRP = T_OWN // 256
                drain(phaseA1(0), 10000)
                drain(phaseA2(0), 10000)
                for gi in range(NGRP):
                    gbuild(gi)
                    if gi + 1 < NGRP:
                        passes_and_epilogue(gi, phaseA1(gi + 1), phaseA2(gi + 1))
                    else:
                        passes_and_epilogue(gi, None, None)
                p.barrier()

        if "C" in stages:
            stage_c()
        p.barrier(skip=())
        p.emit()
    return nc


def host_inputs(inp, c):
    b, q = c // 4, c % 4
    f32 = np.float32
    xp = inp["x_prompt"][b]
    xs = inp["x_sample"][b]

    def win(x, lo, hi):
        n = x.shape[0]
        out = np.zeros((hi - lo, x.shape[1]), f32)
        a, bnd = max(lo, 0), min(hi, n)
        out[a - lo:bnd - lo] = x[a:bnd]
        return out

    own = []
    emask = np.zeros((128, NG_OWN, 2), f32)
    gi = 0
    for (x, L) in ((xp, 2048), (xs, 1024)):
        for g in range(L // 512):
            lo = q * L + g * 512 - 128
            own.append(win(x, lo, lo + 768))
            if lo < 0:
                emask[:, gi, 0] = -1e30
            if lo + 768 > x.shape[0]:
                emask[:, gi, 1] = -1e30
            gi += 1
    oth = []
    omask = np.zeros((128, NG_OTH, 4), f32)
    gi = 0
    for (x, L) in ((xp, 2048), (xs, 1024)):
        for j in [jj for jj in range(4) if jj != q]:
            for g in range(L // 512):
                lo = j * L + g * 512 - 2
                oth.append(win(x, lo, lo + 516))
                mf = 1.0 if j < q else 0.0
                omask[:, gi, :] = [mf, 1 - mf, 1 - mf, mf]
                gi += 1
    x_res = np.concatenate([xp[q * 2048:(q + 1) * 2048], xs[q * 1024:(q + 1) * 1024]], axis=0)
    rep = lambda v: np.broadcast_to(np.asarray(v, f32).reshape(1, -1), (128, np.asarray(v).size)).copy()
    rep_d = np.stack([rep(inp["g_mix"][0]), rep(inp["g_ffn"][0]), rep(inp["g_final"]), rep(inp["g_ssm_norm"][0])], axis=1)
    rep_s = np.zeros((128, 160), f32)
    rep_s[:, 0:32] = inp["a_log_f"][0]
    rep_s[:, 32:64] = inp["a_log_b"][0]
    rep_s[:, 64:96] = inp["d_skip"][0]
    rep_s[:, 96:112] = inp["attn_sink"][0]
    slopes = np.exp2(-8.0 * np.arange(1, 17, dtype=np.float64) / 16)
    qi = np.arange(128)[:, None]
    km = np.arange(384)[None, :]
    rel = qi - km + 128
    ab = np.where(np.abs(rel) <= 128, -np.abs(rel).astype(np.float64), -1e30).astype(f32)
    convw = np.ascontiguousarray(inp["conv_w"][0].T.reshape(24, 128, 5).transpose(1, 0, 2))
    convb = np.ascontiguousarray(inp["conv_b"][0].reshape(24, 128).T)
    dtb = np.concatenate([inp["dt_bias_f"][0], inp["dt_bias_b"][0]]).reshape(64, 1).astype(f32)
    cst = np.zeros((128, 9, 128), f32)
    s_ = np.arange(128)[:, None]
    l_ = np.arange(128)[None, :]
    cst[:, 0] = np.eye(128)
    cst[:, 1] = (s_ <= l_)
    cst[:, 2] = (s_ >= l_)
    cst[:, 3] = 1.0
    cst[:, 4] = l_
    import ml_dtypes
    negm = np.zeros((128, 2, 512), f32)
    negm[:, 0] = np.tile(np.where(s_ > l_, NEG, 0.0), (1, 4))
    negm[:, 1] = np.tile(np.where(s_ < l_, NEG, 0.0), (1, 4))
    sel = np.zeros((32, 32, 128), f32)
    for h in range(32):
        sel[h, h, :] = 1.0
    return {
        "x_own": np.stack(own), "x_oth": np.stack(oth), "x_res": x_res,
        "w_in": inp["w_in"][0], "w_ao": inp["w_attn_o"][0], "w_so": inp["w_ssm_o"][0], "w_out": inp["w_out"][0],
        "w_q": inp["w_query"][0], "keys": inp["sub_keys"][0].reshape(16, 128, 128),
        "exp_u": inp["expert_u"][0], "exp_v": inp["expert_v"][0],
        "rep_d": rep_d, "rep_s": rep_s, "attn_bias": ab, "emask": emask, "omask": omask,
        "convw": convw, "convb": convb, "dtb": dtb, "cst": cst, "negm": negm.astype(ml_dtypes.bfloat16), "sel": sel,
    }


def kernel(**inputs):
    inp = {k: np.asarray(v) for k, v in inputs.items()}
    nc = build()
    in_maps = [host_inputs(inp, c) for c in range(NCORES)]
    res = run_bass_kernel_spmd(nc, in_maps, core_ids=list(range(NCORES)))
    yp = np.zeros((2, 8192, D), np.float32)
    ys = np.zeros((2, 4096, D), np.float32)
    for c in range(NCORES):
        b, q = c // 4, c % 4
        y = res.results[c]["y_out"]
        yp[b, q * 2048:(q + 1) * 2048] = y[0:2048]
        ys[b, q * 1024:(q + 1) * 1024] = y[2048:3072]
    return (yp, ys)
```

```python
import contextlib
import numpy as np
import concourse.bass as bass
import concourse.mybir as mybir
from concourse.bass_utils import run_bass_kernel_spmd

F32 = mybir.dt.float32
BF16 = mybir.dt.bfloat16
U32 = mybir.dt.uint32
AF = mybir.ActivationFunctionType
ALU = mybir.AluOpType
AX = mybir.AxisListType
ENGS = ("pe", "act", "dve", "pool", "sp")

D = 2048
INW = 10816
NCORES = 8
Q_END, K_END, V_END, Z_END, XBC_END, DT_END = 1024, 1280, 1536, 3584, 6656, 6720
NEG = -30000.0
EPS = 1e-6
OWN_GROUPS = [(0, 4), (1, 2)]
NG_OWN = 6
NG_OTH = 18
T_OWN = 3072


class Buf:
    __slots__ = ("w", "r")

    def __init__(self):
        self.w = None
        self.r = []


class Prog:
    def __init__(self, nc):
        self.nc = nc
        self.ops = {e: [] for e in ENGS}
        self.dma_sems = {}
        self.waited = {e: {} for e in ENGS}

    def _need(self, eng, dep, waits):
        kind, key, val = dep
        k = (kind, key)
        if self.waited[eng].get(k, -1) >= val:
            return
        self.waited[eng][k] = val
        waits.append(dep)
        if kind == "e":
            self.ops[key][val]["inc"] = True

    def _deps(self, eng, reads, writes, pe_accum):
        waits = []
        for b in reads:
            if b.w is not None:
                self._need(eng, b.w, waits)
        for b in writes:
            if b.w is not None:
                if not (pe_accum and eng == "pe" and b.w[0] == "e" and b.w[1] == "pe"):
                    self._need(eng, b.w, waits)
            for d in b.r:
                self._need(eng, d, waits)
        return waits

    def op(self, eng, fn, reads=(), writes=(), pe_accum=False):
        waits = self._deps(eng, reads, writes, pe_accum)
        idx = len(self.ops[eng])
        self.ops[eng].append(dict(fn=fn, waits=waits, inc=False, dma=None))
        me = ("e", eng, idx)
        for b in reads:
            b.r.append(me)
        for b in writes:
            b.w = me
            b.r = []
        return me

    def dma(self, eng, fn, sem, reads=(), writes=()):
        waits = self._deps(eng, reads, writes, False)
        self.dma_sems[sem] = self.dma_sems.get(sem, 0) + 16
        val = self.dma_sems[sem]
        self.ops[eng].append(dict(fn=fn, waits=waits, inc=False, dma=(sem, 16)))
        me = ("d", sem, val)
        for b in reads:
            b.r.append(me)
        for b in writes:
            b.w = me
            b.r = []
        return me

    def barrier(self, skip=tuple(["cast_w", "cast_uv"] + [f"ci{i}" for i in range(32)])):
        deps = []
        for e in ENGS:
            for i in range(len(self.ops[e]) - 1, -1, -1):
                o = self.ops[e][i]
                if o["fn"] is not None and o["dma"] is None:
                    deps.append(("e", e, i))
                    break
        for s, v in self.dma_sems.items():
            if s not in skip:
                deps.append(("d", s, v))
        for e in ENGS:
            waits = []
            for d in deps:
                self._need(e, d, waits)
            self.ops[e].append(dict(fn=None, waits=waits, inc=False, dma=None))

    def emit(self):
        nc = self.nc
        with contextlib.ExitStack() as st:
            esem = {e: st.enter_context(nc.semaphore("s_" + e)) for e in ENGS}
            dsem = {n: st.enter_context(nc.semaphore("d_" + n)) for n in self.dma_sems}
            cum = {}
            for e in ENGS:
                c = 0
                arr = []
                for o in self.ops[e]:
                    if o["inc"]:
                        c += 1
                    arr.append(c)
                cum[e] = arr
            block = st.enter_context(nc.Block())

            def run(engname, engobj):
                for o in self.ops[engname]:
                    for (kind, key, val) in o["waits"]:
                        if kind == "e":
                            engobj.wait_ge(esem[key], cum[key][val])
                        else:
                            engobj.wait_ge(dsem[key], val)
                    if o["fn"] is None:
                        continue
                    ins = o["fn"](engobj)
                    if o["dma"] is not None:
                        ins.then_inc(dsem[o["dma"][0]], o["dma"][1])
                    elif o["inc"]:
                        ins.then_inc(esem[engname], 1)

            block.tensor(lambda e: run("pe", e))
            block.scalar(lambda e: run("act", e))
            block.vector(lambda e: run("dve", e))
            block.gpsimd(lambda e: run("pool", e))
            block.sync(lambda e: run("sp", e))


def bc(ap, shape):
    return ap.to_broadcast(list(shape))


class TilePool:
    def __init__(self, nc, stack):
        self.nc, self.stack, self.tiles, self.bufs, self.i, self.first = nc, stack, {}, [], 0, True

    def begin(self):
        self.first = (len(self.tiles) == 0)
        self.i = 0

    def sb(self, name, shape, dt):
        if name not in self.tiles:
            self.tiles[name] = self.stack.enter_context(self.nc.sbuf_tensor(name, list(shape), dt))
        return self.tiles[name]

    def buf(self):
        if self.i == len(self.bufs):
            self.bufs.append(Buf())
        b = self.bufs[self.i]
        self.i += 1
        return b


def build(stages=("W", "A", "S", "B", "C"), dbg=()):
    nc = bass.Bass("TRN2", target_bir_lowering=False)
    p = Prog(nc)

    def din(name, shape, dt=F32):
        return nc.dram_tensor(name, list(shape), dt, kind="ExternalInput").ap()

    def dscr(name, shape, dt):
        kind = "ExternalOutput" if name in dbg else "Internal"
        return nc.dram_tensor(name, list(shape), dt, kind=kind).ap()

    x_own = din("x_own", [NG_OWN, 768, D])
    x_oth = din("x_oth", [NG_OTH, 516, D])
    x_res = din("x_res", [T_OWN, D])
    w_in = din("w_in", [D, INW])
    w_ao = din("w_ao", [1024, D])
    w_so = din("w_so", [D, D])
    w_out = din("w_out", [D, D])
    w_q = din("w_q", [D, D])
    keys = din("keys", [16, 128, 128])
    exp_u = din("exp_u", [16384, D])
    exp_v = din("exp_v", [16384, D])
    rep_d = din("rep_d", [128, 4, D])
    rep_s = din("rep_s", [128, 160])
    attn_bias = din("attn_bias", [128, 384])
    emask = din("emask", [128, NG_OWN, 2])
    omask = din("omask", [128, NG_OTH, 4])
    convw = din("convw", [128, 24, 5])
    convb = din("convb", [128, 24])
    dtb = din("dtb", [64, 1])
    cst = din("cst", [128, 9, 128])
    negm = din("negm", [128, 2, 512], BF16)
    sel = din("sel", [32, 32, 128])
    y_out = nc.dram_tensor("y_out", [T_OWN, D], F32, kind="ExternalOutput").ap()

    w_in_b = dscr("w_in_b", [D, INW], BF16)
    w_ao_b = dscr("w_ao_b", [1024, D], BF16)
    w_so_b = dscr("w_so_b", [D, D], BF16)
    w_out_b = dscr("w_out_b", [D, D], BF16)
    w_q_b = dscr("w_q_b", [D, D], BF16)
    v_b = dscr("v_b", [16384, D], BF16)
    u_b = dscr("u_b", [16384, D], BF16)
    ut_b = dscr("ut_b", [128, 128, D], BF16)
    zs_d = dscr("zs_d", [T_OWN, D], BF16)
    gT_d = dscr("gT_d", [32, 128, T_OWN], BF16)
    Xs_d = dscr("Xs_d", [T_OWN, D], BF16)
    Bs_d = dscr("Bs_d", [T_OWN, 512], BF16)
    BT_d = dscr("BT_d", [4, 128, T_OWN], BF16)
    CT_d = dscr("CT_d", [4, 128, T_OWN], BF16)
    dt_d = dscr("dt_d", [T_OWN, 64], F32)
    aT_d = dscr("aT_d", [8, 128, T_OWN], BF16)
    hb_d = dscr("hb_d", [24, 128, D], BF16)
    hin_d = dscr("hin_d", [4, 128, D], F32)
    x1_d = dscr("x1_d", [T_OWN, D], F32)

    with contextlib.ExitStack() as top:
        def SB(stack, name, shape, dt):
            return stack.enter_context(nc.sbuf_tensor(name, list(shape), dt))

        banks = [top.enter_context(nc.psum_tensor(f"bank{i}", [128, 512], F32)) for i in range(8)]
        bank_buf = [Buf() for _ in range(8)]
        bank_ctr = [0]

        def nextbank():
            i = bank_ctr[0] % 8
            bank_ctr[0] += 1
            return banks[i], bank_buf[i]

        c_cst = SB(top, "c_cst", [128, 9, 128], F32)
        c_idb = SB(top, "c_idb", [128, 128], BF16)
        c_reps = SB(top, "c_reps", [128, 160], F32)
        c_arep = SB(top, "c_arep", [128, 64], F32)
        B_const = Buf()
        p.dma("sp", lambda e: e.dma_start(out=c_cst[:], in_=cst), "c0", writes=[B_const])
        p.dma("sp", lambda e: e.dma_start(out=c_reps[:], in_=rep_s), "c0", writes=[B_const])
        p.op("dve", lambda e: e.tensor_copy(out=c_idb[:], in_=c_cst[:, 0, :]), reads=[B_const], writes=[B_const])
        p.op("act", lambda e: e.activation(out=c_arep[:], in_=c_reps[:, 0:64], func=AF.Exp), reads=[B_const], writes=[B_const])
        p.op("dve", lambda e: e.tensor_scalar(out=c_arep[:], in0=c_arep[:], scalar1=-1.0, scalar2=None, op0=ALU.mult),
             reads=[B_const], writes=[B_const])
        IDF = c_cst[:, 0, :]
        TRIF = c_cst[:, 1, :]
        TRIB = c_cst[:, 2, :]
        ONES = c_cst[:, 3, :]
        IOTA = c_cst[:, 4, :]

        B_win, B_w, B_wuv = {}, Buf(), Buf()
        lazy_casts = []

        def emit_casts(n):
            for _ in range(n):
                if lazy_casts:
                    lazy_casts.pop(0)()
        WBLOCKS = ([(Z_END + i * 512, 512) for i in range(6)] + [(XBC_END, 64)] + [(V_END + i * 512, 512) for i in range(4)]
                   + [(DT_END + i * 512, 512) for i in range(8)] + [(0, 512), (512, 512), (Q_END, 512)])
        if "W" in stages:
            def cast(dst, src, rows, cols, rblk, sem, buf):
                for r0 in range(0, rows, rblk):
                    lazy_casts.append(lambda r0=r0, dst=dst, src=src, rblk=rblk, sem=sem, buf=buf: p.dma(
                        "pool", lambda e: e.dma_start(out=dst[r0:r0 + rblk, :], in_=src[r0:r0 + rblk, :], max_dma_last_dim=4096), sem, writes=[buf]))
            for i, (c0, ncol) in enumerate(WBLOCKS):
                B_win[c0] = Buf()
                p.dma("pool", lambda e, c0=c0, ncol=ncol: e.dma_start(out=w_in_b[:, c0:c0 + ncol], in_=w_in[:, c0:c0 + ncol], max_dma_last_dim=4096),
                      f"ci{i}", writes=[B_win[c0]])
            cast(w_ao_b, w_ao, 1024, D, 512, "cast_w", B_w)
            cast(w_so_b, w_so, D, D, 512, "cast_w", B_w)
            cast(w_out_b, w_out, D, D, 512, "cast_w", B_w)
            cast(w_q_b, w_q, D, D, 512, "cast_w", B_w)
            if "C" in stages:
                cast(u_b, exp_u, 16384, D, 1024, "cast_uv", B_wuv)
                cast(v_b, exp_v, 16384, D, 1024, "cast_uv", B_wuv)

        def rms_to_hT(st, xt_tiles, ntiles, hT, g_idx, col0s, nrows=None):
            pass

        def prep_group(pool, xsrc, nwin, own, gidx, tok0):
            pool.begin()
            Buf = pool.buf
            tg = "A" if own else "S"
            lo = 128 if own else 2
            ntile = (nwin + 127) // 128
            hT = pool.sb(f"hT{tg}", [128, 16, nwin], BF16)
            B_hT = Buf()
            xin = [pool.sb(f"xin{tg}_{i}", [128, D], F32) for i in range(2)]
            xn = [pool.sb(f"xn{tg}_{i}", [128, D], BF16) for i in range(2)]
            c_g = pool.sb(f"cg{tg}", [128, D], F32)
            B_cg = Buf()
            if pool.first:
                p.dma("sp", lambda e: e.dma_start(out=c_g[:], in_=rep_d[:, 0, :]), "c1", writes=[B_cg])
            stat = pool.sb(f"stat{tg}", [128, 8, 4], F32)
            B_xin = [Buf(), Buf()]
            B_xn = [Buf(), Buf()]
            B_stat = Buf()
            for ti in range(ntile):
                r0 = ti * 128
                rows = min(128, nwin - r0)
                s = ti % 2
                p.dma("sp", lambda e, s=s, r0=r0, rows=rows: e.dma_start(out=xin[s][0:rows, :], in_=xsrc[r0:r0 + rows, :]),
                      f"xin{s}", writes=[B_xin[s]])
                p.op("act", lambda e, s=s, rows=rows, ti=ti: e.activation(out=xn[s][0:rows, :], in_=xin[s][0:rows, :], func=AF.Square,
                                                                         accum_out=stat[0:rows, ti, 0:1]),
                     reads=[B_xin[s]], writes=[B_xn[s], B_stat])
                p.op("act", lambda e, rows=rows, ti=ti: e.activation(out=stat[0:rows, ti, 1:2], in_=stat[0:rows, ti, 0:1], func=AF.Sqrt,
                                                                    scale=1.0 / D, bias=EPS), reads=[B_stat], writes=[B_stat])
                p.op("dve", lambda e, rows=rows, ti=ti: e.reciprocal(out=stat[0:rows, ti, 2:3], in_=stat[0:rows, ti, 1:2]),
                     reads=[B_stat], writes=[B_stat])
                p.op("dve", lambda e, s=s, rows=rows, ti=ti: e.scalar_tensor_tensor(
                    out=xn[s][0:rows, :], in0=xin[s][0:rows, :], scalar=stat[0:rows, ti, 2:3], in1=c_g[0:rows, :],
                    op0=ALU.mult, op1=ALU.mult), reads=[B_xin[s], B_stat, B_cg], writes=[B_xn[s]])
                for half in range(2):
                    bk, bb = nextbank()
                    bkb = bk[:].bitcast(BF16)
                    for kk in range(8):
                        k = half * 8 + kk
                        p.op("pe", lambda e, s=s, rows=rows, k=k, kk=kk, bkb=bkb: e.transpose(
                            out=bkb[:, kk * 128:kk * 128 + rows], in_=xn[s][0:rows, k * 128:(k + 1) * 128], identity=c_idb[0:rows, 0:rows]),
                            reads=[B_xn[s], B_const], writes=[bb], pe_accum=True)
                    eng = "act" if half == 0 else "dve"
                    src = bkb.rearrange("p (k t) -> p k t", t=128)[:, :, 0:rows]
                    dst = hT[:, half * 8:half * 8 + 8, r0:r0 + rows]
                    if eng == "act":
                        p.op("act", lambda e, src=src, dst=dst: e.copy(out=dst, in_=src), reads=[bb], writes=[B_hT])
                    else:
                        p.op("dve", lambda e, src=src, dst=dst: e.tensor_copy(out=dst, in_=src), reads=[bb], writes=[B_hT])

            wt = [pool.sb(f"wt{tg}_{i}", [128, 16, 512], BF16) for i in range(2)]
            B_wt = [Buf(), Buf()]
            wctr = [0]

            def load_w(c0, ncol):
                s = wctr[0] % 2
                wctr[0] += 1
                src = w_in_b[:, c0:c0 + ncol].rearrange("(k p) c -> p k c", p=128)
                p.dma("sp", lambda e, s=s, src=src, ncol=ncol: e.dma_start(out=wt[s][:, :, 0:ncol], in_=src), f"wt{s}",
                      reads=[B_win[c0]], writes=[B_wt[s]])
                return wt[s], B_wt[s]

            def fm_proj(wtile, wb, wc0, M, t0, N):
                bk, bb = nextbank()
                for k in range(16):
                    p.op("pe", lambda e, k=k, bk=bk: e.matmul(bk[0:M, 0:N], lhsT=wtile[:, k, wc0:wc0 + M], rhs=hT[:, k, t0:t0 + N],
                                                               start=(k == 0), stop=(k == 15)),
                         reads=[wb, B_hT], writes=[bb], pe_accum=True)
                return bk, bb

            def tm_proj(wtile, wb, wc0, Ncol, t0, rows=128):
                bk, bb = nextbank()
                for k in range(16):
                    p.op("pe", lambda e, k=k, bk=bk: e.matmul(bk[0:rows, 0:Ncol], lhsT=hT[:, k, t0:t0 + rows], rhs=wtile[:, k, wc0:wc0 + Ncol],
                                                               start=(k == 0), stop=(k == 15)),
                         reads=[wb, B_hT], writes=[bb], pe_accum=True)
                return bk, bb

            res = {}
            NT = 512
            nconv = 24 if own else 20
            xbc = [pool.sb(f"xbc{tg}_{i}", [128, 516], BF16) for i in range(2)]
            dg = [pool.sb(f"dg{tg}_{i}", [128, 5, 128], BF16) for i in range(2)]
            B_dg = [Buf(), Buf()]
            B_xbc = [Buf(), Buf()]
            csil = [pool.sb(f"csil{tg}_{i}", [128, 512], BF16) for i in range(2)]
            B_csil = [Buf(), Buf()]
            c_cw = pool.sb(f"cw{tg}", [128, 24, 5], F32)
            c_cb = pool.sb(f"cb{tg}", [128, 24], F32)
            c_dtb = pool.sb(f"dtb{tg}", [64, 1], F32)
            B_cw = Buf()
            if pool.first:
                p.dma("sp", lambda e: e.dma_start(out=c_cw[:], in_=convw), "c1", writes=[B_cw])
                p.dma("sp", lambda e: e.dma_start(out=c_cb[:], in_=convb), "c1", writes=[B_cw])
                p.dma("sp", lambda e: e.dma_start(out=c_dtb[:], in_=dtb), "c1", writes=[B_cw])
            Xtm = pool.sb(f"Xtm{tg}", [128, 4, D], BF16)
            Btm = pool.sb(f"Btm{tg}", [128, 4, 512], BF16)
            dttm = pool.sb(f"dttm{tg}", [128, 4, 64], F32)
            B_Xtm, B_Btm, B_dttm = Buf(), Buf(), Buf()
            w0 = lo - 2
            pending = [None]
            pending_tr = [None]
            pending_conv = [None]
            own_stores = []
            for cc4 in range(0, nconv, 4):
                wtile, wb = load_w(V_END + D + cc4 * 128, 512)
                for ci in range(4):
                    cc = cc4 + ci
                    s = cc % 2
                    for hf in range(2):
                        bk, bb = fm_proj(wtile, wb, ci * 128, 128, w0 + hf * 258, 258)
                        p.op("act", lambda e, s=s, hf=hf, bk=bk: e.copy(out=xbc[s][:, hf * 258:(hf + 1) * 258], in_=bk[:, 0:258]),
                             reads=[bb], writes=[B_xbc[s]])
                    if pending[0] is not None:
                        pending[0]()
                        pending[0] = None
                    if pending_conv[0] is not None:
                        pending_conv[0]()
                        pending_conv[0] = None
                        pending[0], pending_tr[0] = pending_tr[0], None
                    p.op("dve", lambda e, s=s, cc=cc: e.tensor_tensor(out=dg[s][:], in0=bc(c_idb[:].unsqueeze(1), [128, 5, 128]),
                                                                      in1=bc(c_cw[:, cc, :].unsqueeze(2), [128, 5, 128]), op=ALU.mult),
                         reads=[B_cw, B_const], writes=[B_dg[s]])

                    def conv_chunk(s=s, cc=cc):
                        bkc, bbc = nextbank()
                        for j in range(5):
                            p.op("pe", lambda e, j=j: e.matmul(bkc[:, 0:512], lhsT=dg[s][:, j, :], rhs=xbc[s][:, j:j + 512], start=(j == 0), stop=(j == 4)),
                                 reads=[B_dg[s], B_xbc[s]], writes=[bbc], pe_accum=True)
                        p.op("act", lambda e: e.activation(out=csil[s][:], in_=bkc[:, 0:512], func=AF.Silu, bias=c_cb[:, cc:cc + 1]),
                             reads=[bbc, B_cw], writes=[B_csil[s]])
                        if own and cc >= 16:
                            g = (cc - 16) % 4
                            dst = (BT_d if cc < 20 else CT_d)[g, :, tok0:tok0 + NT]
                            p.dma("pool", lambda e: e.dma_start(out=dst, in_=csil[s][:]), f"stc{s}", reads=[B_csil[s]])
                    pending_conv[0] = conv_chunk
                    if cc < 20:
                        def tr_chunk(s=s, cc=cc):
                            bk, bb = nextbank()
                            bkb = bk[:].bitcast(BF16)
                            for ti in range(4):
                                p.op("pe", lambda e, ti=ti: e.transpose(out=bkb[:, ti * 128:(ti + 1) * 128],
                                                                         in_=csil[s][:, ti * 128:(ti + 1) * 128], identity=c_idb[:]),
                                     reads=[B_csil[s], B_const], writes=[bb], pe_accum=True)
                            src = bkb[:, 0:512].rearrange("p (t c) -> p t c", c=128)
                            if cc < 16:
                                p.op("act", lambda e: e.copy(out=Xtm[:, :, cc * 128:(cc + 1) * 128], in_=src), reads=[bb], writes=[B_Xtm])
                            else:
                                p.op("act", lambda e: e.copy(out=Btm[:, :, (cc - 16) * 128:(cc - 15) * 128], in_=src), reads=[bb], writes=[B_Btm])
                        pending_tr[0] = tr_chunk
            wtile, wb = load_w(XBC_END, 64)
            bk, bb = fm_proj(wtile, wb, 0, 64, lo, NT)
            for q_ in (pending, pending_conv, pending_tr):
                if q_[0] is not None:
                    q_[0]()
                    q_[0] = None
            dtf = pool.sb(f"dtf{tg}", [64, 512], F32)
            B_dtf = Buf()
            p.op("act", lambda e, bk=bk: e.activation(out=dtf[:], in_=bk[0:64, 0:512], func=AF.Exp, bias=c_dtb[:, 0:1]),
                 reads=[bb, B_cw], writes=[B_dtf])
            p.op("act", lambda e: e.activation(out=dtf[:], in_=dtf[:], func=AF.Ln, bias=1.0), reads=[B_dtf], writes=[B_dtf])
            bk, bb = nextbank()
            for ti in range(4):
                p.op("pe", lambda e, ti=ti, bk=bk: e.transpose(out=bk[:, ti * 64:(ti + 1) * 64], in_=dtf[:, ti * 128:(ti + 1) * 128],
                                                                identity=c_cst[0:64, 0, 0:64]),
                     reads=[B_dtf, B_const], writes=[bb], pe_accum=True)
            p.op("dve", lambda e, bk=bk: e.tensor_copy(out=dttm[:], in_=bk[:, 0:256].rearrange("p (t c) -> p t c", c=64)),
                 reads=[bb], writes=[B_dttm])
            res.update(Xtm=Xtm, Btm=Btm, dttm=dttm, B_Xtm=B_Xtm, B_Btm=B_Btm, B_dttm=B_dttm)
            if not own:
                return res
            for ti in range(4):
                t = tok0 + ti * 128
                p.dma("pool", lambda e, ti=ti, t=t: e.dma_start(out=Xs_d[t:t + 128, :], in_=Xtm[:, ti, :]), "stX", reads=[B_Xtm])
                p.dma("pool", lambda e, ti=ti, t=t: e.dma_start(out=Bs_d[t:t + 128, :], in_=Btm[:, ti, :]), "stX", reads=[B_Btm])
                p.dma("pool", lambda e, ti=ti, t=t: e.dma_start(out=dt_d[t:t + 128, :], in_=dttm[:, ti, :]), "stX", reads=[B_dttm])

            zt = [pool.sb(f"zt{tg}_{i}", [128, 512], BF16) for i in range(2)]
            B_zt = [Buf(), Buf()]
            zc = 0
            for cb4 in range(4):
                wtile, wb = load_w(V_END + cb4 * 512, 512)
                for ti in range(4):
                    bk, bb = tm_proj(wtile, wb, 0, 512, lo + ti * 128)
                    s = zc % 2
                    zc += 1
                    p.op("act", lambda e, s=s, bk=bk: e.activation(out=zt[s][:], in_=bk[:], func=AF.Silu), reads=[bb], writes=[B_zt[s]])
                    t = tok0 + ti * 128
                    p.dma("pool", lambda e, s=s, t=t, cb4=cb4: e.dma_start(out=zs_d[t:t + 128, cb4 * 512:(cb4 + 1) * 512], in_=zt[s][:]),
                          f"stz{s}", reads=[B_zt[s]])
            gt = [pool.sb(f"gt{tg}_{i}", [128, 512], BF16) for i in range(2)]
            B_gt = [Buf(), Buf()]
            for cb4 in range(8):
                wtile, wb = load_w(DT_END + cb4 * 512, 512)
                for ci in range(4):
                    cc = cb4 * 4 + ci
                    bk, bb = fm_proj(wtile, wb, ci * 128, 128, lo, NT)
                    s = cc % 2
                    p.op("act", lambda e, s=s, bk=bk: e.activation(out=gt[s][:], in_=bk[:], func=AF.Sigmoid), reads=[bb], writes=[B_gt[s]])
                    p.dma("pool", lambda e, s=s, cc=cc: e.dma_start(out=gT_d[cc, :, tok0:tok0 + NT], in_=gt[s][:]), f"stg{s}", reads=[B_gt[s]])
            qT = pool.sb(f"qT{tg}", [64, 16, 512], BF16)
            kT = pool.sb(f"kT{tg}", [64, 4, 768], BF16)
            vt = pool.sb(f"vt{tg}", [128, 6, 256], BF16)
            B_qT, B_kT, B_vt = Buf(), Buf(), Buf()
            for cb4 in range(2):
                wtile, wb = load_w(cb4 * 512, 512)
                for hh in range(8):
                    h = cb4 * 8 + hh
                    bk, bb = fm_proj(wtile, wb, hh * 64, 64, lo, NT)
                    p.op("act", lambda e, h=h, bk=bk: e.activation(out=qT[:, h, :], in_=bk[0:64, :], func=AF.Copy, scale=0.125),
                         reads=[bb], writes=[B_qT])
            wtile, wb = load_w(Q_END, 512)
            for kv in range(4):
                for hf in range(2):
                    bk, bb = fm_proj(wtile, wb, kv * 64, 64, hf * 384, 384)
                    p.op("dve", lambda e, kv=kv, hf=hf, bk=bk: e.tensor_copy(out=kT[:, kv, hf * 384:(hf + 1) * 384], in_=bk[0:64, 0:384]),
                         reads=[bb], writes=[B_kT])
            for ti in range(6):
                bk, bb = tm_proj(wtile, wb, 256, 256, ti * 128)
                p.op("act", lambda e, ti=ti, bk=bk: e.copy(out=vt[:, ti, :], in_=bk[:, 0:256]), reads=[bb], writes=[B_vt])

            c_ab = pool.sb(f"ab{tg}", [128, 384], F32)
            c_em = pool.sb(f"em{tg}", [128, NG_OWN, 2], F32)
            B_ab = Buf()
            if pool.first:
                p.dma("sp", lambda e: e.dma_start(out=c_ab[:], in_=attn_bias), "c1", writes=[B_ab])
                p.dma("sp", lambda e: e.dma_start(out=c_em[:], in_=emask), "c1", writes=[B_ab])
            sc = [pool.sb(f"sc{tg}_{i}", [128, 4, 384], F32) for i in range(1)] * 2
            pr = [pool.sb(f"pr{tg}_{i}", [128, 4, 384], BF16) for i in range(1)] * 2
            prT = [pool.sb(f"prT{tg}_{i}", [128, 4, 3, 128], BF16) for i in range(1)] * 2
            ast = [pool.sb(f"ast{tg}_{i}", [128, 4, 8], F32) for i in range(2)]
            B_sc, B_pr, B_prT, B_ast = [Buf()] * 2, [Buf()] * 2, [Buf()] * 2, [Buf(), Buf()]
            atm = pool.sb(f"atm{tg}", [128, 1024], BF16)
            B_atm = Buf()
            aTt = pool.sb(f"aTt{tg}", [128, 8, 512], BF16)
            B_aTt = Buf()
            it = 0
            for j in range(4):
                for kv in range(4):
                    s = it % 2
                    it += 1
                    sbanks = []
                    for g in range(4):
                        h = kv * 4 + g
                        bk, bb = nextbank()
                        p.op("pe", lambda e, h=h, kv=kv, j=j, bk=bk: e.matmul(bk[:, 0:384], lhsT=qT[:, h, j * 128:(j + 1) * 128],
                                                                              rhs=kT[:, kv, j * 128:j * 128 + 384], start=True, stop=True),
                             reads=[B_qT, B_kT], writes=[bb])
                        p.op("dve", lambda e, s=s, g=g, h=h, bk=bk: e.scalar_tensor_tensor(out=sc[s][:, g, :], in0=c_ab[:], scalar=float(2.0 ** (-8.0 * (h + 1) / 16)),
                                                                                           in1=bk[:, 0:384], op0=ALU.mult, op1=ALU.add),
                             reads=[bb, B_ab], writes=[B_sc[s]])
                    if j == 0:
                        p.op("dve", lambda e, s=s: e.tensor_scalar(out=sc[s][:, :, 0:128], in0=sc[s][:, :, 0:128], scalar1=c_em[:, gidx, 0:1],
                                                                   scalar2=None, op0=ALU.add), reads=[B_sc[s], B_ab], writes=[B_sc[s]])
                    if j == 3:
                        p.op("dve", lambda e, s=s: e.tensor_scalar(out=sc[s][:, :, 256:384], in0=sc[s][:, :, 256:384], scalar1=c_em[:, gidx, 1:2],
                                                                   scalar2=None, op0=ALU.add), reads=[B_sc[s], B_ab], writes=[B_sc[s]])
                    p.op("dve", lambda e, s=s: e.tensor_reduce(out=ast[s][:, :, 0], in_=sc[s][:], axis=AX.X, op=ALU.max),
                         reads=[B_sc[s]], writes=[B_ast[s]])
                    p.op("dve", lambda e, s=s, kv=kv: e.tensor_tensor(out=ast[s][:, :, 1], in0=ast[s][:, :, 0], in1=c_reps[:, 96 + kv * 4:100 + kv * 4],
                                                                      op=ALU.max), reads=[B_ast[s], B_const], writes=[B_ast[s]])
                    p.op("dve", lambda e, s=s: e.tensor_scalar(out=ast[s][:, :, 2], in0=ast[s][:, :, 1], scalar1=-1.0, scalar2=None, op0=ALU.mult),
                         reads=[B_ast[s]], writes=[B_ast[s]])
                    for g in range(4):
                        p.op("act", lambda e, s=s, g=g: e.activation(out=pr[s][:, g, :], in_=sc[s][:, g, :], func=AF.Exp, bias=ast[s][:, g, 2:3],
                                                                     accum_out=ast[s][:, g, 3:4]), reads=[B_sc[s], B_ast[s]], writes=[B_pr[s], B_ast[s]])
                    p.op("dve", lambda e, s=s, kv=kv: e.tensor_tensor(out=ast[s][:, :, 4], in0=c_reps[:, 96 + kv * 4:100 + kv * 4], in1=ast[s][:, :, 1],
                                                                      op=ALU.subtract), reads=[B_ast[s], B_const], writes=[B_ast[s]])
                    p.op("act", lambda e, s=s: e.activation(out=ast[s][:, :, 5], in_=ast[s][:, :, 4], func=AF.Exp), reads=[B_ast[s]], writes=[B_ast[s]])
                    p.op("dve", lambda e, s=s: e.tensor_tensor(out=ast[s][:, :, 6], in0=ast[s][:, :, 5], in1=ast[s][:, :, 3], op=ALU.add),
                         reads=[B_ast[s]], writes=[B_ast[s]])
                    p.op("dve", lambda e, s=s: e.reciprocal(out=ast[s][:, :, 7], in_=ast[s][:, :, 6]), reads=[B_ast[s]], writes=[B_ast[s]])
                    for g in range(4):
                        bk, bb = nextbank()
                        bkb = bk[:].bitcast(BF16)
                        for m in range(3):
                            p.op("pe", lambda e, s=s, g=g, m=m, bkb=bkb: e.transpose(out=bkb[:, m * 128:(m + 1) * 128], in_=pr[s][:, g, m * 128:(m + 1) * 128],
                                                                                    identity=c_idb[:]), reads=[B_pr[s], B_const], writes=[bb], pe_accum=True)
                        eng = "act" if g % 2 == 0 else "dve"
                        if eng == "act":
                            p.op("act", lambda e, s=s, g=g, bkb=bkb: e.copy(out=prT[s][:, g, :, :], in_=bkb[:, 0:384].rearrange("p (m t) -> p m t", t=128)),
                                 reads=[bb], writes=[B_prT[s]])
                        else:
                            p.op("dve", lambda e, s=s, g=g, bkb=bkb: e.tensor_copy(out=prT[s][:, g, :, :], in_=bkb[:, 0:384].rearrange("p (m t) -> p m t", t=128)),
                                 reads=[bb], writes=[B_prT[s]])
                    bk, bb = nextbank()
                    for g in range(4):
                        for m in range(3):
                            p.op("pe", lambda e, s=s, g=g, m=m, kv=kv, j=j, bk=bk: e.matmul(bk[:, g * 64:(g + 1) * 64], lhsT=prT[s][:, g, m, :],
                                                                                         rhs=vt[:, j + m, kv * 64:(kv + 1) * 64], start=(m == 0), stop=(m == 2)),
                                 reads=[B_prT[s], B_vt], writes=[bb], pe_accum=True)
                    p.op("dve", lambda e, s=s, kv=kv, bk=bk: e.tensor_tensor(
                        out=atm[:, kv * 256:(kv + 1) * 256].rearrange("p (g d) -> p g d", d=64),
                        in0=bk[:, 0:256].rearrange("p (g d) -> p g d", d=64),
                        in1=bc(ast[s][:, :, 7:8], [128, 4, 64]), op=ALU.mult), reads=[bb, B_ast[s]], writes=[B_atm])
                bk, bb = nextbank()
                bkb = bk[:].bitcast(BF16)
                for k in range(8):
                    p.op("pe", lambda e, k=k, bkb=bkb: e.transpose(out=bkb[:, k * 128:(k + 1) * 128], in_=atm[:, k * 128:(k + 1) * 128], identity=c_idb[:]),
                         reads=[B_atm, B_const], writes=[bb], pe_accum=True)
                p.op("act", lambda e, j=j, bkb=bkb: e.copy(out=aTt[:, :, j * 128:(j + 1) * 128], in_=bkb.rearrange("p (k t) -> p k t", t=128)),
                     reads=[bb], writes=[B_aTt])
            for k in range(8):
                p.dma("pool", lambda e, k=k: e.dma_start(out=aT_d[k, :, tok0:tok0 + NT], in_=aTt[:, k, :]), "sta", reads=[B_aTt])
            return res

        if "A" in stages:
            gi = 0
            with contextlib.ExitStack() as stA:
                poolA = TilePool(nc, stA)
                for seg, ng in OWN_GROUPS:
                    for g in range(ng):
                        tok0 = (0 if seg == 0 else 2048) + g * 512
                        prep_group(poolA, x_own[gi], 768, True, gi, tok0)
                        emit_casts(2)
                        gi += 1
                p.barrier()


        ssd_stack = contextlib.ExitStack()
        Hst = SB(ssd_stack, "Hst", [128, 4, D], F32)
        B_H = [Buf() for _ in range(4)]

        def ssd_work(st, tag, two_xs=False):
            W = {}
            for nm, shp, dt in (("dA", [128, 64], F32), ("cumsb", [128, 128], F32), ("tmp", [128, 64], F32), ("dte", [128, 64], F32),
                                ("dec", [128, 64], F32), ("w", [128, 64], F32), ("xs", [128, D], BF16), ("xs2", [128, D], BF16), ("deff", [128, 64], F32),
                                ("tg", [128, 512], F32)):
                if nm == "xs2" and not two_xs:
                    continue
                W[nm] = SB(st, nm + tag, shp, dt)
                W["B_" + nm] = Buf()
            return W

        def chunk_pre(W, dt_ap, B_dt):
            p.op("dve", lambda e: e.tensor_tensor(out=W["dA"][:], in0=dt_ap, in1=c_arep[:], op=ALU.mult), reads=[B_dt, B_const], writes=[W["B_dA"]])
            bk, bb = nextbank()
            p.op("pe", lambda e, bk=bk: e.matmul(bk[:, 0:32], lhsT=TRIF, rhs=W["dA"][:, 0:32], start=True, stop=True), reads=[W["B_dA"], B_const], writes=[bb])
            p.op("pe", lambda e, bk=bk: e.matmul(bk[:, 32:64], lhsT=TRIB, rhs=W["dA"][:, 32:64], start=True, stop=True), reads=[W["B_dA"], B_const], writes=[bb], pe_accum=True)
            p.op("pe", lambda e, bk=bk: e.matmul(bk[:, 64:128], lhsT=ONES, rhs=W["dA"][:, 0:64], start=True, stop=True), reads=[W["B_dA"], B_const], writes=[bb], pe_accum=True)
            p.op("act", lambda e, bk=bk: e.copy(out=W["cumsb"][:], in_=bk[:, 0:128]), reads=[bb], writes=[W["B_cumsb"]])
            p.op("dve", lambda e: e.tensor_tensor(out=W["tmp"][:], in0=W["cumsb"][:, 64:128], in1=W["cumsb"][:, 0:64], op=ALU.subtract),
                 reads=[W["B_cumsb"]], writes=[W["B_tmp"]])
            p.op("act", lambda e: e.activation(out=W["dte"][:], in_=W["tmp"][:], func=AF.Exp), reads=[W["B_tmp"]], writes=[W["B_dte"]])
            p.op("act", lambda e: e.activation(out=W["dec"][:], in_=W["cumsb"][:, 64:128], func=AF.Exp), reads=[W["B_cumsb"]], writes=[W["B_dec"]])
            p.op("dve", lambda e: e.tensor_tensor(out=W["w"][:], in0=dt_ap, in1=W["dte"][:], op=ALU.mult), reads=[B_dt, W["B_dte"]], writes=[W["B_w"]])

        def chunk_states(W, X_ap, B_X, Bt_ap, B_Bt, d, xk="xs"):
            p.op("dve", lambda e: e.tensor_tensor(out=W[xk][:].rearrange("p (h d) -> p h d", d=64), in0=X_ap.rearrange("p (h d) -> p h d", d=64),
                                                  in1=bc(W["w"][:, d * 32:(d + 1) * 32].unsqueeze(2), [128, 32, 64]), op=ALU.mult),
                 reads=[B_X, W["B_w"]], writes=[W["B_" + xk]])
            out = []
            for g in range(4):
                bk, bb = nextbank()
                p.op("pe", lambda e, g=g, bk=bk: e.matmul(bk[:, 0:512], lhsT=Bt_ap[:, g * 128:(g + 1) * 128], rhs=W[xk][:, g * 512:(g + 1) * 512],
                                                         start=True, stop=True), reads=[B_Bt, W["B_" + xk]], writes=[bb])
                out.append((bk, bb))
            return out

        def h_update(W, hi, d, sbanks):
            Hv = Hst[:, hi, :]
            p.op("dve", lambda e: e.tensor_tensor(out=Hv.rearrange("p (h d) -> p h d", d=64), in0=Hv.rearrange("p (h d) -> p h d", d=64),
                                                  in1=bc(W["dec"][:, d * 32:(d + 1) * 32].unsqueeze(2), [128, 32, 64]), op=ALU.mult),
                 reads=[W["B_dec"], B_H[hi]], writes=[B_H[hi]])
            for g, (bk, bb) in enumerate(sbanks):
                p.op("dve", lambda e, g=g, bk=bk: e.tensor_tensor(out=Hst[:, hi, g * 512:(g + 1) * 512], in0=bk[:, 0:512], in1=Hst[:, hi, g * 512:(g + 1) * 512], op=ALU.add),
                     reads=[bb, B_H[hi]], writes=[B_H[hi]])

        if "S" in stages:
            with contextlib.ExitStack() as st0:
                Pb = SB(st0, "Pb", [128, 2, 32], F32)
                wpb = SB(st0, "wpb", [128, 32], F32)
                c_om = SB(st0, "c_om", [128, NG_OTH, 4], F32)
                B_Pb, B_wpb, B_om = Buf(), Buf(), Buf()
                p.dma("sp", lambda e: e.dma_start(out=c_om[:], in_=omask), "c1", writes=[B_om])
                p.op("pool", lambda e: e.memset(Pb[:], 1.0), writes=[B_Pb])
                for hi in range(4):
                    p.op("pool", lambda e, hi=hi: e.memset(Hst[:, hi, :], 0.0), writes=[B_H[hi]])
                poolS = TilePool(nc, st0)
                W_S = [ssd_work(st0, "S0", True), ssd_work(st0, "S1", True)]

                def other_group(gi):
                    seg = 0 if gi < 12 else 1
                    if True:
                        r = prep_group(poolS, x_oth[gi], 516, False, 100 + gi, 0)

                        def xs_scale(W, ti, d, xk):
                            p.op("dve", lambda e: e.tensor_tensor(out=W[xk][:].rearrange("p (h d) -> p h d", d=64), in0=r["Xtm"][:, ti, :].rearrange("p (h d) -> p h d", d=64),
                                                                  in1=bc(W["w"][:, d * 32:(d + 1) * 32].unsqueeze(2), [128, 32, 64]), op=ALU.mult),
                                 reads=[r["B_Xtm"], W["B_w"]], writes=[W["B_" + xk]])

                        def tile_pre(ti, W):
                            chunk_pre(W, r["dttm"][:, ti, :], r["B_dttm"])
                            xs_scale(W, ti, 0, "xs")
                            xs_scale(W, ti, 1, "xs2")

                        def st_mm(W, ti, g, xk):
                            bk, bb = nextbank()
                            p.op("pe", lambda e: e.matmul(bk[:, 0:512], lhsT=r["Btm"][:, ti, g * 128:(g + 1) * 128], rhs=W[xk][:, g * 512:(g + 1) * 512],
                                                          start=True, stop=True), reads=[r["B_Btm"], W["B_" + xk]], writes=[bb])
                            return bk, bb

                        def tile_main(ti, W):
                            hi = seg * 2
                            p.op("dve", lambda e: e.tensor_scalar(out=W["deff"][:, 0:32], in0=W["dec"][:, 0:32], scalar1=c_om[:, gi, 0:1], scalar2=c_om[:, gi, 1:2],
                                                                  op0=ALU.mult, op1=ALU.add), reads=[W["B_dec"], B_om], writes=[W["B_deff"]])
                            Hv = Hst[:, hi, :]
                            p.op("dve", lambda e: e.tensor_tensor(out=Hv.rearrange("p (h d) -> p h d", d=64), in0=Hv.rearrange("p (h d) -> p h d", d=64),
                                                                  in1=bc(W["deff"][:, 0:32].unsqueeze(2), [128, 32, 64]), op=ALU.mult),
                                 reads=[W["B_deff"], B_H[hi]], writes=[B_H[hi]])
                            for g in range(4):
                                bk, bb = st_mm(W, ti, g, "xs")
                                p.op("dve", lambda e, g=g, bk=bk, hi=hi: e.scalar_tensor_tensor(
                                    out=Hst[:, hi, g * 512:(g + 1) * 512], in0=bk[:, 0:512], scalar=c_om[:, gi, 0:1], in1=Hst[:, hi, g * 512:(g + 1) * 512],
                                    op0=ALU.mult, op1=ALU.add), reads=[bb, B_H[hi], B_om], writes=[B_H[hi]])
                            hi = seg * 2 + 1
                            p.op("dve", lambda e: e.tensor_scalar(out=wpb[:], in0=Pb[:, seg, :], scalar1=c_om[:, gi, 2:3], scalar2=None, op0=ALU.mult),
                                 reads=[B_Pb, B_om], writes=[B_wpb])
                            for g in range(4):
                                bk, bb = st_mm(W, ti, g, "xs2")
                                p.op("dve", lambda e, g=g, bk=bk: e.tensor_tensor(out=W["tg"][:].rearrange("p (h d) -> p h d", d=64),
                                                                                 in0=bk[:, 0:512].rearrange("p (h d) -> p h d", d=64),
                                                                                 in1=bc(wpb[:, g * 8:(g + 1) * 8].unsqueeze(2), [128, 8, 64]), op=ALU.mult),
                                     reads=[bb, B_wpb], writes=[W["B_tg"]])
                                p.op("dve", lambda e, g=g, hi=hi: e.tensor_tensor(out=Hst[:, hi, g * 512:(g + 1) * 512], in0=Hst[:, hi, g * 512:(g + 1) * 512],
                                                                                   in1=W["tg"][:], op=ALU.add), reads=[W["B_tg"], B_H[hi]], writes=[B_H[hi]])
                            p.op("dve", lambda e: e.tensor_scalar(out=W["deff"][:, 32:64], in0=W["dec"][:, 32:64], scalar1=c_om[:, gi, 2:3], scalar2=c_om[:, gi, 3:4],
                                                                  op0=ALU.mult, op1=ALU.add), reads=[W["B_dec"], B_om], writes=[W["B_deff"]])
                            p.op("dve", lambda e: e.tensor_tensor(out=Pb[:, seg, :], in0=Pb[:, seg, :], in1=W["deff"][:, 32:64], op=ALU.mult),
                                 reads=[W["B_deff"], B_Pb, B_wpb], writes=[B_Pb])

                        tile_pre(0, W_S[0])
                        for ti in range(4):
                            if ti + 1 < 4:
                                tile_pre(ti + 1, W_S[(ti + 1) % 2])
                            tile_main(ti, W_S[ti % 2])
                for gi in range(NG_OTH):
                    other_group(gi)
                    emit_casts(2)
                emit_casts(1000)
                if "hin_d" in dbg:
                    for hi in range(4):
                        p.dma("pool", lambda e, hi=hi: e.dma_start(out=hin_d[hi], in_=Hst[:, hi, :]), "sth", reads=[B_H[hi]])
                p.barrier()

        SEGS = [(0, 0, 16), (1, 2048, 8)]
        def stage_b1():
            with contextlib.ExitStack() as st:
                W2 = [ssd_work(st, "B1a"), ssd_work(st, "B1b")]
                Xc = [SB(st, f"b1X{i}", [128, D], BF16) for i in range(2)]
                Bc = [SB(st, f"b1B{i}", [128, 512], BF16) for i in range(2)]
                dc = [SB(st, f"b1d{i}", [128, 64], F32) for i in range(2)]
                hbt = [SB(st, f"b1h{i}", [128, D], BF16) for i in range(2)]
                B_Xc, B_Bc, B_dc, B_hbt = [Buf(), Buf()], [Buf(), Buf()], [Buf(), Buf()], [Buf(), Buf()]
                it = 0
                cbase = 0
                for seg, tok0, nch in SEGS:
                    hi = seg * 2 + 1
                    for c in range(nch - 1, -1, -1):
                        s = it % 2
                        W = W2[s]
                        it += 1
                        t = tok0 + c * 128
                        p.dma("sp", lambda e, s=s, t=t: e.dma_start(out=Xc[s][:], in_=Xs_d[t:t + 128, :]), f"b1l{s}", writes=[B_Xc[s]])
                        p.dma("sp", lambda e, s=s, t=t: e.dma_start(out=Bc[s][:], in_=Bs_d[t:t + 128, :]), f"b1l{s}", writes=[B_Bc[s]])
                        p.dma("sp", lambda e, s=s, t=t: e.dma_start(out=dc[s][:], in_=dt_d[t:t + 128, :]), f"b1l{s}", writes=[B_dc[s]])
                        p.op("act", lambda e, s=s, hi=hi: e.copy(out=hbt[s][:], in_=Hst[:, hi, :]), reads=[B_H[hi]], writes=[B_hbt[s]])
                        p.dma("pool", lambda e, s=s, cc=cbase + c: e.dma_start(out=hb_d[cc], in_=hbt[s][:]), f"b1s{s}", reads=[B_hbt[s]])
                        chunk_pre(W, dc[s][:], B_dc[s])
                        sb_ = chunk_states(W, Xc[s][:], B_Xc[s], Bc[s][:], B_Bc[s], 1)
                        h_update(W, hi, 1, sb_)
                    cbase += nch
                p.barrier()

        def stage_b2():
            with contextlib.ExitStack() as st:
                W2b = [ssd_work(st, "B2a"), ssd_work(st, "B2b")]
                Xc2 = [SB(st, f"b2X{i}", [128, D], BF16) for i in range(2)]
                Bc2 = [SB(st, f"b2B{i}", [128, 512], BF16) for i in range(2)]
                dc2 = [SB(st, f"b2d{i}", [128, 64], F32) for i in range(2)]
                zc2 = [SB(st, f"b2z{i}", [128, D], BF16) for i in range(2)]
                BTc2 = [SB(st, f"b2BT{i}", [128, 4, 128], BF16) for i in range(2)]
                CTc2 = [SB(st, f"b2CT{i}", [128, 4, 128], BF16) for i in range(2)]
                hbt2 = [SB(st, f"b2hb{i}", [128, D], BF16) for i in range(2)]
                hft = SB(st, "b2hf", [128, D], BF16)
                B_ld2, B_hbt2, B_hft = [Buf(), Buf()], [Buf(), Buf()], Buf()
                xdt = SB(st, "b2xdt", [128, 2, D], BF16)
                cumT = SB(st, "b2cumT", [32, 2, 128], F32)
                ecum = SB(st, "b2ecum", [128, 64], F32)
                ncum = SB(st, "b2ncum", [128, 64], F32)
                GTm = SB(st, "b2GTm", [128, 2, 4, 128], BF16)
                LT = SB(st, "b2LT", [128, 8, 128], BF16)
                MT = SB(st, "b2MT", [128, 8, 128], BF16)
                yv = SB(st, "b2y", [128, D], F32)
                t1 = SB(st, "b2t1", [128, 512], F32)
                gst = SB(st, "b2gst", [128, 4, 4], F32)
                ssm = SB(st, "b2ssm", [128, D], BF16)
                c_neg = SB(st, "b2neg", [128, 2, 512], BF16)
                c_gn = SB(st, "b2gn", [128, D], F32)
                ssmT = SB(st, "b2ssmT", [128, 16, 512], BF16)
                aTl = SB(st, "b2aT", [128, 8, 512], BF16)
                mT = SB(st, "b2mT", [128, 16, 512], BF16)
                B_xdt, B_cumT, B_ecum, B_GTm, B_LT, B_MT, B_y, B_t1, B_gst, B_ssm, B_c2, B_ssmT, B_aTl, B_mT = [Buf() for _ in range(14)]
                p.dma("sp", lambda e: e.dma_start(out=c_neg[:], in_=negm), "c1", writes=[B_c2])
                p.dma("sp", lambda e: e.dma_start(out=c_gn[:], in_=rep_d[:, 3, :]), "c1", writes=[B_c2])
                wo1 = [SB(st, f"b2wa{i}", [128, 8, 128], BF16) for i in range(2)]
                wo2 = [SB(st, f"b2ws{i}", [128, 16, 128], BF16) for i in range(2)]
                gl = [SB(st, f"b2gl{i}", [128, 2, 512], BF16) for i in range(2)]
                B_wo, B_gl = [Buf(), Buf()], [Buf(), Buf()]
                wo3 = SB(st, "b2wo", [128, 16, 512], BF16)
                B_wo3 = Buf()
                xr = SB(st, "b2xr", [128, 512], F32)
                B_xr = Buf()
                cbase = 0
                for seg, tok0, nch in SEGS:
                    hif, hib = seg * 2, seg * 2 + 1

                    def do_chunk(c, W, Xc, Bc, dc, zc, BTc, CTc, hbt, B_ld, B_hbt, seg=seg, tok0=tok0, hif=hif, hib=hib, cbase=cbase):
                        t = tok0 + c * 128
                        for dst, src in ((Xc[:], Xs_d[t:t + 128, :]), (Bc[:], Bs_d[t:t + 128, :]), (dc[:], dt_d[t:t + 128, :]), (zc[:], zs_d[t:t + 128, :]),
                                         (BTc[:], BT_d[:, :, t:t + 128].rearrange("g p t -> p g t")), (CTc[:], CT_d[:, :, t:t + 128].rearrange("g p t -> p g t"))):
                            p.dma("sp", lambda e, dst=dst, src=src: e.dma_start(out=dst, in_=src), f"b2l{(cbase + c) % 2}", writes=[B_ld])
                        p.dma("sp", lambda e, cc=cbase + c: e.dma_start(out=hbt[:], in_=hb_d[cc]), f"b2l{(cbase + c) % 2}", writes=[B_hbt])
                        p.op("act", lambda e, hif=hif: e.copy(out=hft[:], in_=Hst[:, hif, :]), reads=[B_H[hif]], writes=[B_hft])
                        chunk_pre(W, dc[:], B_ld)
                        bk, bb = nextbank()
                        p.op("pe", lambda e, bk=bk: e.matmul(bk[0:32, 0:128], lhsT=W["dA"][:, 0:32], rhs=TRIF, start=True, stop=True), reads=[W["B_dA"], B_const], writes=[bb])
                        p.op("pe", lambda e, bk=bk: e.matmul(bk[0:32, 128:256], lhsT=W["dA"][:, 32:64], rhs=TRIB, start=True, stop=True), reads=[W["B_dA"], B_const], writes=[bb], pe_accum=True)
                        p.op("act", lambda e, bk=bk: e.copy(out=cumT[:], in_=bk[0:32, 0:256].rearrange("p (d t) -> p d t", t=128)), reads=[bb], writes=[B_cumT])
                        p.op("act", lambda e: e.activation(out=ecum[:], in_=W["cumsb"][:, 0:64], func=AF.Exp), reads=[W["B_cumsb"]], writes=[B_ecum])
                        p.op("dve", lambda e: e.tensor_scalar(out=ncum[:], in0=W["cumsb"][:, 0:64], scalar1=-1.0, scalar2=None, op0=ALU.mult), reads=[W["B_cumsb"]], writes=[B_ecum])
                        for d in range(2):
                            p.op("dve" if d == 0 else "pool", lambda e, d=d: e.tensor_tensor(
                                out=xdt[:, d, :].rearrange("p (h d) -> p h d", d=64), in0=Xc[:].rearrange("p (h d) -> p h d", d=64),
                                in1=bc(dc[:, d * 32:(d + 1) * 32].unsqueeze(2), [128, 32, 64]), op=ALU.mult), reads=[B_ld], writes=[B_xdt])
                        bk, bb = nextbank()
                        for g in range(4):
                            p.op("pe", lambda e, g=g, bk=bk: e.matmul(bk[:, g * 128:(g + 1) * 128], lhsT=BTc[:, g, :], rhs=CTc[:, g, :], start=True, stop=True),
                                 reads=[B_ld], writes=[bb], pe_accum=True)
                        for d in range(2):
                            tri = TRIF if d == 0 else TRIB
                            p.op("dve", lambda e, d=d, tri=tri, bk=bk: e.tensor_tensor(out=GTm[:, d, :, :], in0=bk[:, 0:512].rearrange("p (g t) -> p g t", t=128),
                                                                                      in1=bc(tri.unsqueeze(1), [128, 4, 128]), op=ALU.mult),
                                 reads=[bb, B_const], writes=[B_GTm])
                        first = True
                        for d in range(2):
                            hsrc, B_hs = (hft, B_hft) if d == 0 else (hbt, B_hbt)
                            for g in range(4):
                                for half in range(2):
                                    bk, bb = nextbank()
                                    p.op("pe", lambda e, d=d, bk=bk: e.matmul(bk[:, 0:512], lhsT=c_idb[:], rhs=c_neg[:, d, :], start=True, stop=False),
                                         reads=[B_c2, B_const], writes=[bb])
                                    for j in range(4):
                                        h = g * 8 + half * 4 + j
                                        p.op("pe", lambda e, d=d, j=j, h=h, bk=bk: e.matmul(bk[:, j * 128:(j + 1) * 128], lhsT=bc(c_cst[0:32, 0, h:h + 1], [32, 128]),
                                                                                             rhs=cumT[:, d, :], start=False, stop=(j == 3)),
                                             reads=[B_cumT, B_const], writes=[bb], pe_accum=True)
                                    for j in range(4):
                                        h = g * 8 + half * 4 + j
                                        p.op("act", lambda e, d=d, j=j, h=h, half=half, bk=bk: e.activation(out=LT[:, half * 4 + j, :], in_=bk[:, j * 128:(j + 1) * 128], func=AF.Exp,
                                                                                                        bias=ncum[:, d * 32 + h:d * 32 + h + 1]),
                                             reads=[bb, B_ecum], writes=[B_LT])
                                p.op("dve", lambda e, d=d, g=g: e.tensor_tensor(out=MT[:], in0=LT[:], in1=bc(GTm[:, d, g, :].unsqueeze(1), [128, 8, 128]), op=ALU.mult),
                                     reads=[B_LT, B_GTm], writes=[B_MT])
                                bkd, bbd = nextbank()
                                for j in range(8):
                                    h = g * 8 + j
                                    p.op("pe", lambda e, d=d, j=j, h=h, bkd=bkd: e.matmul(bkd[:, j * 64:(j + 1) * 64], lhsT=MT[:, j, :], rhs=xdt[:, d, h * 64:(h + 1) * 64],
                                                                                         start=True, stop=True), reads=[B_MT, B_xdt], writes=[bbd], pe_accum=True)
                                bko, bbo = nextbank()
                                p.op("pe", lambda e, g=g, bko=bko, hsrc=hsrc: e.matmul(bko[:, 0:512], lhsT=CTc[:, g, :], rhs=hsrc[:, g * 512:(g + 1) * 512], start=True, stop=True),
                                     reads=[B_ld, B_hs], writes=[bbo])
                                p.op("dve", lambda e, d=d, g=g, bko=bko: e.tensor_tensor(out=t1[:].rearrange("p (h d) -> p h d", d=64),
                                                                                        in0=bko[:, 0:512].rearrange("p (h d) -> p h d", d=64),
                                                                                        in1=bc(ecum[:, d * 32 + g * 8:d * 32 + g * 8 + 8].unsqueeze(2), [128, 8, 64]), op=ALU.mult),
                                     reads=[bbo, B_ecum], writes=[B_t1])
                                if d == 0:
                                    p.op("dve", lambda e, g=g, bkd=bkd: e.tensor_tensor(out=yv[:, g * 512:(g + 1) * 512], in0=bkd[:, 0:512], in1=t1[:], op=ALU.add),
                                         reads=[bbd, B_t1], writes=[B_y])
                                else:
                                    p.op("dve", lambda e, g=g, bkd=bkd: e.tensor_tensor(out=t1[:], in0=bkd[:, 0:512], in1=t1[:], op=ALU.add),
                                         reads=[bbd, B_t1], writes=[B_t1])
                                    p.op("pool", lambda e, g=g: e.tensor_tensor(out=yv[:, g * 512:(g + 1) * 512], in0=yv[:, g * 512:(g + 1) * 512], in1=t1[:], op=ALU.add),
                                         reads=[B_t1, B_y], writes=[B_y])
                        p.op("dve", lambda e: e.tensor_tensor(out=xdt[:, 0, :].rearrange("p (h d) -> p h d", d=64), in0=Xc[:].rearrange("p (h d) -> p h d", d=64),
                                                              in1=bc(c_reps[:, 64:96].unsqueeze(2), [128, 32, 64]), op=ALU.mult),
                             reads=[B_ld, B_const, B_xdt], writes=[B_xdt])
                        p.op("dve", lambda e: e.tensor_tensor(out=yv[:], in0=yv[:], in1=xdt[:, 0, :], op=ALU.add), reads=[B_xdt, B_y], writes=[B_y])
                        p.op("dve", lambda e: e.tensor_tensor(out=yv[:], in0=yv[:], in1=zc[:], op=ALU.mult), reads=[B_ld, B_y], writes=[B_y])
                        for g in range(4):
                            p.op("act", lambda e, g=g: e.activation(out=ssm[:, g * 512:(g + 1) * 512], in_=yv[:, g * 512:(g + 1) * 512], func=AF.Square, accum_out=gst[:, g, 0:1]),
                                 reads=[B_y], writes=[B_ssm, B_gst])
                        p.op("act", lambda e: e.activation(out=gst[:, :, 1], in_=gst[:, :, 0], func=AF.Sqrt, scale=1.0 / 512, bias=EPS), reads=[B_gst], writes=[B_gst])
                        p.op("dve", lambda e: e.reciprocal(out=gst[:, :, 2], in_=gst[:, :, 1]), reads=[B_gst], writes=[B_gst])
                        p.op("dve", lambda e: e.tensor_tensor(out=yv[:].rearrange("p (g d) -> p g d", d=512), in0=yv[:].rearrange("p (g d) -> p g d", d=512),
                                                              in1=bc(gst[:, :, 2:3], [128, 4, 512]), op=ALU.mult), reads=[B_gst, B_y], writes=[B_y])
                        p.op("dve", lambda e: e.tensor_tensor(out=ssm[:], in0=yv[:], in1=c_gn[:], op=ALU.mult), reads=[B_y, B_c2, B_ssm], writes=[B_ssm])
                        ci = c % 4
                        for half in range(2):
                            bk, bb = nextbank()
                            bkb = bk[:].bitcast(BF16)
                            for kk in range(8):
                                k = half * 8 + kk
                                p.op("pe", lambda e, k=k, kk=kk, bkb=bkb: e.transpose(out=bkb[:, kk * 128:(kk + 1) * 128], in_=ssm[:, k * 128:(k + 1) * 128], identity=c_idb[:]),
                                     reads=[B_ssm, B_const], writes=[bb], pe_accum=True)
                            p.op("act", lambda e, half=half, ci=ci, bkb=bkb: e.copy(out=ssmT[:, half * 8:half * 8 + 8, ci * 128:(ci + 1) * 128],
                                                                                    in_=bkb.rearrange("p (k t) -> p k t", t=128)), reads=[bb], writes=[B_ssmT])
                        sb_ = chunk_states(W, Xc[:], B_ld, Bc[:], B_ld, 0)
                        h_update(W, hif, 0, sb_)
                        if ci == 3:
                            g0 = t - 384
                            p.dma("sp", lambda e, g0=g0: e.dma_start(out=aTl[:], in_=aT_d[:, :, g0:g0 + 512].rearrange("k p t -> p k t")), "b2a", writes=[B_aTl])
                            for cc in range(16):
                                s = cc % 2
                                p.dma("sp", lambda e, s=s, cc=cc: e.dma_start(out=wo1[s][:], in_=w_ao_b[:, cc * 128:(cc + 1) * 128].rearrange("(k p) c -> p k c", p=128)),
                                      f"b2w{s}", reads=[B_w], writes=[B_wo[s]])
                                p.dma("sp", lambda e, s=s, cc=cc: e.dma_start(out=wo2[s][:], in_=w_so_b[:, cc * 128:(cc + 1) * 128].rearrange("(k p) c -> p k c", p=128)),
                                      f"b2w{s}", reads=[B_w], writes=[B_wo[s]])
                                p.dma("sp", lambda e, s=s, cc=cc, g0=g0: e.dma_start(out=gl[s][:, 0, :], in_=gT_d[cc, :, g0:g0 + 512]), f"b2g{s}", writes=[B_gl[s]])
                                p.dma("sp", lambda e, s=s, cc=cc, g0=g0: e.dma_start(out=gl[s][:, 1, :], in_=gT_d[16 + cc, :, g0:g0 + 512]), f"b2g{s}", writes=[B_gl[s]])
                                bka, bba = nextbank()
                                for k in range(8):
                                    p.op("pe", lambda e, s=s, k=k, bka=bka: e.matmul(bka[:, 0:512], lhsT=wo1[s][:, k, :], rhs=aTl[:, k, :], start=(k == 0), stop=(k == 7)),
                                         reads=[B_wo[s], B_aTl], writes=[bba], pe_accum=True)
                                bks, bbs = nextbank()
                                for k in range(16):
                                    p.op("pe", lambda e, s=s, k=k, bks=bks: e.matmul(bks[:, 0:512], lhsT=wo2[s][:, k, :], rhs=ssmT[:, k, :], start=(k == 0), stop=(k == 15)),
                                         reads=[B_wo[s], B_ssmT], writes=[bbs], pe_accum=True)
                                p.op("dve", lambda e, s=s, bka=bka: e.tensor_tensor(out=t1[:], in0=bka[:, 0:512], in1=gl[s][:, 0, :], op=ALU.mult),
                                     reads=[bba, B_gl[s], B_t1], writes=[B_t1])
                                p.op("dve", lambda e, s=s, bks=bks: e.tensor_tensor(out=yv[:, 0:512], in0=bks[:, 0:512], in1=gl[s][:, 1, :], op=ALU.mult),
                                     reads=[bbs, B_gl[s], B_y], writes=[B_y])
                                p.op("dve", lambda e, cc=cc: e.tensor_tensor(out=mT[:, cc, :], in0=t1[:], in1=yv[:, 0:512], op=ALU.add),
                                     reads=[B_t1, B_y], writes=[B_mT])
                            for cb in range(4):
                                p.dma("sp", lambda e, cb=cb: e.dma_start(out=wo3[:], in_=w_out_b[:, cb * 512:(cb + 1) * 512].rearrange("(k p) c -> p k c", p=128)),
                                      "b2w3", reads=[B_w], writes=[B_wo3])
                                for ti in range(4):
                                    tt = g0 + ti * 128
                                    p.dma("sp", lambda e, tt=tt, cb=cb: e.dma_start(out=xr[:], in_=x_res[tt:tt + 128, cb * 512:(cb + 1) * 512]), "b2x", writes=[B_xr])
                                    bk, bb = nextbank()
                                    for k in range(16):
                                        p.op("pe", lambda e, k=k, ti=ti, bk=bk: e.matmul(bk[:, 0:512], lhsT=mT[:, k, ti * 128:(ti + 1) * 128], rhs=wo3[:, k, :],
                                                                                          start=(k == 0), stop=(k == 15)), reads=[B_mT, B_wo3], writes=[bb], pe_accum=True)
                                    p.op("dve", lambda e, bk=bk: e.tensor_tensor(out=xr[:], in0=bk[:, 0:512], in1=xr[:], op=ALU.add), reads=[bb, B_xr], writes=[B_xr])
                                    p.dma("pool", lambda e, tt=tt, cb=cb: e.dma_start(out=x1_d[tt:tt + 128, cb * 512:(cb + 1) * 512], in_=xr[:]), "b2xs", reads=[B_xr])
                    for c in range(nch):
                        s2 = (cbase + c) % 2
                        do_chunk(c, W2b[s2], Xc2[s2], Bc2[s2], dc2[s2], zc2[s2], BTc2[s2], CTc2[s2], hbt2[s2], B_ld2[s2], B_hbt2[s2])
                    cbase += nch
                p.barrier()
        if "B" in stages:
            stage_b1()
            stage_b2()
        p.barrier()
        ssd_stack.close()

        def stage_c():
            with contextlib.ExitStack() as st:
                keysT = SB(st, "keysT", [128, 16, 128], BF16)
                iob = SB(st, "iob", [128, 128], BF16)
                c_gf = SB(st, "c_gf", [128, 1, D], F32)
                B_kT, B_cc, B_gf = Buf(), Buf(), Buf()
                p.op("dve", lambda e: e.tensor_copy(out=iob[:], in_=IOTA), reads=[B_const], writes=[B_cc])
                with contextlib.ExitStack() as st2:
                    kf = SB(st2, "kf", [128, 16, 128], F32)
                    kb = SB(st2, "kb", [128, 16, 128], BF16)
                    B_kf = Buf()
                    p.dma("sp", lambda e: e.dma_start(out=kf[:], in_=keys.rearrange("a n d -> n a d")), "c1", writes=[B_kf])
                    p.op("dve", lambda e: e.tensor_copy(out=kb[:], in_=kf[:]), reads=[B_kf], writes=[B_kf])
                    for half in range(2):
                        bk, bb = nextbank()
                        bkb = bk[:].bitcast(BF16)
                        for kk in range(8):
                            p.op("pe", lambda e, a=half * 8 + kk, kk=kk, bkb=bkb: e.transpose(out=bkb[:, kk * 128:(kk + 1) * 128], in_=kb[:, a, :], identity=c_idb[:]),
                                 reads=[B_kf, B_const], writes=[bb], pe_accum=True)
                        p.op("act", lambda e, half=half, bkb=bkb: e.copy(out=keysT[:, half * 8:half * 8 + 8, :], in_=bkb.rearrange("p (k t) -> p k t", t=128)),
                             reads=[bb], writes=[B_kT])
                    ul = [SB(st2, f"ul{i}", [128, D], BF16) for i in range(2)]
                    ut = [SB(st2, f"ut{i}", [128, D], BF16) for i in range(2)]
                    B_ul, B_ut = [Buf(), Buf()], [Buf(), Buf()]
                    for c in range(128):
                        s_ = c % 2
                        p.dma("sp", lambda e, s_=s_, c=c: e.dma_start(out=ul[s_][:], in_=u_b[c * 128:(c + 1) * 128, :]), f"ul{s_}", reads=[B_wuv], writes=[B_ul[s_]])
                        for half in range(2):
                            bk, bb = nextbank()
                            bkb = bk[:].bitcast(BF16)
                            for kk in range(8):
                                k = half * 8 + kk
                                p.op("pe", lambda e, s_=s_, k=k, kk=kk, bkb=bkb: e.transpose(out=bkb[:, kk * 128:(kk + 1) * 128], in_=ul[s_][:, k * 128:(k + 1) * 128], identity=c_idb[:]),
                                     reads=[B_ul[s_], B_const], writes=[bb], pe_accum=True)
                            if half == 0:
                                p.op("act", lambda e, s_=s_, bkb=bkb: e.copy(out=ut[s_][:, 0:1024], in_=bkb), reads=[bb], writes=[B_ut[s_]])
                            else:
                                p.op("dve", lambda e, s_=s_, bkb=bkb: e.tensor_copy(out=ut[s_][:, 1024:2048], in_=bkb), reads=[bb], writes=[B_ut[s_]])
                        p.dma("pool", lambda e, s_=s_, c=c: e.dma_start(out=ut_b[c], in_=ut[s_][:]), f"us{s_}", reads=[B_ut[s_]])
                    p.barrier()

                x1t = SB(st, "x1t", [128, 1, D], F32)
                xn = SB(st, "cxn", [128, D], BF16)
                cst_ = SB(st, "cst_", [128, 2, 4], F32)
                cst2 = SB(st, "cst2", [128, 2, 4], F32)
                xnT2 = [SB(st, f"xnT{i}", [128, 16, 256], BF16) for i in range(2)]
                qT = SB(st, "cqT", [128, 16, 256], BF16)
                wq = [SB(st, f"wq{i}", [128, 16, 128], BF16) for i in range(2)]
                P2g = SB(st, "P2g", [128, 32, 128], BF16)
                OH1 = SB(st, "OH1", [128, 32, 128], BF16)
                scr = P2g[:].bitcast(F32).rearrange("p a b -> p (a b)").rearrange("p (h n) -> p h n", n=128)
                eq = OH1[:].bitcast(F32).rearrange("p a b -> p (a b)").rearrange("p (h k j) -> p h k j", k=16, j=16)
                wk = SB(st, "cwk", [128, 256], F32)
                topv2 = [SB(st, f"topv{i}", [128, 16, 16], F32) for i in range(2)]
                idxu2 = [SB(st, f"idxu{i}", [128, 16, 16], U32) for i in range(2)]
                idxf = SB(st, "idxf", [128, 16, 16], F32)
                cand = SB(st, "cand", [128, 8, 16, 16], F32)
                best = SB(st, "best", [128, 8, 16], F32)
                posu = SB(st, "posu", [128, 8, 16], U32)
                ku = SB(st, "ku", [128, 2, 8, 16], U32)
                kf_ = SB(st, "kf_", [128, 2, 8, 16], F32)
                gat = SB(st, "gat", [128, 8, 16], F32)
                gz = SB(st, "gz", [128, 8, 2], F32)
                I12_2 = [SB(st, f"I12_{i}", [128, 3, 128], F32) for i in range(2)]
                I12T = SB(st, "I12T", [128, 3, 128], BF16)
                Gs = SB(st, "Gs", [128, 128, 256], BF16)
                NSL = 4
                strm = [SB(st, f"strm{i}", [128, 2, D], BF16) for i in range(NSL)]
                ge = [SB(st, f"ge{i}", [128, 256], BF16) for i in range(2)]
                (B_x1t, B_xn, B_cst, B_cst2, B_qT, B_wk, B_idxf, B_cand, B_best, B_posu, B_ku, B_kf2, B_gat, B_gz,
                 B_I12T, B_OH1, B_P2g, B_Gs) = [Buf() for _ in range(18)]
                B_xnT2, B_topv2, B_idxu2, B_I12_2 = [Buf(), Buf()], [Buf(), Buf()], [Buf(), Buf()], [Buf(), Buf()]
                B_wq, B_ge = [Buf(), Buf()], [Buf(), Buf()]
                B_strm = [Buf() for _ in range(NSL)]
                B_scr, B_eq = B_P2g, B_OH1
                B_P2ga, B_P2gb = Buf(), Buf()
                sctr = [0]

                def stage1_hc(ti, hc):
                    topv, idxu, B_topv, B_idxu = topv2[ti], idxu2[ti], B_topv2[ti], B_idxu2[ti]
                    p.op("dve", lambda e: e.max(out=topv[:, hc, 0:8], in_=scr[:, hc, :]), reads=[B_scr], writes=[B_topv])
                    p.op("dve", lambda e: e.match_replace(out=wk[:, 0:128], in_to_replace=topv[:, hc, 0:8], in_values=scr[:, hc, :], imm_value=-1e30),
                         reads=[B_scr, B_topv], writes=[B_wk])
                    p.op("dve", lambda e: e.max(out=topv[:, hc, 8:16], in_=wk[:, 0:128]), reads=[B_wk], writes=[B_topv])
                    p.op("dve", lambda e: e.max_index(out=idxu[:, hc, 0:8], in_max=topv[:, hc, 0:8], in_values=scr[:, hc, :]), reads=[B_scr, B_topv], writes=[B_idxu])
                    p.op("dve", lambda e: e.max_index(out=idxu[:, hc, 8:16], in_max=topv[:, hc, 8:16], in_values=scr[:, hc, :]), reads=[B_scr, B_topv], writes=[B_idxu])

                def scores_q(ti, qd, xsl):
                    bk, bb = nextbank()
                    for j in range(4):
                        hc = qd * 4 + j
                        p.op("pe", lambda e, hc=hc, j=j: e.matmul(bk[:, j * 128:(j + 1) * 128], lhsT=qT[:, hc, ti * 128:(ti + 1) * 128], rhs=keysT[:, hc, :],
                                                                    start=True, stop=True), reads=[B_qT, B_kT], writes=[bb], pe_accum=True)
                    p.op("act", lambda e: e.copy(out=scr[:, qd * 4:qd * 4 + 4, :], in_=bk[:, 0:512].rearrange("p (a n) -> p a n", n=128)),
                         reads=[bb, B_P2ga, B_P2gb], writes=[B_scr])

                def phaseA1(gi):
                    tok0 = gi * 256
                    xnT, B_xnT = xnT2[gi % 2], B_xnT2[gi % 2]
                    p.dma("sp", lambda e: e.dma_start(out=c_gf[:, 0, :], in_=rep_d[:, 1, :]), "cgf", writes=[B_gf])
                    for ti in range(2):
                        t = tok0 + ti * 128
                        p.dma("sp", lambda e, t=t: e.dma_start(out=x1t[:, 0, :], in_=x1_d[t:t + 128, :]), "cx", writes=[B_x1t])
                        p.op("act", lambda e, ti=ti: e.activation(out=xn[:], in_=x1t[:, 0, :], func=AF.Square, accum_out=cst2[:, ti, 0:1]), reads=[B_x1t], writes=[B_xn, B_cst2])
                        p.op("act", lambda e, ti=ti: e.activation(out=cst2[:, ti, 1:2], in_=cst2[:, ti, 0:1], func=AF.Sqrt, scale=1.0 / D, bias=EPS), reads=[B_cst2], writes=[B_cst2])
                        p.op("dve", lambda e, ti=ti: e.reciprocal(out=cst2[:, ti, 2:3], in_=cst2[:, ti, 1:2]), reads=[B_cst2], writes=[B_cst2])
                        p.op("dve", lambda e, ti=ti: e.scalar_tensor_tensor(out=xn[:], in0=x1t[:, 0, :], scalar=cst2[:, ti, 2:3], in1=c_gf[:, 0, :], op0=ALU.mult, op1=ALU.mult),
                             reads=[B_x1t, B_cst2, B_gf, B_xn], writes=[B_xn])
                        yield
                        for half in range(2):
                            bk, bb = nextbank()
                            bkb = bk[:].bitcast(BF16)
                            for kk in range(8):
                                k = half * 8 + kk
                                p.op("pe", lambda e, k=k, kk=kk, bkb=bkb: e.transpose(out=bkb[:, kk * 128:(kk + 1) * 128], in_=xn[:, k * 128:(k + 1) * 128], identity=c_idb[:]),
                                     reads=[B_xn, B_const], writes=[bb], pe_accum=True)
                            p.op("act", lambda e, half=half, ti=ti, bkb=bkb: e.copy(out=xnT[:, half * 8:half * 8 + 8, ti * 128:(ti + 1) * 128], in_=bkb.rearrange("p (k t) -> p k t", t=128)),
                                 reads=[bb], writes=[B_xnT])
                            yield
                    def ld_wq(cc):
                        s_ = cc % 2
                        p.dma("sp", lambda e: e.dma_start(out=wq[s_][:], in_=w_q_b[:, cc * 128:(cc + 1) * 128].rearrange("(k p) c -> p k c", p=128)),
                              f"cwq{s_}", reads=[B_w], writes=[B_wq[s_]])
                    ld_wq(0)
                    for cc in range(16):
                        s_ = cc % 2
                        bk, bb = nextbank()
                        for k in range(16):
                            p.op("pe", lambda e, s_=s_, k=k, bk=bk: e.matmul(bk[:, 0:256], lhsT=wq[s_][:, k, :], rhs=xnT[:, k, :], start=(k == 0), stop=(k == 15)),
                                 reads=[B_wq[s_], B_xnT], writes=[bb], pe_accum=True)
                        p.op("act", lambda e, cc=cc, bk=bk: e.copy(out=qT[:, cc, :], in_=bk[:, 0:256]), reads=[bb], writes=[B_qT])
                        if cc + 1 < 16:
                            ld_wq(cc + 1)
                        yield
                        if cc % 4 != 3:
                            yield
                    for qd in range(4):
                        scores_q(0, qd, None)
                        yield
                    for hc in range(16):
                        stage1_hc(0, hc)
                        yield
                    for qd in range(4):
                        scores_q(1, qd, None)
                        yield

                def phaseA2(gi):
                    for hc in range(16):
                        stage1_hc(1, hc)
                        yield
                    for ti in range(2):
                        topv, idxu, B_topv, B_idxu = topv2[ti], idxu2[ti], B_topv2[ti], B_idxu2[ti]
                        I12, B_I12 = I12_2[ti], B_I12_2[ti]
                        p.op("dve", lambda e, idxu=idxu: e.tensor_copy(out=idxf[:], in_=idxu[:]), reads=[B_idxu], writes=[B_idxf])
                        tv = topv[:].rearrange("p (h c) k -> p h c k", c=2)
                        p.op("dve", lambda e, tv=tv: e.tensor_tensor(out=cand[:], in0=bc(tv[:, :, 0, :].unsqueeze(3), [128, 8, 16, 16]), in1=bc(tv[:, :, 1, :].unsqueeze(2), [128, 8, 16, 16]), op=ALU.add),
                             reads=[B_topv], writes=[B_cand])
                        yield
                        for h in range(8):
                            cv = cand[:, h, :, :].rearrange("p a b -> p (a b)")
                            p.op("dve", lambda e, h=h, cv=cv: e.max(out=best[:, h, 0:8], in_=cv), reads=[B_cand], writes=[B_best])
                            p.op("dve", lambda e, h=h, cv=cv: e.match_replace(out=wk[:], in_to_replace=best[:, h, 0:8], in_values=cv, imm_value=-1e30), reads=[B_cand, B_best], writes=[B_wk])
                            p.op("dve", lambda e, h=h: e.max(out=best[:, h, 8:16], in_=wk[:]), reads=[B_wk], writes=[B_best])
                            p.op("dve", lambda e, h=h, cv=cv: e.max_index(out=posu[:, h, 0:8], in_max=best[:, h, 0:8], in_values=cv), reads=[B_cand, B_best], writes=[B_posu])
                            p.op("dve", lambda e, h=h, cv=cv: e.max_index(out=posu[:, h, 8:16], in_max=best[:, h, 8:16], in_values=cv), reads=[B_cand, B_best], writes=[B_posu])
                            yield
                        p.op("dve", lambda e: e.tensor_tensor(out=gat[:], in0=best[:], in1=bc(best[:, :, 0:1], [128, 8, 16]), op=ALU.subtract), reads=[B_best], writes=[B_gat])
                        p.op("act", lambda e: e.activation(out=gat[:], in_=gat[:], func=AF.Exp), reads=[B_gat], writes=[B_gat])
                        p.op("dve", lambda e: e.tensor_reduce(out=gz[:, :, 0], in_=gat[:], axis=AX.X, op=ALU.add), reads=[B_gat], writes=[B_gz])
                        p.op("dve", lambda e: e.reciprocal(out=gz[:, :, 1], in_=gz[:, :, 0]), reads=[B_gz], writes=[B_gz])
                        p.op("dve", lambda e, I12=I12: e.tensor_tensor(out=I12[:, 2, :].rearrange("p (h k) -> p h k", k=16), in0=gat[:], in1=bc(gz[:, :, 1:2], [128, 8, 16]), op=ALU.mult),
                             reads=[B_gat, B_gz], writes=[B_I12])
                        p.op("dve", lambda e: e.tensor_single_scalar(out=ku[:, 0, :, :], in_=posu[:], scalar=4, op=ALU.logical_shift_right), reads=[B_posu], writes=[B_ku])
                        p.op("dve", lambda e: e.tensor_single_scalar(out=ku[:, 1, :, :], in_=posu[:], scalar=15, op=ALU.bitwise_and), reads=[B_posu], writes=[B_ku])
                        p.op("dve", lambda e: e.tensor_copy(out=kf_[:], in_=ku[:]), reads=[B_ku], writes=[B_kf2])
                        yield
                        iv = idxf[:].rearrange("p (h c) k -> p h c k", c=2)
                        for c_ in range(2):
                            p.op("dve", lambda e, c_=c_: e.tensor_tensor(out=eq, in0=bc(kf_[:, c_, :, :].unsqueeze(3), [128, 8, 16, 16]),
                                                                          in1=bc(IOTA[:, 0:16].unsqueeze(1).unsqueeze(1), [128, 8, 16, 16]), op=ALU.is_equal),
                                 reads=[B_kf2, B_const], writes=[B_eq])
                            p.op("dve", lambda e, c_=c_, iv=iv: e.tensor_tensor(out=eq, in0=eq, in1=bc(iv[:, :, c_, :].unsqueeze(2), [128, 8, 16, 16]), op=ALU.mult),
                                 reads=[B_idxf, B_eq], writes=[B_eq])
                            p.op("dve", lambda e, c_=c_, I12=I12: e.tensor_reduce(out=I12[:, c_, :], in_=eq.rearrange("p h k j -> p (h k) j"), axis=AX.X, op=ALU.add),
                                 reads=[B_eq], writes=[B_I12])
                            yield

                def gbuild(gi):
                    for ti in range(2):
                        I12, B_I12 = I12_2[ti], B_I12_2[ti]
                        bk, bb = nextbank()
                        for w_ in range(3):
                            p.op("pe", lambda e, w_=w_, bk=bk, I12=I12: e.transpose(out=bk[:, w_ * 128:(w_ + 1) * 128], in_=I12[:, w_, :], identity=IDF), reads=[B_I12, B_const], writes=[bb], pe_accum=True)
                        p.op("act", lambda e, bk=bk: e.copy(out=I12T[:], in_=bk[:, 0:384].rearrange("p (w t) -> p w t", t=128)), reads=[bb], writes=[B_I12T])
                        for hf in range(4):
                            tsl = slice(hf * 32, (hf + 1) * 32)
                            p.op("dve", lambda e, tsl=tsl: e.tensor_tensor(out=OH1[:], in0=bc(iob[:].unsqueeze(1), [128, 32, 128]), in1=bc(I12T[:, 0, tsl].unsqueeze(2), [128, 32, 128]), op=ALU.is_equal),
                                 reads=[B_I12T, B_cc], writes=[B_OH1])
                            p.op("dve", lambda e, tsl=tsl: e.tensor_tensor(out=P2g[:], in0=bc(iob[:].unsqueeze(1), [128, 32, 128]), in1=bc(I12T[:, 1, tsl].unsqueeze(2), [128, 32, 128]), op=ALU.is_equal),
                                 reads=[B_I12T, B_cc], writes=[B_P2g, B_P2ga, B_P2gb])
                            p.op("dve", lambda e, hf=hf: e.tensor_tensor(out=P2g[:, 0:20, :], in0=P2g[:, 0:20, :], in1=bc(I12T[:, 2, hf * 32:hf * 32 + 20].unsqueeze(2), [128, 20, 128]), op=ALU.mult),
                                 reads=[B_I12T, B_P2g], writes=[B_P2ga])
                            p.op("pool", lambda e, hf=hf: e.tensor_tensor(out=P2g[:, 20:32, :], in0=P2g[:, 20:32, :], in1=bc(I12T[:, 2, hf * 32 + 20:hf * 32 + 32].unsqueeze(2), [128, 12, 128]), op=ALU.mult),
                                 reads=[B_I12T, B_P2g], writes=[B_P2gb])
                            for q4 in range(8):
                                bk, bb = nextbank()
                                for j in range(4):
                                    tl = q4 * 4 + j
                                    p.op("pe", lambda e, tl=tl, j=j, bk=bk: e.matmul(bk[:, j * 128:(j + 1) * 128], lhsT=P2g[:, tl, :], rhs=OH1[:, tl, :], start=True, stop=True),
                                         reads=[B_P2g, B_P2ga, B_P2gb, B_OH1], writes=[bb], pe_accum=True)
                                tg0 = ti * 128 + hf * 32 + q4 * 4
                                src = bk[:, 0:512].rearrange("p (t i) -> p i t", i=128)
                                p.op("act", lambda e, tg0=tg0, src=src: e.copy(out=Gs[:, :, tg0:tg0 + 4], in_=src), reads=[bb], writes=[B_Gs])

                def drain(gen, n):
                    if gen is None:
                        return None
                    for _ in range(n):
                        try:
                            next(gen)
                        except StopIteration:
                            return None
                    return gen

                def passes_and_epilogue(gi, ga, gb):
                    tok0 = gi * 256
                    xnT, B_xnT = xnT2[gi % 2], B_xnT2[gi % 2]
                    for c2 in range(64):
                        sl = sctr[0] % NSL
                        sctr[0] += 1
                        p.dma("sp", lambda e, sl=sl, c2=c2: e.dma_start(out=strm[sl][:], in_=ut_b[2 * c2:2 * c2 + 2].rearrange("c p f -> p c f")), f"cs{sl}", writes=[B_strm[sl]])
                        for cj in range(2):
                            c = 2 * c2 + cj
                            s_ = c % 2
                            bk, bb = nextbank()
                            for k in range(16):
                                p.op("pe", lambda e, sl=sl, cj=cj, k=k, bk=bk: e.matmul(bk[:, 0:256], lhsT=strm[sl][:, cj, k * 128:(k + 1) * 128], rhs=xnT[:, k, :], start=(k == 0), stop=(k == 15)),
                                     reads=[B_strm[sl], B_xnT], writes=[bb], pe_accum=True)
                            p.op("act", lambda e, s_=s_, bk=bk: e.activation(out=ge[s_][:], in_=bk[:, 0:256], func=AF.Gelu), reads=[bb], writes=[B_ge[s_]])
                            p.op("dve", lambda e, s_=s_, c=c: e.tensor_tensor(out=Gs[:, c, :], in0=Gs[:, c, :], in1=ge[s_][:], op=ALU.mult),
                                 reads=[B_ge[s_], B_Gs], writes=[B_Gs])
                        if ga is not None:
                            ga = drain(ga, 1)
                        else:
                            gb = drain(gb, 1)
                    ga = drain(ga, 10000)
                    for c2 in range(64):
                        sl = sctr[0] % NSL
                        sctr[0] += 1
                        p.dma("sp", lambda e, sl=sl, c2=c2: e.dma_start(out=strm[sl][:], in_=v_b[c2 * 256:(c2 + 1) * 256, :].rearrange("(c p) f -> p c f", p=128)), f"cs{sl}",
                              reads=[B_wuv], writes=[B_strm[sl]])
                        for cj in range(2):
                            c = 2 * c2 + cj
                            for ti in range(2):
                                for db in range(4):
                                    bi = ti * 4 + db
                                    p.op("pe", lambda e, sl=sl, cj=cj, c=c, ti=ti, db=db, bi=bi: e.matmul(banks[bi][:, 0:512], lhsT=Gs[:, c, ti * 128:(ti + 1) * 128], rhs=strm[sl][:, cj, db * 512:(db + 1) * 512],
                                                                                                   start=(c == 0), stop=(c == 127)), reads=[B_strm[sl], B_Gs], writes=[bank_buf[bi]], pe_accum=True)
                        gb = drain(gb, 1)
                    gb = drain(gb, 10000)
                    p.dma("sp", lambda e: e.dma_start(out=c_gf[:, 0, :], in_=rep_d[:, 2, :]), "cgf", writes=[B_gf])
                    for ti in range(2):
                        t = tok0 + ti * 128
                        p.dma("sp", lambda e, t=t: e.dma_start(out=x1t[:, 0, :], in_=x1_d[t:t + 128, :]), "cx", writes=[B_x1t])
                        for db in range(4):
                            bi = ti * 4 + db
                            p.op("dve", lambda e, db=db, bi=bi: e.tensor_tensor(out=x1t[:, 0, db * 512:(db + 1) * 512], in0=banks[bi][:, 0:512], in1=x1t[:, 0, db * 512:(db + 1) * 512], op=ALU.add),
                                 reads=[bank_buf[bi], B_x1t], writes=[B_x1t])
                        p.op("act", lambda e, ti=ti: e.activation(out=xn[:], in_=x1t[:, 0, :], func=AF.Square, accum_out=cst_[:, ti, 0:1]), reads=[B_x1t, B_xn], writes=[B_xn, B_cst])
                        p.op("act", lambda e, ti=ti: e.activation(out=cst_[:, ti, 1:2], in_=cst_[:, ti, 0:1], func=AF.Sqrt, scale=1.0 / D, bias=EPS), reads=[B_cst], writes=[B_cst])
                        p.op("dve", lambda e, ti=ti: e.reciprocal(out=cst_[:, ti, 2:3], in_=cst_[:, ti, 1:2]), reads=[B_cst], writes=[B_cst])
                        p.op("dve", lambda e, ti=ti: e.scalar_tensor_tensor(out=x1t[:, 0, :], in0=x1t[:, 0, :], scalar=cst_[:, ti, 2:3], in1=c_gf[:, 0, :], op0=ALU.mult, op1=ALU.mult),
                             reads=[B_cst, B_gf, B_x1t], writes=[B_x1t])
                        p.dma("pool", lambda e, t=t: e.dma_start(out=y_out[t:t + 128, :], in_=x1t[:, 0, :]), "cy", reads=[B_x1t])

                NGRP = T_OWN // 256
                drain(phaseA1(0), 10000)
                drain(phaseA2(0), 10000)
                for gi in range(NGRP):
                    gbuild(gi)
                    if gi + 1 < NGRP:
                        passes_and_epilogue(gi, phaseA1(gi + 1), phaseA2(gi + 1))
                    else:
                        passes_and_epilogue(gi, None, None)
                p.barrier()

        if "C" in stages:
            stage_c()
        p.barrier(skip=())
        p.emit()
    return nc


def host_inputs(inp, c):
    b, q = c // 4, c % 4
    f32 = np.float32
    xp = inp["x_prompt"][b]
    xs = inp["x_sample"][b]

    def win(x, lo, hi):
        n = x.shape[0]
        out = np.zeros((hi - lo, x.shape[1]), f32)
        a, bnd = max(lo, 0), min(hi, n)
        out[a - lo:bnd - lo] = x[a:bnd]
        return out

    own = []
    emask = np.zeros((128, NG_OWN, 2), f32)
    gi = 0
    for (x, L) in ((xp, 2048), (xs, 1024)):
        for g in range(L // 512):
            lo = q * L + g * 512 - 128
            own.append(win(x, lo, lo + 768))
            if lo < 0:
                emask[:, gi, 0] = -1e30
            if lo + 768 > x.shape[0]:
                emask[:, gi, 1] = -1e30
            gi += 1
    oth = []
    omask = np.zeros((128, NG_OTH, 4), f32)
    gi = 0
    for (x, L) in ((xp, 2048), (xs, 1024)):
        for j in [jj for jj in range(4) if jj != q]:
            for g in range(L // 512):
                lo = j * L + g * 512 - 2
                oth.append(win(x, lo, lo + 516))
                mf = 1.0 if j < q else 0.0
                omask[:, gi, :] = [mf, 1 - mf, 1 - mf, mf]
                gi += 1
    x_res = np.concatenate([xp[q * 2048:(q + 1) * 2048], xs[q * 1024:(q + 1) * 1024]], axis=0)
    rep = lambda v: np.broadcast_to(np.asarray(v, f32).reshape(1, -1), (128, np.asarray(v).size)).copy()
    rep_d = np.stack([rep(inp["g_mix"][0]), rep(inp["g_ffn"][0]), rep(inp["g_final"]), rep(inp["g_ssm_norm"][0])], axis=1)
    rep_s = np.zeros((128, 160), f32)
    rep_s[:, 0:32] = inp["a_log_f"][0]
    rep_s[:, 32:64] = inp["a_log_b"][0]
    rep_s[:, 64:96] = inp["d_skip"][0]
    rep_s[:, 96:112] = inp["attn_sink"][0]
    slopes = np.exp2(-8.0 * np.arange(1, 17, dtype=np.float64) / 16)
    qi = np.arange(128)[:, None]
    km = np.arange(384)[None, :]
    rel = qi - km + 128
    ab = np.where(np.abs(rel) <= 128, -np.abs(rel).astype(np.float64), -1e30).astype(f32)
    convw = np.ascontiguousarray(inp["conv_w"][0].T.reshape(24, 128, 5).transpose(1, 0, 2))
    convb = np.ascontiguousarray(inp["conv_b"][0].reshape(24, 128).T)
    dtb = np.concatenate([inp["dt_bias_f"][0], inp["dt_bias_b"][0]]).reshape(64, 1).astype(f32)
    cst = np.zeros((128, 9, 128), f32)
    s_ = np.arange(128)[:, None]
    l_ = np.arange(128)[None, :]
    cst[:, 0] = np.eye(128)
    cst[:, 1] = (s_ <= l_)
    cst[:, 2] = (s_ >= l_)
    cst[:, 3] = 1.0
    cst[:, 4] = l_
    import ml_dtypes
    negm = np.zeros((128, 2, 512), f32)
    negm[:, 0] = np.tile(np.where(s_ > l_, NEG, 0.0), (1, 4))
    negm[:, 1] = np.tile(np.where(s_ < l_, NEG, 0.0), (1, 4))
    sel = np.zeros((32, 32, 128), f32)
    for h in range(32):
        sel[h, h, :] = 1.0
    return {
        "x_own": np.stack(own), "x_oth": np.stack(oth), "x_res": x_res,
        "w_in": inp["w_in"][0], "w_ao": inp["w_attn_o"][0], "w_so": inp["w_ssm_o"][0], "w_out": inp["w_out"][0],
        "w_q": inp["w_query"][0], "keys": inp["sub_keys"][0].reshape(16, 128, 128),
        "exp_u": inp["expert_u"][0], "exp_v": inp["expert_v"][0],
        "rep_d": rep_d, "rep_s": rep_s, "attn_bias": ab, "emask": emask, "omask": omask,
        "convw": convw, "convb": convb, "dtb": dtb, "cst": cst, "negm": negm.astype(ml_dtypes.bfloat16), "sel": sel,
    }


def kernel(**inputs):
    inp = {k: np.asarray(v) for k, v in inputs.items()}
    nc = build()
    in_maps = [host_inputs(inp, c) for c in range(NCORES)]
    res = run_bass_kernel_spmd(nc, in_maps, core_ids=list(range(NCORES)))
    yp = np.zeros((2, 8192, D), np.float32)
    ys = np.zeros((2, 4096, D), np.float32)
    for c in range(NCORES):
        b, q = c // 4, c % 4
        y = res.results[c]["y_out"]
        yp[b, q * 2048:(q + 1) * 2048] = y[0:2048]
        ys[b, q * 1024:(q + 1) * 1024] = y[2048:3072]
    return (yp, ys)
```

```python
import contextlib
import numpy as np
import concourse.bass as bass
import concourse.mybir as mybir
from concourse.bass_utils import run_bass_kernel_spmd

F32 = mybir.dt.float32
BF16 = mybir.dt.bfloat16
U32 = mybir.dt.uint32
AF = mybir.ActivationFunctionType
ALU = mybir.AluOpType
AX = mybir.AxisListType
ENGS = ("pe", "act", "dve", "pool", "sp")

D = 2048
INW = 10816
NCORES = 8
Q_END, K_END, V_END, Z_END, XBC_END, DT_END = 1024, 1280, 1536, 3584, 6656, 6720
NEG = -30000.0
EPS = 1e-6
OWN_GROUPS = [(0, 4), (1, 2)]
NG_OWN = 6
NG_OTH = 18
T_OWN = 3072


class Buf:
    __slots__ = ("w", "r")

    def __init__(self):
        self.w = None
        self.r = []


class Prog:
    def __init__(self, nc):
        self.nc = nc
        self.ops = {e: [] for e in ENGS}
        self.dma_sems = {}
        self.waited = {e: {} for e in ENGS}

    def _need(self, eng, dep, waits):
        kind, key, val = dep
        k = (kind, key)
        if self.waited[eng].get(k, -1) >= val:
            return
        self.waited[eng][k] = val
        waits.append(dep)
        if kind == "e":
            self.ops[key][val]["inc"] = True

    def _deps(self, eng, reads, writes, pe_accum):
        waits = []
        for b in reads:
            if b.w is not None:
                self._need(eng, b.w, waits)
        for b in writes:
            if b.w is not None:
                if not (pe_accum and eng == "pe" and b.w[0] == "e" and b.w[1] == "pe"):
                    self._need(eng, b.w, waits)
            for d in b.r:
                self._need(eng, d, waits)
        return waits

    def op(self, eng, fn, reads=(), writes=(), pe_accum=False):
        waits = self._deps(eng, reads, writes, pe_accum)
        idx = len(self.ops[eng])
        self.ops[eng].append(dict(fn=fn, waits=waits, inc=False, dma=None))
        me = ("e", eng, idx)
        for b in reads:
            b.r.append(me)
        for b in writes:
            b.w = me
            b.r = []
        return me

    def dma(self, eng, fn, sem, reads=(), writes=()):
        waits = self._deps(eng, reads, writes, False)
        self.dma_sems[sem] = self.dma_sems.get(sem, 0) + 16
        val = self.dma_sems[sem]
        self.ops[eng].append(dict(fn=fn, waits=waits, inc=False, dma=(sem, 16)))
        me = ("d", sem, val)
        for b in reads:
            b.r.append(me)
        for b in writes:
            b.w = me
            b.r = []
        return me

    def barrier(self, skip=tuple(["cast_w", "cast_uv"] + [f"ci{i}" for i in range(32)])):
        deps = []
        for e in ENGS:
            for i in range(len(self.ops[e]) - 1, -1, -1):
                o = self.ops[e][i]
                if o["fn"] is not None and o["dma"] is None:
                    deps.append(("e", e, i))
                    break
        for s, v in self.dma_sems.items():
            if s not in skip:
                deps.append(("d", s, v))
        for e in ENGS:
            waits = []
            for d in deps:
                self._need(e, d, waits)
            self.ops[e].append(dict(fn=None, waits=waits, inc=False, dma=None))

    def emit(self):
        nc = self.nc
        with contextlib.ExitStack() as st:
            esem = {e: st.enter_context(nc.semaphore("s_" + e)) for e in ENGS}
            dsem = {n: st.enter_context(nc.semaphore("d_" + n)) for n in self.dma_sems}
            cum = {}
            for e in ENGS:
                c = 0
                arr = []
                for o in self.ops[e]:
                    if o["inc"]:
                        c += 1
                    arr.append(c)
                cum[e] = arr
            block = st.enter_context(nc.Block())

            def run(engname, engobj):
                for o in self.ops[engname]:
                    for (kind, key, val) in o["waits"]:
                        if kind == "e":
                            engobj.wait_ge(esem[key], cum[key][val])
                        else:
                            engobj.wait_ge(dsem[key], val)
                    if o["fn"] is None:
                        continue
                    ins = o["fn"](engobj)
                    if o["dma"] is not None:
                        ins.then_inc(dsem[o["dma"][0]], o["dma"][1])
                    elif o["inc"]:
                        ins.then_inc(esem[engname], 1)

            block.tensor(lambda e: run("pe", e))
            block.scalar(lambda e: run("act", e))
            block.vector(lambda e: run("dve", e))
            block.gpsimd(lambda e: run("pool", e))
            block.sync(lambda e: run("sp", e))


def bc(ap, shape):
    return ap.to_broadcast(list(shape))


class TilePool:
    def __init__(self, nc, stack):
        self.nc, self.stack, self.tiles, self.bufs, self.i, self.first = nc, stack, {}, [], 0, True

    def begin(self):
        self.first = (len(self.tiles) == 0)
        self.i = 0

    def sb(self, name, shape, dt):
        if name not in self.tiles:
            self.tiles[name] = self.stack.enter_context(self.nc.sbuf_tensor(name, list(shape), dt))
        return self.tiles[name]

    def buf(self):
        if self.i == len(self.bufs):
            self.bufs.append(Buf())
        b = self.bufs[self.i]
        self.i += 1
        return b


def build(stages=("W", "A", "S", "B", "C"), dbg=()):
    nc = bass.Bass("TRN2", target_bir_lowering=False)
    p = Prog(nc)

    def din(name, shape, dt=F32):
        return nc.dram_tensor(name, list(shape), dt, kind="ExternalInput").ap()

    def dscr(name, shape, dt):
        kind = "ExternalOutput" if name in dbg else "Internal"
        return nc.dram_tensor(name, list(shape), dt, kind=kind).ap()

    x_own = din("x_own", [NG_OWN, 768, D])
    x_oth = din("x_oth", [NG_OTH, 516, D])
    x_res = din("x_res", [T_OWN, D])
    w_in = din("w_in", [D, INW])
    w_ao = din("w_ao", [1024, D])
    w_so = din("w_so", [D, D])
    w_out = din("w_out", [D, D])
    w_q = din("w_q", [D, D])
    keys = din("keys", [16, 128, 128])
    exp_u = din("exp_u", [16384, D])
    exp_v = din("exp_v", [16384, D])
    rep_d = din("rep_d", [128, 4, D])
    rep_s = din("rep_s", [128, 160])
    attn_bias = din("attn_bias", [128, 384])
    emask = din("emask", [128, NG_OWN, 2])
    omask = din("omask", [128, NG_OTH, 4])
    convw = din("convw", [128, 24, 5])
    convb = din("convb", [128, 24])
    dtb = din("dtb", [64, 1])
    cst = din("cst", [128, 9, 128])
    negm = din("negm", [128, 2, 512], BF16)
    sel = din("sel", [32, 32, 128])
    y_out = nc.dram_tensor("y_out", [T_OWN, D], F32, kind="ExternalOutput").ap()

    w_in_b = dscr("w_in_b", [D, INW], BF16)
    w_ao_b = dscr("w_ao_b", [1024, D], BF16)
    w_so_b = dscr("w_so_b", [D, D], BF16)
    w_out_b = dscr("w_out_b", [D, D], BF16)
    w_q_b = dscr("w_q_b", [D, D], BF16)
    v_b = dscr("v_b", [16384, D], BF16)
    u_b = dscr("u_b", [16384, D], BF16)
    ut_b = dscr("ut_b", [128, 128, D], BF16)
    zs_d = dscr("zs_d", [T_OWN, D], BF16)
    gT_d = dscr("gT_d", [32, 128, T_OWN], BF16)
    Xs_d = dscr("Xs_d", [T_OWN, D], BF16)
    Bs_d = dscr("Bs_d", [T_OWN, 512], BF16)
    BT_d = dscr("BT_d", [4, 128, T_OWN], BF16)
    CT_d = dscr("CT_d", [4, 128, T_OWN], BF16)
    dt_d = dscr("dt_d", [T_OWN, 64], F32)
    aT_d = dscr("aT_d", [8, 128, T_OWN], BF16)
    hb_d = dscr("hb_d", [24, 128, D], BF16)
    hin_d = dscr("hin_d", [4, 128, D], F32)
    x1_d = dscr("x1_d", [T_OWN, D], F32)

    with contextlib.ExitStack() as top:
        def SB(stack, name, shape, dt):
            return stack.enter_context(nc.sbuf_tensor(name, list(shape), dt))

        banks = [top.enter_context(nc.psum_tensor(f"bank{i}", [128, 512], F32)) for i in range(8)]
        bank_buf = [Buf() for _ in range(8)]
        bank_ctr = [0]

        def nextbank():
            i = bank_ctr[0] % 8
            bank_ctr[0] += 1
            return banks[i], bank_buf[i]

        c_cst = SB(top, "c_cst", [128, 9, 128], F32)
        c_idb = SB(top, "c_idb", [128, 128], BF16)
        c_reps = SB(top, "c_reps", [128, 160], F32)
        c_arep = SB(top, "c_arep", [128, 64], F32)
        B_const = Buf()
        p.dma("sp", lambda e: e.dma_start(out=c_cst[:], in_=cst), "c0", writes=[B_const])
        p.dma("sp", lambda e: e.dma_start(out=c_reps[:], in_=rep_s), "c0", writes=[B_const])
        p.op("dve", lambda e: e.tensor_copy(out=c_idb[:], in_=c_cst[:, 0, :]), reads=[B_const], writes=[B_const])
        p.op("act", lambda e: e.activation(out=c_arep[:], in_=c_reps[:, 0:64], func=AF.Exp), reads=[B_const], writes=[B_const])
        p.op("dve", lambda e: e.tensor_scalar(out=c_arep[:], in0=c_arep[:], scalar1=-1.0, scalar2=None, op0=ALU.mult),
             reads=[B_const], writes=[B_const])
        IDF = c_cst[:, 0, :]
        TRIF = c_cst[:, 1, :]
        TRIB = c_cst[:, 2, :]
        ONES = c_cst[:, 3, :]
        IOTA = c_cst[:, 4, :]

        B_win, B_w, B_wuv = {}, Buf(), Buf()
        lazy_casts = []

        def emit_casts(n):
            for _ in range(n):
                if lazy_casts:
                    lazy_casts.pop(0)()
        WBLOCKS = ([(Z_END + i * 512, 512) for i in range(6)] + [(XBC_END, 64)] + [(V_END + i * 512, 512) for i in range(4)]
                   + [(DT_END + i * 512, 512) for i in range(8)] + [(0, 512), (512, 512), (Q_END, 512)])
        if "W" in stages:
            def cast(dst, src, rows, cols, rblk, sem, buf):
                for r0 in range(0, rows, rblk):
                    lazy_casts.append(lambda r0=r0, dst=dst, src=src, rblk=rblk, sem=sem, buf=buf: p.dma(
                        "pool", lambda e: e.dma_start(out=dst[r0:r0 + rblk, :], in_=src[r0:r0 + rblk, :], max_dma_last_dim=4096), sem, writes=[buf]))
            for i, (c0, ncol) in enumerate(WBLOCKS):
                B_win[c0] = Buf()
                p.dma("pool", lambda e, c0=c0, ncol=ncol: e.dma_start(out=w_in_b[:, c0:c0 + ncol], in_=w_in[:, c0:c0 + ncol], max_dma_last_dim=4096),
                      f"ci{i}", writes=[B_win[c0]])
            cast(w_ao_b, w_ao, 1024, D, 512, "cast_w", B_w)
            cast(w_so_b, w_so, D, D, 512, "cast_w", B_w)
            cast(w_out_b, w_out, D, D, 512, "cast_w", B_w)
            cast(w_q_b, w_q, D, D, 512, "cast_w", B_w)
            if "C" in stages:
                cast(u_b, exp_u, 16384, D, 1024, "cast_uv", B_wuv)
                cast(v_b, exp_v, 16384, D, 1024, "cast_uv", B_wuv)

        def rms_to_hT(st, xt_tiles, ntiles, hT, g_idx, col0s, nrows=None):
            pass

        def prep_group(pool, xsrc, nwin, own, gidx, tok0, par=0, res=None):
            pool.begin()
            Buf = pool.buf
            tg = "A" if own else "S"
            lo = 128 if own else 2
            ntile = (nwin + 127) // 128
            hT = pool.sb(f"hT{tg}", [128, 16, nwin], BF16)
            B_hT = Buf()
            xin = [pool.sb(f"xin{tg}_{i}", [128, D], F32) for i in range(2)]
            xn = [pool.sb(f"xn{tg}_{i}", [128, D], BF16) for i in range(2)]
            c_g = pool.sb(f"cg{tg}", [128, D], F32)
            B_cg = Buf()
            if pool.first:
                p.dma("sp", lambda e: e.dma_start(out=c_g[:], in_=rep_d[:, 0, :]), "c1", writes=[B_cg])
            stat = pool.sb(f"stat{tg}", [128, 8, 4], F32)
            B_xin = [Buf(), Buf()]
            B_xn = [Buf(), Buf()]
            B_stat = Buf()
            for ti in range(ntile):
                r0 = ti * 128
                rows = min(128, nwin - r0)
                s = ti % 2
                p.dma("sp", lambda e, s=s, r0=r0, rows=rows: e.dma_start(out=xin[s][0:rows, :], in_=xsrc[r0:r0 + rows, :]),
                      f"xin{s}", writes=[B_xin[s]])
                p.op("act", lambda e, s=s, rows=rows, ti=ti: e.activation(out=xn[s][0:rows, :], in_=xin[s][0:rows, :], func=AF.Square,
                                                                         accum_out=stat[0:rows, ti, 0:1]),
                     reads=[B_xin[s]], writes=[B_xn[s], B_stat])
                p.op("act", lambda e, rows=rows, ti=ti: e.activation(out=stat[0:rows, ti, 1:2], in_=stat[0:rows, ti, 0:1], func=AF.Sqrt,
                                                                    scale=1.0 / D, bias=EPS), reads=[B_stat], writes=[B_stat])
                p.op("dve", lambda e, rows=rows, ti=ti: e.reciprocal(out=stat[0:rows, ti, 2:3], in_=stat[0:rows, ti, 1:2]),
                     reads=[B_stat], writes=[B_stat])
                p.op("dve", lambda e, s=s, rows=rows, ti=ti: e.scalar_tensor_tensor(
                    out=xn[s][0:rows, :], in0=xin[s][0:rows, :], scalar=stat[0:rows, ti, 2:3], in1=c_g[0:rows, :],
                    op0=ALU.mult, op1=ALU.mult), reads=[B_xin[s], B_stat, B_cg], writes=[B_xn[s]])
                for half in range(2):
                    bk, bb = nextbank()
                    bkb = bk[:].bitcast(BF16)
                    for kk in range(8):
                        k = half * 8 + kk
                        p.op("pe", lambda e, s=s, rows=rows, k=k, kk=kk, bkb=bkb: e.transpose(
                            out=bkb[:, kk * 128:kk * 128 + rows], in_=xn[s][0:rows, k * 128:(k + 1) * 128], identity=c_idb[0:rows, 0:rows]),
                            reads=[B_xn[s], B_const], writes=[bb], pe_accum=True)
                    eng = "act" if half == 0 else "dve"
                    src = bkb.rearrange("p (k t) -> p k t", t=128)[:, :, 0:rows]
                    dst = hT[:, half * 8:half * 8 + 8, r0:r0 + rows]
                    if eng == "act":
                        p.op("act", lambda e, src=src, dst=dst: e.copy(out=dst, in_=src), reads=[bb], writes=[B_hT])
                    else:
                        p.op("dve", lambda e, src=src, dst=dst: e.tensor_copy(out=dst, in_=src), reads=[bb], writes=[B_hT])
                yield

            wt = [pool.sb(f"wt{tg}_{i}", [128, 16, 512], BF16) for i in range(2)]
            B_wt = [Buf(), Buf()]
            wctr = [0]

            def load_w(c0, ncol):
                s = wctr[0] % 2
                wctr[0] += 1
                src = w_in_b[:, c0:c0 + ncol].rearrange("(k p) c -> p k c", p=128)
                p.dma("sp", lambda e, s=s, src=src, ncol=ncol: e.dma_start(out=wt[s][:, :, 0:ncol], in_=src), f"wt{s}",
                      reads=[B_win[c0]], writes=[B_wt[s]])
                return wt[s], B_wt[s]

            def fm_proj(wtile, wb, wc0, M, t0, N):
                bk, bb = nextbank()
                for k in range(16):
                    p.op("pe", lambda e, k=k, bk=bk: e.matmul(bk[0:M, 0:N], lhsT=wtile[:, k, wc0:wc0 + M], rhs=hT[:, k, t0:t0 + N],
                                                               start=(k == 0), stop=(k == 15)),
                         reads=[wb, B_hT], writes=[bb], pe_accum=True)
                return bk, bb

            def tm_proj(wtile, wb, wc0, Ncol, t0, rows=128):
                bk, bb = nextbank()
                for k in range(16):
                    p.op("pe", lambda e, k=k, bk=bk: e.matmul(bk[0:rows, 0:Ncol], lhsT=hT[:, k, t0:t0 + rows], rhs=wtile[:, k, wc0:wc0 + Ncol],
                                                               start=(k == 0), stop=(k == 15)),
                         reads=[wb, B_hT], writes=[bb], pe_accum=True)
                return bk, bb

            NT = 512
            nconv = 24 if own else 20
            xbc = [pool.sb(f"xbc{tg}_{i}", [128, 516], BF16) for i in range(2)]
            dg = [pool.sb(f"dg{tg}_{i}", [128, 5, 128], BF16) for i in range(2)]
            B_dg = [Buf(), Buf()]
            B_xbc = [Buf(), Buf()]
            csil = [pool.sb(f"csil{tg}_{i}", [128, 512], BF16) for i in range(2)]
            B_csil = [Buf(), Buf()]
            c_cw = pool.sb(f"cw{tg}", [128, 24, 5], F32)
            c_cb = pool.sb(f"cb{tg}", [128, 24], F32)
            c_dtb = pool.sb(f"dtb{tg}", [64, 1], F32)
            B_cw = Buf()
            if pool.first:
                p.dma("sp", lambda e: e.dma_start(out=c_cw[:], in_=convw), "c1", writes=[B_cw])
                p.dma("sp", lambda e: e.dma_start(out=c_cb[:], in_=convb), "c1", writes=[B_cw])
                p.dma("sp", lambda e: e.dma_start(out=c_dtb[:], in_=dtb), "c1", writes=[B_cw])
            npar = 1 if own else 2
            Xtm = [pool.sb(f"Xtm{tg}{i}", [128, 4, D], BF16) for i in range(npar)][par]
            Btm = [pool.sb(f"Btm{tg}{i}", [128, 4, 512], BF16) for i in range(npar)][par]
            dttm = [pool.sb(f"dttm{tg}{i}", [128, 4, 64], F32) for i in range(npar)][par]
            B_Xtm = [Buf() for _ in range(npar)][par]
            B_Btm = [Buf() for _ in range(npar)][par]
            B_dttm = [Buf() for _ in range(npar)][par]
            w0 = lo - 2
            pending = [None]
            pending_tr = [None]
            pending_conv = [None]
            own_stores = []
            for cc4 in range(0, nconv, 4):
                wtile, wb = load_w(V_END + D + cc4 * 128, 512)
                for ci in range(4):
                    cc = cc4 + ci
                    s = cc % 2
                    for hf in range(2):
                        bk, bb = fm_proj(wtile, wb, ci * 128, 128, w0 + hf * 258, 258)
                        p.op("act", lambda e, s=s, hf=hf, bk=bk: e.copy(out=xbc[s][:, hf * 258:(hf + 1) * 258], in_=bk[:, 0:258]),
                             reads=[bb], writes=[B_xbc[s]])
                    if pending[0] is not None:
                        pending[0]()
                        pending[0] = None
                    if pending_conv[0] is not None:
                        pending_conv[0]()
                        pending_conv[0] = None
                        pending[0], pending_tr[0] = pending_tr[0], None
                    p.op("dve", lambda e, s=s, cc=cc: e.tensor_tensor(out=dg[s][:], in0=bc(c_idb[:].unsqueeze(1), [128, 5, 128]),
                                                                      in1=bc(c_cw[:, cc, :].unsqueeze(2), [128, 5, 128]), op=ALU.mult),
                         reads=[B_cw, B_const], writes=[B_dg[s]])

                    def conv_chunk(s=s, cc=cc):
                        bkc, bbc = nextbank()
                        for j in range(5):
                            p.op("pe", lambda e, j=j: e.matmul(bkc[:, 0:512], lhsT=dg[s][:, j, :], rhs=xbc[s][:, j:j + 512], start=(j == 0), stop=(j == 4)),
                                 reads=[B_dg[s], B_xbc[s]], writes=[bbc], pe_accum=True)
                        p.op("act", lambda e: e.activation(out=csil[s][:], in_=bkc[:, 0:512], func=AF.Silu, bias=c_cb[:, cc:cc + 1]),
                             reads=[bbc, B_cw], writes=[B_csil[s]])
                        if own and cc >= 16:
                            g = (cc - 16) % 4
                            dst = (BT_d if cc < 20 else CT_d)[g, :, tok0:tok0 + NT]
                            p.dma("pool", lambda e: e.dma_start(out=dst, in_=csil[s][:]), f"stc{s}", reads=[B_csil[s]])
                    pending_conv[0] = conv_chunk
                    if cc < 20:
                        def tr_chunk(s=s, cc=cc):
                            bk, bb = nextbank()
                            bkb = bk[:].bitcast(BF16)
                            for ti in range(4):
                                p.op("pe", lambda e, ti=ti: e.transpose(out=bkb[:, ti * 128:(ti + 1) * 128],
                                                                         in_=csil[s][:, ti * 128:(ti + 1) * 128], identity=c_idb[:]),
                                     reads=[B_csil[s], B_const], writes=[bb], pe_accum=True)
                            src = bkb[:, 0:512].rearrange("p (t c) -> p t c", c=128)
                            if cc < 16:
                                p.op("act", lambda e: e.copy(out=Xtm[:, :, cc * 128:(cc + 1) * 128], in_=src), reads=[bb], writes=[B_Xtm])
                            else:
                                p.op("act", lambda e: e.copy(out=Btm[:, :, (cc - 16) * 128:(cc - 15) * 128], in_=src), reads=[bb], writes=[B_Btm])
                        pending_tr[0] = tr_chunk
                    yield
            wtile, wb = load_w(XBC_END, 64)
            bk, bb = fm_proj(wtile, wb, 0, 64, lo, NT)
            for q_ in (pending, pending_conv, pending_tr):
                if q_[0] is not None:
                    q_[0]()
                    q_[0] = None
            dtf = pool.sb(f"dtf{tg}", [64, 512], F32)
            B_dtf = Buf()
            p.op("act", lambda e, bk=bk: e.activation(out=dtf[:], in_=bk[0:64, 0:512], func=AF.Exp, bias=c_dtb[:, 0:1]),
                 reads=[bb, B_cw], writes=[B_dtf])
            p.op("act", lambda e: e.activation(out=dtf[:], in_=dtf[:], func=AF.Ln, bias=1.0), reads=[B_dtf], writes=[B_dtf])
            bk, bb = nextbank()
            for ti in range(4):
                p.op("pe", lambda e, ti=ti, bk=bk: e.transpose(out=bk[:, ti * 64:(ti + 1) * 64], in_=dtf[:, ti * 128:(ti + 1) * 128],
                                                                identity=c_cst[0:64, 0, 0:64]),
                     reads=[B_dtf, B_const], writes=[bb], pe_accum=True)
            p.op("dve", lambda e, bk=bk: e.tensor_copy(out=dttm[:], in_=bk[:, 0:256].rearrange("p (t c) -> p t c", c=64)),
                 reads=[bb], writes=[B_dttm])
            if res is not None:
                res.update(Xtm=Xtm, Btm=Btm, dttm=dttm, B_Xtm=B_Xtm, B_Btm=B_Btm, B_dttm=B_dttm)
            yield
            if not own:
                return
            for ti in range(4):
                t = tok0 + ti * 128
                p.dma("pool", lambda e, ti=ti, t=t: e.dma_start(out=Xs_d[t:t + 128, :], in_=Xtm[:, ti, :]), "stX", reads=[B_Xtm])
                p.dma("pool", lambda e, ti=ti, t=t: e.dma_start(out=Bs_d[t:t + 128, :], in_=Btm[:, ti, :]), "stX", reads=[B_Btm])
                p.dma("pool", lambda e, ti=ti, t=t: e.dma_start(out=dt_d[t:t + 128, :], in_=dttm[:, ti, :]), "stX", reads=[B_dttm])

            zt = [pool.sb(f"zt{tg}_{i}", [128, 512], BF16) for i in range(2)]
            B_zt = [Buf(), Buf()]
            zc = 0
            for cb4 in range(4):
                wtile, wb = load_w(V_END + cb4 * 512, 512)
                for ti in range(4):
                    bk, bb = tm_proj(wtile, wb, 0, 512, lo + ti * 128)
                    s = zc % 2
                    zc += 1
                    p.op("act", lambda e, s=s, bk=bk: e.activation(out=zt[s][:], in_=bk[:], func=AF.Silu), reads=[bb], writes=[B_zt[s]])
                    t = tok0 + ti * 128
                    p.dma("pool", lambda e, s=s, t=t, cb4=cb4: e.dma_start(out=zs_d[t:t + 128, cb4 * 512:(cb4 + 1) * 512], in_=zt[s][:]),
                          f"stz{s}", reads=[B_zt[s]])
            gt = [pool.sb(f"gt{tg}_{i}", [128, 512], BF16) for i in range(2)]
            B_gt = [Buf(), Buf()]
            for cb4 in range(8):
                wtile, wb = load_w(DT_END + cb4 * 512, 512)
                for ci in range(4):
                    cc = cb4 * 4 + ci
                    bk, bb = fm_proj(wtile, wb, ci * 128, 128, lo, NT)
                    s = cc % 2
                    p.op("act", lambda e, s=s, bk=bk: e.activation(out=gt[s][:], in_=bk[:], func=AF.Sigmoid), reads=[bb], writes=[B_gt[s]])
                    p.dma("pool", lambda e, s=s, cc=cc: e.dma_start(out=gT_d[cc, :, tok0:tok0 + NT], in_=gt[s][:]), f"stg{s}", reads=[B_gt[s]])
            qT = pool.sb(f"qT{tg}", [64, 16, 512], BF16)
            kT = pool.sb(f"kT{tg}", [64, 4, 768], BF16)
            vt = pool.sb(f"vt{tg}", [128, 6, 256], BF16)
            B_qT, B_kT, B_vt = Buf(), Buf(), Buf()
            for cb4 in range(2):
                wtile, wb = load_w(cb4 * 512, 512)
                for hh in range(8):
                    h = cb4 * 8 + hh
                    bk, bb = fm_proj(wtile, wb, hh * 64, 64, lo, NT)
                    p.op("act", lambda e, h=h, bk=bk: e.activation(out=qT[:, h, :], in_=bk[0:64, :], func=AF.Copy, scale=0.125),
                         reads=[bb], writes=[B_qT])
            wtile, wb = load_w(Q_END, 512)
            for kv in range(4):
                for hf in range(2):
                    bk, bb = fm_proj(wtile, wb, kv * 64, 64, hf * 384, 384)
                    p.op("dve", lambda e, kv=kv, hf=hf, bk=bk: e.tensor_copy(out=kT[:, kv, hf * 384:(hf + 1) * 384], in_=bk[0:64, 0:384]),
                         reads=[bb], writes=[B_kT])
            for ti in range(6):
                bk, bb = tm_proj(wtile, wb, 256, 256, ti * 128)
                p.op("act", lambda e, ti=ti, bk=bk: e.copy(out=vt[:, ti, :], in_=bk[:, 0:256]), reads=[bb], writes=[B_vt])

            c_ab = pool.sb(f"ab{tg}", [128, 384], F32)
            c_em = pool.sb(f"em{tg}", [128, NG_OWN, 2], F32)
            B_ab = Buf()
            if pool.first:
                p.dma("sp", lambda e: e.dma_start(out=c_ab[:], in_=attn_bias), "c1", writes=[B_ab])
                p.dma("sp", lambda e: e.dma_start(out=c_em[:], in_=emask), "c1", writes=[B_ab])
            sc = [pool.sb(f"sc{tg}_{i}", [128, 4, 384], F32) for i in range(2)]
            pr = [pool.sb(f"pr{tg}_{i}", [128, 4, 384], BF16) for i in range(2)]
            prT = [pool.sb(f"prT{tg}_{i}", [128, 4, 3, 128], BF16) for i in range(2)]
            ast = [pool.sb(f"ast{tg}_{i}", [128, 4, 8], F32) for i in range(2)]
            B_sc, B_pr, B_prT, B_ast = [Buf(), Buf()], [Buf(), Buf()], [Buf(), Buf()], [Buf(), Buf()]
            atm = pool.sb(f"atm{tg}", [128, 1024], BF16)
            B_atm = Buf()
            aTt = pool.sb(f"aTt{tg}", [128, 8, 512], BF16)
            B_aTt = Buf()
            def att_head(j, kv, s):
                sbanks = []
                for g in range(4):
                    h = kv * 4 + g
                    bk, bb = nextbank()
                    p.op("pe", lambda e, h=h, kv=kv, j=j, bk=bk: e.matmul(bk[:, 0:384], lhsT=qT[:, h, j * 128:(j + 1) * 128],
                                                                          rhs=kT[:, kv, j * 128:j * 128 + 384], start=True, stop=True),
                         reads=[B_qT, B_kT], writes=[bb])
                    p.op("dve", lambda e, s=s, g=g, h=h, bk=bk: e.scalar_tensor_tensor(out=sc[s][:, g, :], in0=c_ab[:], scalar=float(2.0 ** (-8.0 * (h + 1) / 16)),
                                                                                       in1=bk[:, 0:384], op0=ALU.mult, op1=ALU.add),
                         reads=[bb, B_ab], writes=[B_sc[s]])
                if j == 0:
                    p.op("dve", lambda e, s=s: e.tensor_scalar(out=sc[s][:, :, 0:128], in0=sc[s][:, :, 0:128], scalar1=c_em[:, gidx, 0:1],
                                                               scalar2=None, op0=ALU.add), reads=[B_sc[s], B_ab], writes=[B_sc[s]])
                if j == 3:
                    p.op("dve", lambda e, s=s: e.tensor_scalar(out=sc[s][:, :, 256:384], in0=sc[s][:, :, 256:384], scalar1=c_em[:, gidx, 1:2],
                                                               scalar2=None, op0=ALU.add), reads=[B_sc[s], B_ab], writes=[B_sc[s]])
                p.op("dve", lambda e, s=s: e.tensor_reduce(out=ast[s][:, :, 0], in_=sc[s][:], axis=AX.X, op=ALU.max),
                     reads=[B_sc[s]], writes=[B_ast[s]])
                p.op("dve", lambda e, s=s, kv=kv: e.tensor_tensor(out=ast[s][:, :, 1], in0=ast[s][:, :, 0], in1=c_reps[:, 96 + kv * 4:100 + kv * 4],
                                                                  op=ALU.max), reads=[B_ast[s], B_const], writes=[B_ast[s]])
                p.op("dve", lambda e, s=s: e.tensor_scalar(out=ast[s][:, :, 2], in0=ast[s][:, :, 1], scalar1=-1.0, scalar2=None, op0=ALU.mult),
                     reads=[B_ast[s]], writes=[B_ast[s]])
                for g in range(4):
                    p.op("act", lambda e, s=s, g=g: e.activation(out=pr[s][:, g, :], in_=sc[s][:, g, :], func=AF.Exp, bias=ast[s][:, g, 2:3],
                                                                 accum_out=ast[s][:, g, 3:4]), reads=[B_sc[s], B_ast[s]], writes=[B_pr[s], B_ast[s]])
                p.op("dve", lambda e, s=s, kv=kv: e.tensor_tensor(out=ast[s][:, :, 4], in0=c_reps[:, 96 + kv * 4:100 + kv * 4], in1=ast[s][:, :, 1],
                                                                  op=ALU.subtract), reads=[B_ast[s], B_const], writes=[B_ast[s]])
                p.op("act", lambda e, s=s: e.activation(out=ast[s][:, :, 5], in_=ast[s][:, :, 4], func=AF.Exp), reads=[B_ast[s]], writes=[B_ast[s]])
                p.op("dve", lambda e, s=s: e.tensor_tensor(out=ast[s][:, :, 6], in0=ast[s][:, :, 5], in1=ast[s][:, :, 3], op=ALU.add),
                     reads=[B_ast[s]], writes=[B_ast[s]])
                p.op("dve", lambda e, s=s: e.reciprocal(out=ast[s][:, :, 7], in_=ast[s][:, :, 6]), reads=[B_ast[s]], writes=[B_ast[s]])

            def att_tail(j, kv, s):
                for g in range(4):
                    bk, bb = nextbank()
                    bkb = bk[:].bitcast(BF16)
                    for m in range(3):
                        p.op("pe", lambda e, s=s, g=g, m=m, bkb=bkb: e.transpose(out=bkb[:, m * 128:(m + 1) * 128], in_=pr[s][:, g, m * 128:(m + 1) * 128],
                                                                                identity=c_idb[:]), reads=[B_pr[s], B_const], writes=[bb], pe_accum=True)
                    eng = "act" if g % 2 == 0 else "dve"
                    if eng == "act":
                        p.op("act", lambda e, s=s, g=g, bkb=bkb: e.copy(out=prT[s][:, g, :, :], in_=bkb[:, 0:384].rearrange("p (m t) -> p m t", t=128)),
                             reads=[bb], writes=[B_prT[s]])
                    else:
                        p.op("dve", lambda e, s=s, g=g, bkb=bkb: e.tensor_copy(out=prT[s][:, g, :, :], in_=bkb[:, 0:384].rearrange("p (m t) -> p m t", t=128)),
                             reads=[bb], writes=[B_prT[s]])
                bk, bb = nextbank()
                for g in range(4):
                    for m in range(3):
                        p.op("pe", lambda e, s=s, g=g, m=m, kv=kv, j=j, bk=bk: e.matmul(bk[:, g * 64:(g + 1) * 64], lhsT=prT[s][:, g, m, :],
                                                                                     rhs=vt[:, j + m, kv * 64:(kv + 1) * 64], start=(m == 0), stop=(m == 2)),
                             reads=[B_prT[s], B_vt], writes=[bb], pe_accum=True)
                p.op("dve", lambda e, s=s, kv=kv, bk=bk: e.tensor_tensor(
                    out=atm[:, kv * 256:(kv + 1) * 256].rearrange("p (g d) -> p g d", d=64),
                    in0=bk[:, 0:256].rearrange("p (g d) -> p g d", d=64),
                    in1=bc(ast[s][:, :, 7:8], [128, 4, 64]), op=ALU.mult), reads=[bb, B_ast[s]], writes=[B_atm])
                if kv == 3:
                    bk, bb = nextbank()
                    bkb = bk[:].bitcast(BF16)
                    for k in range(8):
                        p.op("pe", lambda e, k=k, bkb=bkb: e.transpose(out=bkb[:, k * 128:(k + 1) * 128], in_=atm[:, k * 128:(k + 1) * 128], identity=c_idb[:]),
                             reads=[B_atm, B_const], writes=[bb], pe_accum=True)
                    p.op("act", lambda e, j=j, bkb=bkb: e.copy(out=aTt[:, :, j * 128:(j + 1) * 128], in_=bkb.rearrange("p (k t) -> p k t", t=128)),
                         reads=[bb], writes=[B_aTt])

            items = [(j, kv) for j in range(4) for kv in range(4)]
            att_head(items[0][0], items[0][1], 0)
            for i_ in range(1, 16):
                att_head(items[i_][0], items[i_][1], i_ % 2)
                att_tail(items[i_ - 1][0], items[i_ - 1][1], (i_ - 1) % 2)
            att_tail(items[15][0], items[15][1], 1)
            for k in range(8):
                p.dma("pool", lambda e, k=k: e.dma_start(out=aT_d[k, :, tok0:tok0 + NT], in_=aTt[:, k, :]), "sta", reads=[B_aTt])

        if "A" in stages:
            gi = 0
            with contextlib.ExitStack() as stA:
                poolA = TilePool(nc, stA)
                for seg, ng in OWN_GROUPS:
                    for g in range(ng):
                        tok0 = (0 if seg == 0 else 2048) + g * 512
                        for _ in prep_group(poolA, x_own[gi], 768, True, gi, tok0):
                            pass
                        emit_casts(2)
                        gi += 1
                p.barrier()


        ssd_stack = contextlib.ExitStack()
        Hst = SB(ssd_stack, "Hst", [128, 4, D], F32)
        B_H = [Buf() for _ in range(4)]

        def ssd_work(st, tag, two_xs=False):
            W = {}
            for nm, shp, dt in (("dA", [128, 64], F32), ("cumsb", [128, 128], F32), ("tmp", [128, 64], F32), ("dte", [128, 64], F32),
                                ("dec", [128, 64], F32), ("w", [128, 64], F32), ("xs", [128, D], BF16), ("xs2", [128, D], BF16), ("deff", [128, 64], F32),
                                ("tg", [128, 512], F32)):
                if nm == "xs2" and not two_xs:
                    continue
                W[nm] = SB(st, nm + tag, shp, dt)
                W["B_" + nm] = Buf()
            return W

        def chunk_pre(W, dt_ap, B_dt):
            p.op("dve", lambda e: e.tensor_tensor(out=W["dA"][:], in0=dt_ap, in1=c_arep[:], op=ALU.mult), reads=[B_dt, B_const], writes=[W["B_dA"]])
            bk, bb = nextbank()
            p.op("pe", lambda e, bk=bk: e.matmul(bk[:, 0:32], lhsT=TRIF, rhs=W["dA"][:, 0:32], start=True, stop=True), reads=[W["B_dA"], B_const], writes=[bb])
            p.op("pe", lambda e, bk=bk: e.matmul(bk[:, 32:64], lhsT=TRIB, rhs=W["dA"][:, 32:64], start=True, stop=True), reads=[W["B_dA"], B_const], writes=[bb], pe_accum=True)
            p.op("pe", lambda e, bk=bk: e.matmul(bk[:, 64:128], lhsT=ONES, rhs=W["dA"][:, 0:64], start=True, stop=True), reads=[W["B_dA"], B_const], writes=[bb], pe_accum=True)
            p.op("act", lambda e, bk=bk: e.copy(out=W["cumsb"][:], in_=bk[:, 0:128]), reads=[bb], writes=[W["B_cumsb"]])
            p.op("dve", lambda e: e.tensor_tensor(out=W["tmp"][:], in0=W["cumsb"][:, 64:128], in1=W["cumsb"][:, 0:64], op=ALU.subtract),
                 reads=[W["B_cumsb"]], writes=[W["B_tmp"]])
            p.op("act", lambda e: e.activation(out=W["dte"][:], in_=W["tmp"][:], func=AF.Exp), reads=[W["B_tmp"]], writes=[W["B_dte"]])
            p.op("act", lambda e: e.activation(out=W["dec"][:], in_=W["cumsb"][:, 64:128], func=AF.Exp), reads=[W["B_cumsb"]], writes=[W["B_dec"]])
            p.op("dve", lambda e: e.tensor_tensor(out=W["w"][:], in0=dt_ap, in1=W["dte"][:], op=ALU.mult), reads=[B_dt, W["B_dte"]], writes=[W["B_w"]])

        def chunk_states(W, X_ap, B_X, Bt_ap, B_Bt, d, xk="xs"):
            p.op("dve", lambda e: e.tensor_tensor(out=W[xk][:].rearrange("p (h d) -> p h d", d=64), in0=X_ap.rearrange("p (h d) -> p h d", d=64),
                                                  in1=bc(W["w"][:, d * 32:(d + 1) * 32].unsqueeze(2), [128, 32, 64]), op=ALU.mult),
                 reads=[B_X, W["B_w"]], writes=[W["B_" + xk]])
            out = []
            for g in range(4):
                bk, bb = nextbank()
                p.op("pe", lambda e, g=g, bk=bk: e.matmul(bk[:, 0:512], lhsT=Bt_ap[:, g * 128:(g + 1) * 128], rhs=W[xk][:, g * 512:(g + 1) * 512],
                                                         start=True, stop=True), reads=[B_Bt, W["B_" + xk]], writes=[bb])
                out.append((bk, bb))
            return out

        def h_update(W, hi, d, sbanks):
            Hv = Hst[:, hi, :]
            p.op("dve", lambda e: e.tensor_tensor(out=Hv.rearrange("p (h d) -> p h d", d=64), in0=Hv.rearrange("p (h d) -> p h d", d=64),
                                                  in1=bc(W["dec"][:, d * 32:(d + 1) * 32].unsqueeze(2), [128, 32, 64]), op=ALU.mult),
                 reads=[W["B_dec"], B_H[hi]], writes=[B_H[hi]])
            for g, (bk, bb) in enumerate(sbanks):
                p.op("dve", lambda e, g=g, bk=bk: e.tensor_tensor(out=Hst[:, hi, g * 512:(g + 1) * 512], in0=bk[:, 0:512], in1=Hst[:, hi, g * 512:(g + 1) * 512], op=ALU.add),
                     reads=[bb, B_H[hi]], writes=[B_H[hi]])

        if "S" in stages:
            with contextlib.ExitStack() as st0:
                Pb = SB(st0, "Pb", [128, 2, 32], F32)
                wpb = SB(st0, "wpb", [128, 32], F32)
                c_om = SB(st0, "c_om", [128, NG_OTH, 4], F32)
                B_Pb, B_wpb, B_om = Buf(), Buf(), Buf()
                p.dma("sp", lambda e: e.dma_start(out=c_om[:], in_=omask), "c1", writes=[B_om])
                p.op("pool", lambda e: e.memset(Pb[:], 1.0), writes=[B_Pb])
                for hi in range(4):
                    p.op("pool", lambda e, hi=hi: e.memset(Hst[:, hi, :], 0.0), writes=[B_H[hi]])
                poolS = TilePool(nc, st0)
                W_S = [ssd_work(st0, "S0", True), ssd_work(st0, "S1", True)]

                def other_group(gi, r, nxt):
                    seg = 0 if gi < 12 else 1
                    if True:

                        def xs_scale(W, ti, d, xk):
                            p.op("dve", lambda e: e.tensor_tensor(out=W[xk][:].rearrange("p (h d) -> p h d", d=64), in0=r["Xtm"][:, ti, :].rearrange("p (h d) -> p h d", d=64),
                                                                  in1=bc(W["w"][:, d * 32:(d + 1) * 32].unsqueeze(2), [128, 32, 64]), op=ALU.mult),
                                 reads=[r["B_Xtm"], W["B_w"]], writes=[W["B_" + xk]])

                        def tile_pre(ti, W):
                            chunk_pre(W, r["dttm"][:, ti, :], r["B_dttm"])
                            xs_scale(W, ti, 0, "xs")
                            xs_scale(W, ti, 1, "xs2")

                        def st_mm(W, ti, g, xk):
                            bk, bb = nextbank()
                            p.op("pe", lambda e: e.matmul(bk[:, 0:512], lhsT=r["Btm"][:, ti, g * 128:(g + 1) * 128], rhs=W[xk][:, g * 512:(g + 1) * 512],
                                                          start=True, stop=True), reads=[r["B_Btm"], W["B_" + xk]], writes=[bb])
                            return bk, bb

                        def tile_main(ti, W):
                            hi = seg * 2
                            p.op("dve", lambda e: e.tensor_scalar(out=W["deff"][:, 0:32], in0=W["dec"][:, 0:32], scalar1=c_om[:, gi, 0:1], scalar2=c_om[:, gi, 1:2],
                                                                  op0=ALU.mult, op1=ALU.add), reads=[W["B_dec"], B_om], writes=[W["B_deff"]])
                            Hv = Hst[:, hi, :]
                            p.op("dve", lambda e: e.tensor_tensor(out=Hv.rearrange("p (h d) -> p h d", d=64), in0=Hv.rearrange("p (h d) -> p h d", d=64),
                                                                  in1=bc(W["deff"][:, 0:32].unsqueeze(2), [128, 32, 64]), op=ALU.mult),
                                 reads=[W["B_deff"], B_H[hi]], writes=[B_H[hi]])
                            for g in range(4):
                                bk, bb = st_mm(W, ti, g, "xs")
                                p.op("dve", lambda e, g=g, bk=bk, hi=hi: e.scalar_tensor_tensor(
                                    out=Hst[:, hi, g * 512:(g + 1) * 512], in0=bk[:, 0:512], scalar=c_om[:, gi, 0:1], in1=Hst[:, hi, g * 512:(g + 1) * 512],
                                    op0=ALU.mult, op1=ALU.add), reads=[bb, B_H[hi], B_om], writes=[B_H[hi]])
                            hi = seg * 2 + 1
                            p.op("dve", lambda e: e.tensor_scalar(out=wpb[:], in0=Pb[:, seg, :], scalar1=c_om[:, gi, 2:3], scalar2=None, op0=ALU.mult),
                                 reads=[B_Pb, B_om], writes=[B_wpb])
                            for g in range(4):
                                bk, bb = st_mm(W, ti, g, "xs2")
                                p.op("dve", lambda e, g=g, bk=bk: e.tensor_tensor(out=W["tg"][:].rearrange("p (h d) -> p h d", d=64),
                                                                                 in0=bk[:, 0:512].rearrange("p (h d) -> p h d", d=64),
                                                                                 in1=bc(wpb[:, g * 8:(g + 1) * 8].unsqueeze(2), [128, 8, 64]), op=ALU.mult),
                                     reads=[bb, B_wpb], writes=[W["B_tg"]])
                                p.op("dve", lambda e, g=g, hi=hi: e.tensor_tensor(out=Hst[:, hi, g * 512:(g + 1) * 512], in0=Hst[:, hi, g * 512:(g + 1) * 512],
                                                                                   in1=W["tg"][:], op=ALU.add), reads=[W["B_tg"], B_H[hi]], writes=[B_H[hi]])
                            p.op("dve", lambda e: e.tensor_scalar(out=W["deff"][:, 32:64], in0=W["dec"][:, 32:64], scalar1=c_om[:, gi, 2:3], scalar2=c_om[:, gi, 3:4],
                                                                  op0=ALU.mult, op1=ALU.add), reads=[W["B_dec"], B_om], writes=[W["B_deff"]])
                            p.op("dve", lambda e: e.tensor_tensor(out=Pb[:, seg, :], in0=Pb[:, seg, :], in1=W["deff"][:, 32:64], op=ALU.mult),
                                 reads=[W["B_deff"], B_Pb, B_wpb], writes=[B_Pb])

                        def dr(n):
                            for _ in range(n):
                                if nxt[0] is not None:
                                    try:
                                        next(nxt[0])
                                    except StopIteration:
                                        nxt[0] = None

                        tile_pre(0, W_S[0])
                        for ti in range(4):
                            dr(3)
                            if ti + 1 < 4:
                                tile_pre(ti + 1, W_S[(ti + 1) % 2])
                            dr(4)
                            tile_main(ti, W_S[ti % 2])
                        dr(1000)
                rs = [dict() for _ in range(NG_OTH)]
                for _ in prep_group(poolS, x_oth[0], 516, False, 100, 0, 0, rs[0]):
                    pass
                for gi in range(NG_OTH):
                    nxt = [prep_group(poolS, x_oth[gi + 1], 516, False, 101 + gi, 0, (gi + 1) % 2, rs[gi + 1])] if gi + 1 < NG_OTH else [None]
                    other_group(gi, rs[gi], nxt)
                    emit_casts(2)
                emit_casts(1000)
                if "hin_d" in dbg:
                    for hi in range(4):
                        p.dma("pool", lambda e, hi=hi: e.dma_start(out=hin_d[hi], in_=Hst[:, hi, :]), "sth", reads=[B_H[hi]])
                p.barrier()

        SEGS = [(0, 0, 16), (1, 2048, 8)]
        def stage_b1():
            with contextlib.ExitStack() as st:
                W2 = [ssd_work(st, "B1a"), ssd_work(st, "B1b")]
                Xc = [SB(st, f"b1X{i}", [128, D], BF16) for i in range(2)]
                Bc = [SB(st, f"b1B{i}", [128, 512], BF16) for i in range(2)]
                dc = [SB(st, f"b1d{i}", [128, 64], F32) for i in range(2)]
                hbt = [SB(st, f"b1h{i}", [128, D], BF16) for i in range(2)]
                B_Xc, B_Bc, B_dc, B_hbt = [Buf(), Buf()], [Buf(), Buf()], [Buf(), Buf()], [Buf(), Buf()]
                it = 0
                cbase = 0
                for seg, tok0, nch in SEGS:
                    hi = seg * 2 + 1
                    for c in range(nch - 1, -1, -1):
                        s = it % 2
                        W = W2[s]
                        it += 1
                        t = tok0 + c * 128
                        p.dma("sp", lambda e, s=s, t=t: e.dma_start(out=Xc[s][:], in_=Xs_d[t:t + 128, :]), f"b1l{s}", writes=[B_Xc[s]])
                        p.dma("sp", lambda e, s=s, t=t: e.dma_start(out=Bc[s][:], in_=Bs_d[t:t + 128, :]), f"b1l{s}", writes=[B_Bc[s]])
                        p.dma("sp", lambda e, s=s, t=t: e.dma_start(out=dc[s][:], in_=dt_d[t:t + 128, :]), f"b1l{s}", writes=[B_dc[s]])
                        p.op("act", lambda e, s=s, hi=hi: e.copy(out=hbt[s][:], in_=Hst[:, hi, :]), reads=[B_H[hi]], writes=[B_hbt[s]])
                        p.dma("pool", lambda e, s=s, cc=cbase + c: e.dma_start(out=hb_d[cc], in_=hbt[s][:]), f"b1s{s}", reads=[B_hbt[s]])
                        chunk_pre(W, dc[s][:], B_dc[s])
                        sb_ = chunk_states(W, Xc[s][:], B_Xc[s], Bc[s][:], B_Bc[s], 1)
                        h_update(W, hi, 1, sb_)
                    cbase += nch
                p.barrier()

        def stage_b2():
            with contextlib.ExitStack() as st:
                W2b = [ssd_work(st, "B2a"), ssd_work(st, "B2b")]
                Xc2 = [SB(st, f"b2X{i}", [128, D], BF16) for i in range(2)]
                Bc2 = [SB(st, f"b2B{i}", [128, 512], BF16) for i in range(2)]
                dc2 = [SB(st, f"b2d{i}", [128, 64], F32) for i in range(2)]
                zc2 = [SB(st, f"b2z{i}", [128, D], BF16) for i in range(2)]
                BTc2 = [SB(st, f"b2BT{i}", [128, 4, 128], BF16) for i in range(2)]
                CTc2 = [SB(st, f"b2CT{i}", [128, 4, 128], BF16) for i in range(2)]
                hbt2 = [SB(st, f"b2hb{i}", [128, D], BF16) for i in range(2)]
                hft = SB(st, "b2hf", [128, D], BF16)
                B_ld2, B_hbt2, B_hft = [Buf(), Buf()], [Buf(), Buf()], Buf()
                xdt = SB(st, "b2xdt", [128, 2, D], BF16)
                cumT = SB(st, "b2cumT", [32, 2, 128], F32)
                ecum = SB(st, "b2ecum", [128, 64], F32)
                ncum = SB(st, "b2ncum", [128, 64], F32)
                GTm = SB(st, "b2GTm", [128, 2, 4, 128], BF16)
                LT = SB(st, "b2LT", [128, 8, 128], BF16)
                MT = SB(st, "b2MT", [128, 8, 128], BF16)
                yv = SB(st, "b2y", [128, D], F32)
                t1 = SB(st, "b2t1", [128, 512], F32)
                gst = SB(st, "b2gst", [128, 4, 4], F32)
                ssm = SB(st, "b2ssm", [128, D], BF16)
                c_neg = SB(st, "b2neg", [128, 2, 512], BF16)
                c_gn = SB(st, "b2gn", [128, D], F32)
                ssmT = SB(st, "b2ssmT", [128, 16, 512], BF16)
                aTl = SB(st, "b2aT", [128, 8, 512], BF16)
                mT = SB(st, "b2mT", [128, 16, 512], BF16)
                B_xdt, B_cumT, B_ecum, B_GTm, B_LT, B_MT, B_y, B_t1, B_gst, B_ssm, B_c2, B_ssmT, B_aTl, B_mT = [Buf() for _ in range(14)]
                p.dma("sp", lambda e: e.dma_start(out=c_neg[:], in_=negm), "c1", writes=[B_c2])
                p.dma("sp", lambda e: e.dma_start(out=c_gn[:], in_=rep_d[:, 3, :]), "c1", writes=[B_c2])
                wo1 = [SB(st, f"b2wa{i}", [128, 8, 128], BF16) for i in range(2)]
                wo2 = [SB(st, f"b2ws{i}", [128, 16, 128], BF16) for i in range(2)]
                gl = [SB(st, f"b2gl{i}", [128, 2, 512], BF16) for i in range(2)]
                B_wo, B_gl = [Buf(), Buf()], [Buf(), Buf()]
                wo3 = SB(st, "b2wo", [128, 16, 512], BF16)
                B_wo3 = Buf()
                xr = SB(st, "b2xr", [128, 512], F32)
                B_xr = Buf()
                cbase = 0
                for seg, tok0, nch in SEGS:
                    hif, hib = seg * 2, seg * 2 + 1

                    def do_chunk(c, W, Xc, Bc, dc, zc, BTc, CTc, hbt, B_ld, B_hbt, seg=seg, tok0=tok0, hif=hif, hib=hib, cbase=cbase):
                        t = tok0 + c * 128
                        for dst, src in ((Xc[:], Xs_d[t:t + 128, :]), (Bc[:], Bs_d[t:t + 128, :]), (dc[:], dt_d[t:t + 128, :]), (zc[:], zs_d[t:t + 128, :]),
                                         (BTc[:], BT_d[:, :, t:t + 128].rearrange("g p t -> p g t")), (CTc[:], CT_d[:, :, t:t + 128].rearrange("g p t -> p g t"))):
                            p.dma("sp", lambda e, dst=dst, src=src: e.dma_start(out=dst, in_=src), f"b2l{(cbase + c) % 2}", writes=[B_ld])
                        p.dma("sp", lambda e, cc=cbase + c: e.dma_start(out=hbt[:], in_=hb_d[cc]), f"b2l{(cbase + c) % 2}", writes=[B_hbt])
                        p.op("act", lambda e, hif=hif: e.copy(out=hft[:], in_=Hst[:, hif, :]), reads=[B_H[hif]], writes=[B_hft])
                        chunk_pre(W, dc[:], B_ld)
                        bk, bb = nextbank()
                        p.op("pe", lambda e, bk=bk: e.matmul(bk[0:32, 0:128], lhsT=W["dA"][:, 0:32], rhs=TRIF, start=True, stop=True), reads=[W["B_dA"], B_const], writes=[bb])
                        p.op("pe", lambda e, bk=bk: e.matmul(bk[0:32, 128:256], lhsT=W["dA"][:, 32:64], rhs=TRIB, start=True, stop=True), reads=[W["B_dA"], B_const], writes=[bb], pe_accum=True)
                        p.op("act", lambda e, bk=bk: e.copy(out=cumT[:], in_=bk[0:32, 0:256].rearrange("p (d t) -> p d t", t=128)), reads=[bb], writes=[B_cumT])
                        p.op("act", lambda e: e.activation(out=ecum[:], in_=W["cumsb"][:, 0:64], func=AF.Exp), reads=[W["B_cumsb"]], writes=[B_ecum])
                        p.op("dve", lambda e: e.tensor_scalar(out=ncum[:], in0=W["cumsb"][:, 0:64], scalar1=-1.0, scalar2=None, op0=ALU.mult), reads=[W["B_cumsb"]], writes=[B_ecum])
                        for d in range(2):
                            p.op("dve" if d == 0 else "pool", lambda e, d=d: e.tensor_tensor(
                                out=xdt[:, d, :].rearrange("p (h d) -> p h d", d=64), in0=Xc[:].rearrange("p (h d) -> p h d", d=64),
                                in1=bc(dc[:, d * 32:(d + 1) * 32].unsqueeze(2), [128, 32, 64]), op=ALU.mult), reads=[B_ld], writes=[B_xdt])
                        bk, bb = nextbank()
                        for g in range(4):
                            p.op("pe", lambda e, g=g, bk=bk: e.matmul(bk[:, g * 128:(g + 1) * 128], lhsT=BTc[:, g, :], rhs=CTc[:, g, :], start=True, stop=True),
                                 reads=[B_ld], writes=[bb], pe_accum=True)
                        for d in range(2):
                            tri = TRIF if d == 0 else TRIB
                            p.op("dve", lambda e, d=d, tri=tri, bk=bk: e.tensor_tensor(out=GTm[:, d, :, :], in0=bk[:, 0:512].rearrange("p (g t) -> p g t", t=128),
                                                                                      in1=bc(tri.unsqueeze(1), [128, 4, 128]), op=ALU.mult),
                                 reads=[bb, B_const], writes=[B_GTm])
                        first = True
                        for d in range(2):
                            hsrc, B_hs = (hft, B_hft) if d == 0 else (hbt, B_hbt)
                            for g in range(4):
                                for half in range(2):
                                    bk, bb = nextbank()
                                    p.op("pe", lambda e, d=d, bk=bk: e.matmul(bk[:, 0:512], lhsT=c_idb[:], rhs=c_neg[:, d, :], start=True, stop=False),
                                         reads=[B_c2, B_const], writes=[bb])
                                    for j in range(4):
                                        h = g * 8 + half * 4 + j
                                        p.op("pe", lambda e, d=d, j=j, h=h, bk=bk: e.matmul(bk[:, j * 128:(j + 1) * 128], lhsT=bc(c_cst[0:32, 0, h:h + 1], [32, 128]),
                                                                                             rhs=cumT[:, d, :], start=False, stop=(j == 3)),
                                             reads=[B_cumT, B_const], writes=[bb], pe_accum=True)
                                    for j in range(4):
                                        h = g * 8 + half * 4 + j
                                        p.op("act", lambda e, d=d, j=j, h=h, half=half, bk=bk: e.activation(out=LT[:, half * 4 + j, :], in_=bk[:, j * 128:(j + 1) * 128], func=AF.Exp,
                                                                                                        bias=ncum[:, d * 32 + h:d * 32 + h + 1]),
                                             reads=[bb, B_ecum], writes=[B_LT])
                                p.op("dve", lambda e, d=d, g=g: e.tensor_tensor(out=MT[:], in0=LT[:], in1=bc(GTm[:, d, g, :].unsqueeze(1), [128, 8, 128]), op=ALU.mult),
                                     reads=[B_LT, B_GTm], writes=[B_MT])
                                bkd, bbd = nextbank()
                                for j in range(8):
                                    h = g * 8 + j
                                    p.op("pe", lambda e, d=d, j=j, h=h, bkd=bkd: e.matmul(bkd[:, j * 64:(j + 1) * 64], lhsT=MT[:, j, :], rhs=xdt[:, d, h * 64:(h + 1) * 64],
                                                                                         start=True, stop=True), reads=[B_MT, B_xdt], writes=[bbd], pe_accum=True)
                                bko, bbo = nextbank()
                                p.op("pe", lambda e, g=g, bko=bko, hsrc=hsrc: e.matmul(bko[:, 0:512], lhsT=CTc[:, g, :], rhs=hsrc[:, g * 512:(g + 1) * 512], start=True, stop=True),
                                     reads=[B_ld, B_hs], writes=[bbo])
                                p.op("dve", lambda e, d=d, g=g, bko=bko: e.tensor_tensor(out=t1[:].rearrange("p (h d) -> p h d", d=64),
                                                                                        in0=bko[:, 0:512].rearrange("p (h d) -> p h d", d=64),
                                                                                        in1=bc(ecum[:, d * 32 + g * 8:d * 32 + g * 8 + 8].unsqueeze(2), [128, 8, 64]), op=ALU.mult),
                                     reads=[bbo, B_ecum], writes=[B_t1])
                                if d == 0:
                                    p.op("dve", lambda e, g=g, bkd=bkd: e.tensor_tensor(out=yv[:, g * 512:(g + 1) * 512], in0=bkd[:, 0:512], in1=t1[:], op=ALU.add),
                                         reads=[bbd, B_t1], writes=[B_y])
                                else:
                                    p.op("dve", lambda e, g=g, bkd=bkd: e.tensor_tensor(out=t1[:], in0=bkd[:, 0:512], in1=t1[:], op=ALU.add),
                                         reads=[bbd, B_t1], writes=[B_t1])
                                    p.op("pool", lambda e, g=g: e.tensor_tensor(out=yv[:, g * 512:(g + 1) * 512], in0=yv[:, g * 512:(g + 1) * 512], in1=t1[:], op=ALU.add),
                                         reads=[B_t1, B_y], writes=[B_y])
                        p.op("dve", lambda e: e.tensor_tensor(out=xdt[:, 0, :].rearrange("p (h d) -> p h d", d=64), in0=Xc[:].rearrange("p (h d) -> p h d", d=64),
                                                              in1=bc(c_reps[:, 64:96].unsqueeze(2), [128, 32, 64]), op=ALU.mult),
                             reads=[B_ld, B_const, B_xdt], writes=[B_xdt])
                        p.op("dve", lambda e: e.tensor_tensor(out=yv[:], in0=yv[:], in1=xdt[:, 0, :], op=ALU.add), reads=[B_xdt, B_y], writes=[B_y])
                        p.op("dve", lambda e: e.tensor_tensor(out=yv[:], in0=yv[:], in1=zc[:], op=ALU.mult), reads=[B_ld, B_y], writes=[B_y])
                        for g in range(4):
                            p.op("act", lambda e, g=g: e.activation(out=ssm[:, g * 512:(g + 1) * 512], in_=yv[:, g * 512:(g + 1) * 512], func=AF.Square, accum_out=gst[:, g, 0:1]),
                                 reads=[B_y], writes=[B_ssm, B_gst])
                        p.op("act", lambda e: e.activation(out=gst[:, :, 1], in_=gst[:, :, 0], func=AF.Sqrt, scale=1.0 / 512, bias=EPS), reads=[B_gst], writes=[B_gst])
                        p.op("dve", lambda e: e.reciprocal(out=gst[:, :, 2], in_=gst[:, :, 1]), reads=[B_gst], writes=[B_gst])
                        p.op("dve", lambda e: e.tensor_tensor(out=yv[:].rearrange("p (g d) -> p g d", d=512), in0=yv[:].rearrange("p (g d) -> p g d", d=512),
                                                              in1=bc(gst[:, :, 2:3], [128, 4, 512]), op=ALU.mult), reads=[B_gst, B_y], writes=[B_y])
                        p.op("dve", lambda e: e.tensor_tensor(out=ssm[:], in0=yv[:], in1=c_gn[:], op=ALU.mult), reads=[B_y, B_c2, B_ssm], writes=[B_ssm])
                        ci = c % 4
                        for half in range(2):
                            bk, bb = nextbank()
                            bkb = bk[:].bitcast(BF16)
                            for kk in range(8):
                                k = half * 8 + kk
                                p.op("pe", lambda e, k=k, kk=kk, bkb=bkb: e.transpose(out=bkb[:, kk * 128:(kk + 1) * 128], in_=ssm[:, k * 128:(k + 1) * 128], identity=c_idb[:]),
                                     reads=[B_ssm, B_const], writes=[bb], pe_accum=True)
                            p.op("act", lambda e, half=half, ci=ci, bkb=bkb: e.copy(out=ssmT[:, half * 8:half * 8 + 8, ci * 128:(ci + 1) * 128],
                                                                                    in_=bkb.rearrange("p (k t) -> p k t", t=128)), reads=[bb], writes=[B_ssmT])
                        sb_ = chunk_states(W, Xc[:], B_ld, Bc[:], B_ld, 0)
                        h_update(W, hif, 0, sb_)
                        if ci == 3:
                            g0 = t - 384
                            p.dma("sp", lambda e, g0=g0: e.dma_start(out=aTl[:], in_=aT_d[:, :, g0:g0 + 512].rearrange("k p t -> p k t")), "b2a", writes=[B_aTl])
                            for cc in range(16):
                                s = cc % 2
                                p.dma("sp", lambda e, s=s, cc=cc: e.dma_start(out=wo1[s][:], in_=w_ao_b[:, cc * 128:(cc + 1) * 128].rearrange("(k p) c -> p k c", p=128)),
                                      f"b2w{s}", reads=[B_w], writes=[B_wo[s]])
                                p.dma("sp", lambda e, s=s, cc=cc: e.dma_start(out=wo2[s][:], in_=w_so_b[:, cc * 128:(cc + 1) * 128].rearrange("(k p) c -> p k c", p=128)),
                                      f"b2w{s}", reads=[B_w], writes=[B_wo[s]])
                                p.dma("sp", lambda e, s=s, cc=cc, g0=g0: e.dma_start(out=gl[s][:, 0, :], in_=gT_d[cc, :, g0:g0 + 512]), f"b2g{s}", writes=[B_gl[s]])
                                p.dma("sp", lambda e, s=s, cc=cc, g0=g0: e.dma_start(out=gl[s][:, 1, :], in_=gT_d[16 + cc, :, g0:g0 + 512]), f"b2g{s}", writes=[B_gl[s]])
                                bka, bba = nextbank()
                                for k in range(8):
                                    p.op("pe", lambda e, s=s, k=k, bka=bka: e.matmul(bka[:, 0:512], lhsT=wo1[s][:, k, :], rhs=aTl[:, k, :], start=(k == 0), stop=(k == 7)),
                                         reads=[B_wo[s], B_aTl], writes=[bba], pe_accum=True)
                                bks, bbs = nextbank()
                                for k in range(16):
                                    p.op("pe", lambda e, s=s, k=k, bks=bks: e.matmul(bks[:, 0:512], lhsT=wo2[s][:, k, :], rhs=ssmT[:, k, :], start=(k == 0), stop=(k == 15)),
                                         reads=[B_wo[s], B_ssmT], writes=[bbs], pe_accum=True)
                                p.op("dve", lambda e, s=s, bka=bka: e.tensor_tensor(out=t1[:], in0=bka[:, 0:512], in1=gl[s][:, 0, :], op=ALU.mult),
                                     reads=[bba, B_gl[s], B_t1], writes=[B_t1])
                                p.op("dve", lambda e, s=s, bks=bks: e.tensor_tensor(out=yv[:, 0:512], in0=bks[:, 0:512], in1=gl[s][:, 1, :], op=ALU.mult),
                                     reads=[bbs, B_gl[s], B_y], writes=[B_y])
                                p.op("dve", lambda e, cc=cc: e.tensor_tensor(out=mT[:, cc, :], in0=t1[:], in1=yv[:, 0:512], op=ALU.add),
                                     reads=[B_t1, B_y], writes=[B_mT])
                            for cb in range(4):
                                p.dma("sp", lambda e, cb=cb: e.dma_start(out=wo3[:], in_=w_out_b[:, cb * 512:(cb + 1) * 512].rearrange("(k p) c -> p k c", p=128)),
                                      "b2w3", reads=[B_w], writes=[B_wo3])
                                for ti in range(4):
                                    tt = g0 + ti * 128
                                    p.dma("sp", lambda e, tt=tt, cb=cb: e.dma_start(out=xr[:], in_=x_res[tt:tt + 128, cb * 512:(cb + 1) * 512]), "b2x", writes=[B_xr])
                                    bk, bb = nextbank()
                                    for k in range(16):
                                        p.op("pe", lambda e, k=k, ti=ti, bk=bk: e.matmul(bk[:, 0:512], lhsT=mT[:, k, ti * 128:(ti + 1) * 128], rhs=wo3[:, k, :],
                                                                                          start=(k == 0), stop=(k == 15)), reads=[B_mT, B_wo3], writes=[bb], pe_accum=True)
                                    p.op("dve", lambda e, bk=bk: e.tensor_tensor(out=xr[:], in0=bk[:, 0:512], in1=xr[:], op=ALU.add), reads=[bb, B_xr], writes=[B_xr])
                                    p.dma("pool", lambda e, tt=tt, cb=cb: e.dma_start(out=x1_d[tt:tt + 128, cb * 512:(cb + 1) * 512], in_=xr[:]), "b2xs", reads=[B_xr])
                    for c in range(nch):
                        s2 = (cbase + c) % 2
                        do_chunk(c, W2b[s2], Xc2[s2], Bc2[s2], dc2[s2], zc2[s2], BTc2[s2], CTc2[s2], hbt2[s2], B_ld2[s2], B_hbt2[s2])
                    cbase += nch
                p.barrier()
        if "B" in stages:
            stage_b1()
            stage_b2()
        p.barrier()
        ssd_stack.close()

        def stage_c():
            with contextlib.ExitStack() as st:
                keysT = SB(st, "keysT", [128, 16, 128], BF16)
                iob = SB(st, "iob", [128, 128], BF16)
                c_gf = SB(st, "c_gf", [128, 1, D], F32)
                B_kT, B_cc, B_gf = Buf(), Buf(), Buf()
                p.op("dve", lambda e: e.tensor_copy(out=iob[:], in_=IOTA), reads=[B_const], writes=[B_cc])
                with contextlib.ExitStack() as st2:
                    kf = SB(st2, "kf", [128, 16, 128], F32)
                    kb = SB(st2, "kb", [128, 16, 128], BF16)
                    B_kf = Buf()
                    p.dma("sp", lambda e: e.dma_start(out=kf[:], in_=keys.rearrange("a n d -> n a d")), "c1", writes=[B_kf])
                    p.op("dve", lambda e: e.tensor_copy(out=kb[:], in_=kf[:]), reads=[B_kf], writes=[B_kf])
                    for half in range(2):
                        bk, bb = nextbank()
                        bkb = bk[:].bitcast(BF16)
                        for kk in range(8):
                            p.op("pe", lambda e, a=half * 8 + kk, kk=kk, bkb=bkb: e.transpose(out=bkb[:, kk * 128:(kk + 1) * 128], in_=kb[:, a, :], identity=c_idb[:]),
                                 reads=[B_kf, B_const], writes=[bb], pe_accum=True)
                        p.op("act", lambda e, half=half, bkb=bkb: e.copy(out=keysT[:, half * 8:half * 8 + 8, :], in_=bkb.rearrange("p (k t) -> p k t", t=128)),
                             reads=[bb], writes=[B_kT])
                    ul = [SB(st2, f"ul{i}", [128, D], BF16) for i in range(2)]
                    ut = [SB(st2, f"ut{i}", [128, D], BF16) for i in range(2)]
                    B_ul, B_ut = [Buf(), Buf()], [Buf(), Buf()]
                    for c in range(128):
                        s_ = c % 2
                        p.dma("sp", lambda e, s_=s_, c=c: e.dma_start(out=ul[s_][:], in_=u_b[c * 128:(c + 1) * 128, :]), f"ul{s_}", reads=[B_wuv], writes=[B_ul[s_]])
                        for half in range(2):
                            bk, bb = nextbank()
                            bkb = bk[:].bitcast(BF16)
                            for kk in range(8):
                                k = half * 8 + kk
                                p.op("pe", lambda e, s_=s_, k=k, kk=kk, bkb=bkb: e.transpose(out=bkb[:, kk * 128:(kk + 1) * 128], in_=ul[s_][:, k * 128:(k + 1) * 128], identity=c_idb[:]),
                                     reads=[B_ul[s_], B_const], writes=[bb], pe_accum=True)
                            if half == 0:
                                p.op("act", lambda e, s_=s_, bkb=bkb: e.copy(out=ut[s_][:, 0:1024], in_=bkb), reads=[bb], writes=[B_ut[s_]])
                            else:
                                p.op("dve", lambda e, s_=s_, bkb=bkb: e.tensor_copy(out=ut[s_][:, 1024:2048], in_=bkb), reads=[bb], writes=[B_ut[s_]])
                        p.dma("pool", lambda e, s_=s_, c=c: e.dma_start(out=ut_b[c], in_=ut[s_][:]), f"us{s_}", reads=[B_ut[s_]])
                    p.barrier()

                x1t = SB(st, "x1t", [128, 1, D], F32)
                xn = SB(st, "cxn", [128, D], BF16)
                cst_ = SB(st, "cst_", [128, 2, 4], F32)
                cst2 = SB(st, "cst2", [128, 2, 4], F32)
                xnT2 = [SB(st, f"xnT{i}", [128, 16, 256], BF16) for i in range(2)]
                qT = SB(st, "cqT", [128, 16, 256], BF16)
                wq = [SB(st, f"wq{i}", [128, 16, 128], BF16) for i in range(2)]
                P2g = SB(st, "P2g", [128, 32, 128], BF16)
                OH1 = SB(st, "OH1", [128, 32, 128], BF16)
                scr = P2g[:].bitcast(F32).rearrange("p a b -> p (a b)").rearrange("p (h n) -> p h n", n=128)
                eq = OH1[:].bitcast(F32).rearrange("p a b -> p (a b)").rearrange("p (h k j) -> p h k j", k=16, j=16)
                wk = SB(st, "cwk", [128, 256], F32)
                topv2 = [SB(st, f"topv{i}", [128, 16, 16], F32) for i in range(2)]
                idxu2 = [SB(st, f"idxu{i}", [128, 16, 16], U32) for i in range(2)]
                idxf = SB(st, "idxf", [128, 16, 16], F32)
                cand = SB(st, "cand", [128, 8, 16, 16], F32)
                best = SB(st, "best", [128, 8, 16], F32)
                posu = SB(st, "posu", [128, 8, 16], U32)
                ku = SB(st, "ku", [128, 2, 8, 16], U32)
                kf_ = SB(st, "kf_", [128, 2, 8, 16], F32)
                gat = SB(st, "gat", [128, 8, 16], F32)
                gz = SB(st, "gz", [128, 8, 2], F32)
                I12_2 = [SB(st, f"I12_{i}", [128, 3, 128], F32) for i in range(2)]
                I12T = SB(st, "I12T", [128, 3, 128], BF16)
                Gs = SB(st, "Gs", [128, 128, 256], BF16)
                NSL = 4
                strm = [SB(st, f"strm{i}", [128, 2, D], BF16) for i in range(NSL)]
                ge = [SB(st, f"ge{i}", [128, 256], BF16) for i in range(2)]
                (B_x1t, B_xn, B_cst, B_cst2, B_qT, B_wk, B_idxf, B_cand, B_best, B_posu, B_ku, B_kf2, B_gat, B_gz,
                 B_I12T, B_OH1, B_P2g, B_Gs) = [Buf() for _ in range(18)]
                B_xnT2, B_topv2, B_idxu2, B_I12_2 = [Buf(), Buf()], [Buf(), Buf()], [Buf(), Buf()], [Buf(), Buf()]
                B_wq, B_ge = [Buf(), Buf()], [Buf(), Buf()]
                B_strm = [Buf() for _ in range(NSL)]
                B_scr, B_eq = B_P2g, B_OH1
                B_P2ga, B_P2gb = Buf(), Buf()
                sctr = [0]

                def stage1_hc(ti, hc):
                    topv, idxu, B_topv, B_idxu = topv2[ti], idxu2[ti], B_topv2[ti], B_idxu2[ti]
                    p.op("dve", lambda e: e.max(out=topv[:, hc, 0:8], in_=scr[:, hc, :]), reads=[B_scr], writes=[B_topv])
                    p.op("dve", lambda e: e.match_replace(out=wk[:, 0:128], in_to_replace=topv[:, hc, 0:8], in_values=scr[:, hc, :], imm_value=-1e30),
                         reads=[B_scr, B_topv], writes=[B_wk])
                    p.op("dve", lambda e: e.max(out=topv[:, hc, 8:16], in_=wk[:, 0:128]), reads=[B_wk], writes=[B_topv])
                    p.op("dve", lambda e: e.max_index(out=idxu[:, hc, 0:8], in_max=topv[:, hc, 0:8], in_values=scr[:, hc, :]), reads=[B_scr, B_topv], writes=[B_idxu])
                    p.op("dve", lambda e: e.max_index(out=idxu[:, hc, 8:16], in_max=topv[:, hc, 8:16], in_values=scr[:, hc, :]), reads=[B_scr, B_topv], writes=[B_idxu])

                def scores_q(ti, qd, xsl):
                    bk, bb = nextbank()
                    for j in range(4):
                        hc = qd * 4 + j
                        p.op("pe", lambda e, hc=hc, j=j: e.matmul(bk[:, j * 128:(j + 1) * 128], lhsT=qT[:, hc, ti * 128:(ti + 1) * 128], rhs=keysT[:, hc, :],
                                                                    start=True, stop=True), reads=[B_qT, B_kT], writes=[bb], pe_accum=True)
                    p.op("act", lambda e: e.copy(out=scr[:, qd * 4:qd * 4 + 4, :], in_=bk[:, 0:512].rearrange("p (a n) -> p a n", n=128)),
                         reads=[bb, B_P2ga, B_P2gb], writes=[B_scr])

                def phaseA1(gi):
                    tok0 = gi * 256
                    xnT, B_xnT = xnT2[gi % 2], B_xnT2[gi % 2]
                    p.dma("sp", lambda e: e.dma_start(out=c_gf[:, 0, :], in_=rep_d[:, 1, :]), "cgf", writes=[B_gf])
                    for ti in range(2):
                        t = tok0 + ti * 128
                        p.dma("sp", lambda e, t=t: e.dma_start(out=x1t[:, 0, :], in_=x1_d[t:t + 128, :]), "cx", writes=[B_x1t])
                        p.op("act", lambda e, ti=ti: e.activation(out=xn[:], in_=x1t[:, 0, :], func=AF.Square, accum_out=cst2[:, ti, 0:1]), reads=[B_x1t], writes=[B_xn, B_cst2])
                        p.op("act", lambda e, ti=ti: e.activation(out=cst2[:, ti, 1:2], in_=cst2[:, ti, 0:1], func=AF.Sqrt, scale=1.0 / D, bias=EPS), reads=[B_cst2], writes=[B_cst2])
                        p.op("dve", lambda e, ti=ti: e.reciprocal(out=cst2[:, ti, 2:3], in_=cst2[:, ti, 1:2]), reads=[B_cst2], writes=[B_cst2])
                        p.op("dve", lambda e, ti=ti: e.scalar_tensor_tensor(out=xn[:], in0=x1t[:, 0, :], scalar=cst2[:, ti, 2:3], in1=c_gf[:, 0, :], op0=ALU.mult, op1=ALU.mult),
                             reads=[B_x1t, B_cst2, B_gf, B_xn], writes=[B_xn])
                        yield
                        for half in range(2):
                            bk, bb = nextbank()
                            bkb = bk[:].bitcast(BF16)
                            for kk in range(8):
                                k = half * 8 + kk
                                p.op("pe", lambda e, k=k, kk=kk, bkb=bkb: e.transpose(out=bkb[:, kk * 128:(kk + 1) * 128], in_=xn[:, k * 128:(k + 1) * 128], identity=c_idb[:]),
                                     reads=[B_xn, B_const], writes=[bb], pe_accum=True)
                            p.op("act", lambda e, half=half, ti=ti, bkb=bkb: e.copy(out=xnT[:, half * 8:half * 8 + 8, ti * 128:(ti + 1) * 128], in_=bkb.rearrange("p (k t) -> p k t", t=128)),
                                 reads=[bb], writes=[B_xnT])
                            yield
                    def ld_wq(cc):
                        s_ = cc % 2
                        p.dma("sp", lambda e: e.dma_start(out=wq[s_][:], in_=w_q_b[:, cc * 128:(cc + 1) * 128].rearrange("(k p) c -> p k c", p=128)),
                              f"cwq{s_}", reads=[B_w], writes=[B_wq[s_]])
                    ld_wq(0)
                    for cc in range(16):
                        s_ = cc % 2
                        bk, bb = nextbank()
                        for k in range(16):
                            p.op("pe", lambda e, s_=s_, k=k, bk=bk: e.matmul(bk[:, 0:256], lhsT=wq[s_][:, k, :], rhs=xnT[:, k, :], start=(k == 0), stop=(k == 15)),
                                 reads=[B_wq[s_], B_xnT], writes=[bb], pe_accum=True)
                        p.op("act", lambda e, cc=cc, bk=bk: e.copy(out=qT[:, cc, :], in_=bk[:, 0:256]), reads=[bb], writes=[B_qT])
                        if cc + 1 < 16:
                            ld_wq(cc + 1)
                        yield
                        if cc % 4 != 3:
                            yield
                    for qd in range(4):
                        scores_q(0, qd, None)
                        yield
                    for hc in range(16):
                        stage1_hc(0, hc)
                        yield
                    for qd in range(4):
                        scores_q(1, qd, None)
                        yield

                def phaseA2(gi):
                    for hc in range(16):
                        stage1_hc(1, hc)
                        yield
                    for ti in range(2):
                        topv, idxu, B_topv, B_idxu = topv2[ti], idxu2[ti], B_topv2[ti], B_idxu2[ti]
                        I12, B_I12 = I12_2[ti], B_I12_2[ti]
                        p.op("dve", lambda e, idxu=idxu: e.tensor_copy(out=idxf[:], in_=idxu[:]), reads=[B_idxu], writes=[B_idxf])
                        tv = topv[:].rearrange("p (h c) k -> p h c k", c=2)
                        p.op("dve", lambda e, tv=tv: e.tensor_tensor(out=cand[:], in0=bc(tv[:, :, 0, :].unsqueeze(3), [128, 8, 16, 16]), in1=bc(tv[:, :, 1, :].unsqueeze(2), [128, 8, 16, 16]), op=ALU.add),
                             reads=[B_topv], writes=[B_cand])
                        yield
                        for h in range(8):
                            cv = cand[:, h, :, :].rearrange("p a b -> p (a b)")
                            p.op("dve", lambda e, h=h, cv=cv: e.max(out=best[:, h, 0:8], in_=cv), reads=[B_cand], writes=[B_best])
                            p.op("dve", lambda e, h=h, cv=cv: e.match_replace(out=wk[:], in_to_replace=best[:, h, 0:8], in_values=cv, imm_value=-1e30), reads=[B_cand, B_best], writes=[B_wk])
                            p.op("dve", lambda e, h=h: e.max(out=best[:, h, 8:16], in_=wk[:]), reads=[B_wk], writes=[B_best])
                            p.op("dve", lambda e, h=h, cv=cv: e.max_index(out=posu[:, h, 0:8], in_max=best[:, h, 0:8], in_values=cv), reads=[B_cand, B_best], writes=[B_posu])
                            p.op("dve", lambda e, h=h, cv=cv: e.max_index(out=posu[:, h, 8:16], in_max=best[:, h, 8:16], in_values=cv), reads=[B_cand, B_best], writes=[B_posu])
                            yield
                        p.op("dve", lambda e: e.tensor_tensor(out=gat[:], in0=best[:], in1=bc(best[:, :, 0:1], [128, 8, 16]), op=ALU.subtract), reads=[B_best], writes=[B_gat])
                        p.op("act", lambda e: e.activation(out=gat[:], in_=gat[:], func=AF.Exp), reads=[B_gat], writes=[B_gat])
                        p.op("dve", lambda e: e.tensor_reduce(out=gz[:, :, 0], in_=gat[:], axis=AX.X, op=ALU.add), reads=[B_gat], writes=[B_gz])
                        p.op("dve", lambda e: e.reciprocal(out=gz[:, :, 1], in_=gz[:, :, 0]), reads=[B_gz], writes=[B_gz])
                        p.op("dve", lambda e, I12=I12: e.tensor_tensor(out=I12[:, 2, :].rearrange("p (h k) -> p h k", k=16), in0=gat[:], in1=bc(gz[:, :, 1:2], [128, 8, 16]), op=ALU.mult),
                             reads=[B_gat, B_gz], writes=[B_I12])
                        p.op("dve", lambda e: e.tensor_single_scalar(out=ku[:, 0, :, :], in_=posu[:], scalar=4, op=ALU.logical_shift_right), reads=[B_posu], writes=[B_ku])
                        p.op("dve", lambda e: e.tensor_single_scalar(out=ku[:, 1, :, :], in_=posu[:], scalar=15, op=ALU.bitwise_and), reads=[B_posu], writes=[B_ku])
                        p.op("dve", lambda e: e.tensor_copy(out=kf_[:], in_=ku[:]), reads=[B_ku], writes=[B_kf2])
                        yield
                        iv = idxf[:].rearrange("p (h c) k -> p h c k", c=2)
                        for c_ in range(2):
                            p.op("dve", lambda e, c_=c_: e.tensor_tensor(out=eq, in0=bc(kf_[:, c_, :, :].unsqueeze(3), [128, 8, 16, 16]),
                                                                          in1=bc(IOTA[:, 0:16].unsqueeze(1).unsqueeze(1), [128, 8, 16, 16]), op=ALU.is_equal),
                                 reads=[B_kf2, B_const], writes=[B_eq])
                            p.op("dve", lambda e, c_=c_, iv=iv: e.tensor_tensor(out=eq, in0=eq, in1=bc(iv[:, :, c_, :].unsqueeze(2), [128, 8, 16, 16]), op=ALU.mult),
                                 reads=[B_idxf, B_eq], writes=[B_eq])
                            p.op("dve", lambda e, c_=c_, I12=I12: e.tensor_reduce(out=I12[:, c_, :], in_=eq.rearrange("p h k j -> p (h k) j"), axis=AX.X, op=ALU.add),
                                 reads=[B_eq], writes=[B_I12])
                            yield

                def gbuild(gi):
                    for ti in range(2):
                        I12, B_I12 = I12_2[ti], B_I12_2[ti]
                        bk, bb = nextbank()
                        for w_ in range(3):
                            p.op("pe", lambda e, w_=w_, bk=bk, I12=I12: e.transpose(out=bk[:, w_ * 128:(w_ + 1) * 128], in_=I12[:, w_, :], identity=IDF), reads=[B_I12, B_const], writes=[bb], pe_accum=True)
                        p.op("act", lambda e, bk=bk: e.copy(out=I12T[:], in_=bk[:, 0:384].rearrange("p (w t) -> p w t", t=128)), reads=[bb], writes=[B_I12T])
                        for hf in range(4):
                            tsl = slice(hf * 32, (hf + 1) * 32)
                            p.op("dve", lambda e, tsl=tsl: e.tensor_tensor(out=OH1[:], in0=bc(iob[:].unsqueeze(1), [128, 32, 128]), in1=bc(I12T[:, 0, tsl].unsqueeze(2), [128, 32, 128]), op=ALU.is_equal),
                                 reads=[B_I12T, B_cc], writes=[B_OH1])
                            p.op("dve", lambda e, tsl=tsl: e.tensor_tensor(out=P2g[:], in0=bc(iob[:].unsqueeze(1), [128, 32, 128]), in1=bc(I12T[:, 1, tsl].unsqueeze(2), [128, 32, 128]), op=ALU.is_equal),
                                 reads=[B_I12T, B_cc], writes=[B_P2g, B_P2ga, B_P2gb])
                            p.op("dve", lambda e, hf=hf: e.tensor_tensor(out=P2g[:, 0:20, :], in0=P2g[:, 0:20, :], in1=bc(I12T[:, 2, hf * 32:hf * 32 + 20].unsqueeze(2), [128, 20, 128]), op=ALU.mult),
                                 reads=[B_I12T, B_P2g], writes=[B_P2ga])
                            p.op("pool", lambda e, hf=hf: e.tensor_tensor(out=P2g[:, 20:32, :], in0=P2g[:, 20:32, :], in1=bc(I12T[:, 2, hf * 32 + 20:hf * 32 + 32].unsqueeze(2), [128, 12, 128]), op=ALU.mult),
                                 reads=[B_I12T, B_P2g], writes=[B_P2gb])
                            for q4 in range(8):
                                bk, bb = nextbank()
                                for j in range(4):
                                    tl = q4 * 4 + j
                                    p.op("pe", lambda e, tl=tl, j=j, bk=bk: e.matmul(bk[:, j * 128:(j + 1) * 128], lhsT=P2g[:, tl, :], rhs=OH1[:, tl, :], start=True, stop=True),
                                         reads=[B_P2g, B_P2ga, B_P2gb, B_OH1], writes=[bb], pe_accum=True)
                                tg0 = ti * 128 + hf * 32 + q4 * 4
                                src = bk[:, 0:512].rearrange("p (t i) -> p i t", i=128)
                                p.op("act", lambda e, tg0=tg0, src=src: e.copy(out=Gs[:, :, tg0:tg0 + 4], in_=src), reads=[bb], writes=[B_Gs])

                def drain(gen, n):
                    if gen is None:
                        return None
                    for _ in range(n):
                        try:
                            next(gen)
                        except StopIteration:
                            return None
                    return gen

                def passes_and_epilogue(gi, ga, gb):
                    tok0 = gi * 256
                    xnT, B_xnT = xnT2[gi % 2], B_xnT2[gi % 2]
                    for c2 in range(64):
                        sl = sctr[0] % NSL
                        sctr[0] += 1
                        p.dma("sp", lambda e, sl=sl, c2=c2: e.dma_start(out=strm[sl][:], in_=ut_b[2 * c2:2 * c2 + 2].rearrange("c p f -> p c f")), f"cs{sl}", writes=[B_strm[sl]])
                        for cj in range(2):
                            c = 2 * c2 + cj
                            s_ = c % 2
                            bk, bb = nextbank()
                            for k in range(16):
                                p.op("pe", lambda e, sl=sl, cj=cj, k=k, bk=bk: e.matmul(bk[:, 0:256], lhsT=strm[sl][:, cj, k * 128:(k + 1) * 128], rhs=xnT[:, k, :], start=(k == 0), stop=(k == 15)),
                                     reads=[B_strm[sl], B_xnT], writes=[bb], pe_accum=True)
                            p.op("act", lambda e, s_=s_, bk=bk: e.activation(out=ge[s_][:], in_=bk[:, 0:256], func=AF.Gelu), reads=[bb], writes=[B_ge[s_]])
                            p.op("dve", lambda e, s_=s_, c=c: e.tensor_tensor(out=Gs[:, c, :], in0=Gs[:, c, :], in1=ge[s_][:], op=ALU.mult),
                                 reads=[B_ge[s_], B_Gs], writes=[B_Gs])
                        if ga is not None:
                            ga = drain(ga, 1)
                        else:
                            gb = drain(gb, 1)
                    ga = drain(ga, 10000)
                    for c2 in range(64):
                        sl = sctr[0] % NSL
                        sctr[0] += 1
                        p.dma("sp", lambda e, sl=sl, c2=c2: e.dma_start(out=strm[sl][:], in_=v_b[c2 * 256:(c2 + 1) * 256, :].rearrange("(c p) f -> p c f", p=128)), f"cs{sl}",
                              reads=[B_wuv], writes=[B_strm[sl]])
                        for cj in range(2):
                            c = 2 * c2 + cj
                            for ti in range(2):
                                for db in range(4):
                                    bi = ti * 4 + db
                                    p.op("pe", lambda e, sl=sl, cj=cj, c=c, ti=ti, db=db, bi=bi: e.matmul(banks[bi][:, 0:512], lhsT=Gs[:, c, ti * 128:(ti + 1) * 128], rhs=strm[sl][:, cj, db * 512:(db + 1) * 512],
                                                                                                   start=(c == 0), stop=(c == 127)), reads=[B_strm[sl], B_Gs], writes=[bank_buf[bi]], pe_accum=True)
                        gb = drain(gb, 1)
                    gb = drain(gb, 10000)
                    p.dma("sp", lambda e: e.dma_start(out=c_gf[:, 0, :], in_=rep_d[:, 2, :]), "cgf", writes=[B_gf])
                    for ti in range(2):
                        t = tok0 + ti * 128
                        p.dma("sp", lambda e, t=t: e.dma_start(out=x1t[:, 0, :], in_=x1_d[t:t + 128, :]), "cx", writes=[B_x1t])
                        for db in range(4):
                            bi = ti * 4 + db
                            p.op("dve", lambda e, db=db, bi=bi: e.tensor_tensor(out=x1t[:, 0, db * 512:(db + 1) * 512], in0=banks[bi][:, 0:512], in1=x1t[:, 0, db * 512:(db + 1) * 512], op=ALU.add),
                                 reads=[bank_buf[bi], B_x1t], writes=[B_x1t])
                        p.op("act", lambda e, ti=ti: e.activation(out=xn[:], in_=x1t[:, 0, :], func=AF.Square, accum_out=cst_[:, ti, 0:1]), reads=[B_x1t, B_xn], writes=[B_xn, B_cst])
                        p.op("act", lambda e, ti=ti: e.activation(out=cst_[:, ti, 1:2], in_=cst_[:, ti, 0:1], func=AF.Sqrt, scale=1.0 / D, bias=EPS), reads=[B_cst], writes=[B_cst])
                        p.op("dve", lambda e, ti=ti: e.reciprocal(out=cst_[:, ti, 2:3], in_=cst_[:, ti, 1:2]), reads=[B_cst], writes=[B_cst])
                        p.op("dve", lambda e, ti=ti: e.scalar_tensor_tensor(out=x1t[:, 0, :], in0=x1t[:, 0, :], scalar=cst_[:, ti, 2:3], in1=c_gf[:, 0, :], op0=ALU.mult, op1=ALU.mult),
                             reads=[B_cst, B_gf, B_x1t], writes=[B_x1t])
                        p.dma("pool", lambda e, t=t: e.dma_start(out=y_out[t:t + 128, :], in_=x1t[:, 0, :]), "cy", reads=[B_x1t])

                NGRP = T_OWN // 256
                drain(phaseA1(0), 10000)
                drain(phaseA2(0), 10000)
                for gi in range(NGRP):
                    gbuild(gi)
                    if gi + 1 < NGRP:
                        passes_and_epilogue(gi, phaseA1(gi + 1), phaseA2(gi + 1))
                    else:
                        passes_and_epilogue(gi, None, None)
                p.barrier()

        if "C" in stages:
            stage_c()
        p.barrier(skip=())
        p.emit()
    return nc


def host_inputs(inp, c):
    b, q = c // 4, c % 4
    f32 = np.float32
    xp = inp["x_prompt"][b]
    xs = inp["x_sample"][b]

    def win(x, lo, hi):
        n = x.shape[0]
        out = np.zeros((hi - lo, x.shape[1]), f32)
        a, bnd = max(lo, 0), min(hi, n)
        out[a - lo:bnd - lo] = x[a:bnd]
        return out

    own = []
    emask = np.zeros((128, NG_OWN, 2), f32)
    gi = 0
    for (x, L) in ((xp, 2048), (xs, 1024)):
        for g in range(L // 512):
            lo = q * L + g * 512 - 128
            own.append(win(x, lo, lo + 768))
            if lo < 0:
                emask[:, gi, 0] = -1e30
            if lo + 768 > x.shape[0]:
                emask[:, gi, 1] = -1e30
            gi += 1
    oth = []
    omask = np.zeros((128, NG_OTH, 4), f32)
    gi = 0
    for (x, L) in ((xp, 2048), (xs, 1024)):
        for j in [jj for jj in range(4) if jj != q]:
            for g in range(L // 512):
                lo = j * L + g * 512 - 2
                oth.append(win(x, lo, lo + 516))
                mf = 1.0 if j < q else 0.0
                omask[:, gi, :] = [mf, 1 - mf, 1 - mf, mf]
                gi += 1
    x_res = np.concatenate([xp[q * 2048:(q + 1) * 2048], xs[q * 1024:(q + 1) * 1024]], axis=0)
    rep = lambda v: np.broadcast_to(np.asarray(v, f32).reshape(1, -1), (128, np.asarray(v).size)).copy()
    rep_d = np.stack([rep(inp["g_mix"][0]), rep(inp["g_ffn"][0]), rep(inp["g_final"]), rep(inp["g_ssm_norm"][0])], axis=1)
    rep_s = np.zeros((128, 160), f32)
    rep_s[:, 0:32] = inp["a_log_f"][0]
    rep_s[:, 32:64] = inp["a_log_b"][0]
    rep_s[:, 64:96] = inp["d_skip"][0]
    rep_s[:, 96:112] = inp["attn_sink"][0]
    slopes = np.exp2(-8.0 * np.arange(1, 17, dtype=np.float64) / 16)
    qi = np.arange(128)[:, None]
    km = np.arange(384)[None, :]
    rel = qi - km + 128
    ab = np.where(np.abs(rel) <= 128, -np.abs(rel).astype(np.float64), -1e30).astype(f32)
    convw = np.ascontiguousarray(inp["conv_w"][0].T.reshape(24, 128, 5).transpose(1, 0, 2))
    convb = np.ascontiguousarray(inp["conv_b"][0].reshape(24, 128).T)
    dtb = np.concatenate([inp["dt_bias_f"][0], inp["dt_bias_b"][0]]).reshape(64, 1).astype(f32)
    cst = np.zeros((128, 9, 128), f32)
    s_ = np.arange(128)[:, None]
    l_ = np.arange(128)[None, :]
    cst[:, 0] = np.eye(128)
    cst[:, 1] = (s_ <= l_)
    cst[:, 2] = (s_ >= l_)
    cst[:, 3] = 1.0
    cst[:, 4] = l_
    import ml_dtypes
    negm = np.zeros((128, 2, 512), f32)
    negm[:, 0] = np.tile(np.where(s_ > l_, NEG, 0.0), (1, 4))
    negm[:, 1] = np.tile(np.where(s_ < l_, NEG, 0.0), (1, 4))
    sel = np.zeros((32, 32, 128), f32)
    for h in range(32):
        sel[h, h, :] = 1.0
    return {
        "x_own": np.stack(own), "x_oth": np.stack(oth), "x_res": x_res,
        "w_in": inp["w_in"][0], "w_ao": inp["w_attn_o"][0], "w_so": inp["w_ssm_o"][0], "w_out": inp["w_out"][0],
        "w_q": inp["w_query"][0], "keys": inp["sub_keys"][0].reshape(16, 128, 128),
        "exp_u": inp["expert_u"][0], "exp_v": inp["expert_v"][0],
        "rep_d": rep_d, "rep_s": rep_s, "attn_bias": ab, "emask": emask, "omask": omask,
        "convw": convw, "convb": convb, "dtb": dtb, "cst": cst, "negm": negm.astype(ml_dtypes.bfloat16), "sel": sel,
    }


def kernel(**inputs):
    inp = {k: np.asarray(v) for k, v in inputs.items()}
    nc = build()
    in_maps = [host_inputs(inp, c) for c in range(NCORES)]
    res = run_bass_kernel_spmd(nc, in_maps, core_ids=list(range(NCORES)))
    yp = np.zeros((2, 8192, D), np.float32)
    ys = np.zeros((2, 4096, D), np.float32)
    for c in range(NCORES):
        b, q = c // 4, c % 4
        y = res.results[c]["y_out"]
        yp[b, q * 2048:(q + 1) * 2048] = y[0:2048]
        ys[b, q * 1024:(q + 1) * 1024] = y[2048:3072]
    return (yp, ys)
```

```python
import contextlib
import numpy as np
import concourse.bass as bass
import concourse.mybir as mybir
from concourse.bass_utils import run_bass_kernel_spmd

F32 = mybir.dt.float32
BF16 = mybir.dt.bfloat16
U32 = mybir.dt.uint32
AF = mybir.ActivationFunctionType
ALU = mybir.AluOpType
AX = mybir.AxisListType
ENGS = ("pe", "act", "dve", "pool", "sp")

D = 2048
INW = 10816
NCORES = 8
Q_END, K_END, V_END, Z_END, XBC_END, DT_END = 1024, 1280, 1536, 3584, 6656, 6720
NEG = -30000.0
EPS = 1e-6
OWN_GROUPS = [(0, 4), (1, 2)]
NG_OWN = 6
NG_OTH = 18
T_OWN = 3072


class Buf:
    __slots__ = ("w", "r")

    def __init__(self):
        self.w = None
        self.r = []


class Prog:
    def __init__(self, nc):
        self.nc = nc
        self.ops = {e: [] for e in ENGS}
        self.dma_sems = {}
        self.waited = {e: {} for e in ENGS}

    def _need(self, eng, dep, waits):
        kind, key, val = dep
        k = (kind, key)
        if self.waited[eng].get(k, -1) >= val:
            return
        self.waited[eng][k] = val
        waits.append(dep)
        if kind == "e":
            self.ops[key][val]["inc"] = True

    def _deps(self, eng, reads, writes, pe_accum):
        waits = []
        for b in reads:
            if b.w is not None:
                self._need(eng, b.w, waits)
        for b in writes:
            if b.w is not None:
                if not (pe_accum and eng == "pe" and b.w[0] == "e" and b.w[1] == "pe"):
                    self._need(eng, b.w, waits)
            for d in b.r:
                self._need(eng, d, waits)
        return waits

    def op(self, eng, fn, reads=(), writes=(), pe_accum=False):
        waits = self._deps(eng, reads, writes, pe_accum)
        idx = len(self.ops[eng])
        self.ops[eng].append(dict(fn=fn, waits=waits, inc=False, dma=None))
        me = ("e", eng, idx)
        for b in reads:
            b.r.append(me)
        for b in writes:
            b.w = me
            b.r = []
        return me

    def dma(self, eng, fn, sem, reads=(), writes=()):
        waits = self._deps(eng, reads, writes, False)
        self.dma_sems[sem] = self.dma_sems.get(sem, 0) + 16
        val = self.dma_sems[sem]
        self.ops[eng].append(dict(fn=fn, waits=waits, inc=False, dma=(sem, 16)))
        me = ("d", sem, val)
        for b in reads:
            b.r.append(me)
        for b in writes:
            b.w = me
            b.r = []
        return me

    def barrier(self, skip=tuple(["cast_w", "cast_uv"] + [f"ci{i}" for i in range(32)])):
        deps = []
        for e in ENGS:
            for i in range(len(self.ops[e]) - 1, -1, -1):
                o = self.ops[e][i]
                if o["fn"] is not None and o["dma"] is None:
                    deps.append(("e", e, i))
                    break
        for s, v in self.dma_sems.items():
            if s not in skip:
                deps.append(("d", s, v))
        for e in ENGS:
            waits = []
            for d in deps:
                self._need(e, d, waits)
            self.ops[e].append(dict(fn=None, waits=waits, inc=False, dma=None))

    def emit(self):
        nc = self.nc
        with contextlib.ExitStack() as st:
            esem = {e: st.enter_context(nc.semaphore("s_" + e)) for e in ENGS}
            dsem = {n: st.enter_context(nc.semaphore("d_" + n)) for n in self.dma_sems}
            cum = {}
            for e in ENGS:
                c = 0
                arr = []
                for o in self.ops[e]:
                    if o["inc"]:
                        c += 1
                    arr.append(c)
                cum[e] = arr
            block = st.enter_context(nc.Block())

            def run(engname, engobj):
                for o in self.ops[engname]:
                    for (kind, key, val) in o["waits"]:
                        if kind == "e":
                            engobj.wait_ge(esem[key], cum[key][val])
                        else:
                            engobj.wait_ge(dsem[key], val)
                    if o["fn"] is None:
                        continue
                    ins = o["fn"](engobj)
                    if o["dma"] is not None:
                        ins.then_inc(dsem[o["dma"][0]], o["dma"][1])
                    elif o["inc"]:
                        ins.then_inc(esem[engname], 1)

            block.tensor(lambda e: run("pe", e))
            block.scalar(lambda e: run("act", e))
            block.vector(lambda e: run("dve", e))
            block.gpsimd(lambda e: run("pool", e))
            block.sync(lambda e: run("sp", e))


def bc(ap, shape):
    return ap.to_broadcast(list(shape))


class TilePool:
    def __init__(self, nc, stack):
        self.nc, self.stack, self.tiles, self.bufs, self.i, self.first = nc, stack, {}, [], 0, True

    def begin(self):
        self.first = (len(self.tiles) == 0)
        self.i = 0

    def sb(self, name, shape, dt):
        if name not in self.tiles:
            self.tiles[name] = self.stack.enter_context(self.nc.sbuf_tensor(name, list(shape), dt))
        return self.tiles[name]

    def buf(self):
        if self.i == len(self.bufs):
            self.bufs.append(Buf())
        b = self.bufs[self.i]
        self.i += 1
        return b


def build(stages=("W", "A", "S", "B", "C"), dbg=()):
    nc = bass.Bass("TRN2", target_bir_lowering=False)
    p = Prog(nc)

    def din(name, shape, dt=F32):
        return nc.dram_tensor(name, list(shape), dt, kind="ExternalInput").ap()

    def dscr(name, shape, dt):
        kind = "ExternalOutput" if name in dbg else "Internal"
        return nc.dram_tensor(name, list(shape), dt, kind=kind).ap()

    x_own = din("x_own", [NG_OWN, 768, D])
    x_oth = din("x_oth", [NG_OTH, 516, D])
    x_res = din("x_res", [T_OWN, D])
    w_in = din("w_in", [D, INW])
    w_ao = din("w_ao", [1024, D])
    w_so = din("w_so", [D, D])
    w_out = din("w_out", [D, D])
    w_q = din("w_q", [D, D])
    keys = din("keys", [16, 128, 128])
    exp_u = din("exp_u", [16384, D])
    exp_v = din("exp_v", [16384, D])
    rep_d = din("rep_d", [128, 4, D])
    rep_s = din("rep_s", [128, 160])
    attn_bias = din("attn_bias", [128, 384])
    emask = din("emask", [128, NG_OWN, 2])
    omask = din("omask", [128, NG_OTH, 4])
    convw = din("convw", [128, 24, 5])
    convb = din("convb", [128, 24])
    dtb = din("dtb", [64, 1])
    cst = din("cst", [128, 9, 128])
    negm = din("negm", [128, 2, 512], BF16)
    sel = din("sel", [32, 32, 128])
    y_out = nc.dram_tensor("y_out", [T_OWN, D], F32, kind="ExternalOutput").ap()

    w_in_b = dscr("w_in_b", [D, INW], BF16)
    w_ao_b = dscr("w_ao_b", [1024, D], BF16)
    w_so_b = dscr("w_so_b", [D, D], BF16)
    w_out_b = dscr("w_out_b", [D, D], BF16)
    w_q_b = dscr("w_q_b", [D, D], BF16)
    v_b = dscr("v_b", [16384, D], BF16)
    u_b = dscr("u_b", [16384, D], BF16)
    ut_b = dscr("ut_b", [128, 128, D], BF16)
    zs_d = dscr("zs_d", [T_OWN, D], BF16)
    gT_d = dscr("gT_d", [32, 128, T_OWN], BF16)
    Xs_d = dscr("Xs_d", [T_OWN, D], BF16)
    Bs_d = dscr("Bs_d", [T_OWN, 512], BF16)
    BT_d = dscr("BT_d", [4, 128, T_OWN], BF16)
    CT_d = dscr("CT_d", [4, 128, T_OWN], BF16)
    dt_d = dscr("dt_d", [T_OWN, 64], F32)
    aT_d = dscr("aT_d", [8, 128, T_OWN], BF16)
    hb_d = dscr("hb_d", [24, 128, D], BF16)
    hin_d = dscr("hin_d", [4, 128, D], F32)
    x1_d = dscr("x1_d", [T_OWN, D], F32)

    with contextlib.ExitStack() as top:
        def SB(stack, name, shape, dt):
            return stack.enter_context(nc.sbuf_tensor(name, list(shape), dt))

        banks = [top.enter_context(nc.psum_tensor(f"bank{i}", [128, 512], F32)) for i in range(8)]
        bank_buf = [Buf() for _ in range(8)]
        bank_ctr = [0]

        def nextbank():
            i = bank_ctr[0] % 8
            bank_ctr[0] += 1
            return banks[i], bank_buf[i]

        c_cst = SB(top, "c_cst", [128, 9, 128], F32)
        c_idb = SB(top, "c_idb", [128, 128], BF16)
        c_reps = SB(top, "c_reps", [128, 160], F32)
        c_arep = SB(top, "c_arep", [128, 64], F32)
        B_const = Buf()
        p.dma("sp", lambda e: e.dma_start(out=c_cst[:], in_=cst), "c0", writes=[B_const])
        p.dma("sp", lambda e: e.dma_start(out=c_reps[:], in_=rep_s), "c0", writes=[B_const])
        p.op("dve", lambda e: e.tensor_copy(out=c_idb[:], in_=c_cst[:, 0, :]), reads=[B_const], writes=[B_const])
        p.op("act", lambda e: e.activation(out=c_arep[:], in_=c_reps[:, 0:64], func=AF.Exp), reads=[B_const], writes=[B_const])
        p.op("dve", lambda e: e.tensor_scalar(out=c_arep[:], in0=c_arep[:], scalar1=-1.0, scalar2=None, op0=ALU.mult),
             reads=[B_const], writes=[B_const])
        IDF = c_cst[:, 0, :]
        TRIF = c_cst[:, 1, :]
        TRIB = c_cst[:, 2, :]
        ONES = c_cst[:, 3, :]
        IOTA = c_cst[:, 4, :]

        B_win, B_w, B_wuv = {}, Buf(), Buf()
        lazy_casts = []

        def emit_casts(n):
            for _ in range(n):
                if lazy_casts:
                    lazy_casts.pop(0)()
        WBLOCKS = ([(Z_END + i * 512, 512) for i in range(6)] + [(XBC_END, 64)] + [(V_END + i * 512, 512) for i in range(4)]
                   + [(DT_END + i * 512, 512) for i in range(8)] + [(0, 512), (512, 512), (Q_END, 512)])
        if "W" in stages:
            def cast(dst, src, rows, cols, rblk, sem, buf):
                for r0 in range(0, rows, rblk):
                    lazy_casts.append(lambda r0=r0, dst=dst, src=src, rblk=rblk, sem=sem, buf=buf: p.dma(
                        "pool", lambda e: e.dma_start(out=dst[r0:r0 + rblk, :], in_=src[r0:r0 + rblk, :], max_dma_last_dim=4096), sem, writes=[buf]))
            for i, (c0, ncol) in enumerate(WBLOCKS):
                B_win[c0] = Buf()
                p.dma("pool", lambda e, c0=c0, ncol=ncol: e.dma_start(out=w_in_b[:, c0:c0 + ncol], in_=w_in[:, c0:c0 + ncol], max_dma_last_dim=4096),
                      f"ci{i}", writes=[B_win[c0]])
            cast(w_ao_b, w_ao, 1024, D, 512, "cast_w", B_w)
            cast(w_so_b, w_so, D, D, 512, "cast_w", B_w)
            cast(w_out_b, w_out, D, D, 512, "cast_w", B_w)
            cast(w_q_b, w_q, D, D, 512, "cast_w", B_w)
            if "C" in stages:
                cast(u_b, exp_u, 16384, D, 1024, "cast_uv", B_wuv)
                cast(v_b, exp_v, 16384, D, 1024, "cast_uv", B_wuv)

        def rms_to_hT(st, xt_tiles, ntiles, hT, g_idx, col0s, nrows=None):
            pass

        def prep_group(pool, xsrc, nwin, own, gidx, tok0, par=0, res=None):
            pool.begin()
            Buf = pool.buf
            tg = "A" if own else "S"
            lo = 128 if own else 2
            ntile = (nwin + 127) // 128
            hT = pool.sb(f"hT{tg}", [128, 16, nwin], BF16)
            B_hT = Buf()
            xin = [pool.sb(f"xin{tg}_{i}", [128, D], F32) for i in range(2)]
            xn = [pool.sb(f"xn{tg}_{i}", [128, D], BF16) for i in range(2)]
            c_g = pool.sb(f"cg{tg}", [128, D], F32)
            B_cg = Buf()
            if pool.first:
                p.dma("sp", lambda e: e.dma_start(out=c_g[:], in_=rep_d[:, 0, :]), "c1", writes=[B_cg])
            stat = pool.sb(f"stat{tg}", [128, 8, 4], F32)
            B_xin = [Buf(), Buf()]
            B_xn = [Buf(), Buf()]
            B_stat = Buf()
            for ti in range(ntile):
                r0 = ti * 128
                rows = min(128, nwin - r0)
                s = ti % 2
                p.dma("sp", lambda e, s=s, r0=r0, rows=rows: e.dma_start(out=xin[s][0:rows, :], in_=xsrc[r0:r0 + rows, :]),
                      f"xin{s}", writes=[B_xin[s]])
                p.op("act", lambda e, s=s, rows=rows, ti=ti: e.activation(out=xn[s][0:rows, :], in_=xin[s][0:rows, :], func=AF.Square,
                                                                         accum_out=stat[0:rows, ti, 0:1]),
                     reads=[B_xin[s]], writes=[B_xn[s], B_stat])
                p.op("act", lambda e, rows=rows, ti=ti: e.activation(out=stat[0:rows, ti, 1:2], in_=stat[0:rows, ti, 0:1], func=AF.Sqrt,
                                                                    scale=1.0 / D, bias=EPS), reads=[B_stat], writes=[B_stat])
                p.op("dve", lambda e, rows=rows, ti=ti: e.reciprocal(out=stat[0:rows, ti, 2:3], in_=stat[0:rows, ti, 1:2]),
                     reads=[B_stat], writes=[B_stat])
                p.op("dve", lambda e, s=s, rows=rows, ti=ti: e.scalar_tensor_tensor(
                    out=xn[s][0:rows, :], in0=xin[s][0:rows, :], scalar=stat[0:rows, ti, 2:3], in1=c_g[0:rows, :],
                    op0=ALU.mult, op1=ALU.mult), reads=[B_xin[s], B_stat, B_cg], writes=[B_xn[s]])
                for half in range(2):
                    bk, bb = nextbank()
                    bkb = bk[:].bitcast(BF16)
                    for kk in range(8):
                        k = half * 8 + kk
                        p.op("pe", lambda e, s=s, rows=rows, k=k, kk=kk, bkb=bkb: e.transpose(
                            out=bkb[:, kk * 128:kk * 128 + rows], in_=xn[s][0:rows, k * 128:(k + 1) * 128], identity=c_idb[0:rows, 0:rows]),
                            reads=[B_xn[s], B_const], writes=[bb], pe_accum=True)
                    eng = "act" if half == 0 else "dve"
                    src = bkb.rearrange("p (k t) -> p k t", t=128)[:, :, 0:rows]
                    dst = hT[:, half * 8:half * 8 + 8, r0:r0 + rows]
                    if eng == "act":
                        p.op("act", lambda e, src=src, dst=dst: e.copy(out=dst, in_=src), reads=[bb], writes=[B_hT])
                    else:
                        p.op("dve", lambda e, src=src, dst=dst: e.tensor_copy(out=dst, in_=src), reads=[bb], writes=[B_hT])
                yield

            wt = [pool.sb(f"wt{tg}_{i}", [128, 16, 512], BF16) for i in range(2)]
            B_wt = [Buf(), Buf()]
            wctr = [0]

            def load_w(c0, ncol):
                s = wctr[0] % 2
                wctr[0] += 1
                src = w_in_b[:, c0:c0 + ncol].rearrange("(k p) c -> p k c", p=128)
                p.dma("sp", lambda e, s=s, src=src, ncol=ncol: e.dma_start(out=wt[s][:, :, 0:ncol], in_=src), f"wt{s}",
                      reads=[B_win[c0]], writes=[B_wt[s]])
                return wt[s], B_wt[s]

            def fm_proj(wtile, wb, wc0, M, t0, N):
                bk, bb = nextbank()
                for k in range(16):
                    p.op("pe", lambda e, k=k, bk=bk: e.matmul(bk[0:M, 0:N], lhsT=wtile[:, k, wc0:wc0 + M], rhs=hT[:, k, t0:t0 + N],
                                                               start=(k == 0), stop=(k == 15)),
                         reads=[wb, B_hT], writes=[bb], pe_accum=True)
                return bk, bb

            def tm_proj(wtile, wb, wc0, Ncol, t0, rows=128):
                bk, bb = nextbank()
                for k in range(16):
                    p.op("pe", lambda e, k=k, bk=bk: e.matmul(bk[0:rows, 0:Ncol], lhsT=hT[:, k, t0:t0 + rows], rhs=wtile[:, k, wc0:wc0 + Ncol],
                                                               start=(k == 0), stop=(k == 15)),
                         reads=[wb, B_hT], writes=[bb], pe_accum=True)
                return bk, bb

            NT = 512
            nconv = 24 if own else 20
            xbc = [pool.sb(f"xbc{tg}_{i}", [128, 516], BF16) for i in range(2)]
            dg = [pool.sb(f"dg{tg}_{i}", [128, 5, 128], BF16) for i in range(2)]
            B_dg = [Buf(), Buf()]
            B_xbc = [Buf(), Buf()]
            csil = [pool.sb(f"csil{tg}_{i}", [128, 512], BF16) for i in range(2)]
            B_csil = [Buf(), Buf()]
            c_cw = pool.sb(f"cw{tg}", [128, 24, 5], F32)
            c_cb = pool.sb(f"cb{tg}", [128, 24], F32)
            c_dtb = pool.sb(f"dtb{tg}", [64, 1], F32)
            B_cw = Buf()
            if pool.first:
                p.dma("sp", lambda e: e.dma_start(out=c_cw[:], in_=convw), "c1", writes=[B_cw])
                p.dma("sp", lambda e: e.dma_start(out=c_cb[:], in_=convb), "c1", writes=[B_cw])
                p.dma("sp", lambda e: e.dma_start(out=c_dtb[:], in_=dtb), "c1", writes=[B_cw])
            npar = 1 if own else 2
            Xtm = [pool.sb(f"Xtm{tg}{i}", [128, 4, D], BF16) for i in range(npar)][par]
            Btm = [pool.sb(f"Btm{tg}{i}", [128, 4, 512], BF16) for i in range(npar)][par]
            dttm = [pool.sb(f"dttm{tg}{i}", [128, 4, 64], F32) for i in range(npar)][par]
            B_Xtm = [Buf() for _ in range(npar)][par]
            B_Btm = [Buf() for _ in range(npar)][par]
            B_dttm = [Buf() for _ in range(npar)][par]
            w0 = lo - 2
            pending = [None]
            pending_tr = [None]
            pending_conv = [None]
            own_stores = []
            for cc4 in range(0, nconv, 4):
                wtile, wb = load_w(V_END + D + cc4 * 128, 512)
                for ci in range(4):
                    cc = cc4 + ci
                    s = cc % 2
                    for hf in range(2):
                        bk, bb = fm_proj(wtile, wb, ci * 128, 128, w0 + hf * 258, 258)
                        p.op("act", lambda e, s=s, hf=hf, bk=bk: e.copy(out=xbc[s][:, hf * 258:(hf + 1) * 258], in_=bk[:, 0:258]),
                             reads=[bb], writes=[B_xbc[s]])
                    if pending[0] is not None:
                        pending[0]()
                        pending[0] = None
                    if pending_conv[0] is not None:
                        pending_conv[0]()
                        pending_conv[0] = None
                        pending[0], pending_tr[0] = pending_tr[0], None
                    p.op("dve", lambda e, s=s, cc=cc: e.tensor_tensor(out=dg[s][:], in0=bc(c_idb[:].unsqueeze(1), [128, 5, 128]),
                                                                      in1=bc(c_cw[:, cc, :].unsqueeze(2), [128, 5, 128]), op=ALU.mult),
                         reads=[B_cw, B_const], writes=[B_dg[s]])

                    def conv_chunk(s=s, cc=cc):
                        bkc, bbc = nextbank()
                        for j in range(5):
                            p.op("pe", lambda e, j=j: e.matmul(bkc[:, 0:512], lhsT=dg[s][:, j, :], rhs=xbc[s][:, j:j + 512], start=(j == 0), stop=(j == 4)),
                                 reads=[B_dg[s], B_xbc[s]], writes=[bbc], pe_accum=True)
                        p.op("act", lambda e: e.activation(out=csil[s][:], in_=bkc[:, 0:512], func=AF.Silu, bias=c_cb[:, cc:cc + 1]),
                             reads=[bbc, B_cw], writes=[B_csil[s]])
                        if own and cc >= 16:
                            g = (cc - 16) % 4
                            dst = (BT_d if cc < 20 else CT_d)[g, :, tok0:tok0 + NT]
                            p.dma("pool", lambda e: e.dma_start(out=dst, in_=csil[s][:]), f"stc{s}", reads=[B_csil[s]])
                    pending_conv[0] = conv_chunk
                    if cc < 20:
                        def tr_chunk(s=s, cc=cc):
                            bk, bb = nextbank()
                            bkb = bk[:].bitcast(BF16)
                            for ti in range(4):
                                p.op("pe", lambda e, ti=ti: e.transpose(out=bkb[:, ti * 128:(ti + 1) * 128],
                                                                         in_=csil[s][:, ti * 128:(ti + 1) * 128], identity=c_idb[:]),
                                     reads=[B_csil[s], B_const], writes=[bb], pe_accum=True)
                            src = bkb[:, 0:512].rearrange("p (t c) -> p t c", c=128)
                            if cc < 16:
                                p.op("act", lambda e: e.copy(out=Xtm[:, :, cc * 128:(cc + 1) * 128], in_=src), reads=[bb], writes=[B_Xtm])
                            else:
                                p.op("act", lambda e: e.copy(out=Btm[:, :, (cc - 16) * 128:(cc - 15) * 128], in_=src), reads=[bb], writes=[B_Btm])
                        pending_tr[0] = tr_chunk
                    yield
            wtile, wb = load_w(XBC_END, 64)
            bk, bb = fm_proj(wtile, wb, 0, 64, lo, NT)
            for q_ in (pending, pending_conv, pending_tr):
                if q_[0] is not None:
                    q_[0]()
                    q_[0] = None
            dtf = pool.sb(f"dtf{tg}", [64, 512], F32)
            B_dtf = Buf()
            p.op("act", lambda e, bk=bk: e.activation(out=dtf[:], in_=bk[0:64, 0:512], func=AF.Exp, bias=c_dtb[:, 0:1]),
                 reads=[bb, B_cw], writes=[B_dtf])
            p.op("act", lambda e: e.activation(out=dtf[:], in_=dtf[:], func=AF.Ln, bias=1.0), reads=[B_dtf], writes=[B_dtf])
            bk, bb = nextbank()
            for ti in range(4):
                p.op("pe", lambda e, ti=ti, bk=bk: e.transpose(out=bk[:, ti * 64:(ti + 1) * 64], in_=dtf[:, ti * 128:(ti + 1) * 128],
                                                                identity=c_cst[0:64, 0, 0:64]),
                     reads=[B_dtf, B_const], writes=[bb], pe_accum=True)
            p.op("dve", lambda e, bk=bk: e.tensor_copy(out=dttm[:], in_=bk[:, 0:256].rearrange("p (t c) -> p t c", c=64)),
                 reads=[bb], writes=[B_dttm])
            if res is not None:
                res.update(Xtm=Xtm, Btm=Btm, dttm=dttm, B_Xtm=B_Xtm, B_Btm=B_Btm, B_dttm=B_dttm)
            yield
            if not own:
                return
            for ti in range(4):
                t = tok0 + ti * 128
                p.dma("pool", lambda e, ti=ti, t=t: e.dma_start(out=Xs_d[t:t + 128, :], in_=Xtm[:, ti, :]), "stX", reads=[B_Xtm])
                p.dma("pool", lambda e, ti=ti, t=t: e.dma_start(out=Bs_d[t:t + 128, :], in_=Btm[:, ti, :]), "stX", reads=[B_Btm])
                p.dma("pool", lambda e, ti=ti, t=t: e.dma_start(out=dt_d[t:t + 128, :], in_=dttm[:, ti, :]), "stX", reads=[B_dttm])

            zt = [pool.sb(f"zt{tg}_{i}", [128, 512], BF16) for i in range(2)]
            B_zt = [Buf(), Buf()]
            zc = 0
            for cb4 in range(4):
                wtile, wb = load_w(V_END + cb4 * 512, 512)
                for ti in range(4):
                    bk, bb = tm_proj(wtile, wb, 0, 512, lo + ti * 128)
                    s = zc % 2
                    zc += 1
                    p.op("act", lambda e, s=s, bk=bk: e.activation(out=zt[s][:], in_=bk[:], func=AF.Silu), reads=[bb], writes=[B_zt[s]])
                    t = tok0 + ti * 128
                    p.dma("pool", lambda e, s=s, t=t, cb4=cb4: e.dma_start(out=zs_d[t:t + 128, cb4 * 512:(cb4 + 1) * 512], in_=zt[s][:]),
                          f"stz{s}", reads=[B_zt[s]])
            gt = [pool.sb(f"gt{tg}_{i}", [128, 512], BF16) for i in range(2)]
            B_gt = [Buf(), Buf()]
            for cb4 in range(8):
                wtile, wb = load_w(DT_END + cb4 * 512, 512)
                for ci in range(4):
                    cc = cb4 * 4 + ci
                    bk, bb = fm_proj(wtile, wb, ci * 128, 128, lo, NT)
                    s = cc % 2
                    p.op("act", lambda e, s=s, bk=bk: e.activation(out=gt[s][:], in_=bk[:], func=AF.Sigmoid), reads=[bb], writes=[B_gt[s]])
                    p.dma("pool", lambda e, s=s, cc=cc: e.dma_start(out=gT_d[cc, :, tok0:tok0 + NT], in_=gt[s][:]), f"stg{s}", reads=[B_gt[s]])
            qT = pool.sb(f"qT{tg}", [64, 16, 512], BF16)
            kT = pool.sb(f"kT{tg}", [64, 4, 768], BF16)
            vt = pool.sb(f"vt{tg}", [128, 6, 256], BF16)
            B_qT, B_kT, B_vt = Buf(), Buf(), Buf()
            for cb4 in range(2):
                wtile, wb = load_w(cb4 * 512, 512)
                for hh in range(8):
                    h = cb4 * 8 + hh
                    bk, bb = fm_proj(wtile, wb, hh * 64, 64, lo, NT)
                    p.op("act", lambda e, h=h, bk=bk: e.activation(out=qT[:, h, :], in_=bk[0:64, :], func=AF.Copy, scale=0.125),
                         reads=[bb], writes=[B_qT])
            wtile, wb = load_w(Q_END, 512)
            for kv in range(4):
                for hf in range(2):
                    bk, bb = fm_proj(wtile, wb, kv * 64, 64, hf * 384, 384)
                    p.op("dve", lambda e, kv=kv, hf=hf, bk=bk: e.tensor_copy(out=kT[:, kv, hf * 384:(hf + 1) * 384], in_=bk[0:64, 0:384]),
                         reads=[bb], writes=[B_kT])
            for ti in range(6):
                bk, bb = tm_proj(wtile, wb, 256, 256, ti * 128)
                p.op("act", lambda e, ti=ti, bk=bk: e.copy(out=vt[:, ti, :], in_=bk[:, 0:256]), reads=[bb], writes=[B_vt])

            c_ab = pool.sb(f"ab{tg}", [128, 384], F32)
            c_em = pool.sb(f"em{tg}", [128, NG_OWN, 2], F32)
            B_ab = Buf()
            if pool.first:
                p.dma("sp", lambda e: e.dma_start(out=c_ab[:], in_=attn_bias), "c1", writes=[B_ab])
                p.dma("sp", lambda e: e.dma_start(out=c_em[:], in_=emask), "c1", writes=[B_ab])
            sc = [pool.sb(f"sc{tg}_{i}", [128, 4, 384], F32) for i in range(2)]
            pr = [pool.sb(f"pr{tg}_{i}", [128, 4, 384], BF16) for i in range(2)]
            prT = [pool.sb(f"prT{tg}_{i}", [128, 4, 3, 128], BF16) for i in range(2)]
            ast = [pool.sb(f"ast{tg}_{i}", [128, 4, 8], F32) for i in range(2)]
            B_sc, B_pr, B_prT, B_ast = [Buf(), Buf()], [Buf(), Buf()], [Buf(), Buf()], [Buf(), Buf()]
            atm = pool.sb(f"atm{tg}", [128, 1024], BF16)
            B_atm = Buf()
            aTt = pool.sb(f"aTt{tg}", [128, 8, 512], BF16)
            B_aTt = Buf()
            def att_head(j, kv, s):
                sbanks = []
                for g in range(4):
                    h = kv * 4 + g
                    bk, bb = nextbank()
                    p.op("pe", lambda e, h=h, kv=kv, j=j, bk=bk: e.matmul(bk[:, 0:384], lhsT=qT[:, h, j * 128:(j + 1) * 128],
                                                                          rhs=kT[:, kv, j * 128:j * 128 + 384], start=True, stop=True),
                         reads=[B_qT, B_kT], writes=[bb])
                    p.op("dve", lambda e, s=s, g=g, h=h, bk=bk: e.scalar_tensor_tensor(out=sc[s][:, g, :], in0=c_ab[:], scalar=float(2.0 ** (-8.0 * (h + 1) / 16)),
                                                                                       in1=bk[:, 0:384], op0=ALU.mult, op1=ALU.add),
                         reads=[bb, B_ab], writes=[B_sc[s]])
                if j == 0:
                    p.op("dve", lambda e, s=s: e.tensor_scalar(out=sc[s][:, :, 0:128], in0=sc[s][:, :, 0:128], scalar1=c_em[:, gidx, 0:1],
                                                               scalar2=None, op0=ALU.add), reads=[B_sc[s], B_ab], writes=[B_sc[s]])
                if j == 3:
                    p.op("dve", lambda e, s=s: e.tensor_scalar(out=sc[s][:, :, 256:384], in0=sc[s][:, :, 256:384], scalar1=c_em[:, gidx, 1:2],
                                                               scalar2=None, op0=ALU.add), reads=[B_sc[s], B_ab], writes=[B_sc[s]])
                p.op("dve", lambda e, s=s: e.tensor_reduce(out=ast[s][:, :, 0], in_=sc[s][:], axis=AX.X, op=ALU.max),
                     reads=[B_sc[s]], writes=[B_ast[s]])
                p.op("dve", lambda e, s=s, kv=kv: e.tensor_tensor(out=ast[s][:, :, 1], in0=ast[s][:, :, 0], in1=c_reps[:, 96 + kv * 4:100 + kv * 4],
                                                                  op=ALU.max), reads=[B_ast[s], B_const], writes=[B_ast[s]])
                p.op("dve", lambda e, s=s: e.tensor_scalar(out=ast[s][:, :, 2], in0=ast[s][:, :, 1], scalar1=-1.0, scalar2=None, op0=ALU.mult),
                     reads=[B_ast[s]], writes=[B_ast[s]])
                for g in range(4):
                    p.op("act", lambda e, s=s, g=g: e.activation(out=pr[s][:, g, :], in_=sc[s][:, g, :], func=AF.Exp, bias=ast[s][:, g, 2:3],
                                                                 accum_out=ast[s][:, g, 3:4]), reads=[B_sc[s], B_ast[s]], writes=[B_pr[s], B_ast[s]])
                p.op("dve", lambda e, s=s, kv=kv: e.tensor_tensor(out=ast[s][:, :, 4], in0=c_reps[:, 96 + kv * 4:100 + kv * 4], in1=ast[s][:, :, 1],
                                                                  op=ALU.subtract), reads=[B_ast[s], B_const], writes=[B_ast[s]])
                p.op("act", lambda e, s=s: e.activation(out=ast[s][:, :, 5], in_=ast[s][:, :, 4], func=AF.Exp), reads=[B_ast[s]], writes=[B_ast[s]])
                p.op("dve", lambda e, s=s: e.tensor_tensor(out=ast[s][:, :, 6], in0=ast[s][:, :, 5], in1=ast[s][:, :, 3], op=ALU.add),
                     reads=[B_ast[s]], writes=[B_ast[s]])
                p.op("dve", lambda e, s=s: e.reciprocal(out=ast[s][:, :, 7], in_=ast[s][:, :, 6]), reads=[B_ast[s]], writes=[B_ast[s]])

            def att_tail(j, kv, s):
                for g in range(4):
                    bk, bb = nextbank()
                    bkb = bk[:].bitcast(BF16)
                    for m in range(3):
                        p.op("pe", lambda e, s=s, g=g, m=m, bkb=bkb: e.transpose(out=bkb[:, m * 128:(m + 1) * 128], in_=pr[s][:, g, m * 128:(m + 1) * 128],
                                                                                identity=c_idb[:]), reads=[B_pr[s], B_const], writes=[bb], pe_accum=True)
                    eng = "act" if g % 2 == 0 else "dve"
                    if eng == "act":
                        p.op("act", lambda e, s=s, g=g, bkb=bkb: e.copy(out=prT[s][:, g, :, :], in_=bkb[:, 0:384].rearrange("p (m t) -> p m t", t=128)),
                             reads=[bb], writes=[B_prT[s]])
                    else:
                        p.op("dve", lambda e, s=s, g=g, bkb=bkb: e.tensor_copy(out=prT[s][:, g, :, :], in_=bkb[:, 0:384].rearrange("p (m t) -> p m t", t=128)),
                             reads=[bb], writes=[B_prT[s]])
                bk, bb = nextbank()
                for g in range(4):
                    for m in range(3):
                        p.op("pe", lambda e, s=s, g=g, m=m, kv=kv, j=j, bk=bk: e.matmul(bk[:, g * 64:(g + 1) * 64], lhsT=prT[s][:, g, m, :],
                                                                                     rhs=vt[:, j + m, kv * 64:(kv + 1) * 64], start=(m == 0), stop=(m == 2)),
                             reads=[B_prT[s], B_vt], writes=[bb], pe_accum=True)
                p.op("dve", lambda e, s=s, kv=kv, bk=bk: e.tensor_tensor(
                    out=atm[:, kv * 256:(kv + 1) * 256].rearrange("p (g d) -> p g d", d=64),
                    in0=bk[:, 0:256].rearrange("p (g d) -> p g d", d=64),
                    in1=bc(ast[s][:, :, 7:8], [128, 4, 64]), op=ALU.mult), reads=[bb, B_ast[s]], writes=[B_atm])
                if kv == 3:
                    bk, bb = nextbank()
                    bkb = bk[:].bitcast(BF16)
                    for k in range(8):
                        p.op("pe", lambda e, k=k, bkb=bkb: e.transpose(out=bkb[:, k * 128:(k + 1) * 128], in_=atm[:, k * 128:(k + 1) * 128], identity=c_idb[:]),
                             reads=[B_atm, B_const], writes=[bb], pe_accum=True)
                    p.op("act", lambda e, j=j, bkb=bkb: e.copy(out=aTt[:, :, j * 128:(j + 1) * 128], in_=bkb.rearrange("p (k t) -> p k t", t=128)),
                         reads=[bb], writes=[B_aTt])

            items = [(j, kv) for j in range(4) for kv in range(4)]
            att_head(items[0][0], items[0][1], 0)
            for i_ in range(1, 16):
                att_head(items[i_][0], items[i_][1], i_ % 2)
                att_tail(items[i_ - 1][0], items[i_ - 1][1], (i_ - 1) % 2)
            att_tail(items[15][0], items[15][1], 1)
            for k in range(8):
                p.dma("pool", lambda e, k=k: e.dma_start(out=aT_d[k, :, tok0:tok0 + NT], in_=aTt[:, k, :]), "sta", reads=[B_aTt])

        if "A" in stages:
            gi = 0
            with contextlib.ExitStack() as stA:
                poolA = TilePool(nc, stA)
                for seg, ng in OWN_GROUPS:
                    for g in range(ng):
                        tok0 = (0 if seg == 0 else 2048) + g * 512
                        for _ in prep_group(poolA, x_own[gi], 768, True, gi, tok0):
                            pass
                        emit_casts(2)
                        gi += 1
                p.barrier()


        ssd_stack = contextlib.ExitStack()
        Hst = SB(ssd_stack, "Hst", [128, 4, D], F32)
        B_H = [Buf() for _ in range(4)]

        def ssd_work(st, tag, two_xs=False):
            W = {}
            for nm, shp, dt in (("dA", [128, 64], F32), ("cumsb", [128, 128], F32), ("tmp", [128, 64], F32), ("dte", [128, 64], F32),
                                ("dec", [128, 64], F32), ("w", [128, 64], F32), ("xs", [128, D], BF16), ("xs2", [128, D], BF16), ("deff", [128, 64], F32),
                                ("tg", [128, 512], F32)):
                if nm == "xs2" and not two_xs:
                    continue
                W[nm] = SB(st, nm + tag, shp, dt)
                W["B_" + nm] = Buf()
            return W

        def chunk_pre(W, dt_ap, B_dt):
            p.op("dve", lambda e: e.tensor_tensor(out=W["dA"][:], in0=dt_ap, in1=c_arep[:], op=ALU.mult), reads=[B_dt, B_const], writes=[W["B_dA"]])
            bk, bb = nextbank()
            p.op("pe", lambda e, bk=bk: e.matmul(bk[:, 0:32], lhsT=TRIF, rhs=W["dA"][:, 0:32], start=True, stop=True), reads=[W["B_dA"], B_const], writes=[bb])
            p.op("pe", lambda e, bk=bk: e.matmul(bk[:, 32:64], lhsT=TRIB, rhs=W["dA"][:, 32:64], start=True, stop=True), reads=[W["B_dA"], B_const], writes=[bb], pe_accum=True)
            p.op("pe", lambda e, bk=bk: e.matmul(bk[:, 64:128], lhsT=ONES, rhs=W["dA"][:, 0:64], start=True, stop=True), reads=[W["B_dA"], B_const], writes=[bb], pe_accum=True)
            p.op("act", lambda e, bk=bk: e.copy(out=W["cumsb"][:], in_=bk[:, 0:128]), reads=[bb], writes=[W["B_cumsb"]])
            p.op("dve", lambda e: e.tensor_tensor(out=W["tmp"][:], in0=W["cumsb"][:, 64:128], in1=W["cumsb"][:, 0:64], op=ALU.subtract),
                 reads=[W["B_cumsb"]], writes=[W["B_tmp"]])
            p.op("act", lambda e: e.activation(out=W["dte"][:], in_=W["tmp"][:], func=AF.Exp), reads=[W["B_tmp"]], writes=[W["B_dte"]])
            p.op("act", lambda e: e.activation(out=W["dec"][:], in_=W["cumsb"][:, 64:128], func=AF.Exp), reads=[W["B_cumsb"]], writes=[W["B_dec"]])
            p.op("dve", lambda e: e.tensor_tensor(out=W["w"][:], in0=dt_ap, in1=W["dte"][:], op=ALU.mult), reads=[B_dt, W["B_dte"]], writes=[W["B_w"]])

        def chunk_states(W, X_ap, B_X, Bt_ap, B_Bt, d, xk="xs"):
            p.op("dve", lambda e: e.tensor_tensor(out=W[xk][:].rearrange("p (h d) -> p h d", d=64), in0=X_ap.rearrange("p (h d) -> p h d", d=64),
                                                  in1=bc(W["w"][:, d * 32:(d + 1) * 32].unsqueeze(2), [128, 32, 64]), op=ALU.mult),
                 reads=[B_X, W["B_w"]], writes=[W["B_" + xk]])
            out = []
            for g in range(4):
                bk, bb = nextbank()
                p.op("pe", lambda e, g=g, bk=bk: e.matmul(bk[:, 0:512], lhsT=Bt_ap[:, g * 128:(g + 1) * 128], rhs=W[xk][:, g * 512:(g + 1) * 512],
                                                         start=True, stop=True), reads=[B_Bt, W["B_" + xk]], writes=[bb])
                out.append((bk, bb))
            return out

        def h_update(W, hi, d, sbanks):
            Hv = Hst[:, hi, :]
            p.op("dve", lambda e: e.tensor_tensor(out=Hv.rearrange("p (h d) -> p h d", d=64), in0=Hv.rearrange("p (h d) -> p h d", d=64),
                                                  in1=bc(W["dec"][:, d * 32:(d + 1) * 32].unsqueeze(2), [128, 32, 64]), op=ALU.mult),
                 reads=[W["B_dec"], B_H[hi]], writes=[B_H[hi]])
            for g, (bk, bb) in enumerate(sbanks):
                p.op("dve", lambda e, g=g, bk=bk: e.tensor_tensor(out=Hst[:, hi, g * 512:(g + 1) * 512], in0=bk[:, 0:512], in1=Hst[:, hi, g * 512:(g + 1) * 512], op=ALU.add),
                     reads=[bb, B_H[hi]], writes=[B_H[hi]])

        if "S" in stages:
            with contextlib.ExitStack() as st0:
                Pb = SB(st0, "Pb", [128, 2, 32], F32)
                wpb = SB(st0, "wpb", [128, 32], F32)
                c_om = SB(st0, "c_om", [128, NG_OTH, 4], F32)
                B_Pb, B_wpb, B_om = Buf(), Buf(), Buf()
                p.dma("sp", lambda e: e.dma_start(out=c_om[:], in_=omask), "c1", writes=[B_om])
                p.op("pool", lambda e: e.memset(Pb[:], 1.0), writes=[B_Pb])
                for hi in range(4):
                    p.op("pool", lambda e, hi=hi: e.memset(Hst[:, hi, :], 0.0), writes=[B_H[hi]])
                poolS = TilePool(nc, st0)
                W_S = [ssd_work(st0, "S0", True), ssd_work(st0, "S1", True)]

                def other_group(gi, r, nxt):
                    seg = 0 if gi < 12 else 1
                    if True:

                        def xs_scale(W, ti, d, xk):
                            p.op("dve", lambda e: e.tensor_tensor(out=W[xk][:].rearrange("p (h d) -> p h d", d=64), in0=r["Xtm"][:, ti, :].rearrange("p (h d) -> p h d", d=64),
                                                                  in1=bc(W["w"][:, d * 32:(d + 1) * 32].unsqueeze(2), [128, 32, 64]), op=ALU.mult),
                                 reads=[r["B_Xtm"], W["B_w"]], writes=[W["B_" + xk]])

                        def tile_pre(ti, W):
                            chunk_pre(W, r["dttm"][:, ti, :], r["B_dttm"])
                            xs_scale(W, ti, 0, "xs")
                            xs_scale(W, ti, 1, "xs2")

                        def st_mm(W, ti, g, xk):
                            bk, bb = nextbank()
                            p.op("pe", lambda e: e.matmul(bk[:, 0:512], lhsT=r["Btm"][:, ti, g * 128:(g + 1) * 128], rhs=W[xk][:, g * 512:(g + 1) * 512],
                                                          start=True, stop=True), reads=[r["B_Btm"], W["B_" + xk]], writes=[bb])
                            return bk, bb

                        def tile_main(ti, W):
                            hi = seg * 2
                            p.op("dve", lambda e: e.tensor_scalar(out=W["deff"][:, 0:32], in0=W["dec"][:, 0:32], scalar1=c_om[:, gi, 0:1], scalar2=c_om[:, gi, 1:2],
                                                                  op0=ALU.mult, op1=ALU.add), reads=[W["B_dec"], B_om], writes=[W["B_deff"]])
                            Hv = Hst[:, hi, :]
                            p.op("dve", lambda e: e.tensor_tensor(out=Hv.rearrange("p (h d) -> p h d", d=64), in0=Hv.rearrange("p (h d) -> p h d", d=64),
                                                                  in1=bc(W["deff"][:, 0:32].unsqueeze(2), [128, 32, 64]), op=ALU.mult),
                                 reads=[W["B_deff"], B_H[hi]], writes=[B_H[hi]])
                            for g in range(4):
                                bk, bb = st_mm(W, ti, g, "xs")
                                p.op("dve", lambda e, g=g, bk=bk, hi=hi: e.scalar_tensor_tensor(
                                    out=Hst[:, hi, g * 512:(g + 1) * 512], in0=bk[:, 0:512], scalar=c_om[:, gi, 0:1], in1=Hst[:, hi, g * 512:(g + 1) * 512],
                                    op0=ALU.mult, op1=ALU.add), reads=[bb, B_H[hi], B_om], writes=[B_H[hi]])
                            hi = seg * 2 + 1
                            p.op("dve", lambda e: e.tensor_scalar(out=wpb[:], in0=Pb[:, seg, :], scalar1=c_om[:, gi, 2:3], scalar2=None, op0=ALU.mult),
                                 reads=[B_Pb, B_om], writes=[B_wpb])
                            for g in range(4):
                                bk, bb = st_mm(W, ti, g, "xs2")
                                p.op("dve", lambda e, g=g, bk=bk: e.tensor_tensor(out=W["tg"][:].rearrange("p (h d) -> p h d", d=64),
                                                                                 in0=bk[:, 0:512].rearrange("p (h d) -> p h d", d=64),
                                                                                 in1=bc(wpb[:, g * 8:(g + 1) * 8].unsqueeze(2), [128, 8, 64]), op=ALU.mult),
                                     reads=[bb, B_wpb], writes=[W["B_tg"]])
                                p.op("dve", lambda e, g=g, hi=hi: e.tensor_tensor(out=Hst[:, hi, g * 512:(g + 1) * 512], in0=Hst[:, hi, g * 512:(g + 1) * 512],
                                                                                   in1=W["tg"][:], op=ALU.add), reads=[W["B_tg"], B_H[hi]], writes=[B_H[hi]])
                            p.op("dve", lambda e: e.tensor_scalar(out=W["deff"][:, 32:64], in0=W["dec"][:, 32:64], scalar1=c_om[:, gi, 2:3], scalar2=c_om[:, gi, 3:4],
                                                                  op0=ALU.mult, op1=ALU.add), reads=[W["B_dec"], B_om], writes=[W["B_deff"]])
                            p.op("dve", lambda e: e.tensor_tensor(out=Pb[:, seg, :], in0=Pb[:, seg, :], in1=W["deff"][:, 32:64], op=ALU.mult),
                                 reads=[W["B_deff"], B_Pb, B_wpb], writes=[B_Pb])

                        def dr(n):
                            for _ in range(n):
                                if nxt[0] is not None:
                                    try:
                                        next(nxt[0])
                                    except StopIteration:
                                        nxt[0] = None

                        tile_pre(0, W_S[0])
                        for ti in range(4):
                            dr(3)
                            if ti + 1 < 4:
                                tile_pre(ti + 1, W_S[(ti + 1) % 2])
                            dr(4)
                            tile_main(ti, W_S[ti % 2])
                        dr(1000)
                rs = [dict() for _ in range(NG_OTH)]
                for _ in prep_group(poolS, x_oth[0], 516, False, 100, 0, 0, rs[0]):
                    pass
                for gi in range(NG_OTH):
                    nxt = [prep_group(poolS, x_oth[gi + 1], 516, False, 101 + gi, 0, (gi + 1) % 2, rs[gi + 1])] if gi + 1 < NG_OTH else [None]
                    other_group(gi, rs[gi], nxt)
                    emit_casts(2)
                emit_casts(1000)
                if "hin_d" in dbg:
                    for hi in range(4):
                        p.dma("pool", lambda e, hi=hi: e.dma_start(out=hin_d[hi], in_=Hst[:, hi, :]), "sth", reads=[B_H[hi]])
                p.barrier()

        SEGS = [(0, 0, 16), (1, 2048, 8)]
        def uprep_gen(stk):
            ul = [SB(stk, f"ul{i}", [128, D], BF16) for i in range(2)]
            ut = [SB(stk, f"ut{i}", [128, D], BF16) for i in range(2)]
            B_ul, B_ut = [Buf(), Buf()], [Buf(), Buf()]
            for c in range(128):
                s_ = c % 2
                p.dma("sp", lambda e, s_=s_, c=c: e.dma_start(out=ul[s_][:], in_=u_b[c * 128:(c + 1) * 128, :]), f"ul{s_}", reads=[B_wuv], writes=[B_ul[s_]])
                yield
                for half in range(2):
                    bk, bb = nextbank()
                    bkb = bk[:].bitcast(BF16)
                    for kk in range(8):
                        k = half * 8 + kk
                        p.op("pe", lambda e, s_=s_, k=k, kk=kk, bkb=bkb: e.transpose(out=bkb[:, kk * 128:(kk + 1) * 128], in_=ul[s_][:, k * 128:(k + 1) * 128], identity=c_idb[:]),
                             reads=[B_ul[s_], B_const], writes=[bb], pe_accum=True)
                    if half == 0:
                        p.op("act", lambda e, s_=s_, bkb=bkb: e.copy(out=ut[s_][:, 0:1024], in_=bkb), reads=[bb], writes=[B_ut[s_]])
                    else:
                        p.op("dve", lambda e, s_=s_, bkb=bkb: e.tensor_copy(out=ut[s_][:, 1024:2048], in_=bkb), reads=[bb], writes=[B_ut[s_]])
                p.dma("pool", lambda e, s_=s_, c=c: e.dma_start(out=ut_b[c], in_=ut[s_][:]), f"us{s_}", reads=[B_ut[s_]])
                yield

        def stage_b1():
            with contextlib.ExitStack() as st:
                W2 = [ssd_work(st, "B1a"), ssd_work(st, "B1b")]
                Xc = [SB(st, f"b1X{i}", [128, D], BF16) for i in range(2)]
                Bc = [SB(st, f"b1B{i}", [128, 512], BF16) for i in range(2)]
                dc = [SB(st, f"b1d{i}", [128, 64], F32) for i in range(2)]
                hbt = [SB(st, f"b1h{i}", [128, D], BF16) for i in range(2)]
                B_Xc, B_Bc, B_dc, B_hbt = [Buf(), Buf()], [Buf(), Buf()], [Buf(), Buf()], [Buf(), Buf()]
                it = 0
                cbase = 0
                ug = uprep_gen(st) if "C" in stages else iter(())
                for seg, tok0, nch in SEGS:
                    hi = seg * 2 + 1
                    for c in range(nch - 1, -1, -1):
                        s = it % 2
                        W = W2[s]
                        it += 1
                        t = tok0 + c * 128
                        p.dma("sp", lambda e, s=s, t=t: e.dma_start(out=Xc[s][:], in_=Xs_d[t:t + 128, :]), f"b1l{s}", writes=[B_Xc[s]])
                        p.dma("sp", lambda e, s=s, t=t: e.dma_start(out=Bc[s][:], in_=Bs_d[t:t + 128, :]), f"b1l{s}", writes=[B_Bc[s]])
                        p.dma("sp", lambda e, s=s, t=t: e.dma_start(out=dc[s][:], in_=dt_d[t:t + 128, :]), f"b1l{s}", writes=[B_dc[s]])
                        p.op("act", lambda e, s=s, hi=hi: e.copy(out=hbt[s][:], in_=Hst[:, hi, :]), reads=[B_H[hi]], writes=[B_hbt[s]])
                        p.dma("pool", lambda e, s=s, cc=cbase + c: e.dma_start(out=hb_d[cc], in_=hbt[s][:]), f"b1s{s}", reads=[B_hbt[s]])
                        chunk_pre(W, dc[s][:], B_dc[s])
                        for _ in range(6):
                            next(ug, None)
                        sb_ = chunk_states(W, Xc[s][:], B_Xc[s], Bc[s][:], B_Bc[s], 1)
                        h_update(W, hi, 1, sb_)
                        for _ in range(6):
                            next(ug, None)
                    cbase += nch
                for _ in ug:
                    pass
                p.barrier()

        def stage_b2():
            with contextlib.ExitStack() as st:
                W2b = [ssd_work(st, "B2a"), ssd_work(st, "B2b")]
                Xc2 = [SB(st, f"b2X{i}", [128, D], BF16) for i in range(2)]
                Bc2 = [SB(st, f"b2B{i}", [128, 512], BF16) for i in range(2)]
                dc2 = [SB(st, f"b2d{i}", [128, 64], F32) for i in range(2)]
                zc2 = [SB(st, f"b2z{i}", [128, D], BF16) for i in range(2)]
                BTc2 = [SB(st, f"b2BT{i}", [128, 4, 128], BF16) for i in range(2)]
                CTc2 = [SB(st, f"b2CT{i}", [128, 4, 128], BF16) for i in range(2)]
                hbt2 = [SB(st, f"b2hb{i}", [128, D], BF16) for i in range(2)]
                hft = SB(st, "b2hf", [128, D], BF16)
                B_ld2, B_hbt2, B_hft = [Buf(), Buf()], [Buf(), Buf()], Buf()
                xdt = SB(st, "b2xdt", [128, 2, D], BF16)
                cumT = SB(st, "b2cumT", [32, 2, 128], F32)
                ecum = SB(st, "b2ecum", [128, 64], F32)
                ncum = SB(st, "b2ncum", [128, 64], F32)
                GTm = SB(st, "b2GTm", [128, 2, 4, 128], BF16)
                LT2 = [SB(st, f"b2LT{i}", [128, 8, 128], BF16) for i in range(2)]
                MT2 = [SB(st, f"b2MT{i}", [128, 8, 128], BF16) for i in range(2)]
                B_LT2, B_MT2 = [Buf(), Buf()], [Buf(), Buf()]
                yv = SB(st, "b2y", [128, D], F32)
                t1 = SB(st, "b2t1", [128, 512], F32)
                gst = SB(st, "b2gst", [128, 4, 4], F32)
                ssm = SB(st, "b2ssm", [128, D], BF16)
                c_neg = SB(st, "b2neg", [128, 2, 512], BF16)
                c_gn = SB(st, "b2gn", [128, D], F32)
                ssmT = SB(st, "b2ssmT", [128, 16, 512], BF16)
                aTl = SB(st, "b2aT", [128, 8, 512], BF16)
                mT = SB(st, "b2mT", [128, 16, 512], BF16)
                B_xdt, B_cumT, B_ecum, B_GTm, B_LT, B_MT, B_y, B_t1, B_gst, B_ssm, B_c2, B_ssmT, B_aTl, B_mT = [Buf() for _ in range(14)]
                p.dma("sp", lambda e: e.dma_start(out=c_neg[:], in_=negm), "c1", writes=[B_c2])
                p.dma("sp", lambda e: e.dma_start(out=c_gn[:], in_=rep_d[:, 3, :]), "c1", writes=[B_c2])
                wo1 = [SB(st, f"b2wa{i}", [128, 8, 128], BF16) for i in range(2)]
                wo2 = [SB(st, f"b2ws{i}", [128, 16, 128], BF16) for i in range(2)]
                gl = [SB(st, f"b2gl{i}", [128, 2, 512], BF16) for i in range(2)]
                B_wo, B_gl = [Buf(), Buf()], [Buf(), Buf()]
                wo3 = SB(st, "b2wo", [128, 16, 512], BF16)
                B_wo3 = Buf()
                xr = SB(st, "b2xr", [128, 512], F32)
                B_xr = Buf()
                cbase = 0
                for seg, tok0, nch in SEGS:
                    hif, hib = seg * 2, seg * 2 + 1

                    def do_chunk(c, W, Xc, Bc, dc, zc, BTc, CTc, hbt, B_ld, B_hbt, seg=seg, tok0=tok0, hif=hif, hib=hib, cbase=cbase):
                        t = tok0 + c * 128
                        for dst, src in ((Xc[:], Xs_d[t:t + 128, :]), (Bc[:], Bs_d[t:t + 128, :]), (dc[:], dt_d[t:t + 128, :]), (zc[:], zs_d[t:t + 128, :]),
                                         (BTc[:], BT_d[:, :, t:t + 128].rearrange("g p t -> p g t")), (CTc[:], CT_d[:, :, t:t + 128].rearrange("g p t -> p g t"))):
                            p.dma("sp", lambda e, dst=dst, src=src: e.dma_start(out=dst, in_=src), f"b2l{(cbase + c) % 2}", writes=[B_ld])
                        p.dma("sp", lambda e, cc=cbase + c: e.dma_start(out=hbt[:], in_=hb_d[cc]), f"b2l{(cbase + c) % 2}", writes=[B_hbt])
                        p.op("act", lambda e, hif=hif: e.copy(out=hft[:], in_=Hst[:, hif, :]), reads=[B_H[hif]], writes=[B_hft])
                        chunk_pre(W, dc[:], B_ld)
                        bk, bb = nextbank()
                        p.op("pe", lambda e, bk=bk: e.matmul(bk[0:32, 0:128], lhsT=W["dA"][:, 0:32], rhs=TRIF, start=True, stop=True), reads=[W["B_dA"], B_const], writes=[bb])
                        p.op("pe", lambda e, bk=bk: e.matmul(bk[0:32, 128:256], lhsT=W["dA"][:, 32:64], rhs=TRIB, start=True, stop=True), reads=[W["B_dA"], B_const], writes=[bb], pe_accum=True)
                        p.op("act", lambda e, bk=bk: e.copy(out=cumT[:], in_=bk[0:32, 0:256].rearrange("p (d t) -> p d t", t=128)), reads=[bb], writes=[B_cumT])
                        p.op("act", lambda e: e.activation(out=ecum[:], in_=W["cumsb"][:, 0:64], func=AF.Exp), reads=[W["B_cumsb"]], writes=[B_ecum])
                        p.op("dve", lambda e: e.tensor_scalar(out=ncum[:], in0=W["cumsb"][:, 0:64], scalar1=-1.0, scalar2=None, op0=ALU.mult), reads=[W["B_cumsb"]], writes=[B_ecum])
                        for d in range(2):
                            p.op("dve" if d == 0 else "pool", lambda e, d=d: e.tensor_tensor(
                                out=xdt[:, d, :].rearrange("p (h d) -> p h d", d=64), in0=Xc[:].rearrange("p (h d) -> p h d", d=64),
                                in1=bc(dc[:, d * 32:(d + 1) * 32].unsqueeze(2), [128, 32, 64]), op=ALU.mult), reads=[B_ld], writes=[B_xdt])
                        bk, bb = nextbank()
                        for g in range(4):
                            p.op("pe", lambda e, g=g, bk=bk: e.matmul(bk[:, g * 128:(g + 1) * 128], lhsT=BTc[:, g, :], rhs=CTc[:, g, :], start=True, stop=True),
                                 reads=[B_ld], writes=[bb], pe_accum=True)
                        for d in range(2):
                            tri = TRIF if d == 0 else TRIB
                            p.op("dve", lambda e, d=d, tri=tri, bk=bk: e.tensor_tensor(out=GTm[:, d, :, :], in0=bk[:, 0:512].rearrange("p (g t) -> p g t", t=128),
                                                                                      in1=bc(tri.unsqueeze(1), [128, 4, 128]), op=ALU.mult),
                                 reads=[bb, B_const], writes=[B_GTm])
                        def dg_head(d, g, sl):
                            LT, MT, B_LT, B_MT = LT2[sl], MT2[sl], B_LT2[sl], B_MT2[sl]
                            for half in range(2):
                                bk, bb = nextbank()
                                p.op("pe", lambda e, bk=bk: e.matmul(bk[:, 0:512], lhsT=c_idb[:], rhs=c_neg[:, d, :], start=True, stop=False),
                                     reads=[B_c2, B_const], writes=[bb])
                                for j in range(4):
                                    h = g * 8 + half * 4 + j
                                    p.op("pe", lambda e, j=j, h=h, bk=bk: e.matmul(bk[:, j * 128:(j + 1) * 128], lhsT=bc(c_cst[0:32, 0, h:h + 1], [32, 128]),
                                                                                   rhs=cumT[:, d, :], start=False, stop=(j == 3)),
                                         reads=[B_cumT, B_const], writes=[bb], pe_accum=True)
                                for j in range(4):
                                    h = g * 8 + half * 4 + j
                                    p.op("act", lambda e, j=j, h=h, half=half, bk=bk: e.activation(out=LT[:, half * 4 + j, :], in_=bk[:, j * 128:(j + 1) * 128], func=AF.Exp,
                                                                                                  bias=ncum[:, d * 32 + h:d * 32 + h + 1]),
                                         reads=[bb, B_ecum], writes=[B_LT])
                            p.op("dve", lambda e: e.tensor_tensor(out=MT[:], in0=LT[:], in1=bc(GTm[:, d, g, :].unsqueeze(1), [128, 8, 128]), op=ALU.mult),
                                 reads=[B_LT, B_GTm], writes=[B_MT])

                        def dg_tail(d, g, sl):
                            MT, B_MT = MT2[sl], B_MT2[sl]
                            hsrc, B_hs = (hft, B_hft) if d == 0 else (hbt, B_hbt)
                            bkd, bbd = nextbank()
                            for j in range(8):
                                h = g * 8 + j
                                p.op("pe", lambda e, j=j, h=h: e.matmul(bkd[:, j * 64:(j + 1) * 64], lhsT=MT[:, j, :], rhs=xdt[:, d, h * 64:(h + 1) * 64],
                                                                        start=True, stop=True), reads=[B_MT, B_xdt], writes=[bbd], pe_accum=True)
                            bko, bbo = nextbank()
                            p.op("pe", lambda e: e.matmul(bko[:, 0:512], lhsT=CTc[:, g, :], rhs=hsrc[:, g * 512:(g + 1) * 512], start=True, stop=True),
                                 reads=[B_ld, B_hs], writes=[bbo])
                            p.op("dve", lambda e: e.tensor_tensor(out=t1[:].rearrange("p (h d) -> p h d", d=64),
                                                                  in0=bko[:, 0:512].rearrange("p (h d) -> p h d", d=64),
                                                                  in1=bc(ecum[:, d * 32 + g * 8:d * 32 + g * 8 + 8].unsqueeze(2), [128, 8, 64]), op=ALU.mult),
                                 reads=[bbo, B_ecum], writes=[B_t1])
                            if d == 0:
                                p.op("dve", lambda e: e.tensor_tensor(out=yv[:, g * 512:(g + 1) * 512], in0=bkd[:, 0:512], in1=t1[:], op=ALU.add),
                                     reads=[bbd, B_t1], writes=[B_y])
                            else:
                                p.op("dve", lambda e: e.tensor_tensor(out=t1[:], in0=bkd[:, 0:512], in1=t1[:], op=ALU.add),
                                     reads=[bbd, B_t1], writes=[B_t1])
                                p.op("pool", lambda e: e.tensor_tensor(out=yv[:, g * 512:(g + 1) * 512], in0=yv[:, g * 512:(g + 1) * 512], in1=t1[:], op=ALU.add),
                                     reads=[B_t1, B_y], writes=[B_y])

                        dgs = [(d, g) for d in range(2) for g in range(4)]
                        dg_head(dgs[0][0], dgs[0][1], 0)
                        for i_ in range(1, 8):
                            dg_head(dgs[i_][0], dgs[i_][1], i_ % 2)
                            dg_tail(dgs[i_ - 1][0], dgs[i_ - 1][1], (i_ - 1) % 2)
                        dg_tail(dgs[7][0], dgs[7][1], 1)
                        p.op("dve", lambda e: e.tensor_tensor(out=xdt[:, 0, :].rearrange("p (h d) -> p h d", d=64), in0=Xc[:].rearrange("p (h d) -> p h d", d=64),
                                                              in1=bc(c_reps[:, 64:96].unsqueeze(2), [128, 32, 64]), op=ALU.mult),
                             reads=[B_ld, B_const, B_xdt], writes=[B_xdt])
                        p.op("dve", lambda e: e.tensor_tensor(out=yv[:], in0=yv[:], in1=xdt[:, 0, :], op=ALU.add), reads=[B_xdt, B_y], writes=[B_y])
                        p.op("dve", lambda e: e.tensor_tensor(out=yv[:], in0=yv[:], in1=zc[:], op=ALU.mult), reads=[B_ld, B_y], writes=[B_y])
                        for g in range(4):
                            p.op("act", lambda e, g=g: e.activation(out=ssm[:, g * 512:(g + 1) * 512], in_=yv[:, g * 512:(g + 1) * 512], func=AF.Square, accum_out=gst[:, g, 0:1]),
                                 reads=[B_y], writes=[B_ssm, B_gst])
                        p.op("act", lambda e: e.activation(out=gst[:, :, 1], in_=gst[:, :, 0], func=AF.Sqrt, scale=1.0 / 512, bias=EPS), reads=[B_gst], writes=[B_gst])
                        p.op("dve", lambda e: e.reciprocal(out=gst[:, :, 2], in_=gst[:, :, 1]), reads=[B_gst], writes=[B_gst])
                        p.op("dve", lambda e: e.tensor_tensor(out=yv[:].rearrange("p (g d) -> p g d", d=512), in0=yv[:].rearrange("p (g d) -> p g d", d=512),
                                                              in1=bc(gst[:, :, 2:3], [128, 4, 512]), op=ALU.mult), reads=[B_gst, B_y], writes=[B_y])
                        p.op("dve", lambda e: e.tensor_tensor(out=ssm[:], in0=yv[:], in1=c_gn[:], op=ALU.mult), reads=[B_y, B_c2, B_ssm], writes=[B_ssm])
                        ci = c % 4
                        for half in range(2):
                            bk, bb = nextbank()
                            bkb = bk[:].bitcast(BF16)
                            for kk in range(8):
                                k = half * 8 + kk
                                p.op("pe", lambda e, k=k, kk=kk, bkb=bkb: e.transpose(out=bkb[:, kk * 128:(kk + 1) * 128], in_=ssm[:, k * 128:(k + 1) * 128], identity=c_idb[:]),
                                     reads=[B_ssm, B_const], writes=[bb], pe_accum=True)
                            p.op("act", lambda e, half=half, ci=ci, bkb=bkb: e.copy(out=ssmT[:, half * 8:half * 8 + 8, ci * 128:(ci + 1) * 128],
                                                                                    in_=bkb.rearrange("p (k t) -> p k t", t=128)), reads=[bb], writes=[B_ssmT])
                        sb_ = chunk_states(W, Xc[:], B_ld, Bc[:], B_ld, 0)
                        h_update(W, hif, 0, sb_)
                        if ci == 3:
                            g0 = t - 384
                            p.dma("sp", lambda e, g0=g0: e.dma_start(out=aTl[:], in_=aT_d[:, :, g0:g0 + 512].rearrange("k p t -> p k t")), "b2a", writes=[B_aTl])
                            for cc in range(16):
                                s = cc % 2
                                p.dma("sp", lambda e, s=s, cc=cc: e.dma_start(out=wo1[s][:], in_=w_ao_b[:, cc * 128:(cc + 1) * 128].rearrange("(k p) c -> p k c", p=128)),
                                      f"b2w{s}", reads=[B_w], writes=[B_wo[s]])
                                p.dma("sp", lambda e, s=s, cc=cc: e.dma_start(out=wo2[s][:], in_=w_so_b[:, cc * 128:(cc + 1) * 128].rearrange("(k p) c -> p k c", p=128)),
                                      f"b2w{s}", reads=[B_w], writes=[B_wo[s]])
                                p.dma("sp", lambda e, s=s, cc=cc, g0=g0: e.dma_start(out=gl[s][:, 0, :], in_=gT_d[cc, :, g0:g0 + 512]), f"b2g{s}", writes=[B_gl[s]])
                                p.dma("sp", lambda e, s=s, cc=cc, g0=g0: e.dma_start(out=gl[s][:, 1, :], in_=gT_d[16 + cc, :, g0:g0 + 512]), f"b2g{s}", writes=[B_gl[s]])
                                bka, bba = nextbank()
                                for k in range(8):
                                    p.op("pe", lambda e, s=s, k=k, bka=bka: e.matmul(bka[:, 0:512], lhsT=wo1[s][:, k, :], rhs=aTl[:, k, :], start=(k == 0), stop=(k == 7)),
                                         reads=[B_wo[s], B_aTl], writes=[bba], pe_accum=True)
                                bks, bbs = nextbank()
                                for k in range(16):
                                    p.op("pe", lambda e, s=s, k=k, bks=bks: e.matmul(bks[:, 0:512], lhsT=wo2[s][:, k, :], rhs=ssmT[:, k, :], start=(k == 0), stop=(k == 15)),
                                         reads=[B_wo[s], B_ssmT], writes=[bbs], pe_accum=True)
                                p.op("dve", lambda e, s=s, bka=bka: e.tensor_tensor(out=t1[:], in0=bka[:, 0:512], in1=gl[s][:, 0, :], op=ALU.mult),
                                     reads=[bba, B_gl[s], B_t1], writes=[B_t1])
                                p.op("dve", lambda e, s=s, bks=bks: e.tensor_tensor(out=yv[:, 0:512], in0=bks[:, 0:512], in1=gl[s][:, 1, :], op=ALU.mult),
                                     reads=[bbs, B_gl[s], B_y], writes=[B_y])
                                p.op("dve", lambda e, cc=cc: e.tensor_tensor(out=mT[:, cc, :], in0=t1[:], in1=yv[:, 0:512], op=ALU.add),
                                     reads=[B_t1, B_y], writes=[B_mT])
                            for cb in range(4):
                                p.dma("sp", lambda e, cb=cb: e.dma_start(out=wo3[:], in_=w_out_b[:, cb * 512:(cb + 1) * 512].rearrange("(k p) c -> p k c", p=128)),
                                      "b2w3", reads=[B_w], writes=[B_wo3])
                                for ti in range(4):
                                    tt = g0 + ti * 128
                                    p.dma("sp", lambda e, tt=tt, cb=cb: e.dma_start(out=xr[:], in_=x_res[tt:tt + 128, cb * 512:(cb + 1) * 512]), "b2x", writes=[B_xr])
                                    bk, bb = nextbank()
                                    for k in range(16):
                                        p.op("pe", lambda e, k=k, ti=ti, bk=bk: e.matmul(bk[:, 0:512], lhsT=mT[:, k, ti * 128:(ti + 1) * 128], rhs=wo3[:, k, :],
                                                                                          start=(k == 0), stop=(k == 15)), reads=[B_mT, B_wo3], writes=[bb], pe_accum=True)
                                    p.op("dve", lambda e, bk=bk: e.tensor_tensor(out=xr[:], in0=bk[:, 0:512], in1=xr[:], op=ALU.add), reads=[bb, B_xr], writes=[B_xr])
                                    p.dma("pool", lambda e, tt=tt, cb=cb: e.dma_start(out=x1_d[tt:tt + 128, cb * 512:(cb + 1) * 512], in_=xr[:]), "b2xs", reads=[B_xr])
                    for c in range(nch):
                        s2 = (cbase + c) % 2
                        do_chunk(c, W2b[s2], Xc2[s2], Bc2[s2], dc2[s2], zc2[s2], BTc2[s2], CTc2[s2], hbt2[s2], B_ld2[s2], B_hbt2[s2])
                    cbase += nch
                p.barrier()
        if "B" in stages:
            stage_b1()
            stage_b2()
        p.barrier()
        ssd_stack.close()

        def stage_c():
            with contextlib.ExitStack() as st:
                keysT = SB(st, "keysT", [128, 16, 128], BF16)
                iob = SB(st, "iob", [128, 128], BF16)
                c_gf = SB(st, "c_gf", [128, 1, D], F32)
                B_kT, B_cc, B_gf = Buf(), Buf(), Buf()
                p.op("dve", lambda e: e.tensor_copy(out=iob[:], in_=IOTA), reads=[B_const], writes=[B_cc])
                with contextlib.ExitStack() as st2:
                    kf = SB(st2, "kf", [128, 16, 128], F32)
                    kb = SB(st2, "kb", [128, 16, 128], BF16)
                    B_kf = Buf()
                    p.dma("sp", lambda e: e.dma_start(out=kf[:], in_=keys.rearrange("a n d -> n a d")), "c1", writes=[B_kf])
                    p.op("dve", lambda e: e.tensor_copy(out=kb[:], in_=kf[:]), reads=[B_kf], writes=[B_kf])
                    for half in range(2):
                        bk, bb = nextbank()
                        bkb = bk[:].bitcast(BF16)
                        for kk in range(8):
                            p.op("pe", lambda e, a=half * 8 + kk, kk=kk, bkb=bkb: e.transpose(out=bkb[:, kk * 128:(kk + 1) * 128], in_=kb[:, a, :], identity=c_idb[:]),
                                 reads=[B_kf, B_const], writes=[bb], pe_accum=True)
                        p.op("act", lambda e, half=half, bkb=bkb: e.copy(out=keysT[:, half * 8:half * 8 + 8, :], in_=bkb.rearrange("p (k t) -> p k t", t=128)),
                             reads=[bb], writes=[B_kT])
                    p.barrier()

                x1t = SB(st, "x1t", [128, 1, D], F32)
                xn = SB(st, "cxn", [128, D], BF16)
                cst_ = SB(st, "cst_", [128, 2, 4], F32)
                cst2 = SB(st, "cst2", [128, 2, 4], F32)
                xnT2 = [SB(st, f"xnT{i}", [128, 16, 256], BF16) for i in range(2)]
                qT = SB(st, "cqT", [128, 16, 256], BF16)
                wq = [SB(st, f"wq{i}", [128, 16, 128], BF16) for i in range(2)]
                P2g = SB(st, "P2g", [128, 32, 128], BF16)
                OH1 = SB(st, "OH1", [128, 32, 128], BF16)
                scr = P2g[:].bitcast(F32).rearrange("p a b -> p (a b)").rearrange("p (h n) -> p h n", n=128)
                eq = OH1[:].bitcast(F32).rearrange("p a b -> p (a b)").rearrange("p (h k j) -> p h k j", k=16, j=16)
                wk = SB(st, "cwk", [128, 256], F32)
                topv2 = [SB(st, f"topv{i}", [128, 16, 16], F32) for i in range(2)]
                idxu2 = [SB(st, f"idxu{i}", [128, 16, 16], U32) for i in range(2)]
                idxf = SB(st, "idxf", [128, 16, 16], F32)
                cand = SB(st, "cand", [128, 8, 16, 16], F32)
                best = SB(st, "best", [128, 8, 16], F32)
                posu = SB(st, "posu", [128, 8, 16], U32)
                ku = SB(st, "ku", [128, 2, 8, 16], U32)
                kf_ = SB(st, "kf_", [128, 2, 8, 16], F32)
                gat = SB(st, "gat", [128, 8, 16], F32)
                gz = SB(st, "gz", [128, 8, 2], F32)
                I12_2 = [SB(st, f"I12_{i}", [128, 3, 128], F32) for i in range(2)]
                I12T = SB(st, "I12T", [128, 3, 128], BF16)
                Gs = SB(st, "Gs", [128, 128, 256], BF16)
                NSL = 4
                strm = [SB(st, f"strm{i}", [128, 2, D], BF16) for i in range(NSL)]
                ge = [SB(st, f"ge{i}", [128, 256], BF16) for i in range(2)]
                (B_x1t, B_xn, B_cst, B_cst2, B_qT, B_wk, B_idxf, B_cand, B_best, B_posu, B_ku, B_kf2, B_gat, B_gz,
                 B_I12T, B_OH1, B_P2g, B_Gs) = [Buf() for _ in range(18)]
                B_xnT2, B_topv2, B_idxu2, B_I12_2 = [Buf(), Buf()], [Buf(), Buf()], [Buf(), Buf()], [Buf(), Buf()]
                B_wq, B_ge = [Buf(), Buf()], [Buf(), Buf()]
                B_strm = [Buf() for _ in range(NSL)]
                B_scr, B_eq = B_P2g, B_OH1
                B_P2ga, B_P2gb = Buf(), Buf()
                sctr = [0]

                def stage1_hc(ti, hc):
                    topv, idxu, B_topv, B_idxu = topv2[ti], idxu2[ti], B_topv2[ti], B_idxu2[ti]
                    p.op("dve", lambda e: e.max(out=topv[:, hc, 0:8], in_=scr[:, hc, :]), reads=[B_scr], writes=[B_topv])
                    p.op("dve", lambda e: e.match_replace(out=wk[:, 0:128], in_to_replace=topv[:, hc, 0:8], in_values=scr[:, hc, :], imm_value=-1e30),
                         reads=[B_scr, B_topv], writes=[B_wk])
                    p.op("dve", lambda e: e.max(out=topv[:, hc, 8:16], in_=wk[:, 0:128]), reads=[B_wk], writes=[B_topv])
                    p.op("dve", lambda e: e.max_index(out=idxu[:, hc, 0:8], in_max=topv[:, hc, 0:8], in_values=scr[:, hc, :]), reads=[B_scr, B_topv], writes=[B_idxu])
                    p.op("dve", lambda e: e.max_index(out=idxu[:, hc, 8:16], in_max=topv[:, hc, 8:16], in_values=scr[:, hc, :]), reads=[B_scr, B_topv], writes=[B_idxu])

                def scores_q(ti, qd, xsl):
                    bk, bb = nextbank()
                    for j in range(4):
                        hc = qd * 4 + j
                        p.op("pe", lambda e, hc=hc, j=j: e.matmul(bk[:, j * 128:(j + 1) * 128], lhsT=qT[:, hc, ti * 128:(ti + 1) * 128], rhs=keysT[:, hc, :],
                                                                    start=True, stop=True), reads=[B_qT, B_kT], writes=[bb], pe_accum=True)
                    p.op("act", lambda e: e.copy(out=scr[:, qd * 4:qd * 4 + 4, :], in_=bk[:, 0:512].rearrange("p (a n) -> p a n", n=128)),
                         reads=[bb, B_P2ga, B_P2gb], writes=[B_scr])

                def phaseA1(gi):
                    tok0 = gi * 256
                    xnT, B_xnT = xnT2[gi % 2], B_xnT2[gi % 2]
                    p.dma("sp", lambda e: e.dma_start(out=c_gf[:, 0, :], in_=rep_d[:, 1, :]), "cgf", writes=[B_gf])
                    for ti in range(2):
                        t = tok0 + ti * 128
                        p.dma("sp", lambda e, t=t: e.dma_start(out=x1t[:, 0, :], in_=x1_d[t:t + 128, :]), "cx", writes=[B_x1t])
                        p.op("act", lambda e, ti=ti: e.activation(out=xn[:], in_=x1t[:, 0, :], func=AF.Square, accum_out=cst2[:, ti, 0:1]), reads=[B_x1t], writes=[B_xn, B_cst2])
                        p.op("act", lambda e, ti=ti: e.activation(out=cst2[:, ti, 1:2], in_=cst2[:, ti, 0:1], func=AF.Sqrt, scale=1.0 / D, bias=EPS), reads=[B_cst2], writes=[B_cst2])
                        p.op("dve", lambda e, ti=ti: e.reciprocal(out=cst2[:, ti, 2:3], in_=cst2[:, ti, 1:2]), reads=[B_cst2], writes=[B_cst2])
                        p.op("dve", lambda e, ti=ti: e.scalar_tensor_tensor(out=xn[:], in0=x1t[:, 0, :], scalar=cst2[:, ti, 2:3], in1=c_gf[:, 0, :], op0=ALU.mult, op1=ALU.mult),
                             reads=[B_x1t, B_cst2, B_gf, B_xn], writes=[B_xn])
                        yield
                        for half in range(2):
                            bk, bb = nextbank()
                            bkb = bk[:].bitcast(BF16)
                            for kk in range(8):
                                k = half * 8 + kk
                                p.op("pe", lambda e, k=k, kk=kk, bkb=bkb: e.transpose(out=bkb[:, kk * 128:(kk + 1) * 128], in_=xn[:, k * 128:(k + 1) * 128], identity=c_idb[:]),
                                     reads=[B_xn, B_const], writes=[bb], pe_accum=True)
                            p.op("act", lambda e, half=half, ti=ti, bkb=bkb: e.copy(out=xnT[:, half * 8:half * 8 + 8, ti * 128:(ti + 1) * 128], in_=bkb.rearrange("p (k t) -> p k t", t=128)),
                                 reads=[bb], writes=[B_xnT])
                            yield
                    def ld_wq(cc):
                        s_ = cc % 2
                        p.dma("sp", lambda e: e.dma_start(out=wq[s_][:], in_=w_q_b[:, cc * 128:(cc + 1) * 128].rearrange("(k p) c -> p k c", p=128)),
                              f"cwq{s_}", reads=[B_w], writes=[B_wq[s_]])
                    ld_wq(0)
                    for cc in range(16):
                        s_ = cc % 2
                        bk, bb = nextbank()
                        for k in range(16):
                            p.op("pe", lambda e, s_=s_, k=k, bk=bk: e.matmul(bk[:, 0:256], lhsT=wq[s_][:, k, :], rhs=xnT[:, k, :], start=(k == 0), stop=(k == 15)),
                                 reads=[B_wq[s_], B_xnT], writes=[bb], pe_accum=True)
                        p.op("act", lambda e, cc=cc, bk=bk: e.copy(out=qT[:, cc, :], in_=bk[:, 0:256]), reads=[bb], writes=[B_qT])
                        if cc + 1 < 16:
                            ld_wq(cc + 1)
                        yield
                        if cc % 4 != 3:
                            yield
                    for qd in range(4):
                        scores_q(0, qd, None)
                        yield
                    for hc in range(16):
                        stage1_hc(0, hc)
                        yield
                    for qd in range(4):
                        scores_q(1, qd, None)
                        yield

                def phaseA2(gi):
                    for hc in range(16):
                        stage1_hc(1, hc)
                        yield
                    for ti in range(2):
                        topv, idxu, B_topv, B_idxu = topv2[ti], idxu2[ti], B_topv2[ti], B_idxu2[ti]
                        I12, B_I12 = I12_2[ti], B_I12_2[ti]
                        p.op("dve", lambda e, idxu=idxu: e.tensor_copy(out=idxf[:], in_=idxu[:]), reads=[B_idxu], writes=[B_idxf])
                        tv = topv[:].rearrange("p (h c) k -> p h c k", c=2)
                        p.op("dve", lambda e, tv=tv: e.tensor_tensor(out=cand[:], in0=bc(tv[:, :, 0, :].unsqueeze(3), [128, 8, 16, 16]), in1=bc(tv[:, :, 1, :].unsqueeze(2), [128, 8, 16, 16]), op=ALU.add),
                             reads=[B_topv], writes=[B_cand])
                        yield
                        for h in range(8):
                            cv = cand[:, h, :, :].rearrange("p a b -> p (a b)")
                            p.op("dve", lambda e, h=h, cv=cv: e.max(out=best[:, h, 0:8], in_=cv), reads=[B_cand], writes=[B_best])
                            p.op("dve", lambda e, h=h, cv=cv: e.match_replace(out=wk[:], in_to_replace=best[:, h, 0:8], in_values=cv, imm_value=-1e30), reads=[B_cand, B_best], writes=[B_wk])
                            p.op("dve", lambda e, h=h: e.max(out=best[:, h, 8:16], in_=wk[:]), reads=[B_wk], writes=[B_best])
                            p.op("dve", lambda e, h=h, cv=cv: e.max_index(out=posu[:, h, 0:8], in_max=best[:, h, 0:8], in_values=cv), reads=[B_cand, B_best], writes=[B_posu])
                            p.op("dve", lambda e, h=h, cv=cv: e.max_index(out=posu[:, h, 8:16], in_max=best[:, h, 8:16], in_values=cv), reads=[B_cand, B_best], writes=[B_posu])
                            yield
                        p.op("dve", lambda e: e.tensor_tensor(out=gat[:], in0=best[:], in1=bc(best[:, :, 0:1], [128, 8, 16]), op=ALU.subtract), reads=[B_best], writes=[B_gat])
                        p.op("act", lambda e: e.activation(out=gat[:], in_=gat[:], func=AF.Exp), reads=[B_gat], writes=[B_gat])
                        p.op("dve", lambda e: e.tensor_reduce(out=gz[:, :, 0], in_=gat[:], axis=AX.X, op=ALU.add), reads=[B_gat], writes=[B_gz])
                        p.op("dve", lambda e: e.reciprocal(out=gz[:, :, 1], in_=gz[:, :, 0]), reads=[B_gz], writes=[B_gz])
                        p.op("dve", lambda e, I12=I12: e.tensor_tensor(out=I12[:, 2, :].rearrange("p (h k) -> p h k", k=16), in0=gat[:], in1=bc(gz[:, :, 1:2], [128, 8, 16]), op=ALU.mult),
                             reads=[B_gat, B_gz], writes=[B_I12])
                        p.op("dve", lambda e: e.tensor_single_scalar(out=ku[:, 0, :, :], in_=posu[:], scalar=4, op=ALU.logical_shift_right), reads=[B_posu], writes=[B_ku])
                        p.op("dve", lambda e: e.tensor_single_scalar(out=ku[:, 1, :, :], in_=posu[:], scalar=15, op=ALU.bitwise_and), reads=[B_posu], writes=[B_ku])
                        p.op("dve", lambda e: e.tensor_copy(out=kf_[:], in_=ku[:]), reads=[B_ku], writes=[B_kf2])
                        yield
                        iv = idxf[:].rearrange("p (h c) k -> p h c k", c=2)
                        for c_ in range(2):
                            p.op("dve", lambda e, c_=c_: e.tensor_tensor(out=eq, in0=bc(kf_[:, c_, :, :].unsqueeze(3), [128, 8, 16, 16]),
                                                                          in1=bc(IOTA[:, 0:16].unsqueeze(1).unsqueeze(1), [128, 8, 16, 16]), op=ALU.is_equal),
                                 reads=[B_kf2, B_const], writes=[B_eq])
                            p.op("dve", lambda e, c_=c_, iv=iv: e.tensor_tensor(out=eq, in0=eq, in1=bc(iv[:, :, c_, :].unsqueeze(2), [128, 8, 16, 16]), op=ALU.mult),
                                 reads=[B_idxf, B_eq], writes=[B_eq])
                            p.op("dve", lambda e, c_=c_, I12=I12: e.tensor_reduce(out=I12[:, c_, :], in_=eq.rearrange("p h k j -> p (h k) j"), axis=AX.X, op=ALU.add),
                                 reads=[B_eq], writes=[B_I12])
                            yield

                def gbuild(gi):
                    for ti in range(2):
                        I12, B_I12 = I12_2[ti], B_I12_2[ti]
                        bk, bb = nextbank()
                        for w_ in range(3):
                            p.op("pe", lambda e, w_=w_, bk=bk, I12=I12: e.transpose(out=bk[:, w_ * 128:(w_ + 1) * 128], in_=I12[:, w_, :], identity=IDF), reads=[B_I12, B_const], writes=[bb], pe_accum=True)
                        p.op("act", lambda e, bk=bk: e.copy(out=I12T[:], in_=bk[:, 0:384].rearrange("p (w t) -> p w t", t=128)), reads=[bb], writes=[B_I12T])
                        for hf in range(4):
                            tsl = slice(hf * 32, (hf + 1) * 32)
                            p.op("dve", lambda e, tsl=tsl: e.tensor_tensor(out=OH1[:], in0=bc(iob[:].unsqueeze(1), [128, 32, 128]), in1=bc(I12T[:, 0, tsl].unsqueeze(2), [128, 32, 128]), op=ALU.is_equal),
                                 reads=[B_I12T, B_cc], writes=[B_OH1])
                            p.op("dve", lambda e, tsl=tsl: e.tensor_tensor(out=P2g[:], in0=bc(iob[:].unsqueeze(1), [128, 32, 128]), in1=bc(I12T[:, 1, tsl].unsqueeze(2), [128, 32, 128]), op=ALU.is_equal),
                                 reads=[B_I12T, B_cc], writes=[B_P2g, B_P2ga, B_P2gb])
                            p.op("dve", lambda e, hf=hf: e.tensor_tensor(out=P2g[:, 0:20, :], in0=P2g[:, 0:20, :], in1=bc(I12T[:, 2, hf * 32:hf * 32 + 20].unsqueeze(2), [128, 20, 128]), op=ALU.mult),
                                 reads=[B_I12T, B_P2g], writes=[B_P2ga])
                            p.op("pool", lambda e, hf=hf: e.tensor_tensor(out=P2g[:, 20:32, :], in0=P2g[:, 20:32, :], in1=bc(I12T[:, 2, hf * 32 + 20:hf * 32 + 32].unsqueeze(2), [128, 12, 128]), op=ALU.mult),
                                 reads=[B_I12T, B_P2g], writes=[B_P2gb])
                            for q4 in range(8):
                                bk, bb = nextbank()
                                for j in range(4):
                                    tl = q4 * 4 + j
                                    p.op("pe", lambda e, tl=tl, j=j, bk=bk: e.matmul(bk[:, j * 128:(j + 1) * 128], lhsT=P2g[:, tl, :], rhs=OH1[:, tl, :], start=True, stop=True),
                                         reads=[B_P2g, B_P2ga, B_P2gb, B_OH1], writes=[bb], pe_accum=True)
                                tg0 = ti * 128 + hf * 32 + q4 * 4
                                src = bk[:, 0:512].rearrange("p (t i) -> p i t", i=128)
                                p.op("act", lambda e, tg0=tg0, src=src: e.copy(out=Gs[:, :, tg0:tg0 + 4], in_=src), reads=[bb], writes=[B_Gs])

                def drain(gen, n):
                    if gen is None:
                        return None
                    for _ in range(n):
                        try:
                            next(gen)
                        except StopIteration:
                            return None
                    return gen

                def passes_and_epilogue(gi, ga, gb):
                    tok0 = gi * 256
                    xnT, B_xnT = xnT2[gi % 2], B_xnT2[gi % 2]
                    for c2 in range(64):
                        sl = sctr[0] % NSL
                        sctr[0] += 1
                        p.dma("sp", lambda e, sl=sl, c2=c2: e.dma_start(out=strm[sl][:], in_=ut_b[2 * c2:2 * c2 + 2].rearrange("c p f -> p c f")), f"cs{sl}", writes=[B_strm[sl]])
                        for cj in range(2):
                            c = 2 * c2 + cj
                            s_ = c % 2
                            bk, bb = nextbank()
                            for k in range(16):
                                p.op("pe", lambda e, sl=sl, cj=cj, k=k, bk=bk: e.matmul(bk[:, 0:256], lhsT=strm[sl][:, cj, k * 128:(k + 1) * 128], rhs=xnT[:, k, :], start=(k == 0), stop=(k == 15)),
                                     reads=[B_strm[sl], B_xnT], writes=[bb], pe_accum=True)
                            p.op("act", lambda e, s_=s_, bk=bk: e.activation(out=ge[s_][:], in_=bk[:, 0:256], func=AF.Gelu), reads=[bb], writes=[B_ge[s_]])
                            p.op("dve", lambda e, s_=s_, c=c: e.tensor_tensor(out=Gs[:, c, :], in0=Gs[:, c, :], in1=ge[s_][:], op=ALU.mult),
                                 reads=[B_ge[s_], B_Gs], writes=[B_Gs])
                        if ga is not None:
                            ga = drain(ga, 1)
                        else:
                            gb = drain(gb, 1)
                    ga = drain(ga, 10000)
                    for c2 in range(64):
                        sl = sctr[0] % NSL
                        sctr[0] += 1
                        p.dma("sp", lambda e, sl=sl, c2=c2: e.dma_start(out=strm[sl][:], in_=v_b[c2 * 256:(c2 + 1) * 256, :].rearrange("(c p) f -> p c f", p=128)), f"cs{sl}",
                              reads=[B_wuv], writes=[B_strm[sl]])
                        for cj in range(2):
                            c = 2 * c2 + cj
                            for ti in range(2):
                                for db in range(4):
                                    bi = ti * 4 + db
                                    p.op("pe", lambda e, sl=sl, cj=cj, c=c, ti=ti, db=db, bi=bi: e.matmul(banks[bi][:, 0:512], lhsT=Gs[:, c, ti * 128:(ti + 1) * 128], rhs=strm[sl][:, cj, db * 512:(db + 1) * 512],
                                                                                                   start=(c == 0), stop=(c == 127)), reads=[B_strm[sl], B_Gs], writes=[bank_buf[bi]], pe_accum=True)
                        gb = drain(gb, 1)
                    gb = drain(gb, 10000)
                    p.dma("sp", lambda e: e.dma_start(out=c_gf[:, 0, :], in_=rep_d[:, 2, :]), "cgf", writes=[B_gf])
                    for ti in range(2):
                        t = tok0 + ti * 128
                        p.dma("sp", lambda e, t=t: e.dma_start(out=x1t[:, 0, :], in_=x1_d[t:t + 128, :]), "cx", writes=[B_x1t])
                        for db in range(4):
                            bi = ti * 4 + db
                            p.op("dve", lambda e, db=db, bi=bi: e.tensor_tensor(out=x1t[:, 0, db * 512:(db + 1) * 512], in0=banks[bi][:, 0:512], in1=x1t[:, 0, db * 512:(db + 1) * 512], op=ALU.add),
                                 reads=[bank_buf[bi], B_x1t], writes=[B_x1t])
                        p.op("act", lambda e, ti=ti: e.activation(out=xn[:], in_=x1t[:, 0, :], func=AF.Square, accum_out=cst_[:, ti, 0:1]), reads=[B_x1t, B_xn], writes=[B_xn, B_cst])
                        p.op("act", lambda e, ti=ti: e.activation(out=cst_[:, ti, 1:2], in_=cst_[:, ti, 0:1], func=AF.Sqrt, scale=1.0 / D, bias=EPS), reads=[B_cst], writes=[B_cst])
                        p.op("dve", lambda e, ti=ti: e.reciprocal(out=cst_[:, ti, 2:3], in_=cst_[:, ti, 1:2]), reads=[B_cst], writes=[B_cst])
                        p.op("dve", lambda e, ti=ti: e.scalar_tensor_tensor(out=x1t[:, 0, :], in0=x1t[:, 0, :], scalar=cst_[:, ti, 2:3], in1=c_gf[:, 0, :], op0=ALU.mult, op1=ALU.mult),
                             reads=[B_cst, B_gf, B_x1t], writes=[B_x1t])
                        p.dma("pool", lambda e, t=t: e.dma_start(out=y_out[t:t + 128, :], in_=x1t[:, 0, :]), "cy", reads=[B_x1t])

                NGRP = T_OWN // 256
                drain(phaseA1(0), 10000)
                drain(phaseA2(0), 10000)
                for gi in range(NGRP):
                    gbuild(gi)
                    if gi + 1 < NGRP:
                        passes_and_epilogue(gi, phaseA1(gi + 1), phaseA2(gi + 1))
                    else:
                        passes_and_epilogue(gi, None, None)
                p.barrier()

        if "C" in stages:
            stage_c()
        p.barrier(skip=())
        p.emit()
    return nc


def host_inputs(inp, c):
    b, q = c // 4, c % 4
    f32 = np.float32
    xp = inp["x_prompt"][b]
    xs = inp["x_sample"][b]

    def win(x, lo, hi):
        n = x.shape[0]
        out = np.zeros((hi - lo, x.shape[1]), f32)
        a, bnd = max(lo, 0), min(hi, n)
        out[a - lo:bnd - lo] = x[a:bnd]
        return out

    own = []
    emask = np.zeros((128, NG_OWN, 2), f32)
    gi = 0
    for (x, L) in ((xp, 2048), (xs, 1024)):
        for g in range(L // 512):
            lo = q * L + g * 512 - 128
            own.append(win(x, lo, lo + 768))
            if lo < 0:
                emask[:, gi, 0] = -1e30
            if lo + 768 > x.shape[0]:
                emask[:, gi, 1] = -1e30
            gi += 1
    oth = []
    omask = np.zeros((128, NG_OTH, 4), f32)
    gi = 0
    for (x, L) in ((xp, 2048), (xs, 1024)):
        for j in [jj for jj in range(4) if jj != q]:
            for g in range(L // 512):
                lo = j * L + g * 512 - 2
                oth.append(win(x, lo, lo + 516))
                mf = 1.0 if j < q else 0.0
                omask[:, gi, :] = [mf, 1 - mf, 1 - mf, mf]
                gi += 1
    x_res = np.concatenate([xp[q * 2048:(q + 1) * 2048], xs[q * 1024:(q + 1) * 1024]], axis=0)
    rep = lambda v: np.broadcast_to(np.asarray(v, f32).reshape(1, -1), (128, np.asarray(v).size)).copy()
    rep_d = np.stack([rep(inp["g_mix"][0]), rep(inp["g_ffn"][0]), rep(inp["g_final"]), rep(inp["g_ssm_norm"][0])], axis=1)
    rep_s = np.zeros((128, 160), f32)
    rep_s[:, 0:32] = inp["a_log_f"][0]
    rep_s[:, 32:64] = inp["a_log_b"][0]
    rep_s[:, 64:96] = inp["d_skip"][0]
    rep_s[:, 96:112] = inp["attn_sink"][0]
    slopes = np.exp2(-8.0 * np.arange(1, 17, dtype=np.float64) / 16)
    qi = np.arange(128)[:, None]
    km = np.arange(384)[None, :]
    rel = qi - km + 128
    ab = np.where(np.abs(rel) <= 128, -np.abs(rel).astype(np.float64), -1e30).astype(f32)
    convw = np.ascontiguousarray(inp["conv_w"][0].T.reshape(24, 128, 5).transpose(1, 0, 2))
    convb = np.ascontiguousarray(inp["conv_b"][0].reshape(24, 128).T)
    dtb = np.concatenate([inp["dt_bias_f"][0], inp["dt_bias_b"][0]]).reshape(64, 1).astype(f32)
    cst = np.zeros((128, 9, 128), f32)
    s_ = np.arange(128)[:, None]
    l_ = np.arange(128)[None, :]
    cst[:, 0] = np.eye(128)
    cst[:, 1] = (s_ <= l_)
    cst[:, 2] = (s_ >= l_)
    cst[:, 3] = 1.0
    cst[:, 4] = l_
    import ml_dtypes
    negm = np.zeros((128, 2, 512), f32)
    negm[:, 0] = np.tile(np.where(s_ > l_, NEG, 0.0), (1, 4))
    negm[:, 1] = np.tile(np.where(s_ < l_, NEG, 0.0), (1, 4))
    sel = np.zeros((32, 32, 128), f32)
    for h in range(32):
        sel[h, h, :] = 1.0
    return {
        "x_own": np.stack(own), "x_oth": np.stack(oth), "x_res": x_res,
        "w_in": inp["w_in"][0], "w_ao": inp["w_attn_o"][0], "w_so": inp["w_ssm_o"][0], "w_out": inp["w_out"][0],
        "w_q": inp["w_query"][0], "keys": inp["sub_keys"][0].reshape(16, 128, 128),
        "exp_u": inp["expert_u"][0], "exp_v": inp["expert_v"][0],
        "rep_d": rep_d, "rep_s": rep_s, "attn_bias": ab, "emask": emask, "omask": omask,
        "convw": convw, "convb": convb, "dtb": dtb, "cst": cst, "negm": negm.astype(ml_dtypes.bfloat16), "sel": sel,
    }


def kernel(**inputs):
    inp = {k: np.asarray(v) for k, v in inputs.items()}
    nc = build()
    in_maps = [host_inputs(inp, c) for c in range(NCORES)]
    res = run_bass_kernel_spmd(nc, in_maps, core_ids=list(range(NCORES)))
    yp = np.zeros((2, 8192, D), np.float32)
    ys = np.zeros((2, 4096, D), np.float32)
    for c in range(NCORES):
        b, q = c // 4, c % 4
        y = res.results[c]["y_out"]
        yp[b, q * 2048:(q + 1) * 2048] = y[0:2048]
        ys[b, q * 1024:(q + 1) * 1024] = y[2048:3072]
    return (yp, ys)
```

```python
import contextlib
import numpy as np
import concourse.bass as bass
import concourse.mybir as mybir
from concourse.bass_utils import run_bass_kernel_spmd

F32 = mybir.dt.float32
BF16 = mybir.dt.bfloat16
U32 = mybir.dt.uint32
AF = mybir.ActivationFunctionType
ALU = mybir.AluOpType
AX = mybir.AxisListType
ENGS = ("pe", "act", "dve", "pool", "sp")

D = 2048
INW = 10816
NCORES = 8
Q_END, K_END, V_END, Z_END, XBC_END, DT_END = 1024, 1280, 1536, 3584, 6656, 6720
NEG = -30000.0
EPS = 1e-6
OWN_GROUPS = [(0, 4), (1, 2)]
NG_OWN = 6
NG_OTH = 18
T_OWN = 3072


class Buf:
    __slots__ = ("w", "r")

    def __init__(self):
        self.w = None
        self.r = []


class Prog:
    def __init__(self, nc):
        self.nc = nc
        self.ops = {e: [] for e in ENGS}
        self.dma_sems = {}
        self.waited = {e: {} for e in ENGS}

    def _need(self, eng, dep, waits):
        kind, key, val = dep
        k = (kind, key)
        if self.waited[eng].get(k, -1) >= val:
            return
        self.waited[eng][k] = val
        waits.append(dep)
        if kind == "e":
            self.ops[key][val]["inc"] = True

    def _deps(self, eng, reads, writes, pe_accum):
        waits = []
        for b in reads:
            if b.w is not None:
                self._need(eng, b.w, waits)
        for b in writes:
            if b.w is not None:
                if not (pe_accum and eng == "pe" and b.w[0] == "e" and b.w[1] == "pe"):
                    self._need(eng, b.w, waits)
            for d in b.r:
                self._need(eng, d, waits)
        return waits

    def op(self, eng, fn, reads=(), writes=(), pe_accum=False):
        waits = self._deps(eng, reads, writes, pe_accum)
        idx = len(self.ops[eng])
        self.ops[eng].append(dict(fn=fn, waits=waits, inc=False, dma=None))
        me = ("e", eng, idx)
        for b in reads:
            b.r.append(me)
        for b in writes:
            b.w = me
            b.r = []
        return me

    def dma(self, eng, fn, sem, reads=(), writes=()):
        waits = self._deps(eng, reads, writes, False)
        self.dma_sems[sem] = self.dma_sems.get(sem, 0) + 16
        val = self.dma_sems[sem]
        self.ops[eng].append(dict(fn=fn, waits=waits, inc=False, dma=(sem, 16)))
        me = ("d", sem, val)
        for b in reads:
            b.r.append(me)
        for b in writes:
            b.w = me
            b.r = []
        return me

    def barrier(self, skip=tuple(["cast_w", "cast_uv"] + [f"ci{i}" for i in range(32)])):
        deps = []
        for e in ENGS:
            for i in range(len(self.ops[e]) - 1, -1, -1):
                o = self.ops[e][i]
                if o["fn"] is not None and o["dma"] is None:
                    deps.append(("e", e, i))
                    break
        for s, v in self.dma_sems.items():
            if s not in skip:
                deps.append(("d", s, v))
        for e in ENGS:
            waits = []
            for d in deps:
                self._need(e, d, waits)
            self.ops[e].append(dict(fn=None, waits=waits, inc=False, dma=None))

    def emit(self):
        nc = self.nc
        with contextlib.ExitStack() as st:
            esem = {e: st.enter_context(nc.semaphore("s_" + e)) for e in ENGS}
            dsem = {n: st.enter_context(nc.semaphore("d_" + n)) for n in self.dma_sems}
            cum = {}
            for e in ENGS:
                c = 0
                arr = []
                for o in self.ops[e]:
                    if o["inc"]:
                        c += 1
                    arr.append(c)
                cum[e] = arr
            block = st.enter_context(nc.Block())

            def run(engname, engobj):
                for o in self.ops[engname]:
                    for (kind, key, val) in o["waits"]:
                        if kind == "e":
                            engobj.wait_ge(esem[key], cum[key][val])
                        else:
                            engobj.wait_ge(dsem[key], val)
                    if o["fn"] is None:
                        continue
                    ins = o["fn"](engobj)
                    if o["dma"] is not None:
                        ins.then_inc(dsem[o["dma"][0]], o["dma"][1])
                    elif o["inc"]:
                        ins.then_inc(esem[engname], 1)

            block.tensor(lambda e: run("pe", e))
            block.scalar(lambda e: run("act", e))
            block.vector(lambda e: run("dve", e))
            block.gpsimd(lambda e: run("pool", e))
            block.sync(lambda e: run("sp", e))


def bc(ap, shape):
    return ap.to_broadcast(list(shape))


class TilePool:
    def __init__(self, nc, stack):
        self.nc, self.stack, self.tiles, self.bufs, self.i, self.first = nc, stack, {}, [], 0, True

    def begin(self):
        self.first = (len(self.tiles) == 0)
        self.i = 0

    def sb(self, name, shape, dt):
        if name not in self.tiles:
            self.tiles[name] = self.stack.enter_context(self.nc.sbuf_tensor(name, list(shape), dt))
        return self.tiles[name]

    def buf(self):
        if self.i == len(self.bufs):
            self.bufs.append(Buf())
        b = self.bufs[self.i]
        self.i += 1
        return b


def build(stages=("W", "A", "S", "B", "C"), dbg=()):
    nc = bass.Bass("TRN2", target_bir_lowering=False)
    p = Prog(nc)

    def din(name, shape, dt=F32):
        return nc.dram_tensor(name, list(shape), dt, kind="ExternalInput").ap()

    def dscr(name, shape, dt):
        kind = "ExternalOutput" if name in dbg else "Internal"
        return nc.dram_tensor(name, list(shape), dt, kind=kind).ap()

    x_own = din("x_own", [NG_OWN, 768, D])
    x_oth = din("x_oth", [NG_OTH, 516, D])
    x_res = din("x_res", [T_OWN, D])
    w_in = din("w_in", [D, INW])
    w_ao = din("w_ao", [1024, D])
    w_so = din("w_so", [D, D])
    w_out = din("w_out", [D, D])
    w_q = din("w_q", [D, D])
    keys = din("keys", [16, 128, 128])
    exp_u = din("exp_u", [16384, D])
    exp_v = din("exp_v", [16384, D])
    rep_d = din("rep_d", [128, 4, D])
    rep_s = din("rep_s", [128, 160])
    attn_bias = din("attn_bias", [128, 384])
    emask = din("emask", [128, NG_OWN, 2])
    omask = din("omask", [128, NG_OTH, 4])
    convw = din("convw", [128, 24, 5])
    convb = din("convb", [128, 24])
    dtb = din("dtb", [64, 1])
    cst = din("cst", [128, 9, 128])
    negm = din("negm", [128, 2, 512], BF16)
    sel = din("sel", [32, 32, 128])
    y_out = nc.dram_tensor("y_out", [T_OWN, D], F32, kind="ExternalOutput").ap()

    w_in_b = dscr("w_in_b", [D, INW], BF16)
    w_ao_b = dscr("w_ao_b", [1024, D], BF16)
    w_so_b = dscr("w_so_b", [D, D], BF16)
    w_out_b = dscr("w_out_b", [D, D], BF16)
    w_q_b = dscr("w_q_b", [D, D], BF16)
    v_b = dscr("v_b", [16384, D], BF16)
    u_b = dscr("u_b", [16384, D], BF16)
    ut_b = dscr("ut_b", [128, 128, D], BF16)
    zs_d = dscr("zs_d", [T_OWN, D], BF16)
    gT_d = dscr("gT_d", [32, 128, T_OWN], BF16)
    Xs_d = dscr("Xs_d", [T_OWN, D], BF16)
    Bs_d = dscr("Bs_d", [T_OWN, 512], BF16)
    BT_d = dscr("BT_d", [4, 128, T_OWN], BF16)
    CT_d = dscr("CT_d", [4, 128, T_OWN], BF16)
    dt_d = dscr("dt_d", [T_OWN, 64], F32)
    aT_d = dscr("aT_d", [8, 128, T_OWN], BF16)
    hb_d = dscr("hb_d", [24, 128, D], BF16)
    hin_d = dscr("hin_d", [4, 128, D], F32)
    x1_d = dscr("x1_d", [T_OWN, D], F32)

    with contextlib.ExitStack() as top:
        def SB(stack, name, shape, dt):
            return stack.enter_context(nc.sbuf_tensor(name, list(shape), dt))

        banks = [top.enter_context(nc.psum_tensor(f"bank{i}", [128, 512], F32)) for i in range(8)]
        bank_buf = [Buf() for _ in range(8)]
        bank_ctr = [0]

        def nextbank():
            i = bank_ctr[0] % 8
            bank_ctr[0] += 1
            return banks[i], bank_buf[i]

        c_cst = SB(top, "c_cst", [128, 9, 128], F32)
        c_idb = SB(top, "c_idb", [128, 128], BF16)
        c_reps = SB(top, "c_reps", [128, 160], F32)
        c_arep = SB(top, "c_arep", [128, 64], F32)
        B_const = Buf()
        p.dma("sp", lambda e: e.dma_start(out=c_cst[:], in_=cst), "c0", writes=[B_const])
        p.dma("sp", lambda e: e.dma_start(out=c_reps[:], in_=rep_s), "c0", writes=[B_const])
        p.op("dve", lambda e: e.tensor_copy(out=c_idb[:], in_=c_cst[:, 0, :]), reads=[B_const], writes=[B_const])
        p.op("act", lambda e: e.activation(out=c_arep[:], in_=c_reps[:, 0:64], func=AF.Exp), reads=[B_const], writes=[B_const])
        p.op("dve", lambda e: e.tensor_scalar(out=c_arep[:], in0=c_arep[:], scalar1=-1.0, scalar2=None, op0=ALU.mult),
             reads=[B_const], writes=[B_const])
        IDF = c_cst[:, 0, :]
        TRIF = c_cst[:, 1, :]
        TRIB = c_cst[:, 2, :]
        ONES = c_cst[:, 3, :]
        IOTA = c_cst[:, 4, :]

        B_win, B_w, B_wuv = {}, Buf(), Buf()
        lazy_casts = []

        def emit_casts(n):
            for _ in range(n):
                if lazy_casts:
                    lazy_casts.pop(0)()
        WBLOCKS = ([(Z_END + i * 512, 512) for i in range(6)] + [(XBC_END, 64)] + [(V_END + i * 512, 512) for i in range(4)]
                   + [(DT_END + i * 512, 512) for i in range(8)] + [(0, 512), (512, 512), (Q_END, 512)])
        if "W" in stages:
            def cast(dst, src, rows, cols, rblk, sem, buf):
                for r0 in range(0, rows, rblk):
                    lazy_casts.append(lambda r0=r0, dst=dst, src=src, rblk=rblk, sem=sem, buf=buf: p.dma(
                        "pool", lambda e: e.dma_start(out=dst[r0:r0 + rblk, :], in_=src[r0:r0 + rblk, :], max_dma_last_dim=4096), sem, writes=[buf]))
            for i, (c0, ncol) in enumerate(WBLOCKS):
                B_win[c0] = Buf()
                p.dma("pool", lambda e, c0=c0, ncol=ncol: e.dma_start(out=w_in_b[:, c0:c0 + ncol], in_=w_in[:, c0:c0 + ncol], max_dma_last_dim=4096),
                      f"ci{i}", writes=[B_win[c0]])
            cast(w_ao_b, w_ao, 1024, D, 512, "cast_w", B_w)
            cast(w_so_b, w_so, D, D, 512, "cast_w", B_w)
            cast(w_out_b, w_out, D, D, 512, "cast_w", B_w)
            cast(w_q_b, w_q, D, D, 512, "cast_w", B_w)
            if "C" in stages:
                cast(u_b, exp_u, 16384, D, 1024, "cast_uv", B_wuv)
                cast(v_b, exp_v, 16384, D, 1024, "cast_uv", B_wuv)

        def rms_to_hT(st, xt_tiles, ntiles, hT, g_idx, col0s, nrows=None):
            pass

        def prep_group(pool, xsrc, nwin, own, gidx, tok0, par=0, res=None):
            pool.begin()
            Buf = pool.buf
            tg = "A" if own else "S"
            lo = 128 if own else 2
            ntile = (nwin + 127) // 128
            hT = pool.sb(f"hT{tg}", [128, 16, nwin], BF16)
            B_hT = Buf()
            xin = [pool.sb(f"xin{tg}_{i}", [128, D], F32) for i in range(2)]
            xn = [pool.sb(f"xn{tg}_{i}", [128, D], BF16) for i in range(2)]
            c_g = pool.sb(f"cg{tg}", [128, D], F32)
            B_cg = Buf()
            if pool.first:
                p.dma("sp", lambda e: e.dma_start(out=c_g[:], in_=rep_d[:, 0, :]), "c1", writes=[B_cg])
            stat = pool.sb(f"stat{tg}", [128, 8, 4], F32)
            B_xin = [Buf(), Buf()]
            B_xn = [Buf(), Buf()]
            B_stat = Buf()
            for ti in range(ntile):
                r0 = ti * 128
                rows = min(128, nwin - r0)
                s = ti % 2
                p.dma("sp", lambda e, s=s, r0=r0, rows=rows: e.dma_start(out=xin[s][0:rows, :], in_=xsrc[r0:r0 + rows, :]),
                      f"xin{s}", writes=[B_xin[s]])
                p.op("act", lambda e, s=s, rows=rows, ti=ti: e.activation(out=xn[s][0:rows, :], in_=xin[s][0:rows, :], func=AF.Square,
                                                                         accum_out=stat[0:rows, ti, 0:1]),
                     reads=[B_xin[s]], writes=[B_xn[s], B_stat])
                p.op("act", lambda e, rows=rows, ti=ti: e.activation(out=stat[0:rows, ti, 1:2], in_=stat[0:rows, ti, 0:1], func=AF.Sqrt,
                                                                    scale=1.0 / D, bias=EPS), reads=[B_stat], writes=[B_stat])
                p.op("dve", lambda e, rows=rows, ti=ti: e.reciprocal(out=stat[0:rows, ti, 2:3], in_=stat[0:rows, ti, 1:2]),
                     reads=[B_stat], writes=[B_stat])
                p.op("dve", lambda e, s=s, rows=rows, ti=ti: e.scalar_tensor_tensor(
                    out=xn[s][0:rows, :], in0=xin[s][0:rows, :], scalar=stat[0:rows, ti, 2:3], in1=c_g[0:rows, :],
                    op0=ALU.mult, op1=ALU.mult), reads=[B_xin[s], B_stat, B_cg], writes=[B_xn[s]])
                for half in range(2):
                    bk, bb = nextbank()
                    bkb = bk[:].bitcast(BF16)
                    for kk in range(8):
                        k = half * 8 + kk
                        p.op("pe", lambda e, s=s, rows=rows, k=k, kk=kk, bkb=bkb: e.transpose(
                            out=bkb[:, kk * 128:kk * 128 + rows], in_=xn[s][0:rows, k * 128:(k + 1) * 128], identity=c_idb[0:rows, 0:rows]),
                            reads=[B_xn[s], B_const], writes=[bb], pe_accum=True)
                    eng = "act" if half == 0 else "dve"
                    src = bkb.rearrange("p (k t) -> p k t", t=128)[:, :, 0:rows]
                    dst = hT[:, half * 8:half * 8 + 8, r0:r0 + rows]
                    if eng == "act":
                        p.op("act", lambda e, src=src, dst=dst: e.copy(out=dst, in_=src), reads=[bb], writes=[B_hT])
                    else:
                        p.op("dve", lambda e, src=src, dst=dst: e.tensor_copy(out=dst, in_=src), reads=[bb], writes=[B_hT])
                yield

            wt = [pool.sb(f"wt{tg}_{i}", [128, 16, 512], BF16) for i in range(2)]
            B_wt = [Buf(), Buf()]
            wctr = [0]

            def load_w(c0, ncol):
                s = wctr[0] % 2
                wctr[0] += 1
                src = w_in_b[:, c0:c0 + ncol].rearrange("(k p) c -> p k c", p=128)
                p.dma("sp", lambda e, s=s, src=src, ncol=ncol: e.dma_start(out=wt[s][:, :, 0:ncol], in_=src), f"wt{s}",
                      reads=[B_win[c0]], writes=[B_wt[s]])
                return wt[s], B_wt[s]

            def fm_proj(wtile, wb, wc0, M, t0, N):
                bk, bb = nextbank()
                for k in range(16):
                    p.op("pe", lambda e, k=k, bk=bk: e.matmul(bk[0:M, 0:N], lhsT=wtile[:, k, wc0:wc0 + M], rhs=hT[:, k, t0:t0 + N],
                                                               start=(k == 0), stop=(k == 15)),
                         reads=[wb, B_hT], writes=[bb], pe_accum=True)
                return bk, bb

            def tm_proj(wtile, wb, wc0, Ncol, t0, rows=128):
                bk, bb = nextbank()
                for k in range(16):
                    p.op("pe", lambda e, k=k, bk=bk: e.matmul(bk[0:rows, 0:Ncol], lhsT=hT[:, k, t0:t0 + rows], rhs=wtile[:, k, wc0:wc0 + Ncol],
                                                               start=(k == 0), stop=(k == 15)),
                         reads=[wb, B_hT], writes=[bb], pe_accum=True)
                return bk, bb

            NT = 512
            nconv = 24 if own else 20
            xbc = [pool.sb(f"xbc{tg}_{i}", [128, 516], BF16) for i in range(2)]
            dg = [pool.sb(f"dg{tg}_{i}", [128, 5, 128], BF16) for i in range(2)]
            B_dg = [Buf(), Buf()]
            B_xbc = [Buf(), Buf()]
            csil = [pool.sb(f"csil{tg}_{i}", [128, 512], BF16) for i in range(2)]
            B_csil = [Buf(), Buf()]
            c_cw = pool.sb(f"cw{tg}", [128, 24, 5], F32)
            c_cb = pool.sb(f"cb{tg}", [128, 24], F32)
            c_dtb = pool.sb(f"dtb{tg}", [64, 1], F32)
            B_cw = Buf()
            if pool.first:
                p.dma("sp", lambda e: e.dma_start(out=c_cw[:], in_=convw), "c1", writes=[B_cw])
                p.dma("sp", lambda e: e.dma_start(out=c_cb[:], in_=convb), "c1", writes=[B_cw])
                p.dma("sp", lambda e: e.dma_start(out=c_dtb[:], in_=dtb), "c1", writes=[B_cw])
            npar = 1 if own else 2
            Xtm = [pool.sb(f"Xtm{tg}{i}", [128, 4, D], BF16) for i in range(npar)][par]
            Btm = [pool.sb(f"Btm{tg}{i}", [128, 4, 512], BF16) for i in range(npar)][par]
            dttm = [pool.sb(f"dttm{tg}{i}", [128, 4, 64], F32) for i in range(npar)][par]
            B_Xtm = [Buf() for _ in range(npar)][par]
            B_Btm = [Buf() for _ in range(npar)][par]
            B_dttm = [Buf() for _ in range(npar)][par]
            w0 = lo - 2
            pending = [None]
            pending_tr = [None]
            pending_conv = [None]
            own_stores = []
            for cc4 in range(0, nconv, 4):
                wtile, wb = load_w(V_END + D + cc4 * 128, 512)
                for ci in range(4):
                    cc = cc4 + ci
                    s = cc % 2
                    for hf in range(2):
                        bk, bb = fm_proj(wtile, wb, ci * 128, 128, w0 + hf * 258, 258)
                        p.op("act", lambda e, s=s, hf=hf, bk=bk: e.copy(out=xbc[s][:, hf * 258:(hf + 1) * 258], in_=bk[:, 0:258]),
                             reads=[bb], writes=[B_xbc[s]])
                    if pending[0] is not None:
                        pending[0]()
                        pending[0] = None
                    if pending_conv[0] is not None:
                        pending_conv[0]()
                        pending_conv[0] = None
                        pending[0], pending_tr[0] = pending_tr[0], None
                    p.op("dve", lambda e, s=s, cc=cc: e.tensor_tensor(out=dg[s][:], in0=bc(c_idb[:].unsqueeze(1), [128, 5, 128]),
                                                                      in1=bc(c_cw[:, cc, :].unsqueeze(2), [128, 5, 128]), op=ALU.mult),
                         reads=[B_cw, B_const], writes=[B_dg[s]])

                    def conv_chunk(s=s, cc=cc):
                        bkc, bbc = nextbank()
                        for j in range(5):
                            p.op("pe", lambda e, j=j: e.matmul(bkc[:, 0:512], lhsT=dg[s][:, j, :], rhs=xbc[s][:, j:j + 512], start=(j == 0), stop=(j == 4)),
                                 reads=[B_dg[s], B_xbc[s]], writes=[bbc], pe_accum=True)
                        p.op("act", lambda e: e.activation(out=csil[s][:], in_=bkc[:, 0:512], func=AF.Silu, bias=c_cb[:, cc:cc + 1]),
                             reads=[bbc, B_cw], writes=[B_csil[s]])
                        if own and cc >= 16:
                            g = (cc - 16) % 4
                            dst = (BT_d if cc < 20 else CT_d)[g, :, tok0:tok0 + NT]
                            p.dma("pool", lambda e: e.dma_start(out=dst, in_=csil[s][:]), f"stc{s}", reads=[B_csil[s]])
                    pending_conv[0] = conv_chunk
                    if cc < 20:
                        def tr_chunk(s=s, cc=cc):
                            bk, bb = nextbank()
                            bkb = bk[:].bitcast(BF16)
                            for ti in range(4):
                                p.op("pe", lambda e, ti=ti: e.transpose(out=bkb[:, ti * 128:(ti + 1) * 128],
                                                                         in_=csil[s][:, ti * 128:(ti + 1) * 128], identity=c_idb[:]),
                                     reads=[B_csil[s], B_const], writes=[bb], pe_accum=True)
                            src = bkb[:, 0:512].rearrange("p (t c) -> p t c", c=128)
                            if cc < 16:
                                p.op("act", lambda e: e.copy(out=Xtm[:, :, cc * 128:(cc + 1) * 128], in_=src), reads=[bb], writes=[B_Xtm])
                            else:
                                p.op("act", lambda e: e.copy(out=Btm[:, :, (cc - 16) * 128:(cc - 15) * 128], in_=src), reads=[bb], writes=[B_Btm])
                        pending_tr[0] = tr_chunk
                    yield
            wtile, wb = load_w(XBC_END, 64)
            bk, bb = fm_proj(wtile, wb, 0, 64, lo, NT)
            for q_ in (pending, pending_conv, pending_tr):
                if q_[0] is not None:
                    q_[0]()
                    q_[0] = None
            dtf = pool.sb(f"dtf{tg}", [64, 512], F32)
            B_dtf = Buf()
            p.op("act", lambda e, bk=bk: e.activation(out=dtf[:], in_=bk[0:64, 0:512], func=AF.Exp, bias=c_dtb[:, 0:1]),
                 reads=[bb, B_cw], writes=[B_dtf])
            p.op("act", lambda e: e.activation(out=dtf[:], in_=dtf[:], func=AF.Ln, bias=1.0), reads=[B_dtf], writes=[B_dtf])
            bk, bb = nextbank()
            for ti in range(4):
                p.op("pe", lambda e, ti=ti, bk=bk: e.transpose(out=bk[:, ti * 64:(ti + 1) * 64], in_=dtf[:, ti * 128:(ti + 1) * 128],
                                                                identity=c_cst[0:64, 0, 0:64]),
                     reads=[B_dtf, B_const], writes=[bb], pe_accum=True)
            p.op("dve", lambda e, bk=bk: e.tensor_copy(out=dttm[:], in_=bk[:, 0:256].rearrange("p (t c) -> p t c", c=64)),
                 reads=[bb], writes=[B_dttm])
            if res is not None:
                res.update(Xtm=Xtm, Btm=Btm, dttm=dttm, B_Xtm=B_Xtm, B_Btm=B_Btm, B_dttm=B_dttm)
            yield
            if not own:
                return
            for ti in range(4):
                t = tok0 + ti * 128
                p.dma("pool", lambda e, ti=ti, t=t: e.dma_start(out=Xs_d[t:t + 128, :], in_=Xtm[:, ti, :]), "stX", reads=[B_Xtm])
                p.dma("pool", lambda e, ti=ti, t=t: e.dma_start(out=Bs_d[t:t + 128, :], in_=Btm[:, ti, :]), "stX", reads=[B_Btm])
                p.dma("pool", lambda e, ti=ti, t=t: e.dma_start(out=dt_d[t:t + 128, :], in_=dttm[:, ti, :]), "stX", reads=[B_dttm])

            zt = [pool.sb(f"zt{tg}_{i}", [128, 512], BF16) for i in range(2)]
            B_zt = [Buf(), Buf()]
            zc = 0
            for cb4 in range(4):
                wtile, wb = load_w(V_END + cb4 * 512, 512)
                for ti in range(4):
                    bk, bb = tm_proj(wtile, wb, 0, 512, lo + ti * 128)
                    s = zc % 2
                    zc += 1
                    p.op("act", lambda e, s=s, bk=bk: e.activation(out=zt[s][:], in_=bk[:], func=AF.Silu), reads=[bb], writes=[B_zt[s]])
                    t = tok0 + ti * 128
                    p.dma("pool", lambda e, s=s, t=t, cb4=cb4: e.dma_start(out=zs_d[t:t + 128, cb4 * 512:(cb4 + 1) * 512], in_=zt[s][:]),
                          f"stz{s}", reads=[B_zt[s]])
            gt = [pool.sb(f"gt{tg}_{i}", [128, 512], BF16) for i in range(2)]
            B_gt = [Buf(), Buf()]
            for cb4 in range(8):
                wtile, wb = load_w(DT_END + cb4 * 512, 512)
                for ci in range(4):
                    cc = cb4 * 4 + ci
                    bk, bb = fm_proj(wtile, wb, ci * 128, 128, lo, NT)
                    s = cc % 2
                    p.op("act", lambda e, s=s, bk=bk: e.activation(out=gt[s][:], in_=bk[:], func=AF.Sigmoid), reads=[bb], writes=[B_gt[s]])
                    p.dma("pool", lambda e, s=s, cc=cc: e.dma_start(out=gT_d[cc, :, tok0:tok0 + NT], in_=gt[s][:]), f"stg{s}", reads=[B_gt[s]])
            qT = pool.sb(f"qT{tg}", [64, 16, 512], BF16)
            kT = pool.sb(f"kT{tg}", [64, 4, 768], BF16)
            vt = pool.sb(f"vt{tg}", [128, 6, 256], BF16)
            B_qT, B_kT, B_vt = Buf(), Buf(), Buf()
            for cb4 in range(2):
                wtile, wb = load_w(cb4 * 512, 512)
                for hh in range(8):
                    h = cb4 * 8 + hh
                    bk, bb = fm_proj(wtile, wb, hh * 64, 64, lo, NT)
                    p.op("act", lambda e, h=h, bk=bk: e.activation(out=qT[:, h, :], in_=bk[0:64, :], func=AF.Copy, scale=0.125),
                         reads=[bb], writes=[B_qT])
            wtile, wb = load_w(Q_END, 512)
            for kv in range(4):
                for hf in range(2):
                    bk, bb = fm_proj(wtile, wb, kv * 64, 64, hf * 384, 384)
                    p.op("dve", lambda e, kv=kv, hf=hf, bk=bk: e.tensor_copy(out=kT[:, kv, hf * 384:(hf + 1) * 384], in_=bk[0:64, 0:384]),
                         reads=[bb], writes=[B_kT])
            for ti in range(6):
                bk, bb = tm_proj(wtile, wb, 256, 256, ti * 128)
                p.op("act", lambda e, ti=ti, bk=bk: e.copy(out=vt[:, ti, :], in_=bk[:, 0:256]), reads=[bb], writes=[B_vt])

            c_ab = pool.sb(f"ab{tg}", [128, 384], F32)
            c_em = pool.sb(f"em{tg}", [128, NG_OWN, 2], F32)
            B_ab = Buf()
            if pool.first:
                p.dma("sp", lambda e: e.dma_start(out=c_ab[:], in_=attn_bias), "c1", writes=[B_ab])
                p.dma("sp", lambda e: e.dma_start(out=c_em[:], in_=emask), "c1", writes=[B_ab])
            sc = [pool.sb(f"sc{tg}_{i}", [128, 4, 384], F32) for i in range(2)]
            pr = [pool.sb(f"pr{tg}_{i}", [128, 4, 384], BF16) for i in range(2)]
            prT = [pool.sb(f"prT{tg}_{i}", [128, 4, 3, 128], BF16) for i in range(2)]
            ast = [pool.sb(f"ast{tg}_{i}", [128, 4, 8], F32) for i in range(2)]
            B_sc, B_pr, B_prT, B_ast = [Buf(), Buf()], [Buf(), Buf()], [Buf(), Buf()], [Buf(), Buf()]
            atm = pool.sb(f"atm{tg}", [128, 1024], BF16)
            B_atm = Buf()
            aTt = pool.sb(f"aTt{tg}", [128, 8, 512], BF16)
            B_aTt = Buf()
            def att_head(j, kv, s):
                sbanks = []
                for g in range(4):
                    h = kv * 4 + g
                    bk, bb = nextbank()
                    p.op("pe", lambda e, h=h, kv=kv, j=j, bk=bk: e.matmul(bk[:, 0:384], lhsT=qT[:, h, j * 128:(j + 1) * 128],
                                                                          rhs=kT[:, kv, j * 128:j * 128 + 384], start=True, stop=True),
                         reads=[B_qT, B_kT], writes=[bb])
                    p.op("dve", lambda e, s=s, g=g, h=h, bk=bk: e.scalar_tensor_tensor(out=sc[s][:, g, :], in0=c_ab[:], scalar=float(2.0 ** (-8.0 * (h + 1) / 16)),
                                                                                       in1=bk[:, 0:384], op0=ALU.mult, op1=ALU.add),
                         reads=[bb, B_ab], writes=[B_sc[s]])
                if j == 0:
                    p.op("dve", lambda e, s=s: e.tensor_scalar(out=sc[s][:, :, 0:128], in0=sc[s][:, :, 0:128], scalar1=c_em[:, gidx, 0:1],
                                                               scalar2=None, op0=ALU.add), reads=[B_sc[s], B_ab], writes=[B_sc[s]])
                if j == 3:
                    p.op("dve", lambda e, s=s: e.tensor_scalar(out=sc[s][:, :, 256:384], in0=sc[s][:, :, 256:384], scalar1=c_em[:, gidx, 1:2],
                                                               scalar2=None, op0=ALU.add), reads=[B_sc[s], B_ab], writes=[B_sc[s]])
                p.op("dve", lambda e, s=s: e.tensor_reduce(out=ast[s][:, :, 0], in_=sc[s][:], axis=AX.X, op=ALU.max),
                     reads=[B_sc[s]], writes=[B_ast[s]])
                p.op("dve", lambda e, s=s, kv=kv: e.tensor_tensor(out=ast[s][:, :, 1], in0=ast[s][:, :, 0], in1=c_reps[:, 96 + kv * 4:100 + kv * 4],
                                                                  op=ALU.max), reads=[B_ast[s], B_const], writes=[B_ast[s]])
                p.op("dve", lambda e, s=s: e.tensor_scalar(out=ast[s][:, :, 2], in0=ast[s][:, :, 1], scalar1=-1.0, scalar2=None, op0=ALU.mult),
                     reads=[B_ast[s]], writes=[B_ast[s]])
                for g in range(4):
                    p.op("act", lambda e, s=s, g=g: e.activation(out=pr[s][:, g, :], in_=sc[s][:, g, :], func=AF.Exp, bias=ast[s][:, g, 2:3],
                                                                 accum_out=ast[s][:, g, 3:4]), reads=[B_sc[s], B_ast[s]], writes=[B_pr[s], B_ast[s]])
                p.op("dve", lambda e, s=s, kv=kv: e.tensor_tensor(out=ast[s][:, :, 4], in0=c_reps[:, 96 + kv * 4:100 + kv * 4], in1=ast[s][:, :, 1],
                                                                  op=ALU.subtract), reads=[B_ast[s], B_const], writes=[B_ast[s]])
                p.op("act", lambda e, s=s: e.activation(out=ast[s][:, :, 5], in_=ast[s][:, :, 4], func=AF.Exp), reads=[B_ast[s]], writes=[B_ast[s]])
                p.op("dve", lambda e, s=s: e.tensor_tensor(out=ast[s][:, :, 6], in0=ast[s][:, :, 5], in1=ast[s][:, :, 3], op=ALU.add),
                     reads=[B_ast[s]], writes=[B_ast[s]])
                p.op("dve", lambda e, s=s: e.reciprocal(out=ast[s][:, :, 7], in_=ast[s][:, :, 6]), reads=[B_ast[s]], writes=[B_ast[s]])

            def att_tail(j, kv, s):
                for g in range(4):
                    bk, bb = nextbank()
                    bkb = bk[:].bitcast(BF16)
                    for m in range(3):
                        p.op("pe", lambda e, s=s, g=g, m=m, bkb=bkb: e.transpose(out=bkb[:, m * 128:(m + 1) * 128], in_=pr[s][:, g, m * 128:(m + 1) * 128],
                                                                                identity=c_idb[:]), reads=[B_pr[s], B_const], writes=[bb], pe_accum=True)
                    eng = "act" if g % 2 == 0 else "dve"
                    if eng == "act":
                        p.op("act", lambda e, s=s, g=g, bkb=bkb: e.copy(out=prT[s][:, g, :, :], in_=bkb[:, 0:384].rearrange("p (m t) -> p m t", t=128)),
                             reads=[bb], writes=[B_prT[s]])
                    else:
                        p.op("dve", lambda e, s=s, g=g, bkb=bkb: e.tensor_copy(out=prT[s][:, g, :, :], in_=bkb[:, 0:384].rearrange("p (m t) -> p m t", t=128)),
                             reads=[bb], writes=[B_prT[s]])
                bk, bb = nextbank()
                for g in range(4):
                    for m in range(3):
                        p.op("pe", lambda e, s=s, g=g, m=m, kv=kv, j=j, bk=bk: e.matmul(bk[:, g * 64:(g + 1) * 64], lhsT=prT[s][:, g, m, :],
                                                                                     rhs=vt[:, j + m, kv * 64:(kv + 1) * 64], start=(m == 0), stop=(m == 2)),
                             reads=[B_prT[s], B_vt], writes=[bb], pe_accum=True)
                p.op("dve", lambda e, s=s, kv=kv, bk=bk: e.tensor_tensor(
                    out=atm[:, kv * 256:(kv + 1) * 256].rearrange("p (g d) -> p g d", d=64),
                    in0=bk[:, 0:256].rearrange("p (g d) -> p g d", d=64),
                    in1=bc(ast[s][:, :, 7:8], [128, 4, 64]), op=ALU.mult), reads=[bb, B_ast[s]], writes=[B_atm])
                if kv == 3:
                    bk, bb = nextbank()
                    bkb = bk[:].bitcast(BF16)
                    for k in range(8):
                        p.op("pe", lambda e, k=k, bkb=bkb: e.transpose(out=bkb[:, k * 128:(k + 1) * 128], in_=atm[:, k * 128:(k + 1) * 128], identity=c_idb[:]),
                             reads=[B_atm, B_const], writes=[bb], pe_accum=True)
                    p.op("act", lambda e, j=j, bkb=bkb: e.copy(out=aTt[:, :, j * 128:(j + 1) * 128], in_=bkb.rearrange("p (k t) -> p k t", t=128)),
                         reads=[bb], writes=[B_aTt])

            items = [(j, kv) for j in range(4) for kv in range(4)]
            att_head(items[0][0], items[0][1], 0)
            for i_ in range(1, 16):
                att_head(items[i_][0], items[i_][1], i_ % 2)
                att_tail(items[i_ - 1][0], items[i_ - 1][1], (i_ - 1) % 2)
            att_tail(items[15][0], items[15][1], 1)
            for k in range(8):
                p.dma("pool", lambda e, k=k: e.dma_start(out=aT_d[k, :, tok0:tok0 + NT], in_=aTt[:, k, :]), "sta", reads=[B_aTt])

        if "A" in stages:
            gi = 0
            with contextlib.ExitStack() as stA:
                poolA = TilePool(nc, stA)
                for seg, ng in OWN_GROUPS:
                    for g in range(ng):
                        tok0 = (0 if seg == 0 else 2048) + g * 512
                        for _ in prep_group(poolA, x_own[gi], 768, True, gi, tok0):
                            pass
                        emit_casts(2)
                        gi += 1
                p.barrier()


        ssd_stack = contextlib.ExitStack()
        Hst = SB(ssd_stack, "Hst", [128, 4, D], F32)
        B_H = [Buf() for _ in range(4)]

        def ssd_work(st, tag, two_xs=False):
            W = {}
            for nm, shp, dt in (("dA", [128, 64], F32), ("cumsb", [128, 128], F32), ("tmp", [128, 64], F32), ("dte", [128, 64], F32),
                                ("dec", [128, 64], F32), ("w", [128, 64], F32), ("xs", [128, D], BF16), ("xs2", [128, D], BF16), ("deff", [128, 64], F32),
                                ("tg", [128, 512], F32)):
                if nm == "xs2" and not two_xs:
                    continue
                W[nm] = SB(st, nm + tag, shp, dt)
                W["B_" + nm] = Buf()
            return W

        def chunk_pre(W, dt_ap, B_dt):
            p.op("dve", lambda e: e.tensor_tensor(out=W["dA"][:], in0=dt_ap, in1=c_arep[:], op=ALU.mult), reads=[B_dt, B_const], writes=[W["B_dA"]])
            bk, bb = nextbank()
            p.op("pe", lambda e, bk=bk: e.matmul(bk[:, 0:32], lhsT=TRIF, rhs=W["dA"][:, 0:32], start=True, stop=True), reads=[W["B_dA"], B_const], writes=[bb])
            p.op("pe", lambda e, bk=bk: e.matmul(bk[:, 32:64], lhsT=TRIB, rhs=W["dA"][:, 32:64], start=True, stop=True), reads=[W["B_dA"], B_const], writes=[bb], pe_accum=True)
            p.op("pe", lambda e, bk=bk: e.matmul(bk[:, 64:128], lhsT=ONES, rhs=W["dA"][:, 0:64], start=True, stop=True), reads=[W["B_dA"], B_const], writes=[bb], pe_accum=True)
            p.op("act", lambda e, bk=bk: e.copy(out=W["cumsb"][:], in_=bk[:, 0:128]), reads=[bb], writes=[W["B_cumsb"]])
            p.op("dve", lambda e: e.tensor_tensor(out=W["tmp"][:], in0=W["cumsb"][:, 64:128], in1=W["cumsb"][:, 0:64], op=ALU.subtract),
                 reads=[W["B_cumsb"]], writes=[W["B_tmp"]])
            p.op("act", lambda e: e.activation(out=W["dte"][:], in_=W["tmp"][:], func=AF.Exp), reads=[W["B_tmp"]], writes=[W["B_dte"]])
            p.op("act", lambda e: e.activation(out=W["dec"][:], in_=W["cumsb"][:, 64:128], func=AF.Exp), reads=[W["B_cumsb"]], writes=[W["B_dec"]])
            p.op("dve", lambda e: e.tensor_tensor(out=W["w"][:], in0=dt_ap, in1=W["dte"][:], op=ALU.mult), reads=[B_dt, W["B_dte"]], writes=[W["B_w"]])

        def chunk_states(W, X_ap, B_X, Bt_ap, B_Bt, d, xk="xs"):
            p.op("dve", lambda e: e.tensor_tensor(out=W[xk][:].rearrange("p (h d) -> p h d", d=64), in0=X_ap.rearrange("p (h d) -> p h d", d=64),
                                                  in1=bc(W["w"][:, d * 32:(d + 1) * 32].unsqueeze(2), [128, 32, 64]), op=ALU.mult),
                 reads=[B_X, W["B_w"]], writes=[W["B_" + xk]])
            out = []
            for g in range(4):
                bk, bb = nextbank()
                p.op("pe", lambda e, g=g, bk=bk: e.matmul(bk[:, 0:512], lhsT=Bt_ap[:, g * 128:(g + 1) * 128], rhs=W[xk][:, g * 512:(g + 1) * 512],
                                                         start=True, stop=True), reads=[B_Bt, W["B_" + xk]], writes=[bb])
                out.append((bk, bb))
            return out

        def h_update(W, hi, d, sbanks):
            Hv = Hst[:, hi, :]
            p.op("dve", lambda e: e.tensor_tensor(out=Hv.rearrange("p (h d) -> p h d", d=64), in0=Hv.rearrange("p (h d) -> p h d", d=64),
                                                  in1=bc(W["dec"][:, d * 32:(d + 1) * 32].unsqueeze(2), [128, 32, 64]), op=ALU.mult),
                 reads=[W["B_dec"], B_H[hi]], writes=[B_H[hi]])
            for g, (bk, bb) in enumerate(sbanks):
                p.op("dve", lambda e, g=g, bk=bk: e.tensor_tensor(out=Hst[:, hi, g * 512:(g + 1) * 512], in0=bk[:, 0:512], in1=Hst[:, hi, g * 512:(g + 1) * 512], op=ALU.add),
                     reads=[bb, B_H[hi]], writes=[B_H[hi]])

        if "S" in stages:
            with contextlib.ExitStack() as st0:
                Pb = SB(st0, "Pb", [128, 2, 32], F32)
                wpb = SB(st0, "wpb", [128, 32], F32)
                c_om = SB(st0, "c_om", [128, NG_OTH, 4], F32)
                B_Pb, B_wpb, B_om = Buf(), Buf(), Buf()
                p.dma("sp", lambda e: e.dma_start(out=c_om[:], in_=omask), "c1", writes=[B_om])
                p.op("pool", lambda e: e.memset(Pb[:], 1.0), writes=[B_Pb])
                for hi in range(4):
                    p.op("pool", lambda e, hi=hi: e.memset(Hst[:, hi, :], 0.0), writes=[B_H[hi]])
                poolS = TilePool(nc, st0)
                W_S = [ssd_work(st0, "S0", True), ssd_work(st0, "S1", True)]

                def other_group(gi, r, nxt):
                    seg = 0 if gi < 12 else 1
                    if True:

                        def xs_scale(W, ti, d, xk):
                            p.op("dve", lambda e: e.tensor_tensor(out=W[xk][:].rearrange("p (h d) -> p h d", d=64), in0=r["Xtm"][:, ti, :].rearrange("p (h d) -> p h d", d=64),
                                                                  in1=bc(W["w"][:, d * 32:(d + 1) * 32].unsqueeze(2), [128, 32, 64]), op=ALU.mult),
                                 reads=[r["B_Xtm"], W["B_w"]], writes=[W["B_" + xk]])

                        def tile_pre(ti, W):
                            chunk_pre(W, r["dttm"][:, ti, :], r["B_dttm"])
                            xs_scale(W, ti, 0, "xs")
                            xs_scale(W, ti, 1, "xs2")

                        def st_mm(W, ti, g, xk):
                            bk, bb = nextbank()
                            p.op("pe", lambda e: e.matmul(bk[:, 0:512], lhsT=r["Btm"][:, ti, g * 128:(g + 1) * 128], rhs=W[xk][:, g * 512:(g + 1) * 512],
                                                          start=True, stop=True), reads=[r["B_Btm"], W["B_" + xk]], writes=[bb])
                            return bk, bb

                        def tile_main(ti, W):
                            hi = seg * 2
                            p.op("dve", lambda e: e.tensor_scalar(out=W["deff"][:, 0:32], in0=W["dec"][:, 0:32], scalar1=c_om[:, gi, 0:1], scalar2=c_om[:, gi, 1:2],
                                                                  op0=ALU.mult, op1=ALU.add), reads=[W["B_dec"], B_om], writes=[W["B_deff"]])
                            Hv = Hst[:, hi, :]
                            p.op("dve", lambda e: e.tensor_tensor(out=Hv.rearrange("p (h d) -> p h d", d=64), in0=Hv.rearrange("p (h d) -> p h d", d=64),
                                                                  in1=bc(W["deff"][:, 0:32].unsqueeze(2), [128, 32, 64]), op=ALU.mult),
                                 reads=[W["B_deff"], B_H[hi]], writes=[B_H[hi]])
                            for g in range(4):
                                bk, bb = st_mm(W, ti, g, "xs")
                                p.op("dve", lambda e, g=g, bk=bk, hi=hi: e.scalar_tensor_tensor(
                                    out=Hst[:, hi, g * 512:(g + 1) * 512], in0=bk[:, 0:512], scalar=c_om[:, gi, 0:1], in1=Hst[:, hi, g * 512:(g + 1) * 512],
                                    op0=ALU.mult, op1=ALU.add), reads=[bb, B_H[hi], B_om], writes=[B_H[hi]])
                            hi = seg * 2 + 1
                            p.op("dve", lambda e: e.tensor_scalar(out=wpb[:], in0=Pb[:, seg, :], scalar1=c_om[:, gi, 2:3], scalar2=None, op0=ALU.mult),
                                 reads=[B_Pb, B_om], writes=[B_wpb])
                            for g in range(4):
                                bk, bb = st_mm(W, ti, g, "xs2")
                                p.op("dve", lambda e, g=g, bk=bk: e.tensor_tensor(out=W["tg"][:].rearrange("p (h d) -> p h d", d=64),
                                                                                 in0=bk[:, 0:512].rearrange("p (h d) -> p h d", d=64),
                                                                                 in1=bc(wpb[:, g * 8:(g + 1) * 8].unsqueeze(2), [128, 8, 64]), op=ALU.mult),
                                     reads=[bb, B_wpb], writes=[W["B_tg"]])
                                p.op("dve", lambda e, g=g, hi=hi: e.tensor_tensor(out=Hst[:, hi, g * 512:(g + 1) * 512], in0=Hst[:, hi, g * 512:(g + 1) * 512],
                                                                                   in1=W["tg"][:], op=ALU.add), reads=[W["B_tg"], B_H[hi]], writes=[B_H[hi]])
                            p.op("dve", lambda e: e.tensor_scalar(out=W["deff"][:, 32:64], in0=W["dec"][:, 32:64], scalar1=c_om[:, gi, 2:3], scalar2=c_om[:, gi, 3:4],
                                                                  op0=ALU.mult, op1=ALU.add), reads=[W["B_dec"], B_om], writes=[W["B_deff"]])
                            p.op("dve", lambda e: e.tensor_tensor(out=Pb[:, seg, :], in0=Pb[:, seg, :], in1=W["deff"][:, 32:64], op=ALU.mult),
                                 reads=[W["B_deff"], B_Pb, B_wpb], writes=[B_Pb])

                        def dr(n):
                            for _ in range(n):
                                if nxt[0] is not None:
                                    try:
                                        next(nxt[0])
                                    except StopIteration:
                                        nxt[0] = None

                        tile_pre(0, W_S[0])
                        for ti in range(4):
                            dr(3)
                            if ti + 1 < 4:
                                tile_pre(ti + 1, W_S[(ti + 1) % 2])
                            dr(4)
                            tile_main(ti, W_S[ti % 2])
                        dr(1000)
                rs = [dict() for _ in range(NG_OTH)]
                for _ in prep_group(poolS, x_oth[0], 516, False, 100, 0, 0, rs[0]):
                    pass
                for gi in range(NG_OTH):
                    nxt = [prep_group(poolS, x_oth[gi + 1], 516, False, 101 + gi, 0, (gi + 1) % 2, rs[gi + 1])] if gi + 1 < NG_OTH else [None]
                    other_group(gi, rs[gi], nxt)
                    emit_casts(2)
                emit_casts(1000)
                if "hin_d" in dbg:
                    for hi in range(4):
                        p.dma("pool", lambda e, hi=hi: e.dma_start(out=hin_d[hi], in_=Hst[:, hi, :]), "sth", reads=[B_H[hi]])
                p.barrier()

        SEGS = [(0, 0, 16), (1, 2048, 8)]
        def uprep_gen(stk):
            ul = [SB(stk, f"ul{i}", [128, D], BF16) for i in range(2)]
            ut = [SB(stk, f"ut{i}", [128, D], BF16) for i in range(2)]
            B_ul, B_ut = [Buf(), Buf()], [Buf(), Buf()]
            for c in range(128):
                s_ = c % 2
                p.dma("sp", lambda e, s_=s_, c=c: e.dma_start(out=ul[s_][:], in_=u_b[c * 128:(c + 1) * 128, :]), f"ul{s_}", reads=[B_wuv], writes=[B_ul[s_]])
                yield
                for half in range(2):
                    bk, bb = nextbank()
                    bkb = bk[:].bitcast(BF16)
                    for kk in range(8):
                        k = half * 8 + kk
                        p.op("pe", lambda e, s_=s_, k=k, kk=kk, bkb=bkb: e.transpose(out=bkb[:, kk * 128:(kk + 1) * 128], in_=ul[s_][:, k * 128:(k + 1) * 128], identity=c_idb[:]),
                             reads=[B_ul[s_], B_const], writes=[bb], pe_accum=True)
                    if half == 0:
                        p.op("act", lambda e, s_=s_, bkb=bkb: e.copy(out=ut[s_][:, 0:1024], in_=bkb), reads=[bb], writes=[B_ut[s_]])
                    else:
                        p.op("dve", lambda e, s_=s_, bkb=bkb: e.tensor_copy(out=ut[s_][:, 1024:2048], in_=bkb), reads=[bb], writes=[B_ut[s_]])
                p.dma("pool", lambda e, s_=s_, c=c: e.dma_start(out=ut_b[c], in_=ut[s_][:]), f"us{s_}", reads=[B_ut[s_]])
                yield

        def stage_b1():
            with contextlib.ExitStack() as st:
                W2 = [ssd_work(st, "B1a"), ssd_work(st, "B1b")]
                Xc = [SB(st, f"b1X{i}", [128, D], BF16) for i in range(2)]
                Bc = [SB(st, f"b1B{i}", [128, 512], BF16) for i in range(2)]
                dc = [SB(st, f"b1d{i}", [128, 64], F32) for i in range(2)]
                hbt = [SB(st, f"b1h{i}", [128, D], BF16) for i in range(2)]
                B_Xc, B_Bc, B_dc, B_hbt = [Buf(), Buf()], [Buf(), Buf()], [Buf(), Buf()], [Buf(), Buf()]
                it = 0
                cbase = 0
                ug = uprep_gen(st) if "C" in stages else iter(())
                for seg, tok0, nch in SEGS:
                    hi = seg * 2 + 1
                    for c in range(nch - 1, -1, -1):
                        s = it % 2
                        W = W2[s]
                        it += 1
                        t = tok0 + c * 128
                        p.dma("sp", lambda e, s=s, t=t: e.dma_start(out=Xc[s][:], in_=Xs_d[t:t + 128, :]), f"b1l{s}", writes=[B_Xc[s]])
                        p.dma("sp", lambda e, s=s, t=t: e.dma_start(out=Bc[s][:], in_=Bs_d[t:t + 128, :]), f"b1l{s}", writes=[B_Bc[s]])
                        p.dma("sp", lambda e, s=s, t=t: e.dma_start(out=dc[s][:], in_=dt_d[t:t + 128, :]), f"b1l{s}", writes=[B_dc[s]])
                        p.op("act", lambda e, s=s, hi=hi: e.copy(out=hbt[s][:], in_=Hst[:, hi, :]), reads=[B_H[hi]], writes=[B_hbt[s]])
                        p.dma("pool", lambda e, s=s, cc=cbase + c: e.dma_start(out=hb_d[cc], in_=hbt[s][:]), f"b1s{s}", reads=[B_hbt[s]])
                        chunk_pre(W, dc[s][:], B_dc[s])
                        for _ in range(6):
                            next(ug, None)
                        sb_ = chunk_states(W, Xc[s][:], B_Xc[s], Bc[s][:], B_Bc[s], 1)
                        h_update(W, hi, 1, sb_)
                        for _ in range(6):
                            next(ug, None)
                    cbase += nch
                for _ in ug:
                    pass
                p.barrier()

        def stage_b2():
            with contextlib.ExitStack() as st:
                W2b = [ssd_work(st, "B2a"), ssd_work(st, "B2b")]
                Xc2 = [SB(st, f"b2X{i}", [128, D], BF16) for i in range(2)]
                Bc2 = [SB(st, f"b2B{i}", [128, 512], BF16) for i in range(2)]
                dc2 = [SB(st, f"b2d{i}", [128, 64], F32) for i in range(2)]
                zc2 = [SB(st, f"b2z{i}", [128, D], BF16) for i in range(2)]
                BTc2 = [SB(st, f"b2BT{i}", [128, 4, 128], BF16) for i in range(2)]
                CTc2 = [SB(st, f"b2CT{i}", [128, 4, 128], BF16) for i in range(2)]
                hbt2 = [SB(st, f"b2hb{i}", [128, D], BF16) for i in range(2)]
                hft = SB(st, "b2hf", [128, D], BF16)
                B_ld2, B_hbt2, B_hft = [Buf(), Buf()], [Buf(), Buf()], Buf()
                xdt = SB(st, "b2xdt", [128, 2, D], BF16)
                cumT = SB(st, "b2cumT", [32, 2, 128], F32)
                ecum = SB(st, "b2ecum", [128, 64], F32)
                ncum = SB(st, "b2ncum", [128, 64], F32)
                GTm = SB(st, "b2GTm", [128, 2, 4, 128], BF16)
                LT2 = [SB(st, f"b2LT{i}", [128, 8, 128], BF16) for i in range(2)]
                MT2 = [SB(st, f"b2MT{i}", [128, 8, 128], BF16) for i in range(2)]
                B_LT2, B_MT2 = [Buf(), Buf()], [Buf(), Buf()]
                yv = SB(st, "b2y", [128, D], F32)
                t1 = SB(st, "b2t1", [128, 512], F32)
                gst = SB(st, "b2gst", [128, 4, 4], F32)
                ssm = SB(st, "b2ssm", [128, D], BF16)
                c_neg = SB(st, "b2neg", [128, 2, 512], BF16)
                c_gn = SB(st, "b2gn", [128, D], F32)
                ssmT = SB(st, "b2ssmT", [128, 16, 512], BF16)
                aTl = SB(st, "b2aT", [128, 8, 512], BF16)
                mT = SB(st, "b2mT", [128, 16, 512], BF16)
                B_xdt, B_cumT, B_ecum, B_GTm, B_LT, B_MT, B_y, B_t1, B_gst, B_ssm, B_c2, B_ssmT, B_aTl, B_mT = [Buf() for _ in range(14)]
                p.dma("sp", lambda e: e.dma_start(out=c_neg[:], in_=negm), "c1", writes=[B_c2])
                p.dma("sp", lambda e: e.dma_start(out=c_gn[:], in_=rep_d[:, 3, :]), "c1", writes=[B_c2])
                wo1 = [SB(st, f"b2wa{i}", [128, 8, 128], BF16) for i in range(2)]
                wo2 = [SB(st, f"b2ws{i}", [128, 16, 128], BF16) for i in range(2)]
                gl = [SB(st, f"b2gl{i}", [128, 2, 512], BF16) for i in range(2)]
                B_wo, B_gl = [Buf(), Buf()], [Buf(), Buf()]
                wo3 = SB(st, "b2wo", [128, 16, 512], BF16)
                B_wo3 = Buf()
                xr = SB(st, "b2xr", [128, 512], F32)
                B_xr = Buf()
                cbase = 0
                for seg, tok0, nch in SEGS:
                    hif, hib = seg * 2, seg * 2 + 1

                    def do_chunk(c, W, Xc, Bc, dc, zc, BTc, CTc, hbt, B_ld, B_hbt, seg=seg, tok0=tok0, hif=hif, hib=hib, cbase=cbase):
                        t = tok0 + c * 128
                        for dst, src in ((Xc[:], Xs_d[t:t + 128, :]), (Bc[:], Bs_d[t:t + 128, :]), (dc[:], dt_d[t:t + 128, :]), (zc[:], zs_d[t:t + 128, :]),
                                         (BTc[:], BT_d[:, :, t:t + 128].rearrange("g p t -> p g t")), (CTc[:], CT_d[:, :, t:t + 128].rearrange("g p t -> p g t"))):
                            p.dma("sp", lambda e, dst=dst, src=src: e.dma_start(out=dst, in_=src), f"b2l{(cbase + c) % 2}", writes=[B_ld])
                        p.dma("sp", lambda e, cc=cbase + c: e.dma_start(out=hbt[:], in_=hb_d[cc]), f"b2l{(cbase + c) % 2}", writes=[B_hbt])
                        p.op("act", lambda e, hif=hif: e.copy(out=hft[:], in_=Hst[:, hif, :]), reads=[B_H[hif]], writes=[B_hft])
                        chunk_pre(W, dc[:], B_ld)
                        bk, bb = nextbank()
                        p.op("pe", lambda e, bk=bk: e.matmul(bk[0:32, 0:128], lhsT=W["dA"][:, 0:32], rhs=TRIF, start=True, stop=True), reads=[W["B_dA"], B_const], writes=[bb])
                        p.op("pe", lambda e, bk=bk: e.matmul(bk[0:32, 128:256], lhsT=W["dA"][:, 32:64], rhs=TRIB, start=True, stop=True), reads=[W["B_dA"], B_const], writes=[bb], pe_accum=True)
                        p.op("act", lambda e, bk=bk: e.copy(out=cumT[:], in_=bk[0:32, 0:256].rearrange("p (d t) -> p d t", t=128)), reads=[bb], writes=[B_cumT])
                        p.op("act", lambda e: e.activation(out=ecum[:], in_=W["cumsb"][:, 0:64], func=AF.Exp), reads=[W["B_cumsb"]], writes=[B_ecum])
                        p.op("dve", lambda e: e.tensor_scalar(out=ncum[:], in0=W["cumsb"][:, 0:64], scalar1=-1.0, scalar2=None, op0=ALU.mult), reads=[W["B_cumsb"]], writes=[B_ecum])
                        for d in range(2):
                            p.op("dve" if d == 0 else "pool", lambda e, d=d: e.tensor_tensor(
                                out=xdt[:, d, :].rearrange("p (h d) -> p h d", d=64), in0=Xc[:].rearrange("p (h d) -> p h d", d=64),
                                in1=bc(dc[:, d * 32:(d + 1) * 32].unsqueeze(2), [128, 32, 64]), op=ALU.mult), reads=[B_ld], writes=[B_xdt])
                        bk, bb = nextbank()
                        for g in range(4):
                            p.op("pe", lambda e, g=g, bk=bk: e.matmul(bk[:, g * 128:(g + 1) * 128], lhsT=BTc[:, g, :], rhs=CTc[:, g, :], start=True, stop=True),
                                 reads=[B_ld], writes=[bb], pe_accum=True)
                        for d in range(2):
                            tri = TRIF if d == 0 else TRIB
                            p.op("dve", lambda e, d=d, tri=tri, bk=bk: e.tensor_tensor(out=GTm[:, d, :, :], in0=bk[:, 0:512].rearrange("p (g t) -> p g t", t=128),
                                                                                      in1=bc(tri.unsqueeze(1), [128, 4, 128]), op=ALU.mult),
                                 reads=[bb, B_const], writes=[B_GTm])
                        def dg_head(d, g, sl):
                            LT, MT, B_LT, B_MT = LT2[sl], MT2[sl], B_LT2[sl], B_MT2[sl]
                            for half in range(2):
                                bk, bb = nextbank()
                                p.op("pe", lambda e, bk=bk: e.matmul(bk[:, 0:512], lhsT=c_idb[:], rhs=c_neg[:, d, :], start=True, stop=False),
                                     reads=[B_c2, B_const], writes=[bb])
                                for j in range(4):
                                    h = g * 8 + half * 4 + j
                                    p.op("pe", lambda e, j=j, h=h, bk=bk: e.matmul(bk[:, j * 128:(j + 1) * 128], lhsT=bc(c_cst[0:32, 0, h:h + 1], [32, 128]),
                                                                                   rhs=cumT[:, d, :], start=False, stop=(j == 3)),
                                         reads=[B_cumT, B_const], writes=[bb], pe_accum=True)
                                for j in range(4):
                                    h = g * 8 + half * 4 + j
                                    p.op("act", lambda e, j=j, h=h, half=half, bk=bk: e.activation(out=LT[:, half * 4 + j, :], in_=bk[:, j * 128:(j + 1) * 128], func=AF.Exp,
                                                                                                  bias=ncum[:, d * 32 + h:d * 32 + h + 1]),
                                         reads=[bb, B_ecum], writes=[B_LT])
                            p.op("dve", lambda e: e.tensor_tensor(out=MT[:], in0=LT[:], in1=bc(GTm[:, d, g, :].unsqueeze(1), [128, 8, 128]), op=ALU.mult),
                                 reads=[B_LT, B_GTm], writes=[B_MT])

                        def dg_tail(d, g, sl):
                            MT, B_MT = MT2[sl], B_MT2[sl]
                            hsrc, B_hs = (hft, B_hft) if d == 0 else (hbt, B_hbt)
                            bkd, bbd = nextbank()
                            for j in range(8):
                                h = g * 8 + j
                                p.op("pe", lambda e, j=j, h=h: e.matmul(bkd[:, j * 64:(j + 1) * 64], lhsT=MT[:, j, :], rhs=xdt[:, d, h * 64:(h + 1) * 64],
                                                                        start=True, stop=True), reads=[B_MT, B_xdt], writes=[bbd], pe_accum=True)
                            bko, bbo = nextbank()
                            p.op("pe", lambda e: e.matmul(bko[:, 0:512], lhsT=CTc[:, g, :], rhs=hsrc[:, g * 512:(g + 1) * 512], start=True, stop=True),
                                 reads=[B_ld, B_hs], writes=[bbo])
                            p.op("dve", lambda e: e.tensor_tensor(out=t1[:].rearrange("p (h d) -> p h d", d=64),
                                                                  in0=bko[:, 0:512].rearrange("p (h d) -> p h d", d=64),
                                                                  in1=bc(ecum[:, d * 32 + g * 8:d * 32 + g * 8 + 8].unsqueeze(2), [128, 8, 64]), op=ALU.mult),
                                 reads=[bbo, B_ecum], writes=[B_t1])
                            if d == 0:
                                p.op("dve", lambda e: e.tensor_tensor(out=yv[:, g * 512:(g + 1) * 512], in0=bkd[:, 0:512], in1=t1[:], op=ALU.add),
                                     reads=[bbd, B_t1], writes=[B_y])
                            else:
                                p.op("dve", lambda e: e.tensor_tensor(out=t1[:], in0=bkd[:, 0:512], in1=t1[:], op=ALU.add),
                                     reads=[bbd, B_t1], writes=[B_t1])
                                p.op("pool", lambda e: e.tensor_tensor(out=yv[:, g * 512:(g + 1) * 512], in0=yv[:, g * 512:(g + 1) * 512], in1=t1[:], op=ALU.add),
                                     reads=[B_t1, B_y], writes=[B_y])

                        dgs = [(d, g) for d in range(2) for g in range(4)]
                        dg_head(dgs[0][0], dgs[0][1], 0)
                        for i_ in range(1, 8):
                            dg_head(dgs[i_][0], dgs[i_][1], i_ % 2)
                            dg_tail(dgs[i_ - 1][0], dgs[i_ - 1][1], (i_ - 1) % 2)
                        dg_tail(dgs[7][0], dgs[7][1], 1)
                        p.op("dve", lambda e: e.tensor_tensor(out=xdt[:, 0, :].rearrange("p (h d) -> p h d", d=64), in0=Xc[:].rearrange("p (h d) -> p h d", d=64),
                                                              in1=bc(c_reps[:, 64:96].unsqueeze(2), [128, 32, 64]), op=ALU.mult),
                             reads=[B_ld, B_const, B_xdt], writes=[B_xdt])
                        p.op("dve", lambda e: e.tensor_tensor(out=yv[:], in0=yv[:], in1=xdt[:, 0, :], op=ALU.add), reads=[B_xdt, B_y], writes=[B_y])
                        p.op("dve", lambda e: e.tensor_tensor(out=yv[:], in0=yv[:], in1=zc[:], op=ALU.mult), reads=[B_ld, B_y], writes=[B_y])
                        for g in range(4):
                            p.op("act", lambda e, g=g: e.activation(out=ssm[:, g * 512:(g + 1) * 512], in_=yv[:, g * 512:(g + 1) * 512], func=AF.Square, accum_out=gst[:, g, 0:1]),
                                 reads=[B_y], writes=[B_ssm, B_gst])
                        p.op("act", lambda e: e.activation(out=gst[:, :, 1], in_=gst[:, :, 0], func=AF.Sqrt, scale=1.0 / 512, bias=EPS), reads=[B_gst], writes=[B_gst])
                        p.op("dve", lambda e: e.reciprocal(out=gst[:, :, 2], in_=gst[:, :, 1]), reads=[B_gst], writes=[B_gst])
                        p.op("dve", lambda e: e.tensor_tensor(out=yv[:].rearrange("p (g d) -> p g d", d=512), in0=yv[:].rearrange("p (g d) -> p g d", d=512),
                                                              in1=bc(gst[:, :, 2:3], [128, 4, 512]), op=ALU.mult), reads=[B_gst, B_y], writes=[B_y])
                        p.op("dve", lambda e: e.tensor_tensor(out=ssm[:], in0=yv[:], in1=c_gn[:], op=ALU.mult), reads=[B_y, B_c2, B_ssm], writes=[B_ssm])
                        ci = c % 4
                        for half in range(2):
                            bk, bb = nextbank()
                            bkb = bk[:].bitcast(BF16)
                            for kk in range(8):
                                k = half * 8 + kk
                                p.op("pe", lambda e, k=k, kk=kk, bkb=bkb: e.transpose(out=bkb[:, kk * 128:(kk + 1) * 128], in_=ssm[:, k * 128:(k + 1) * 128], identity=c_idb[:]),
                                     reads=[B_ssm, B_const], writes=[bb], pe_accum=True)
                            p.op("act", lambda e, half=half, ci=ci, bkb=bkb: e.copy(out=ssmT[:, half * 8:half * 8 + 8, ci * 128:(ci + 1) * 128],
                                                                                    in_=bkb.rearrange("p (k t) -> p k t", t=128)), reads=[bb], writes=[B_ssmT])
                        sb_ = chunk_states(W, Xc[:], B_ld, Bc[:], B_ld, 0)
                        h_update(W, hif, 0, sb_)
                        if ci == 3:
                            g0 = t - 384
                            p.dma("sp", lambda e, g0=g0: e.dma_start(out=aTl[:], in_=aT_d[:, :, g0:g0 + 512].rearrange("k p t -> p k t")), "b2a", writes=[B_aTl])
                            for cc in range(16):
                                s = cc % 2
                                p.dma("sp", lambda e, s=s, cc=cc: e.dma_start(out=wo1[s][:], in_=w_ao_b[:, cc * 128:(cc + 1) * 128].rearrange("(k p) c -> p k c", p=128)),
                                      f"b2w{s}", reads=[B_w], writes=[B_wo[s]])
                                p.dma("sp", lambda e, s=s, cc=cc: e.dma_start(out=wo2[s][:], in_=w_so_b[:, cc * 128:(cc + 1) * 128].rearrange("(k p) c -> p k c", p=128)),
                                      f"b2w{s}", reads=[B_w], writes=[B_wo[s]])
                                p.dma("sp", lambda e, s=s, cc=cc, g0=g0: e.dma_start(out=gl[s][:, 0, :], in_=gT_d[cc, :, g0:g0 + 512]), f"b2g{s}", writes=[B_gl[s]])
                                p.dma("sp", lambda e, s=s, cc=cc, g0=g0: e.dma_start(out=gl[s][:, 1, :], in_=gT_d[16 + cc, :, g0:g0 + 512]), f"b2g{s}", writes=[B_gl[s]])
                                bka, bba = nextbank()
                                for k in range(8):
                                    p.op("pe", lambda e, s=s, k=k, bka=bka: e.matmul(bka[:, 0:512], lhsT=wo1[s][:, k, :], rhs=aTl[:, k, :], start=(k == 0), stop=(k == 7)),
                                         reads=[B_wo[s], B_aTl], writes=[bba], pe_accum=True)
                                bks, bbs = nextbank()
                                for k in range(16):
                                    p.op("pe", lambda e, s=s, k=k, bks=bks: e.matmul(bks[:, 0:512], lhsT=wo2[s][:, k, :], rhs=ssmT[:, k, :], start=(k == 0), stop=(k == 15)),
                                         reads=[B_wo[s], B_ssmT], writes=[bbs], pe_accum=True)
                                p.op("dve", lambda e, s=s, bka=bka: e.tensor_tensor(out=t1[:], in0=bka[:, 0:512], in1=gl[s][:, 0, :], op=ALU.mult),
                                     reads=[bba, B_gl[s], B_t1], writes=[B_t1])
                                p.op("dve", lambda e, s=s, bks=bks: e.tensor_tensor(out=yv[:, 0:512], in0=bks[:, 0:512], in1=gl[s][:, 1, :], op=ALU.mult),
                                     reads=[bbs, B_gl[s], B_y], writes=[B_y])
                                p.op("dve", lambda e, cc=cc: e.tensor_tensor(out=mT[:, cc, :], in0=t1[:], in1=yv[:, 0:512], op=ALU.add),
                                     reads=[B_t1, B_y], writes=[B_mT])
                            for cb in range(4):
                                p.dma("sp", lambda e, cb=cb: e.dma_start(out=wo3[:], in_=w_out_b[:, cb * 512:(cb + 1) * 512].rearrange("(k p) c -> p k c", p=128)),
                                      "b2w3", reads=[B_w], writes=[B_wo3])
                                for ti in range(4):
                                    tt = g0 + ti * 128
                                    p.dma("sp", lambda e, tt=tt, cb=cb: e.dma_start(out=xr[:], in_=x_res[tt:tt + 128, cb * 512:(cb + 1) * 512]), "b2x", writes=[B_xr])
                                    bk, bb = nextbank()
                                    for k in range(16):
                                        p.op("pe", lambda e, k=k, ti=ti, bk=bk: e.matmul(bk[:, 0:512], lhsT=mT[:, k, ti * 128:(ti + 1) * 128], rhs=wo3[:, k, :],
                                                                                          start=(k == 0), stop=(k == 15)), reads=[B_mT, B_wo3], writes=[bb], pe_accum=True)
                                    p.op("dve", lambda e, bk=bk: e.tensor_tensor(out=xr[:], in0=bk[:, 0:512], in1=xr[:], op=ALU.add), reads=[bb, B_xr], writes=[B_xr])
                                    p.dma("pool", lambda e, tt=tt, cb=cb: e.dma_start(out=x1_d[tt:tt + 128, cb * 512:(cb + 1) * 512], in_=xr[:]), "b2xs", reads=[B_xr])
                    for c in range(nch):
                        s2 = (cbase + c) % 2
                        do_chunk(c, W2b[s2], Xc2[s2], Bc2[s2], dc2[s2], zc2[s2], BTc2[s2], CTc2[s2], hbt2[s2], B_ld2[s2], B_hbt2[s2])
                    cbase += nch
                p.barrier()
        if "B" in stages:
            stage_b1()
            stage_b2()
        p.barrier()
        ssd_stack.close()

        def stage_c():
            with contextlib.ExitStack() as st:
                keysT = SB(st, "keysT", [128, 16, 128], BF16)
                iob = SB(st, "iob", [128, 128], BF16)
                c_gf = SB(st, "c_gf", [128, 1, D], F32)
                B_kT, B_cc, B_gf = Buf(), Buf(), Buf()
                p.op("dve", lambda e: e.tensor_copy(out=iob[:], in_=IOTA), reads=[B_const], writes=[B_cc])
                with contextlib.ExitStack() as st2:
                    kf = SB(st2, "kf", [128, 16, 128], F32)
                    kb = SB(st2, "kb", [128, 16, 128], BF16)
                    B_kf = Buf()
                    p.dma("sp", lambda e: e.dma_start(out=kf[:], in_=keys.rearrange("a n d -> n a d")), "c1", writes=[B_kf])
                    p.op("dve", lambda e: e.tensor_copy(out=kb[:], in_=kf[:]), reads=[B_kf], writes=[B_kf])
                    for half in range(2):
                        bk, bb = nextbank()
                        bkb = bk[:].bitcast(BF16)
                        for kk in range(8):
                            p.op("pe", lambda e, a=half * 8 + kk, kk=kk, bkb=bkb: e.transpose(out=bkb[:, kk * 128:(kk + 1) * 128], in_=kb[:, a, :], identity=c_idb[:]),
                                 reads=[B_kf, B_const], writes=[bb], pe_accum=True)
                        p.op("act", lambda e, half=half, bkb=bkb: e.copy(out=keysT[:, half * 8:half * 8 + 8, :], in_=bkb.rearrange("p (k t) -> p k t", t=128)),
                             reads=[bb], writes=[B_kT])
                    p.barrier()

                x1t = SB(st, "x1t", [128, 1, D], F32)
                xn = SB(st, "cxn", [128, D], BF16)
                cst_ = SB(st, "cst_", [128, 2, 4], F32)
                cst2 = SB(st, "cst2", [128, 2, 4], F32)
                xnT2 = [SB(st, f"xnT{i}", [128, 16, 256], BF16) for i in range(2)]
                qT = SB(st, "cqT", [128, 16, 256], BF16)
                wq = [SB(st, f"wq{i}", [128, 16, 128], BF16) for i in range(2)]
                P2g = SB(st, "P2g", [128, 32, 128], BF16)
                OH1 = SB(st, "OH1", [128, 32, 128], BF16)
                scr = P2g[:].bitcast(F32).rearrange("p a b -> p (a b)").rearrange("p (h n) -> p h n", n=128)
                eq = OH1[:].bitcast(F32).rearrange("p a b -> p (a b)").rearrange("p (h k j) -> p h k j", k=16, j=16)
                wk = SB(st, "cwk", [128, 256], F32)
                topv2 = [SB(st, f"topv{i}", [128, 16, 16], F32) for i in range(2)]
                idxu2 = [SB(st, f"idxu{i}", [128, 16, 16], U32) for i in range(2)]
                idxf = SB(st, "idxf", [128, 16, 16], F32)
                cand = SB(st, "cand", [128, 8, 16, 16], F32)
                best = SB(st, "best", [128, 8, 16], F32)
                posu = SB(st, "posu", [128, 8, 16], U32)
                ku = SB(st, "ku", [128, 2, 8, 16], U32)
                kf_ = SB(st, "kf_", [128, 2, 8, 16], F32)
                gat = SB(st, "gat", [128, 8, 16], F32)
                gz = SB(st, "gz", [128, 8, 2], F32)
                I12_2 = [SB(st, f"I12_{i}", [128, 3, 128], F32) for i in range(2)]
                I12T = SB(st, "I12T", [128, 3, 128], BF16)
                Gs = SB(st, "Gs", [128, 128, 256], BF16)
                NSL = 4
                strm = [SB(st, f"strm{i}", [128, 2, D], BF16) for i in range(NSL)]
                ge = [SB(st, f"ge{i}", [128, 256], BF16) for i in range(2)]
                (B_x1t, B_xn, B_cst, B_cst2, B_qT, B_wk, B_idxf, B_cand, B_best, B_posu, B_ku, B_kf2, B_gat, B_gz,
                 B_I12T, B_OH1, B_P2g, B_Gs) = [Buf() for _ in range(18)]
                B_xnT2, B_topv2, B_idxu2, B_I12_2 = [Buf(), Buf()], [Buf(), Buf()], [Buf(), Buf()], [Buf(), Buf()]
                B_wq, B_ge = [Buf(), Buf()], [Buf(), Buf()]
                B_strm = [Buf() for _ in range(NSL)]
                B_scr, B_eq = B_P2g, B_OH1
                B_P2ga, B_P2gb = Buf(), Buf()
                HB_OH, HB_P2, HB_Pa, HB_Pb = [Buf(), Buf()], [Buf(), Buf()], [Buf(), Buf()], [Buf(), Buf()]
                fz = SB(st, "fz", [128, 2], F32)
                B_fz = Buf()
                sctr = [0]

                def stage1_hc(ti, hc):
                    topv, idxu, B_topv, B_idxu = topv2[ti], idxu2[ti], B_topv2[ti], B_idxu2[ti]
                    p.op("dve", lambda e: e.max(out=topv[:, hc, 0:8], in_=scr[:, hc, :]), reads=[B_scr], writes=[B_topv])
                    p.op("dve", lambda e: e.match_replace(out=wk[:, 0:128], in_to_replace=topv[:, hc, 0:8], in_values=scr[:, hc, :], imm_value=-1e30),
                         reads=[B_scr, B_topv], writes=[B_wk])
                    p.op("dve", lambda e: e.max(out=topv[:, hc, 8:16], in_=wk[:, 0:128]), reads=[B_wk], writes=[B_topv])
                    p.op("dve", lambda e: e.max_index(out=idxu[:, hc, 0:8], in_max=topv[:, hc, 0:8], in_values=scr[:, hc, :]), reads=[B_scr, B_topv], writes=[B_idxu])
                    p.op("dve", lambda e: e.max_index(out=idxu[:, hc, 8:16], in_max=topv[:, hc, 8:16], in_values=scr[:, hc, :]), reads=[B_scr, B_topv], writes=[B_idxu])

                def scores_q(ti, qd, xsl):
                    bk, bb = nextbank()
                    for j in range(4):
                        hc = qd * 4 + j
                        p.op("pe", lambda e, hc=hc, j=j: e.matmul(bk[:, j * 128:(j + 1) * 128], lhsT=qT[:, hc, ti * 128:(ti + 1) * 128], rhs=keysT[:, hc, :],
                                                                    start=True, stop=True), reads=[B_qT, B_kT], writes=[bb], pe_accum=True)
                    p.op("act", lambda e: e.copy(out=scr[:, qd * 4:qd * 4 + 4, :], in_=bk[:, 0:512].rearrange("p (a n) -> p a n", n=128)),
                         reads=[bb, B_P2ga, B_P2gb], writes=[B_scr])

                def phaseA1(gi):
                    tok0 = gi * 256
                    xnT, B_xnT = xnT2[gi % 2], B_xnT2[gi % 2]
                    p.dma("sp", lambda e: e.dma_start(out=c_gf[:, 0, :], in_=rep_d[:, 1, :]), "cgf", writes=[B_gf])
                    for ti in range(2):
                        t = tok0 + ti * 128
                        p.dma("sp", lambda e, t=t: e.dma_start(out=x1t[:, 0, :], in_=x1_d[t:t + 128, :]), "cx", writes=[B_x1t])
                        p.op("act", lambda e, ti=ti: e.activation(out=xn[:], in_=x1t[:, 0, :], func=AF.Square, accum_out=cst2[:, ti, 0:1]), reads=[B_x1t], writes=[B_xn, B_cst2])
                        p.op("act", lambda e, ti=ti: e.activation(out=cst2[:, ti, 1:2], in_=cst2[:, ti, 0:1], func=AF.Sqrt, scale=1.0 / D, bias=EPS), reads=[B_cst2], writes=[B_cst2])
                        p.op("dve", lambda e, ti=ti: e.reciprocal(out=cst2[:, ti, 2:3], in_=cst2[:, ti, 1:2]), reads=[B_cst2], writes=[B_cst2])
                        p.op("dve", lambda e, ti=ti: e.scalar_tensor_tensor(out=xn[:], in0=x1t[:, 0, :], scalar=cst2[:, ti, 2:3], in1=c_gf[:, 0, :], op0=ALU.mult, op1=ALU.mult),
                             reads=[B_x1t, B_cst2, B_gf, B_xn], writes=[B_xn])
                        yield
                        for half in range(2):
                            bk, bb = nextbank()
                            bkb = bk[:].bitcast(BF16)
                            for kk in range(8):
                                k = half * 8 + kk
                                p.op("pe", lambda e, k=k, kk=kk, bkb=bkb: e.transpose(out=bkb[:, kk * 128:(kk + 1) * 128], in_=xn[:, k * 128:(k + 1) * 128], identity=c_idb[:]),
                                     reads=[B_xn, B_const], writes=[bb], pe_accum=True)
                            p.op("act", lambda e, half=half, ti=ti, bkb=bkb: e.copy(out=xnT[:, half * 8:half * 8 + 8, ti * 128:(ti + 1) * 128], in_=bkb.rearrange("p (k t) -> p k t", t=128)),
                                 reads=[bb], writes=[B_xnT])
                            yield
                    def ld_wq(cc):
                        s_ = cc % 2
                        p.dma("sp", lambda e: e.dma_start(out=wq[s_][:], in_=w_q_b[:, cc * 128:(cc + 1) * 128].rearrange("(k p) c -> p k c", p=128)),
                              f"cwq{s_}", reads=[B_w], writes=[B_wq[s_]])
                    ld_wq(0)
                    for cc in range(16):
                        s_ = cc % 2
                        bk, bb = nextbank()
                        for k in range(16):
                            p.op("pe", lambda e, s_=s_, k=k, bk=bk: e.matmul(bk[:, 0:256], lhsT=wq[s_][:, k, :], rhs=xnT[:, k, :], start=(k == 0), stop=(k == 15)),
                                 reads=[B_wq[s_], B_xnT], writes=[bb], pe_accum=True)
                        p.op("act", lambda e, cc=cc, bk=bk: e.copy(out=qT[:, cc, :], in_=bk[:, 0:256]), reads=[bb], writes=[B_qT])
                        if cc + 1 < 16:
                            ld_wq(cc + 1)
                        yield
                        if cc % 4 != 3:
                            yield
                    for qd in range(4):
                        scores_q(0, qd, None)
                        yield
                    for hc in range(16):
                        stage1_hc(0, hc)
                        yield
                    for qd in range(4):
                        scores_q(1, qd, None)
                        yield

                def phaseA2(gi):
                    for hc in range(16):
                        stage1_hc(1, hc)
                        yield
                    for ti in range(2):
                        topv, idxu, B_topv, B_idxu = topv2[ti], idxu2[ti], B_topv2[ti], B_idxu2[ti]
                        I12, B_I12 = I12_2[ti], B_I12_2[ti]
                        p.op("dve", lambda e, idxu=idxu: e.tensor_copy(out=idxf[:], in_=idxu[:]), reads=[B_idxu], writes=[B_idxf])
                        tv = topv[:].rearrange("p (h c) k -> p h c k", c=2)
                        p.op("dve", lambda e, tv=tv: e.tensor_tensor(out=cand[:], in0=bc(tv[:, :, 0, :].unsqueeze(3), [128, 8, 16, 16]), in1=bc(tv[:, :, 1, :].unsqueeze(2), [128, 8, 16, 16]), op=ALU.add),
                             reads=[B_topv], writes=[B_cand])
                        yield
                        for h in range(8):
                            cv = cand[:, h, :, :].rearrange("p a b -> p (a b)")
                            p.op("dve", lambda e, h=h, cv=cv: e.max(out=best[:, h, 0:8], in_=cv), reads=[B_cand], writes=[B_best])
                            p.op("dve", lambda e, h=h, cv=cv: e.match_replace(out=wk[:], in_to_replace=best[:, h, 0:8], in_values=cv, imm_value=-1e30), reads=[B_cand, B_best], writes=[B_wk])
                            p.op("dve", lambda e, h=h: e.max(out=best[:, h, 8:16], in_=wk[:]), reads=[B_wk], writes=[B_best])
                            p.op("dve", lambda e, h=h, cv=cv: e.max_index(out=posu[:, h, 0:8], in_max=best[:, h, 0:8], in_values=cv), reads=[B_cand, B_best], writes=[B_posu])
                            p.op("dve", lambda e, h=h, cv=cv: e.max_index(out=posu[:, h, 8:16], in_max=best[:, h, 8:16], in_values=cv), reads=[B_cand, B_best], writes=[B_posu])
                            yield
                        p.op("dve", lambda e: e.tensor_tensor(out=gat[:], in0=best[:], in1=bc(best[:, :, 0:1], [128, 8, 16]), op=ALU.subtract), reads=[B_best], writes=[B_gat])
                        p.op("act", lambda e: e.activation(out=gat[:], in_=gat[:], func=AF.Exp), reads=[B_gat], writes=[B_gat])
                        p.op("dve", lambda e: e.tensor_reduce(out=gz[:, :, 0], in_=gat[:], axis=AX.X, op=ALU.add), reads=[B_gat], writes=[B_gz])
                        p.op("dve", lambda e: e.reciprocal(out=gz[:, :, 1], in_=gz[:, :, 0]), reads=[B_gz], writes=[B_gz])
                        p.op("dve", lambda e, I12=I12: e.tensor_tensor(out=I12[:, 2, :].rearrange("p (h k) -> p h k", k=16), in0=gat[:], in1=bc(gz[:, :, 1:2], [128, 8, 16]), op=ALU.mult),
                             reads=[B_gat, B_gz], writes=[B_I12])
                        p.op("dve", lambda e: e.tensor_single_scalar(out=ku[:, 0, :, :], in_=posu[:], scalar=4, op=ALU.logical_shift_right), reads=[B_posu], writes=[B_ku])
                        p.op("dve", lambda e: e.tensor_single_scalar(out=ku[:, 1, :, :], in_=posu[:], scalar=15, op=ALU.bitwise_and), reads=[B_posu], writes=[B_ku])
                        p.op("dve", lambda e: e.tensor_copy(out=kf_[:], in_=ku[:]), reads=[B_ku], writes=[B_kf2])
                        yield
                        iv = idxf[:].rearrange("p (h c) k -> p h c k", c=2)
                        for c_ in range(2):
                            p.op("dve", lambda e, c_=c_: e.tensor_tensor(out=eq, in0=bc(kf_[:, c_, :, :].unsqueeze(3), [128, 8, 16, 16]),
                                                                          in1=bc(IOTA[:, 0:16].unsqueeze(1).unsqueeze(1), [128, 8, 16, 16]), op=ALU.is_equal),
                                 reads=[B_kf2, B_const], writes=[B_eq])
                            p.op("dve", lambda e, c_=c_, iv=iv: e.tensor_tensor(out=eq, in0=eq, in1=bc(iv[:, :, c_, :].unsqueeze(2), [128, 8, 16, 16]), op=ALU.mult),
                                 reads=[B_idxf, B_eq], writes=[B_eq])
                            p.op("dve", lambda e, c_=c_, I12=I12: e.tensor_reduce(out=I12[:, c_, :], in_=eq.rearrange("p h k j -> p (h k) j"), axis=AX.X, op=ALU.add),
                                 reads=[B_eq], writes=[B_I12])
                            yield

                def gbuild(gi):
                    for ti in range(2):
                        I12, B_I12 = I12_2[ti], B_I12_2[ti]
                        bk, bb = nextbank()
                        for w_ in range(3):
                            p.op("pe", lambda e, w_=w_, bk=bk, I12=I12: e.transpose(out=bk[:, w_ * 128:(w_ + 1) * 128], in_=I12[:, w_, :], identity=IDF), reads=[B_I12, B_const], writes=[bb], pe_accum=True)
                        p.op("act", lambda e, bk=bk: e.copy(out=I12T[:], in_=bk[:, 0:384].rearrange("p (w t) -> p w t", t=128)), reads=[bb], writes=[B_I12T])
                        if ti == 0:
                            p.op("dve", lambda e: e.memset(fz[:], 0.0), reads=[], writes=[HB_OH[0], HB_OH[1], HB_P2[0], HB_P2[1], HB_Pa[0], HB_Pa[1], HB_Pb[0], HB_Pb[1],
                                                                                   B_P2g, B_OH1, B_P2ga, B_P2gb, B_fz])
                        for e8 in range(8):
                            x = e8 % 2
                            xs_ = slice(x * 16, (x + 1) * 16)
                            t0_ = e8 * 16
                            p.op("dve", lambda e, xs_=xs_, t0_=t0_: e.tensor_tensor(out=OH1[:, xs_, :], in0=bc(iob[:].unsqueeze(1), [128, 16, 128]),
                                                                                 in1=bc(I12T[:, 0, t0_:t0_ + 16].unsqueeze(2), [128, 16, 128]), op=ALU.is_equal),
                                 reads=[B_I12T, B_cc], writes=[HB_OH[x]])
                            p.op("dve", lambda e, xs_=xs_, t0_=t0_: e.tensor_tensor(out=P2g[:, xs_, :], in0=bc(iob[:].unsqueeze(1), [128, 16, 128]),
                                                                                 in1=bc(I12T[:, 1, t0_:t0_ + 16].unsqueeze(2), [128, 16, 128]), op=ALU.is_equal),
                                 reads=[B_I12T, B_cc], writes=[HB_P2[x], HB_Pa[x], HB_Pb[x]])
                            p.op("dve", lambda e, x=x, t0_=t0_: e.tensor_tensor(out=P2g[:, x * 16:x * 16 + 10, :], in0=P2g[:, x * 16:x * 16 + 10, :],
                                                                             in1=bc(I12T[:, 2, t0_:t0_ + 10].unsqueeze(2), [128, 10, 128]), op=ALU.mult),
                                 reads=[B_I12T, HB_P2[x]], writes=[HB_Pa[x]])
                            p.op("pool", lambda e, x=x, t0_=t0_: e.tensor_tensor(out=P2g[:, x * 16 + 10:x * 16 + 16, :], in0=P2g[:, x * 16 + 10:x * 16 + 16, :],
                                                                              in1=bc(I12T[:, 2, t0_ + 10:t0_ + 16].unsqueeze(2), [128, 6, 128]), op=ALU.mult),
                                 reads=[B_I12T, HB_P2[x]], writes=[HB_Pb[x]])
                            for q4 in range(4):
                                bk, bb = nextbank()
                                for j in range(4):
                                    tl = x * 16 + q4 * 4 + j
                                    p.op("pe", lambda e, tl=tl, j=j, bk=bk: e.matmul(bk[:, j * 128:(j + 1) * 128], lhsT=P2g[:, tl, :], rhs=OH1[:, tl, :], start=True, stop=True),
                                         reads=[HB_P2[x], HB_Pa[x], HB_Pb[x], HB_OH[x]], writes=[bb], pe_accum=True)
                                tg0 = ti * 128 + t0_ + q4 * 4
                                src = bk[:, 0:512].rearrange("p (t i) -> p i t", i=128)
                                p.op("act", lambda e, tg0=tg0, src=src: e.copy(out=Gs[:, :, tg0:tg0 + 4], in_=src), reads=[bb], writes=[B_Gs])
                        if ti == 1:
                            p.op("dve", lambda e: e.memset(fz[:], 0.0), reads=[], writes=[HB_OH[0], HB_OH[1], HB_P2[0], HB_P2[1], HB_Pa[0], HB_Pa[1], HB_Pb[0], HB_Pb[1],
                                                                                   B_P2g, B_OH1, B_P2ga, B_P2gb, B_fz])

                def drain(gen, n):
                    if gen is None:
                        return None
                    for _ in range(n):
                        try:
                            next(gen)
                        except StopIteration:
                            return None
                    return gen

                def passes_and_epilogue(gi, ga, gb):
                    tok0 = gi * 256
                    xnT, B_xnT = xnT2[gi % 2], B_xnT2[gi % 2]
                    for c2 in range(64):
                        sl = sctr[0] % NSL
                        sctr[0] += 1
                        p.dma("sp", lambda e, sl=sl, c2=c2: e.dma_start(out=strm[sl][:], in_=ut_b[2 * c2:2 * c2 + 2].rearrange("c p f -> p c f")), f"cs{sl}", writes=[B_strm[sl]])
                        for cj in range(2):
                            c = 2 * c2 + cj
                            s_ = c % 2
                            bk, bb = nextbank()
                            for k in range(16):
                                p.op("pe", lambda e, sl=sl, cj=cj, k=k, bk=bk: e.matmul(bk[:, 0:256], lhsT=strm[sl][:, cj, k * 128:(k + 1) * 128], rhs=xnT[:, k, :], start=(k == 0), stop=(k == 15)),
                                     reads=[B_strm[sl], B_xnT], writes=[bb], pe_accum=True)
                            p.op("act", lambda e, s_=s_, bk=bk: e.activation(out=ge[s_][:], in_=bk[:, 0:256], func=AF.Gelu), reads=[bb], writes=[B_ge[s_]])
                            p.op("dve", lambda e, s_=s_, c=c: e.tensor_tensor(out=Gs[:, c, :], in0=Gs[:, c, :], in1=ge[s_][:], op=ALU.mult),
                                 reads=[B_ge[s_], B_Gs], writes=[B_Gs])
                        if ga is not None:
                            ga = drain(ga, 1)
                        else:
                            gb = drain(gb, 1)
                    ga = drain(ga, 10000)
                    for c2 in range(64):
                        sl = sctr[0] % NSL
                        sctr[0] += 1
                        p.dma("sp", lambda e, sl=sl, c2=c2: e.dma_start(out=strm[sl][:], in_=v_b[c2 * 256:(c2 + 1) * 256, :].rearrange("(c p) f -> p c f", p=128)), f"cs{sl}",
                              reads=[B_wuv], writes=[B_strm[sl]])
                        for cj in range(2):
                            c = 2 * c2 + cj
                            for ti in range(2):
                                for db in range(4):
                                    bi = ti * 4 + db
                                    p.op("pe", lambda e, sl=sl, cj=cj, c=c, ti=ti, db=db, bi=bi: e.matmul(banks[bi][:, 0:512], lhsT=Gs[:, c, ti * 128:(ti + 1) * 128], rhs=strm[sl][:, cj, db * 512:(db + 1) * 512],
                                                                                                   start=(c == 0), stop=(c == 127)), reads=[B_strm[sl], B_Gs], writes=[bank_buf[bi]], pe_accum=True)
                        gb = drain(gb, 1)
                    gb = drain(gb, 10000)
                    p.dma("sp", lambda e: e.dma_start(out=c_gf[:, 0, :], in_=rep_d[:, 2, :]), "cgf", writes=[B_gf])
                    for ti in range(2):
                        t = tok0 + ti * 128
                        p.dma("sp", lambda e, t=t: e.dma_start(out=x1t[:, 0, :], in_=x1_d[t:t + 128, :]), "cx", writes=[B_x1t])
                        for db in range(4):
                            bi = ti * 4 + db
                            p.op("dve", lambda e, db=db, bi=bi: e.tensor_tensor(out=x1t[:, 0, db * 512:(db + 1) * 512], in0=banks[bi][:, 0:512], in1=x1t[:, 0, db * 512:(db + 1) * 512], op=ALU.add),
                                 reads=[bank_buf[bi], B_x1t], writes=[B_x1t])
                        p.op("act", lambda e, ti=ti: e.activation(out=xn[:], in_=x1t[:, 0, :], func=AF.Square, accum_out=cst_[:, ti, 0:1]), reads=[B_x1t, B_xn], writes=[B_xn, B_cst])
                        p.op("act", lambda e, ti=ti: e.activation(out=cst_[:, ti, 1:2], in_=cst_[:, ti, 0:1], func=AF.Sqrt, scale=1.0 / D, bias=EPS), reads=[B_cst], writes=[B_cst])
                        p.op("dve", lambda e, ti=ti: e.reciprocal(out=cst_[:, ti, 2:3], in_=cst_[:, ti, 1:2]), reads=[B_cst], writes=[B_cst])
                        p.op("dve", lambda e, ti=ti: e.scalar_tensor_tensor(out=x1t[:, 0, :], in0=x1t[:, 0, :], scalar=cst_[:, ti, 2:3], in1=c_gf[:, 0, :], op0=ALU.mult, op1=ALU.mult),
                             reads=[B_cst, B_gf, B_x1t], writes=[B_x1t])
                        p.dma("pool", lambda e, t=t: e.dma_start(out=y_out[t:t + 128, :], in_=x1t[:, 0, :]), "cy", reads=[B_x1t])

                NGRP = T_OWN // 256
                drain(phaseA1(0), 10000)
                drain(phaseA2(0), 10000)
                for gi in range(NGRP):
                    gbuild(gi)
                    if gi + 1 < NGRP:
                        passes_and_epilogue(gi, phaseA1(gi + 1), phaseA2(gi + 1))
                    else:
                        passes_and_epilogue(gi, None, None)
                p.barrier()

        if "C" in stages:
            stage_c()
        p.barrier(skip=())
        p.emit()
    return nc


def host_inputs(inp, c):
    b, q = c // 4, c % 4
    f32 = np.float32
    xp = inp["x_prompt"][b]
    xs = inp["x_sample"][b]

    def win(x, lo, hi):
        n = x.shape[0]
        out = np.zeros((hi - lo, x.shape[1]), f32)
        a, bnd = max(lo, 0), min(hi, n)
        out[a - lo:bnd - lo] = x[a:bnd]
        return out

    own = []
    emask = np.zeros((128, NG_OWN, 2), f32)
    gi = 0
    for (x, L) in ((xp, 2048), (xs, 1024)):
        for g in range(L // 512):
            lo = q * L + g * 512 - 128
            own.append(win(x, lo, lo + 768))
            if lo < 0:
                emask[:, gi, 0] = -1e30
            if lo + 768 > x.shape[0]:
                emask[:, gi, 1] = -1e30
            gi += 1
    oth = []
    omask = np.zeros((128, NG_OTH, 4), f32)
    gi = 0
    for (x, L) in ((xp, 2048), (xs, 1024)):
        for j in [jj for jj in range(4) if jj != q]:
            for g in range(L // 512):
                lo = j * L + g * 512 - 2
                oth.append(win(x, lo, lo + 516))
                mf = 1.0 if j < q else 0.0
                omask[:, gi, :] = [mf, 1 - mf, 1 - mf, mf]
                gi += 1
    x_res = np.concatenate([xp[q * 2048:(q + 1) * 2048], xs[q * 1024:(q + 1) * 1024]], axis=0)
    rep = lambda v: np.broadcast_to(np.asarray(v, f32).reshape(1, -1), (128, np.asarray(v).size)).copy()
    rep_d = np.stack([rep(inp["g_mix"][0]), rep(inp["g_ffn"][0]), rep(inp["g_final"]), rep(inp["g_ssm_norm"][0])], axis=1)
    rep_s = np.zeros((128, 160), f32)
    rep_s[:, 0:32] = inp["a_log_f"][0]
    rep_s[:, 32:64] = inp["a_log_b"][0]
    rep_s[:, 64:96] = inp["d_skip"][0]
    rep_s[:, 96:112] = inp["attn_sink"][0]
    slopes = np.exp2(-8.0 * np.arange(1, 17, dtype=np.float64) / 16)
    qi = np.arange(128)[:, None]
    km = np.arange(384)[None, :]
    rel = qi - km + 128
    ab = np.where(np.abs(rel) <= 128, -np.abs(rel).astype(np.float64), -1e30).astype(f32)
    convw = np.ascontiguousarray(inp["conv_w"][0].T.reshape(24, 128, 5).transpose(1, 0, 2))
    convb = np.ascontiguousarray(inp["conv_b"][0].reshape(24, 128).T)
    dtb = np.concatenate([inp["dt_bias_f"][0], inp["dt_bias_b"][0]]).reshape(64, 1).astype(f32)
    cst = np.zeros((128, 9, 128), f32)
    s_ = np.arange(128)[:, None]
    l_ = np.arange(128)[None, :]
    cst[:, 0] = np.eye(128)
    cst[:, 1] = (s_ <= l_)
    cst[:, 2] = (s_ >= l_)
    cst[:, 3] = 1.0
    cst[:, 4] = l_
    import ml_dtypes
    negm = np.zeros((128, 2, 512), f32)
    negm[:, 0] = np.tile(np.where(s_ > l_, NEG, 0.0), (1, 4))
    negm[:, 1] = np.tile(np.where(s_ < l_, NEG, 0.0), (1, 4))
    sel = np.zeros((32, 32, 128), f32)
    for h in range(32):
        sel[h, h, :] = 1.0
    return {
        "x_own": np.stack(own), "x_oth": np.stack(oth), "x_res": x_res,
        "w_in": inp["w_in"][0], "w_ao": inp["w_attn_o"][0], "w_so": inp["w_ssm_o"][0], "w_out": inp["w_out"][0],
        "w_q": inp["w_query"][0], "keys": inp["sub_keys"][0].reshape(16, 128, 128),
        "exp_u": inp["expert_u"][0], "exp_v": inp["expert_v"][0],
        "rep_d": rep_d, "rep_s": rep_s, "attn_bias": ab, "emask": emask, "omask": omask,
        "convw": convw, "convb": convb, "dtb": dtb, "cst": cst, "negm": negm.astype(ml_dtypes.bfloat16), "sel": sel,
    }


def kernel(**inputs):
    inp = {k: np.asarray(v) for k, v in inputs.items()}
    nc = build()
    in_maps = [host_inputs(inp, c) for c in range(NCORES)]
    res = run_bass_kernel_spmd(nc, in_maps, core_ids=list(range(NCORES)))
    yp = np.zeros((2, 8192, D), np.float32)
    ys = np.zeros((2, 4096, D), np.float32)
    for c in range(NCORES):
        b, q = c // 4, c % 4
        y = res.results[c]["y_out"]
        yp[b, q * 2048:(q + 1) * 2048] = y[0:2048]
        ys[b, q * 1024:(q + 1) * 1024] = y[2048:3072]
    return (yp, ys)
```

```python
import contextlib
import numpy as np
import concourse.bass as bass
import concourse.mybir as mybir
from concourse.bass_utils import run_bass_kernel_spmd

F32 = mybir.dt.float32
BF16 = mybir.dt.bfloat16
U32 = mybir.dt.uint32
AF = mybir.ActivationFunctionType
ALU = mybir.AluOpType
AX = mybir.AxisListType
ENGS = ("pe", "act", "dve", "pool", "sp")

D = 2048
INW = 10816
NCORES = 8
Q_END, K_END, V_END, Z_END, XBC_END, DT_END = 1024, 1280, 1536, 3584, 6656, 6720
NEG = -30000.0
EPS = 1e-6
OWN_GROUPS = [(0, 4), (1, 2)]
NG_OWN = 6
NG_OTH = 18
T_OWN = 3072


class Buf:
    __slots__ = ("w", "r")

    def __init__(self):
        self.w = None
        self.r = []


class Prog:
    def __init__(self, nc):
        self.nc = nc
        self.ops = {e: [] for e in ENGS}
        self.dma_sems = {}
        self.waited = {e: {} for e in ENGS}

    def _need(self, eng, dep, waits):
        kind, key, val = dep
        k = (kind, key)
        if self.waited[eng].get(k, -1) >= val:
            return
        self.waited[eng][k] = val
        waits.append(dep)
        if kind == "e":
            self.ops[key][val]["inc"] = True

    def _deps(self, eng, reads, writes, pe_accum):
        waits = []
        for b in reads:
            if b.w is not None:
                self._need(eng, b.w, waits)
        for b in writes:
            if b.w is not None:
                if not (pe_accum and eng == "pe" and b.w[0] == "e" and b.w[1] == "pe"):
                    self._need(eng, b.w, waits)
            for d in b.r:
                self._need(eng, d, waits)
        return waits

    def op(self, eng, fn, reads=(), writes=(), pe_accum=False):
        waits = self._deps(eng, reads, writes, pe_accum)
        idx = len(self.ops[eng])
        self.ops[eng].append(dict(fn=fn, waits=waits, inc=False, dma=None))
        me = ("e", eng, idx)
        for b in reads:
            b.r.append(me)
        for b in writes:
            b.w = me
            b.r = []
        return me

    def dma(self, eng, fn, sem, reads=(), writes=()):
        waits = self._deps(eng, reads, writes, False)
        self.dma_sems[sem] = self.dma_sems.get(sem, 0) + 16
        val = self.dma_sems[sem]
        self.ops[eng].append(dict(fn=fn, waits=waits, inc=False, dma=(sem, 16)))
        me = ("d", sem, val)
        for b in reads:
            b.r.append(me)
        for b in writes:
            b.w = me
            b.r = []
        return me

    def barrier(self, skip=tuple(["cast_w", "cast_uv"] + [f"ci{i}" for i in range(32)])):
        deps = []
        for e in ENGS:
            for i in range(len(self.ops[e]) - 1, -1, -1):
                o = self.ops[e][i]
                if o["fn"] is not None and o["dma"] is None:
                    deps.append(("e", e, i))
                    break
        for s, v in self.dma_sems.items():
            if s not in skip:
                deps.append(("d", s, v))
        for e in ENGS:
            waits = []
            for d in deps:
                self._need(e, d, waits)
            self.ops[e].append(dict(fn=None, waits=waits, inc=False, dma=None))

    def emit(self):
        nc = self.nc
        with contextlib.ExitStack() as st:
            esem = {e: st.enter_context(nc.semaphore("s_" + e)) for e in ENGS}
            dsem = {n: st.enter_context(nc.semaphore("d_" + n)) for n in self.dma_sems}
            cum = {}
            for e in ENGS:
                c = 0
                arr = []
                for o in self.ops[e]:
                    if o["inc"]:
                        c += 1
                    arr.append(c)
                cum[e] = arr
            block = st.enter_context(nc.Block())

            def run(engname, engobj):
                for o in self.ops[engname]:
                    for (kind, key, val) in o["waits"]:
                        if kind == "e":
                            engobj.wait_ge(esem[key], cum[key][val])
                        else:
                            engobj.wait_ge(dsem[key], val)
                    if o["fn"] is None:
                        continue
                    ins = o["fn"](engobj)
                    if o["dma"] is not None:
                        ins.then_inc(dsem[o["dma"][0]], o["dma"][1])
                    elif o["inc"]:
                        ins.then_inc(esem[engname], 1)

            block.tensor(lambda e: run("pe", e))
            block.scalar(lambda e: run("act", e))
            block.vector(lambda e: run("dve", e))
            block.gpsimd(lambda e: run("pool", e))
            block.sync(lambda e: run("sp", e))


def bc(ap, shape):
    return ap.to_broadcast(list(shape))


class TilePool:
    def __init__(self, nc, stack):
        self.nc, self.stack, self.tiles, self.bufs, self.i, self.first = nc, stack, {}, [], 0, True

    def begin(self):
        self.first = (len(self.tiles) == 0)
        self.i = 0

    def sb(self, name, shape, dt):
        if name not in self.tiles:
            self.tiles[name] = self.stack.enter_context(self.nc.sbuf_tensor(name, list(shape), dt))
        return self.tiles[name]

    def buf(self):
        if self.i == len(self.bufs):
            self.bufs.append(Buf())
        b = self.bufs[self.i]
        self.i += 1
        return b


def build(stages=("W", "A", "S", "B", "C"), dbg=()):
    nc = bass.Bass("TRN2", target_bir_lowering=False)
    p = Prog(nc)

    def din(name, shape, dt=F32):
        return nc.dram_tensor(name, list(shape), dt, kind="ExternalInput").ap()

    def dscr(name, shape, dt):
        kind = "ExternalOutput" if name in dbg else "Internal"
        return nc.dram_tensor(name, list(shape), dt, kind=kind).ap()

    x_own = din("x_own", [NG_OWN, 768, D])
    x_oth = din("x_oth", [NG_OTH, 516, D])
    x_res = din("x_res", [T_OWN, D])
    w_in = din("w_in", [D, INW])
    w_ao = din("w_ao", [1024, D])
    w_so = din("w_so", [D, D])
    w_out = din("w_out", [D, D])
    w_q = din("w_q", [D, D])
    keys = din("keys", [16, 128, 128])
    exp_u = din("exp_u", [16384, D])
    exp_v = din("exp_v", [16384, D])
    rep_d = din("rep_d", [128, 4, D])
    rep_s = din("rep_s", [128, 160])
    attn_bias = din("attn_bias", [128, 384])
    emask = din("emask", [128, NG_OWN, 2])
    omask = din("omask", [128, NG_OTH, 4])
    convw = din("convw", [128, 24, 5])
    convb = din("convb", [128, 24])
    dtb = din("dtb", [64, 1])
    cst = din("cst", [128, 9, 128])
    negm = din("negm", [128, 2, 512], BF16)
    sel = din("sel", [32, 32, 128])
    y_out = nc.dram_tensor("y_out", [T_OWN, D], F32, kind="ExternalOutput").ap()

    w_in_b = dscr("w_in_b", [D, INW], BF16)
    w_ao_b = dscr("w_ao_b", [1024, D], BF16)
    w_so_b = dscr("w_so_b", [D, D], BF16)
    w_out_b = dscr("w_out_b", [D, D], BF16)
    w_q_b = dscr("w_q_b", [D, D], BF16)
    v_b = dscr("v_b", [16384, D], BF16)
    u_b = dscr("u_b", [16384, D], BF16)
    ut_b = dscr("ut_b", [128, 128, D], BF16)
    zs_d = dscr("zs_d", [T_OWN, D], BF16)
    gT_d = dscr("gT_d", [32, 128, T_OWN], BF16)
    Xs_d = dscr("Xs_d", [T_OWN, D], BF16)
    Bs_d = dscr("Bs_d", [T_OWN, 512], BF16)
    BT_d = dscr("BT_d", [4, 128, T_OWN], BF16)
    CT_d = dscr("CT_d", [4, 128, T_OWN], BF16)
    dt_d = dscr("dt_d", [T_OWN, 64], F32)
    aT_d = dscr("aT_d", [8, 128, T_OWN], BF16)
    hb_d = dscr("hb_d", [24, 128, D], BF16)
    hin_d = dscr("hin_d", [4, 128, D], F32)
    x1_d = dscr("x1_d", [T_OWN, D], F32)

    with contextlib.ExitStack() as top:
        def SB(stack, name, shape, dt):
            return stack.enter_context(nc.sbuf_tensor(name, list(shape), dt))

        banks = [top.enter_context(nc.psum_tensor(f"bank{i}", [128, 512], F32)) for i in range(8)]
        bank_buf = [Buf() for _ in range(8)]
        bank_ctr = [0]

        def nextbank():
            i = bank_ctr[0] % 8
            bank_ctr[0] += 1
            return banks[i], bank_buf[i]

        c_cst = SB(top, "c_cst", [128, 9, 128], F32)
        c_idb = SB(top, "c_idb", [128, 128], BF16)
        c_reps = SB(top, "c_reps", [128, 160], F32)
        c_arep = SB(top, "c_arep", [128, 64], F32)
        B_const = Buf()
        p.dma("sp", lambda e: e.dma_start(out=c_cst[:], in_=cst), "c0", writes=[B_const])
        p.dma("sp", lambda e: e.dma_start(out=c_reps[:], in_=rep_s), "c0", writes=[B_const])
        p.op("dve", lambda e: e.tensor_copy(out=c_idb[:], in_=c_cst[:, 0, :]), reads=[B_const], writes=[B_const])
        p.op("act", lambda e: e.activation(out=c_arep[:], in_=c_reps[:, 0:64], func=AF.Exp), reads=[B_const], writes=[B_const])
        p.op("dve", lambda e: e.tensor_scalar(out=c_arep[:], in0=c_arep[:], scalar1=-1.0, scalar2=None, op0=ALU.mult),
             reads=[B_const], writes=[B_const])
        IDF = c_cst[:, 0, :]
        TRIF = c_cst[:, 1, :]
        TRIB = c_cst[:, 2, :]
        ONES = c_cst[:, 3, :]
        IOTA = c_cst[:, 4, :]

        B_win, B_w, B_wuv = {}, Buf(), Buf()
        lazy_casts = []

        def emit_casts(n):
            for _ in range(n):
                if lazy_casts:
                    lazy_casts.pop(0)()
        WBLOCKS = ([(Z_END + i * 512, 512) for i in range(6)] + [(XBC_END, 64)] + [(V_END + i * 512, 512) for i in range(4)]
                   + [(DT_END + i * 512, 512) for i in range(8)] + [(0, 512), (512, 512), (Q_END, 512)])
        if "W" in stages:
            def cast(dst, src, rows, cols, rblk, sem, buf):
                for r0 in range(0, rows, rblk):
                    lazy_casts.append(lambda r0=r0, dst=dst, src=src, rblk=rblk, sem=sem, buf=buf: p.dma(
                        "pool", lambda e: e.dma_start(out=dst[r0:r0 + rblk, :], in_=src[r0:r0 + rblk, :], max_dma_last_dim=4096), sem, writes=[buf]))
            for i, (c0, ncol) in enumerate(WBLOCKS):
                B_win[c0] = Buf()
                p.dma("pool", lambda e, c0=c0, ncol=ncol: e.dma_start(out=w_in_b[:, c0:c0 + ncol], in_=w_in[:, c0:c0 + ncol], max_dma_last_dim=4096),
                      f"ci{i}", writes=[B_win[c0]])
            cast(w_ao_b, w_ao, 1024, D, 512, "cast_w", B_w)
            cast(w_so_b, w_so, D, D, 512, "cast_w", B_w)
            cast(w_out_b, w_out, D, D, 512, "cast_w", B_w)
            cast(w_q_b, w_q, D, D, 512, "cast_w", B_w)
            if "C" in stages:
                cast(u_b, exp_u, 16384, D, 1024, "cast_uv", B_wuv)
                cast(v_b, exp_v, 16384, D, 1024, "cast_uv", B_wuv)

        def rms_to_hT(st, xt_tiles, ntiles, hT, g_idx, col0s, nrows=None):
            pass

        def prep_group(pool, xsrc, nwin, own, gidx, tok0, par=0, res=None):
            pool.begin()
            Buf = pool.buf
            tg = "A" if own else "S"
            lo = 128 if own else 2
            ntile = (nwin + 127) // 128
            hT = pool.sb(f"hT{tg}", [128, 16, nwin], BF16)
            B_hT = Buf()
            xin = [pool.sb(f"xin{tg}_{i}", [128, D], F32) for i in range(2)]
            xn = [pool.sb(f"xn{tg}_{i}", [128, D], BF16) for i in range(2)]
            c_g = pool.sb(f"cg{tg}", [128, D], F32)
            B_cg = Buf()
            if pool.first:
                p.dma("sp", lambda e: e.dma_start(out=c_g[:], in_=rep_d[:, 0, :]), "c1", writes=[B_cg])
            stat = pool.sb(f"stat{tg}", [128, 8, 4], F32)
            B_xin = [Buf(), Buf()]
            B_xn = [Buf(), Buf()]
            B_stat = Buf()
            for ti in range(ntile):
                r0 = ti * 128
                rows = min(128, nwin - r0)
                s = ti % 2
                p.dma("sp", lambda e, s=s, r0=r0, rows=rows: e.dma_start(out=xin[s][0:rows, :], in_=xsrc[r0:r0 + rows, :]),
                      f"xin{s}", writes=[B_xin[s]])
                p.op("act", lambda e, s=s, rows=rows, ti=ti: e.activation(out=xn[s][0:rows, :], in_=xin[s][0:rows, :], func=AF.Square,
                                                                         accum_out=stat[0:rows, ti, 0:1]),
                     reads=[B_xin[s]], writes=[B_xn[s], B_stat])
                p.op("act", lambda e, rows=rows, ti=ti: e.activation(out=stat[0:rows, ti, 1:2], in_=stat[0:rows, ti, 0:1], func=AF.Sqrt,
                                                                    scale=1.0 / D, bias=EPS), reads=[B_stat], writes=[B_stat])
                p.op("dve", lambda e, rows=rows, ti=ti: e.reciprocal(out=stat[0:rows, ti, 2:3], in_=stat[0:rows, ti, 1:2]),
                     reads=[B_stat], writes=[B_stat])
                p.op("dve", lambda e, s=s, rows=rows, ti=ti: e.scalar_tensor_tensor(
                    out=xn[s][0:rows, :], in0=xin[s][0:rows, :], scalar=stat[0:rows, ti, 2:3], in1=c_g[0:rows, :],
                    op0=ALU.mult, op1=ALU.mult), reads=[B_xin[s], B_stat, B_cg], writes=[B_xn[s]])
                for half in range(2):
                    bk, bb = nextbank()
                    bkb = bk[:].bitcast(BF16)
                    for kk in range(8):
                        k = half * 8 + kk
                        p.op("pe", lambda e, s=s, rows=rows, k=k, kk=kk, bkb=bkb: e.transpose(
                            out=bkb[:, kk * 128:kk * 128 + rows], in_=xn[s][0:rows, k * 128:(k + 1) * 128], identity=c_idb[0:rows, 0:rows]),
                            reads=[B_xn[s], B_const], writes=[bb], pe_accum=True)
                    eng = "act" if half == 0 else "dve"
                    src = bkb.rearrange("p (k t) -> p k t", t=128)[:, :, 0:rows]
                    dst = hT[:, half * 8:half * 8 + 8, r0:r0 + rows]
                    if eng == "act":
                        p.op("act", lambda e, src=src, dst=dst: e.copy(out=dst, in_=src), reads=[bb], writes=[B_hT])
                    else:
                        p.op("dve", lambda e, src=src, dst=dst: e.tensor_copy(out=dst, in_=src), reads=[bb], writes=[B_hT])
                yield

            wt = [pool.sb(f"wt{tg}_{i}", [128, 16, 512], BF16) for i in range(2)]
            B_wt = [Buf(), Buf()]
            wctr = [0]

            def load_w(c0, ncol):
                s = wctr[0] % 2
                wctr[0] += 1
                src = w_in_b[:, c0:c0 + ncol].rearrange("(k p) c -> p k c", p=128)
                p.dma("sp", lambda e, s=s, src=src, ncol=ncol: e.dma_start(out=wt[s][:, :, 0:ncol], in_=src), f"wt{s}",
                      reads=[B_win[c0]], writes=[B_wt[s]])
                return wt[s], B_wt[s]

            def fm_proj(wtile, wb, wc0, M, t0, N):
                bk, bb = nextbank()
                for k in range(16):
                    p.op("pe", lambda e, k=k, bk=bk: e.matmul(bk[0:M, 0:N], lhsT=wtile[:, k, wc0:wc0 + M], rhs=hT[:, k, t0:t0 + N],
                                                               start=(k == 0), stop=(k == 15)),
                         reads=[wb, B_hT], writes=[bb], pe_accum=True)
                return bk, bb

            def tm_proj(wtile, wb, wc0, Ncol, t0, rows=128):
                bk, bb = nextbank()
                for k in range(16):
                    p.op("pe", lambda e, k=k, bk=bk: e.matmul(bk[0:rows, 0:Ncol], lhsT=hT[:, k, t0:t0 + rows], rhs=wtile[:, k, wc0:wc0 + Ncol],
                                                               start=(k == 0), stop=(k == 15)),
                         reads=[wb, B_hT], writes=[bb], pe_accum=True)
                return bk, bb

            NT = 512
            nconv = 24 if own else 20
            xbc = [pool.sb(f"xbc{tg}_{i}", [128, 516], BF16) for i in range(2)]
            dg = [pool.sb(f"dg{tg}_{i}", [128, 5, 128], BF16) for i in range(2)]
            B_dg = [Buf(), Buf()]
            B_xbc = [Buf(), Buf()]
            csil = [pool.sb(f"csil{tg}_{i}", [128, 512], BF16) for i in range(2)]
            B_csil = [Buf(), Buf()]
            c_cw = pool.sb(f"cw{tg}", [128, 24, 5], F32)
            c_cb = pool.sb(f"cb{tg}", [128, 24], F32)
            c_dtb = pool.sb(f"dtb{tg}", [64, 1], F32)
            B_cw = Buf()
            if pool.first:
                p.dma("sp", lambda e: e.dma_start(out=c_cw[:], in_=convw), "c1", writes=[B_cw])
                p.dma("sp", lambda e: e.dma_start(out=c_cb[:], in_=convb), "c1", writes=[B_cw])
                p.dma("sp", lambda e: e.dma_start(out=c_dtb[:], in_=dtb), "c1", writes=[B_cw])
            npar = 1 if own else 2
            Xtm = [pool.sb(f"Xtm{tg}{i}", [128, 4, D], BF16) for i in range(npar)][par]
            Btm = [pool.sb(f"Btm{tg}{i}", [128, 4, 512], BF16) for i in range(npar)][par]
            dttm = [pool.sb(f"dttm{tg}{i}", [128, 4, 64], F32) for i in range(npar)][par]
            B_Xtm = [Buf() for _ in range(npar)][par]
            B_Btm = [Buf() for _ in range(npar)][par]
            B_dttm = [Buf() for _ in range(npar)][par]
            w0 = lo - 2
            pending = [None]
            pending_tr = [None]
            pending_conv = [None]
            own_stores = []
            for cc4 in range(0, nconv, 4):
                wtile, wb = load_w(V_END + D + cc4 * 128, 512)
                for ci in range(4):
                    cc = cc4 + ci
                    s = cc % 2
                    for hf in range(2):
                        bk, bb = fm_proj(wtile, wb, ci * 128, 128, w0 + hf * 258, 258)
                        p.op("act", lambda e, s=s, hf=hf, bk=bk: e.copy(out=xbc[s][:, hf * 258:(hf + 1) * 258], in_=bk[:, 0:258]),
                             reads=[bb], writes=[B_xbc[s]])
                    if pending[0] is not None:
                        pending[0]()
                        pending[0] = None
                    if pending_conv[0] is not None:
                        pending_conv[0]()
                        pending_conv[0] = None
                        pending[0], pending_tr[0] = pending_tr[0], None
                    p.op("dve", lambda e, s=s, cc=cc: e.tensor_tensor(out=dg[s][:], in0=bc(c_idb[:].unsqueeze(1), [128, 5, 128]),
                                                                      in1=bc(c_cw[:, cc, :].unsqueeze(2), [128, 5, 128]), op=ALU.mult),
                         reads=[B_cw, B_const], writes=[B_dg[s]])

                    def conv_chunk(s=s, cc=cc):
                        bkc, bbc = nextbank()
                        for j in range(5):
                            p.op("pe", lambda e, j=j: e.matmul(bkc[:, 0:512], lhsT=dg[s][:, j, :], rhs=xbc[s][:, j:j + 512], start=(j == 0), stop=(j == 4)),
                                 reads=[B_dg[s], B_xbc[s]], writes=[bbc], pe_accum=True)
                        p.op("act", lambda e: e.activation(out=csil[s][:], in_=bkc[:, 0:512], func=AF.Silu, bias=c_cb[:, cc:cc + 1]),
                             reads=[bbc, B_cw], writes=[B_csil[s]])
                        if own and cc >= 16:
                            g = (cc - 16) % 4
                            dst = (BT_d if cc < 20 else CT_d)[g, :, tok0:tok0 + NT]
                            p.dma("pool", lambda e: e.dma_start(out=dst, in_=csil[s][:]), f"stc{s}", reads=[B_csil[s]])
                    pending_conv[0] = conv_chunk
                    if cc < 20:
                        def tr_chunk(s=s, cc=cc):
                            bk, bb = nextbank()
                            bkb = bk[:].bitcast(BF16)
                            for ti in range(4):
                                p.op("pe", lambda e, ti=ti: e.transpose(out=bkb[:, ti * 128:(ti + 1) * 128],
                                                                         in_=csil[s][:, ti * 128:(ti + 1) * 128], identity=c_idb[:]),
                                     reads=[B_csil[s], B_const], writes=[bb], pe_accum=True)
                            src = bkb[:, 0:512].rearrange("p (t c) -> p t c", c=128)
                            if cc < 16:
                                p.op("act", lambda e: e.copy(out=Xtm[:, :, cc * 128:(cc + 1) * 128], in_=src), reads=[bb], writes=[B_Xtm])
                            else:
                                p.op("act", lambda e: e.copy(out=Btm[:, :, (cc - 16) * 128:(cc - 15) * 128], in_=src), reads=[bb], writes=[B_Btm])
                        pending_tr[0] = tr_chunk
                    yield
            wtile, wb = load_w(XBC_END, 64)
            bk, bb = fm_proj(wtile, wb, 0, 64, lo, NT)
            for q_ in (pending, pending_conv, pending_tr):
                if q_[0] is not None:
                    q_[0]()
                    q_[0] = None
            dtf = pool.sb(f"dtf{tg}", [64, 512], F32)
            B_dtf = Buf()
            p.op("act", lambda e, bk=bk: e.activation(out=dtf[:], in_=bk[0:64, 0:512], func=AF.Exp, bias=c_dtb[:, 0:1]),
                 reads=[bb, B_cw], writes=[B_dtf])
            p.op("act", lambda e: e.activation(out=dtf[:], in_=dtf[:], func=AF.Ln, bias=1.0), reads=[B_dtf], writes=[B_dtf])
            bk, bb = nextbank()
            for ti in range(4):
                p.op("pe", lambda e, ti=ti, bk=bk: e.transpose(out=bk[:, ti * 64:(ti + 1) * 64], in_=dtf[:, ti * 128:(ti + 1) * 128],
                                                                identity=c_cst[0:64, 0, 0:64]),
                     reads=[B_dtf, B_const], writes=[bb], pe_accum=True)
            p.op("dve", lambda e, bk=bk: e.tensor_copy(out=dttm[:], in_=bk[:, 0:256].rearrange("p (t c) -> p t c", c=64)),
                 reads=[bb], writes=[B_dttm])
            if res is not None:
                res.update(Xtm=Xtm, Btm=Btm, dttm=dttm, B_Xtm=B_Xtm, B_Btm=B_Btm, B_dttm=B_dttm)
            yield
            if not own:
                return
            for ti in range(4):
                t = tok0 + ti * 128
                p.dma("pool", lambda e, ti=ti, t=t: e.dma_start(out=Xs_d[t:t + 128, :], in_=Xtm[:, ti, :]), "stX", reads=[B_Xtm])
                p.dma("pool", lambda e, ti=ti, t=t: e.dma_start(out=Bs_d[t:t + 128, :], in_=Btm[:, ti, :]), "stX", reads=[B_Btm])
                p.dma("pool", lambda e, ti=ti, t=t: e.dma_start(out=dt_d[t:t + 128, :], in_=dttm[:, ti, :]), "stX", reads=[B_dttm])

            zt = [pool.sb(f"zt{tg}_{i}", [128, 512], BF16) for i in range(2)]
            B_zt = [Buf(), Buf()]
            zc = 0
            for cb4 in range(4):
                wtile, wb = load_w(V_END + cb4 * 512, 512)
                for ti in range(4):
                    bk, bb = tm_proj(wtile, wb, 0, 512, lo + ti * 128)
                    s = zc % 2
                    zc += 1
                    p.op("act", lambda e, s=s, bk=bk: e.activation(out=zt[s][:], in_=bk[:], func=AF.Silu), reads=[bb], writes=[B_zt[s]])
                    t = tok0 + ti * 128
                    p.dma("pool", lambda e, s=s, t=t, cb4=cb4: e.dma_start(out=zs_d[t:t + 128, cb4 * 512:(cb4 + 1) * 512], in_=zt[s][:]),
                          f"stz{s}", reads=[B_zt[s]])
            gt = [pool.sb(f"gt{tg}_{i}", [128, 512], BF16) for i in range(2)]
            B_gt = [Buf(), Buf()]
            for cb4 in range(8):
                wtile, wb = load_w(DT_END + cb4 * 512, 512)
                for ci in range(4):
                    cc = cb4 * 4 + ci
                    bk, bb = fm_proj(wtile, wb, ci * 128, 128, lo, NT)
                    s = cc % 2
                    p.op("act", lambda e, s=s, bk=bk: e.activation(out=gt[s][:], in_=bk[:], func=AF.Sigmoid), reads=[bb], writes=[B_gt[s]])
                    p.dma("pool", lambda e, s=s, cc=cc: e.dma_start(out=gT_d[cc, :, tok0:tok0 + NT], in_=gt[s][:]), f"stg{s}", reads=[B_gt[s]])
            qT = pool.sb(f"qT{tg}", [64, 16, 512], BF16)
            kT = pool.sb(f"kT{tg}", [64, 4, 768], BF16)
            vt = pool.sb(f"vt{tg}", [128, 6, 256], BF16)
            B_qT, B_kT, B_vt = Buf(), Buf(), Buf()
            for cb4 in range(2):
                wtile, wb = load_w(cb4 * 512, 512)
                for hh in range(8):
                    h = cb4 * 8 + hh
                    bk, bb = fm_proj(wtile, wb, hh * 64, 64, lo, NT)
                    p.op("act", lambda e, h=h, bk=bk: e.activation(out=qT[:, h, :], in_=bk[0:64, :], func=AF.Copy, scale=0.125),
                         reads=[bb], writes=[B_qT])
            wtile, wb = load_w(Q_END, 512)
            for kv in range(4):
                for hf in range(2):
                    bk, bb = fm_proj(wtile, wb, kv * 64, 64, hf * 384, 384)
                    p.op("dve", lambda e, kv=kv, hf=hf, bk=bk: e.tensor_copy(out=kT[:, kv, hf * 384:(hf + 1) * 384], in_=bk[0:64, 0:384]),
                         reads=[bb], writes=[B_kT])
            for ti in range(6):
                bk, bb = tm_proj(wtile, wb, 256, 256, ti * 128)
                p.op("act", lambda e, ti=ti, bk=bk: e.copy(out=vt[:, ti, :], in_=bk[:, 0:256]), reads=[bb], writes=[B_vt])

            c_ab = pool.sb(f"ab{tg}", [128, 384], F32)
            c_em = pool.sb(f"em{tg}", [128, NG_OWN, 2], F32)
            B_ab = Buf()
            if pool.first:
                p.dma("sp", lambda e: e.dma_start(out=c_ab[:], in_=attn_bias), "c1", writes=[B_ab])
                p.dma("sp", lambda e: e.dma_start(out=c_em[:], in_=emask), "c1", writes=[B_ab])
            sc = [pool.sb(f"sc{tg}_{i}", [128, 4, 384], F32) for i in range(2)]
            pr = [pool.sb(f"pr{tg}_{i}", [128, 4, 384], BF16) for i in range(2)]
            prT = [pool.sb(f"prT{tg}_{i}", [128, 4, 3, 128], BF16) for i in range(2)]
            ast = [pool.sb(f"ast{tg}_{i}", [128, 4, 8], F32) for i in range(2)]
            B_sc, B_pr, B_prT, B_ast = [Buf(), Buf()], [Buf(), Buf()], [Buf(), Buf()], [Buf(), Buf()]
            atm = pool.sb(f"atm{tg}", [128, 1024], BF16)
            B_atm = Buf()
            aTt = pool.sb(f"aTt{tg}", [128, 8, 512], BF16)
            B_aTt = Buf()
            def head_qk(j, kv, s):
                sbanks = []
                for g in range(4):
                    h = kv * 4 + g
                    bk, bb = nextbank()
                    p.op("pe", lambda e, h=h, kv=kv, j=j, bk=bk: e.matmul(bk[:, 0:384], lhsT=qT[:, h, j * 128:(j + 1) * 128],
                                                                          rhs=kT[:, kv, j * 128:j * 128 + 384], start=True, stop=True),
                         reads=[B_qT, B_kT], writes=[bb])
                    p.op("dve", lambda e, s=s, g=g, h=h, bk=bk: e.scalar_tensor_tensor(out=sc[s][:, g, :], in0=c_ab[:], scalar=float(2.0 ** (-8.0 * (h + 1) / 16)),
                                                                                       in1=bk[:, 0:384], op0=ALU.mult, op1=ALU.add),
                         reads=[bb, B_ab], writes=[B_sc[s]])

            def head_a(j, kv, s):
                if j == 0:
                    p.op("dve", lambda e, s=s: e.tensor_scalar(out=sc[s][:, :, 0:128], in0=sc[s][:, :, 0:128], scalar1=c_em[:, gidx, 0:1],
                                                               scalar2=None, op0=ALU.add), reads=[B_sc[s], B_ab], writes=[B_sc[s]])
                if j == 3:
                    p.op("dve", lambda e, s=s: e.tensor_scalar(out=sc[s][:, :, 256:384], in0=sc[s][:, :, 256:384], scalar1=c_em[:, gidx, 1:2],
                                                               scalar2=None, op0=ALU.add), reads=[B_sc[s], B_ab], writes=[B_sc[s]])
                p.op("dve", lambda e, s=s: e.tensor_reduce(out=ast[s][:, :, 0], in_=sc[s][:], axis=AX.X, op=ALU.max),
                     reads=[B_sc[s]], writes=[B_ast[s]])
                p.op("dve", lambda e, s=s, kv=kv: e.tensor_tensor(out=ast[s][:, :, 1], in0=ast[s][:, :, 0], in1=c_reps[:, 96 + kv * 4:100 + kv * 4],
                                                                  op=ALU.max), reads=[B_ast[s], B_const], writes=[B_ast[s]])
                p.op("dve", lambda e, s=s: e.tensor_scalar(out=ast[s][:, :, 2], in0=ast[s][:, :, 1], scalar1=-1.0, scalar2=None, op0=ALU.mult),
                     reads=[B_ast[s]], writes=[B_ast[s]])
                for g in range(4):
                    p.op("act", lambda e, s=s, g=g: e.activation(out=pr[s][:, g, :], in_=sc[s][:, g, :], func=AF.Exp, bias=ast[s][:, g, 2:3],
                                                                 accum_out=ast[s][:, g, 3:4]), reads=[B_sc[s], B_ast[s]], writes=[B_pr[s], B_ast[s]])

            def head_b(j, kv, s):
                p.op("dve", lambda e, s=s, kv=kv: e.tensor_tensor(out=ast[s][:, :, 4], in0=c_reps[:, 96 + kv * 4:100 + kv * 4], in1=ast[s][:, :, 1],
                                                                  op=ALU.subtract), reads=[B_ast[s], B_const], writes=[B_ast[s]])
                p.op("act", lambda e, s=s: e.activation(out=ast[s][:, :, 5], in_=ast[s][:, :, 4], func=AF.Exp), reads=[B_ast[s]], writes=[B_ast[s]])
                p.op("dve", lambda e, s=s: e.tensor_tensor(out=ast[s][:, :, 6], in0=ast[s][:, :, 5], in1=ast[s][:, :, 3], op=ALU.add),
                     reads=[B_ast[s]], writes=[B_ast[s]])
                p.op("dve", lambda e, s=s: e.reciprocal(out=ast[s][:, :, 7], in_=ast[s][:, :, 6]), reads=[B_ast[s]], writes=[B_ast[s]])


            def tail_tr(j, kv, s):
                for g in range(4):
                    bk, bb = nextbank()
                    bkb = bk[:].bitcast(BF16)
                    for m in range(3):
                        p.op("pe", lambda e, s=s, g=g, m=m, bkb=bkb: e.transpose(out=bkb[:, m * 128:(m + 1) * 128], in_=pr[s][:, g, m * 128:(m + 1) * 128],
                                                                                identity=c_idb[:]), reads=[B_pr[s], B_const], writes=[bb], pe_accum=True)
                    eng = "act" if g % 2 == 0 else "dve"
                    if eng == "act":
                        p.op("act", lambda e, s=s, g=g, bkb=bkb: e.copy(out=prT[s][:, g, :, :], in_=bkb[:, 0:384].rearrange("p (m t) -> p m t", t=128)),
                             reads=[bb], writes=[B_prT[s]])
                    else:
                        p.op("dve", lambda e, s=s, g=g, bkb=bkb: e.tensor_copy(out=prT[s][:, g, :, :], in_=bkb[:, 0:384].rearrange("p (m t) -> p m t", t=128)),
                             reads=[bb], writes=[B_prT[s]])

            def tail_pv(j, kv, s):
                bk, bb = nextbank()
                for g in range(4):
                    for m in range(3):
                        p.op("pe", lambda e, s=s, g=g, m=m, kv=kv, j=j, bk=bk: e.matmul(bk[:, g * 64:(g + 1) * 64], lhsT=prT[s][:, g, m, :],
                                                                                     rhs=vt[:, j + m, kv * 64:(kv + 1) * 64], start=(m == 0), stop=(m == 2)),
                             reads=[B_prT[s], B_vt], writes=[bb], pe_accum=True)
                return bk, bb

            def tail_norm(j, kv, s, bk, bb):
                p.op("dve", lambda e, s=s, kv=kv, bk=bk: e.tensor_tensor(
                    out=atm[:, kv * 256:(kv + 1) * 256].rearrange("p (g d) -> p g d", d=64),
                    in0=bk[:, 0:256].rearrange("p (g d) -> p g d", d=64),
                    in1=bc(ast[s][:, :, 7:8], [128, 4, 64]), op=ALU.mult), reads=[bb, B_ast[s]], writes=[B_atm])
                if kv == 3:
                    bk, bb = nextbank()
                    bkb = bk[:].bitcast(BF16)
                    for k in range(8):
                        p.op("pe", lambda e, k=k, bkb=bkb: e.transpose(out=bkb[:, k * 128:(k + 1) * 128], in_=atm[:, k * 128:(k + 1) * 128], identity=c_idb[:]),
                             reads=[B_atm, B_const], writes=[bb], pe_accum=True)
                    p.op("act", lambda e, j=j, bkb=bkb: e.copy(out=aTt[:, :, j * 128:(j + 1) * 128], in_=bkb.rearrange("p (k t) -> p k t", t=128)),
                         reads=[bb], writes=[B_aTt])


            items = [(j, kv) for j in range(4) for kv in range(4)]
            head_qk(items[0][0], items[0][1], 0)
            head_a(items[0][0], items[0][1], 0)
            head_b(items[0][0], items[0][1], 0)
            for i_ in range(1, 17):
                pj, pkv, ps_ = items[i_ - 1][0], items[i_ - 1][1], (i_ - 1) % 2
                if i_ < 16:
                    head_qk(items[i_][0], items[i_][1], i_ % 2)
                tail_tr(pj, pkv, ps_)
                pvb = tail_pv(pj, pkv, ps_)
                if i_ < 16:
                    head_a(items[i_][0], items[i_][1], i_ % 2)
                tail_norm(pj, pkv, ps_, *pvb)
                if i_ < 16:
                    head_b(items[i_][0], items[i_][1], i_ % 2)
            for k in range(8):
                p.dma("pool", lambda e, k=k: e.dma_start(out=aT_d[k, :, tok0:tok0 + NT], in_=aTt[:, k, :]), "sta", reads=[B_aTt])

        if "A" in stages:
            gi = 0
            with contextlib.ExitStack() as stA:
                poolA = TilePool(nc, stA)
                for seg, ng in OWN_GROUPS:
                    for g in range(ng):
                        tok0 = (0 if seg == 0 else 2048) + g * 512
                        for _ in prep_group(poolA, x_own[gi], 768, True, gi, tok0):
                            pass
                        emit_casts(2)
                        gi += 1
                p.barrier()


        ssd_stack = contextlib.ExitStack()
        Hst = SB(ssd_stack, "Hst", [128, 4, D], F32)
        B_H = [Buf() for _ in range(4)]

        def ssd_work(st, tag, two_xs=False):
            W = {}
            for nm, shp, dt in (("dA", [128, 64], F32), ("cumsb", [128, 128], F32), ("tmp", [128, 64], F32), ("dte", [128, 64], F32),
                                ("dec", [128, 64], F32), ("w", [128, 64], F32), ("xs", [128, D], BF16), ("xs2", [128, D], BF16), ("deff", [128, 64], F32),
                                ("tg", [128, 512], F32)):
                if nm == "xs2" and not two_xs:
                    continue
                W[nm] = SB(st, nm + tag, shp, dt)
                W["B_" + nm] = Buf()
            return W

        def chunk_pre(W, dt_ap, B_dt):
            p.op("dve", lambda e: e.tensor_tensor(out=W["dA"][:], in0=dt_ap, in1=c_arep[:], op=ALU.mult), reads=[B_dt, B_const], writes=[W["B_dA"]])
            bk, bb = nextbank()
            p.op("pe", lambda e, bk=bk: e.matmul(bk[:, 0:32], lhsT=TRIF, rhs=W["dA"][:, 0:32], start=True, stop=True), reads=[W["B_dA"], B_const], writes=[bb])
            p.op("pe", lambda e, bk=bk: e.matmul(bk[:, 32:64], lhsT=TRIB, rhs=W["dA"][:, 32:64], start=True, stop=True), reads=[W["B_dA"], B_const], writes=[bb], pe_accum=True)
            p.op("pe", lambda e, bk=bk: e.matmul(bk[:, 64:128], lhsT=ONES, rhs=W["dA"][:, 0:64], start=True, stop=True), reads=[W["B_dA"], B_const], writes=[bb], pe_accum=True)
            p.op("act", lambda e, bk=bk: e.copy(out=W["cumsb"][:], in_=bk[:, 0:128]), reads=[bb], writes=[W["B_cumsb"]])
            p.op("dve", lambda e: e.tensor_tensor(out=W["tmp"][:], in0=W["cumsb"][:, 64:128], in1=W["cumsb"][:, 0:64], op=ALU.subtract),
                 reads=[W["B_cumsb"]], writes=[W["B_tmp"]])
            p.op("act", lambda e: e.activation(out=W["dte"][:], in_=W["tmp"][:], func=AF.Exp), reads=[W["B_tmp"]], writes=[W["B_dte"]])
            p.op("act", lambda e: e.activation(out=W["dec"][:], in_=W["cumsb"][:, 64:128], func=AF.Exp), reads=[W["B_cumsb"]], writes=[W["B_dec"]])
            p.op("dve", lambda e: e.tensor_tensor(out=W["w"][:], in0=dt_ap, in1=W["dte"][:], op=ALU.mult), reads=[B_dt, W["B_dte"]], writes=[W["B_w"]])

        def chunk_states(W, X_ap, B_X, Bt_ap, B_Bt, d, xk="xs"):
            p.op("dve", lambda e: e.tensor_tensor(out=W[xk][:].rearrange("p (h d) -> p h d", d=64), in0=X_ap.rearrange("p (h d) -> p h d", d=64),
                                                  in1=bc(W["w"][:, d * 32:(d + 1) * 32].unsqueeze(2), [128, 32, 64]), op=ALU.mult),
                 reads=[B_X, W["B_w"]], writes=[W["B_" + xk]])
            out = []
            for g in range(4):
                bk, bb = nextbank()
                p.op("pe", lambda e, g=g, bk=bk: e.matmul(bk[:, 0:512], lhsT=Bt_ap[:, g * 128:(g + 1) * 128], rhs=W[xk][:, g * 512:(g + 1) * 512],
                                                         start=True, stop=True), reads=[B_Bt, W["B_" + xk]], writes=[bb])
                out.append((bk, bb))
            return out

        def h_update(W, hi, d, sbanks):
            Hv = Hst[:, hi, :]
            p.op("dve", lambda e: e.tensor_tensor(out=Hv.rearrange("p (h d) -> p h d", d=64), in0=Hv.rearrange("p (h d) -> p h d", d=64),
                                                  in1=bc(W["dec"][:, d * 32:(d + 1) * 32].unsqueeze(2), [128, 32, 64]), op=ALU.mult),
                 reads=[W["B_dec"], B_H[hi]], writes=[B_H[hi]])
            for g, (bk, bb) in enumerate(sbanks):
                p.op("dve", lambda e, g=g, bk=bk: e.tensor_tensor(out=Hst[:, hi, g * 512:(g + 1) * 512], in0=bk[:, 0:512], in1=Hst[:, hi, g * 512:(g + 1) * 512], op=ALU.add),
                     reads=[bb, B_H[hi]], writes=[B_H[hi]])

        if "S" in stages:
            with contextlib.ExitStack() as st0:
                Pb = SB(st0, "Pb", [128, 2, 32], F32)
                wpb = SB(st0, "wpb", [128, 32], F32)
                c_om = SB(st0, "c_om", [128, NG_OTH, 4], F32)
                B_Pb, B_wpb, B_om = Buf(), Buf(), Buf()
                p.dma("sp", lambda e: e.dma_start(out=c_om[:], in_=omask), "c1", writes=[B_om])
                p.op("pool", lambda e: e.memset(Pb[:], 1.0), writes=[B_Pb])
                for hi in range(4):
                    p.op("pool", lambda e, hi=hi: e.memset(Hst[:, hi, :], 0.0), writes=[B_H[hi]])
                poolS = TilePool(nc, st0)
                W_S = [ssd_work(st0, "S0", True), ssd_work(st0, "S1", True)]

                def other_group(gi, r, nxt):
                    seg = 0 if gi < 12 else 1
                    if True:

                        def xs_scale(W, ti, d, xk):
                            p.op("dve", lambda e: e.tensor_tensor(out=W[xk][:].rearrange("p (h d) -> p h d", d=64), in0=r["Xtm"][:, ti, :].rearrange("p (h d) -> p h d", d=64),
                                                                  in1=bc(W["w"][:, d * 32:(d + 1) * 32].unsqueeze(2), [128, 32, 64]), op=ALU.mult),
                                 reads=[r["B_Xtm"], W["B_w"]], writes=[W["B_" + xk]])

                        def tile_pre(ti, W):
                            chunk_pre(W, r["dttm"][:, ti, :], r["B_dttm"])
                            xs_scale(W, ti, 0, "xs")
                            xs_scale(W, ti, 1, "xs2")

                        def st_mm(W, ti, g, xk):
                            bk, bb = nextbank()
                            p.op("pe", lambda e: e.matmul(bk[:, 0:512], lhsT=r["Btm"][:, ti, g * 128:(g + 1) * 128], rhs=W[xk][:, g * 512:(g + 1) * 512],
                                                          start=True, stop=True), reads=[r["B_Btm"], W["B_" + xk]], writes=[bb])
                            return bk, bb

                        def tile_main(ti, W):
                            hi = seg * 2
                            p.op("dve", lambda e: e.tensor_scalar(out=W["deff"][:, 0:32], in0=W["dec"][:, 0:32], scalar1=c_om[:, gi, 0:1], scalar2=c_om[:, gi, 1:2],
                                                                  op0=ALU.mult, op1=ALU.add), reads=[W["B_dec"], B_om], writes=[W["B_deff"]])
                            Hv = Hst[:, hi, :]
                            p.op("dve", lambda e: e.tensor_tensor(out=Hv.rearrange("p (h d) -> p h d", d=64), in0=Hv.rearrange("p (h d) -> p h d", d=64),
                                                                  in1=bc(W["deff"][:, 0:32].unsqueeze(2), [128, 32, 64]), op=ALU.mult),
                                 reads=[W["B_deff"], B_H[hi]], writes=[B_H[hi]])
                            for g in range(4):
                                bk, bb = st_mm(W, ti, g, "xs")
                                p.op("dve", lambda e, g=g, bk=bk, hi=hi: e.scalar_tensor_tensor(
                                    out=Hst[:, hi, g * 512:(g + 1) * 512], in0=bk[:, 0:512], scalar=c_om[:, gi, 0:1], in1=Hst[:, hi, g * 512:(g + 1) * 512],
                                    op0=ALU.mult, op1=ALU.add), reads=[bb, B_H[hi], B_om], writes=[B_H[hi]])
                            hi = seg * 2 + 1
                            p.op("dve", lambda e: e.tensor_scalar(out=wpb[:], in0=Pb[:, seg, :], scalar1=c_om[:, gi, 2:3], scalar2=None, op0=ALU.mult),
                                 reads=[B_Pb, B_om], writes=[B_wpb])
                            for g in range(4):
                                bk, bb = st_mm(W, ti, g, "xs2")
                                p.op("dve", lambda e, g=g, bk=bk: e.tensor_tensor(out=W["tg"][:].rearrange("p (h d) -> p h d", d=64),
                                                                                 in0=bk[:, 0:512].rearrange("p (h d) -> p h d", d=64),
                                                                                 in1=bc(wpb[:, g * 8:(g + 1) * 8].unsqueeze(2), [128, 8, 64]), op=ALU.mult),
                                     reads=[bb, B_wpb], writes=[W["B_tg"]])
                                p.op("dve", lambda e, g=g, hi=hi: e.tensor_tensor(out=Hst[:, hi, g * 512:(g + 1) * 512], in0=Hst[:, hi, g * 512:(g + 1) * 512],
                                                                                   in1=W["tg"][:], op=ALU.add), reads=[W["B_tg"], B_H[hi]], writes=[B_H[hi]])
                            p.op("dve", lambda e: e.tensor_scalar(out=W["deff"][:, 32:64], in0=W["dec"][:, 32:64], scalar1=c_om[:, gi, 2:3], scalar2=c_om[:, gi, 3:4],
                                                                  op0=ALU.mult, op1=ALU.add), reads=[W["B_dec"], B_om], writes=[W["B_deff"]])
                            p.op("dve", lambda e: e.tensor_tensor(out=Pb[:, seg, :], in0=Pb[:, seg, :], in1=W["deff"][:, 32:64], op=ALU.mult),
                                 reads=[W["B_deff"], B_Pb, B_wpb], writes=[B_Pb])

                        def dr(n):
                            for _ in range(n):
                                if nxt[0] is not None:
                                    try:
                                        next(nxt[0])
                                    except StopIteration:
                                        nxt[0] = None

                        tile_pre(0, W_S[0])
                        for ti in range(4):
                            dr(3)
                            if ti + 1 < 4:
                                tile_pre(ti + 1, W_S[(ti + 1) % 2])
                            dr(4)
                            tile_main(ti, W_S[ti % 2])
                        dr(1000)
                rs = [dict() for _ in range(NG_OTH)]
                for _ in prep_group(poolS, x_oth[0], 516, False, 100, 0, 0, rs[0]):
                    pass
                for gi in range(NG_OTH):
                    nxt = [prep_group(poolS, x_oth[gi + 1], 516, False, 101 + gi, 0, (gi + 1) % 2, rs[gi + 1])] if gi + 1 < NG_OTH else [None]
                    other_group(gi, rs[gi], nxt)
                    emit_casts(2)
                emit_casts(1000)
                if "hin_d" in dbg:
                    for hi in range(4):
                        p.dma("pool", lambda e, hi=hi: e.dma_start(out=hin_d[hi], in_=Hst[:, hi, :]), "sth", reads=[B_H[hi]])
                p.barrier()

        SEGS = [(0, 0, 16), (1, 2048, 8)]
        def uprep_gen(stk):
            ul = [SB(stk, f"ul{i}", [128, D], BF16) for i in range(2)]
            ut = [SB(stk, f"ut{i}", [128, D], BF16) for i in range(2)]
            B_ul, B_ut = [Buf(), Buf()], [Buf(), Buf()]
            for c in range(128):
                s_ = c % 2
                p.dma("sp", lambda e, s_=s_, c=c: e.dma_start(out=ul[s_][:], in_=u_b[c * 128:(c + 1) * 128, :]), f"ul{s_}", reads=[B_wuv], writes=[B_ul[s_]])
                yield
                for half in range(2):
                    bk, bb = nextbank()
                    bkb = bk[:].bitcast(BF16)
                    for kk in range(8):
                        k = half * 8 + kk
                        p.op("pe", lambda e, s_=s_, k=k, kk=kk, bkb=bkb: e.transpose(out=bkb[:, kk * 128:(kk + 1) * 128], in_=ul[s_][:, k * 128:(k + 1) * 128], identity=c_idb[:]),
                             reads=[B_ul[s_], B_const], writes=[bb], pe_accum=True)
                    if half == 0:
                        p.op("act", lambda e, s_=s_, bkb=bkb: e.copy(out=ut[s_][:, 0:1024], in_=bkb), reads=[bb], writes=[B_ut[s_]])
                    else:
                        p.op("dve", lambda e, s_=s_, bkb=bkb: e.tensor_copy(out=ut[s_][:, 1024:2048], in_=bkb), reads=[bb], writes=[B_ut[s_]])
                p.dma("pool", lambda e, s_=s_, c=c: e.dma_start(out=ut_b[c], in_=ut[s_][:]), f"us{s_}", reads=[B_ut[s_]])
                yield

        def stage_b1():
            with contextlib.ExitStack() as st:
                W2 = [ssd_work(st, "B1a"), ssd_work(st, "B1b")]
                Xc = [SB(st, f"b1X{i}", [128, D], BF16) for i in range(2)]
                Bc = [SB(st, f"b1B{i}", [128, 512], BF16) for i in range(2)]
                dc = [SB(st, f"b1d{i}", [128, 64], F32) for i in range(2)]
                hbt = [SB(st, f"b1h{i}", [128, D], BF16) for i in range(2)]
                B_Xc, B_Bc, B_dc, B_hbt = [Buf(), Buf()], [Buf(), Buf()], [Buf(), Buf()], [Buf(), Buf()]
                it = 0
                cbase = 0
                ug = uprep_gen(st) if "C" in stages else iter(())
                for seg, tok0, nch in SEGS:
                    hi = seg * 2 + 1
                    for c in range(nch - 1, -1, -1):
                        s = it % 2
                        W = W2[s]
                        it += 1
                        t = tok0 + c * 128
                        p.dma("sp", lambda e, s=s, t=t: e.dma_start(out=Xc[s][:], in_=Xs_d[t:t + 128, :]), f"b1l{s}", writes=[B_Xc[s]])
                        p.dma("sp", lambda e, s=s, t=t: e.dma_start(out=Bc[s][:], in_=Bs_d[t:t + 128, :]), f"b1l{s}", writes=[B_Bc[s]])
                        p.dma("sp", lambda e, s=s, t=t: e.dma_start(out=dc[s][:], in_=dt_d[t:t + 128, :]), f"b1l{s}", writes=[B_dc[s]])
                        p.op("act", lambda e, s=s, hi=hi: e.copy(out=hbt[s][:], in_=Hst[:, hi, :]), reads=[B_H[hi]], writes=[B_hbt[s]])
                        p.dma("pool", lambda e, s=s, cc=cbase + c: e.dma_start(out=hb_d[cc], in_=hbt[s][:]), f"b1s{s}", reads=[B_hbt[s]])
                        chunk_pre(W, dc[s][:], B_dc[s])
                        for _ in range(6):
                            next(ug, None)
                        sb_ = chunk_states(W, Xc[s][:], B_Xc[s], Bc[s][:], B_Bc[s], 1)
                        h_update(W, hi, 1, sb_)
                        for _ in range(6):
                            next(ug, None)
                    cbase += nch
                for _ in ug:
                    pass
                p.barrier()

        def stage_b2():
            with contextlib.ExitStack() as st:
                W2b = [ssd_work(st, "B2a"), ssd_work(st, "B2b")]
                Xc2 = [SB(st, f"b2X{i}", [128, D], BF16) for i in range(2)]
                Bc2 = [SB(st, f"b2B{i}", [128, 512], BF16) for i in range(2)]
                dc2 = [SB(st, f"b2d{i}", [128, 64], F32) for i in range(2)]
                zc2 = [SB(st, f"b2z{i}", [128, D], BF16) for i in range(2)]
                BTc2 = [SB(st, f"b2BT{i}", [128, 4, 128], BF16) for i in range(2)]
                CTc2 = [SB(st, f"b2CT{i}", [128, 4, 128], BF16) for i in range(2)]
                hbt2 = [SB(st, f"b2hb{i}", [128, D], BF16) for i in range(2)]
                hft = SB(st, "b2hf", [128, D], BF16)
                B_ld2, B_hbt2, B_hft = [Buf(), Buf()], [Buf(), Buf()], Buf()
                xdt = SB(st, "b2xdt", [128, 2, D], BF16)
                cumT = SB(st, "b2cumT", [32, 2, 128], F32)
                ecum = SB(st, "b2ecum", [128, 64], F32)
                ncum = SB(st, "b2ncum", [128, 64], F32)
                GTm = SB(st, "b2GTm", [128, 2, 4, 128], BF16)
                LT2 = [SB(st, f"b2LT{i}", [128, 8, 128], BF16) for i in range(2)]
                MT2 = [SB(st, f"b2MT{i}", [128, 8, 128], BF16) for i in range(2)]
                B_LT2, B_MT2 = [Buf(), Buf()], [Buf(), Buf()]
                yv = SB(st, "b2y", [128, D], F32)
                t1 = SB(st, "b2t1", [128, 512], F32)
                gst = SB(st, "b2gst", [128, 4, 4], F32)
                ssm = SB(st, "b2ssm", [128, D], BF16)
                c_neg = SB(st, "b2neg", [128, 2, 512], BF16)
                c_gn = SB(st, "b2gn", [128, D], F32)
                ssmT = SB(st, "b2ssmT", [128, 16, 512], BF16)
                aTl = SB(st, "b2aT", [128, 8, 512], BF16)
                mT = SB(st, "b2mT", [128, 16, 512], BF16)
                B_xdt, B_cumT, B_ecum, B_GTm, B_LT, B_MT, B_y, B_t1, B_gst, B_ssm, B_c2, B_ssmT, B_aTl, B_mT = [Buf() for _ in range(14)]
                p.dma("sp", lambda e: e.dma_start(out=c_neg[:], in_=negm), "c1", writes=[B_c2])
                p.dma("sp", lambda e: e.dma_start(out=c_gn[:], in_=rep_d[:, 3, :]), "c1", writes=[B_c2])
                wo1 = [SB(st, f"b2wa{i}", [128, 8, 128], BF16) for i in range(2)]
                wo2 = [SB(st, f"b2ws{i}", [128, 16, 128], BF16) for i in range(2)]
                gl = [SB(st, f"b2gl{i}", [128, 2, 512], BF16) for i in range(2)]
                B_wo, B_gl = [Buf(), Buf()], [Buf(), Buf()]
                wo3 = SB(st, "b2wo", [128, 16, 512], BF16)
                B_wo3 = Buf()
                xr = SB(st, "b2xr", [128, 512], F32)
                B_xr = Buf()
                cbase = 0
                for seg, tok0, nch in SEGS:
                    hif, hib = seg * 2, seg * 2 + 1

                    def do_chunk(c, W, Xc, Bc, dc, zc, BTc, CTc, hbt, B_ld, B_hbt, seg=seg, tok0=tok0, hif=hif, hib=hib, cbase=cbase):
                        t = tok0 + c * 128
                        for dst, src in ((Xc[:], Xs_d[t:t + 128, :]), (Bc[:], Bs_d[t:t + 128, :]), (dc[:], dt_d[t:t + 128, :]), (zc[:], zs_d[t:t + 128, :]),
                                         (BTc[:], BT_d[:, :, t:t + 128].rearrange("g p t -> p g t")), (CTc[:], CT_d[:, :, t:t + 128].rearrange("g p t -> p g t"))):
                            p.dma("sp", lambda e, dst=dst, src=src: e.dma_start(out=dst, in_=src), f"b2l{(cbase + c) % 2}", writes=[B_ld])
                        p.dma("sp", lambda e, cc=cbase + c: e.dma_start(out=hbt[:], in_=hb_d[cc]), f"b2l{(cbase + c) % 2}", writes=[B_hbt])
                        p.op("act", lambda e, hif=hif: e.copy(out=hft[:], in_=Hst[:, hif, :]), reads=[B_H[hif]], writes=[B_hft])
                        chunk_pre(W, dc[:], B_ld)
                        bk, bb = nextbank()
                        p.op("pe", lambda e, bk=bk: e.matmul(bk[0:32, 0:128], lhsT=W["dA"][:, 0:32], rhs=TRIF, start=True, stop=True), reads=[W["B_dA"], B_const], writes=[bb])
                        p.op("pe", lambda e, bk=bk: e.matmul(bk[0:32, 128:256], lhsT=W["dA"][:, 32:64], rhs=TRIB, start=True, stop=True), reads=[W["B_dA"], B_const], writes=[bb], pe_accum=True)
                        p.op("act", lambda e, bk=bk: e.copy(out=cumT[:], in_=bk[0:32, 0:256].rearrange("p (d t) -> p d t", t=128)), reads=[bb], writes=[B_cumT])
                        p.op("act", lambda e: e.activation(out=ecum[:], in_=W["cumsb"][:, 0:64], func=AF.Exp), reads=[W["B_cumsb"]], writes=[B_ecum])
                        p.op("dve", lambda e: e.tensor_scalar(out=ncum[:], in0=W["cumsb"][:, 0:64], scalar1=-1.0, scalar2=None, op0=ALU.mult), reads=[W["B_cumsb"]], writes=[B_ecum])
                        for d in range(2):
                            p.op("dve" if d == 0 else "pool", lambda e, d=d: e.tensor_tensor(
                                out=xdt[:, d, :].rearrange("p (h d) -> p h d", d=64), in0=Xc[:].rearrange("p (h d) -> p h d", d=64),
                                in1=bc(dc[:, d * 32:(d + 1) * 32].unsqueeze(2), [128, 32, 64]), op=ALU.mult), reads=[B_ld], writes=[B_xdt])
                        bk, bb = nextbank()
                        for g in range(4):
                            p.op("pe", lambda e, g=g, bk=bk: e.matmul(bk[:, g * 128:(g + 1) * 128], lhsT=BTc[:, g, :], rhs=CTc[:, g, :], start=True, stop=True),
                                 reads=[B_ld], writes=[bb], pe_accum=True)
                        for d in range(2):
                            tri = TRIF if d == 0 else TRIB
                            p.op("dve", lambda e, d=d, tri=tri, bk=bk: e.tensor_tensor(out=GTm[:, d, :, :], in0=bk[:, 0:512].rearrange("p (g t) -> p g t", t=128),
                                                                                      in1=bc(tri.unsqueeze(1), [128, 4, 128]), op=ALU.mult),
                                 reads=[bb, B_const], writes=[B_GTm])
                        def dg_head(d, g, sl):
                            LT, MT, B_LT, B_MT = LT2[sl], MT2[sl], B_LT2[sl], B_MT2[sl]
                            for half in range(2):
                                bk, bb = nextbank()
                                p.op("pe", lambda e, bk=bk: e.matmul(bk[:, 0:512], lhsT=c_idb[:], rhs=c_neg[:, d, :], start=True, stop=False),
                                     reads=[B_c2, B_const], writes=[bb])
                                for j in range(4):
                                    h = g * 8 + half * 4 + j
                                    p.op("pe", lambda e, j=j, h=h, bk=bk: e.matmul(bk[:, j * 128:(j + 1) * 128], lhsT=bc(c_cst[0:32, 0, h:h + 1], [32, 128]),
                                                                                   rhs=cumT[:, d, :], start=False, stop=(j == 3)),
                                         reads=[B_cumT, B_const], writes=[bb], pe_accum=True)
                                for j in range(4):
                                    h = g * 8 + half * 4 + j
                                    p.op("act", lambda e, j=j, h=h, half=half, bk=bk: e.activation(out=LT[:, half * 4 + j, :], in_=bk[:, j * 128:(j + 1) * 128], func=AF.Exp,
                                                                                                  bias=ncum[:, d * 32 + h:d * 32 + h + 1]),
                                         reads=[bb, B_ecum], writes=[B_LT])
                            p.op("dve", lambda e: e.tensor_tensor(out=MT[:], in0=LT[:], in1=bc(GTm[:, d, g, :].unsqueeze(1), [128, 8, 128]), op=ALU.mult),
                                 reads=[B_LT, B_GTm], writes=[B_MT])

                        def dg_tail(d, g, sl):
                            MT, B_MT = MT2[sl], B_MT2[sl]
                            hsrc, B_hs = (hft, B_hft) if d == 0 else (hbt, B_hbt)
                            bkd, bbd = nextbank()
                            for j in range(8):
                                h = g * 8 + j
                                p.op("pe", lambda e, j=j, h=h: e.matmul(bkd[:, j * 64:(j + 1) * 64], lhsT=MT[:, j, :], rhs=xdt[:, d, h * 64:(h + 1) * 64],
                                                                        start=True, stop=True), reads=[B_MT, B_xdt], writes=[bbd], pe_accum=True)
                            bko, bbo = nextbank()
                            p.op("pe", lambda e: e.matmul(bko[:, 0:512], lhsT=CTc[:, g, :], rhs=hsrc[:, g * 512:(g + 1) * 512], start=True, stop=True),
                                 reads=[B_ld, B_hs], writes=[bbo])
                            p.op("dve", lambda e: e.tensor_tensor(out=t1[:].rearrange("p (h d) -> p h d", d=64),
                                                                  in0=bko[:, 0:512].rearrange("p (h d) -> p h d", d=64),
                                                                  in1=bc(ecum[:, d * 32 + g * 8:d * 32 + g * 8 + 8].unsqueeze(2), [128, 8, 64]), op=ALU.mult),
                                 reads=[bbo, B_ecum], writes=[B_t1])
                            if d == 0:
                                p.op("dve", lambda e: e.tensor_tensor(out=yv[:, g * 512:(g + 1) * 512], in0=bkd[:, 0:512], in1=t1[:], op=ALU.add),
                                     reads=[bbd, B_t1], writes=[B_y])
                            else:
                                p.op("dve", lambda e: e.tensor_tensor(out=t1[:], in0=bkd[:, 0:512], in1=t1[:], op=ALU.add),
                                     reads=[bbd, B_t1], writes=[B_t1])
                                p.op("pool", lambda e: e.tensor_tensor(out=yv[:, g * 512:(g + 1) * 512], in0=yv[:, g * 512:(g + 1) * 512], in1=t1[:], op=ALU.add),
                                     reads=[B_t1, B_y], writes=[B_y])

                        dgs = [(d, g) for d in range(2) for g in range(4)]
                        dg_head(dgs[0][0], dgs[0][1], 0)
                        for i_ in range(1, 8):
                            dg_head(dgs[i_][0], dgs[i_][1], i_ % 2)
                            dg_tail(dgs[i_ - 1][0], dgs[i_ - 1][1], (i_ - 1) % 2)
                        dg_tail(dgs[7][0], dgs[7][1], 1)
                        p.op("dve", lambda e: e.tensor_tensor(out=xdt[:, 0, :].rearrange("p (h d) -> p h d", d=64), in0=Xc[:].rearrange("p (h d) -> p h d", d=64),
                                                              in1=bc(c_reps[:, 64:96].unsqueeze(2), [128, 32, 64]), op=ALU.mult),
                             reads=[B_ld, B_const, B_xdt], writes=[B_xdt])
                        p.op("dve", lambda e: e.tensor_tensor(out=yv[:], in0=yv[:], in1=xdt[:, 0, :], op=ALU.add), reads=[B_xdt, B_y], writes=[B_y])
                        p.op("dve", lambda e: e.tensor_tensor(out=yv[:], in0=yv[:], in1=zc[:], op=ALU.mult), reads=[B_ld, B_y], writes=[B_y])
                        for g in range(4):
                            p.op("act", lambda e, g=g: e.activation(out=ssm[:, g * 512:(g + 1) * 512], in_=yv[:, g * 512:(g + 1) * 512], func=AF.Square, accum_out=gst[:, g, 0:1]),
                                 reads=[B_y], writes=[B_ssm, B_gst])
                        p.op("act", lambda e: e.activation(out=gst[:, :, 1], in_=gst[:, :, 0], func=AF.Sqrt, scale=1.0 / 512, bias=EPS), reads=[B_gst], writes=[B_gst])
                        p.op("dve", lambda e: e.reciprocal(out=gst[:, :, 2], in_=gst[:, :, 1]), reads=[B_gst], writes=[B_gst])
                        p.op("dve", lambda e: e.tensor_tensor(out=yv[:].rearrange("p (g d) -> p g d", d=512), in0=yv[:].rearrange("p (g d) -> p g d", d=512),
                                                              in1=bc(gst[:, :, 2:3], [128, 4, 512]), op=ALU.mult), reads=[B_gst, B_y], writes=[B_y])
                        p.op("dve", lambda e: e.tensor_tensor(out=ssm[:], in0=yv[:], in1=c_gn[:], op=ALU.mult), reads=[B_y, B_c2, B_ssm], writes=[B_ssm])
                        ci = c % 4
                        for half in range(2):
                            bk, bb = nextbank()
                            bkb = bk[:].bitcast(BF16)
                            for kk in range(8):
                                k = half * 8 + kk
                                p.op("pe", lambda e, k=k, kk=kk, bkb=bkb: e.transpose(out=bkb[:, kk * 128:(kk + 1) * 128], in_=ssm[:, k * 128:(k + 1) * 128], identity=c_idb[:]),
                                     reads=[B_ssm, B_const], writes=[bb], pe_accum=True)
                            p.op("act", lambda e, half=half, ci=ci, bkb=bkb: e.copy(out=ssmT[:, half * 8:half * 8 + 8, ci * 128:(ci + 1) * 128],
                                                                                    in_=bkb.rearrange("p (k t) -> p k t", t=128)), reads=[bb], writes=[B_ssmT])
                        sb_ = chunk_states(W, Xc[:], B_ld, Bc[:], B_ld, 0)
                        h_update(W, hif, 0, sb_)
                        if ci == 3:
                            g0 = t - 384
                            p.dma("sp", lambda e, g0=g0: e.dma_start(out=aTl[:], in_=aT_d[:, :, g0:g0 + 512].rearrange("k p t -> p k t")), "b2a", writes=[B_aTl])
                            for cc in range(16):
                                s = cc % 2
                                p.dma("sp", lambda e, s=s, cc=cc: e.dma_start(out=wo1[s][:], in_=w_ao_b[:, cc * 128:(cc + 1) * 128].rearrange("(k p) c -> p k c", p=128)),
                                      f"b2w{s}", reads=[B_w], writes=[B_wo[s]])
                                p.dma("sp", lambda e, s=s, cc=cc: e.dma_start(out=wo2[s][:], in_=w_so_b[:, cc * 128:(cc + 1) * 128].rearrange("(k p) c -> p k c", p=128)),
                                      f"b2w{s}", reads=[B_w], writes=[B_wo[s]])
                                p.dma("sp", lambda e, s=s, cc=cc, g0=g0: e.dma_start(out=gl[s][:, 0, :], in_=gT_d[cc, :, g0:g0 + 512]), f"b2g{s}", writes=[B_gl[s]])
                                p.dma("sp", lambda e, s=s, cc=cc, g0=g0: e.dma_start(out=gl[s][:, 1, :], in_=gT_d[16 + cc, :, g0:g0 + 512]), f"b2g{s}", writes=[B_gl[s]])
                                bka, bba = nextbank()
                                for k in range(8):
                                    p.op("pe", lambda e, s=s, k=k, bka=bka: e.matmul(bka[:, 0:512], lhsT=wo1[s][:, k, :], rhs=aTl[:, k, :], start=(k == 0), stop=(k == 7)),
                                         reads=[B_wo[s], B_aTl], writes=[bba], pe_accum=True)
                                bks, bbs = nextbank()
                                for k in range(16):
                                    p.op("pe", lambda e, s=s, k=k, bks=bks: e.matmul(bks[:, 0:512], lhsT=wo2[s][:, k, :], rhs=ssmT[:, k, :], start=(k == 0), stop=(k == 15)),
                                         reads=[B_wo[s], B_ssmT], writes=[bbs], pe_accum=True)
                                p.op("dve", lambda e, s=s, bka=bka: e.tensor_tensor(out=t1[:], in0=bka[:, 0:512], in1=gl[s][:, 0, :], op=ALU.mult),
                                     reads=[bba, B_gl[s], B_t1], writes=[B_t1])
                                p.op("dve", lambda e, s=s, bks=bks: e.tensor_tensor(out=yv[:, 0:512], in0=bks[:, 0:512], in1=gl[s][:, 1, :], op=ALU.mult),
                                     reads=[bbs, B_gl[s], B_y], writes=[B_y])
                                p.op("dve", lambda e, cc=cc: e.tensor_tensor(out=mT[:, cc, :], in0=t1[:], in1=yv[:, 0:512], op=ALU.add),
                                     reads=[B_t1, B_y], writes=[B_mT])
                            for cb in range(4):
                                p.dma("sp", lambda e, cb=cb: e.dma_start(out=wo3[:], in_=w_out_b[:, cb * 512:(cb + 1) * 512].rearrange("(k p) c -> p k c", p=128)),
                                      "b2w3", reads=[B_w], writes=[B_wo3])
                                for ti in range(4):
                                    tt = g0 + ti * 128
                                    p.dma("sp", lambda e, tt=tt, cb=cb: e.dma_start(out=xr[:], in_=x_res[tt:tt + 128, cb * 512:(cb + 1) * 512]), "b2x", writes=[B_xr])
                                    bk, bb = nextbank()
                                    for k in range(16):
                                        p.op("pe", lambda e, k=k, ti=ti, bk=bk: e.matmul(bk[:, 0:512], lhsT=mT[:, k, ti * 128:(ti + 1) * 128], rhs=wo3[:, k, :],
                                                                                          start=(k == 0), stop=(k == 15)), reads=[B_mT, B_wo3], writes=[bb], pe_accum=True)
                                    p.op("dve", lambda e, bk=bk: e.tensor_tensor(out=xr[:], in0=bk[:, 0:512], in1=xr[:], op=ALU.add), reads=[bb, B_xr], writes=[B_xr])
                                    p.dma("pool", lambda e, tt=tt, cb=cb: e.dma_start(out=x1_d[tt:tt + 128, cb * 512:(cb + 1) * 512], in_=xr[:]), "b2xs", reads=[B_xr])
                    for c in range(nch):
                        s2 = (cbase + c) % 2
                        do_chunk(c, W2b[s2], Xc2[s2], Bc2[s2], dc2[s2], zc2[s2], BTc2[s2], CTc2[s2], hbt2[s2], B_ld2[s2], B_hbt2[s2])
                    cbase += nch
                p.barrier()
        if "B" in stages:
            stage_b1()
            stage_b2()
        p.barrier()
        ssd_stack.close()

        def stage_c():
            with contextlib.ExitStack() as st:
                keysT = SB(st, "keysT", [128, 16, 128], BF16)
                iob = SB(st, "iob", [128, 128], BF16)
                c_gf = SB(st, "c_gf", [128, 1, D], F32)
                B_kT, B_cc, B_gf = Buf(), Buf(), Buf()
                p.op("dve", lambda e: e.tensor_copy(out=iob[:], in_=IOTA), reads=[B_const], writes=[B_cc])
                with contextlib.ExitStack() as st2:
                    kf = SB(st2, "kf", [128, 16, 128], F32)
                    kb = SB(st2, "kb", [128, 16, 128], BF16)
                    B_kf = Buf()
                    p.dma("sp", lambda e: e.dma_start(out=kf[:], in_=keys.rearrange("a n d -> n a d")), "c1", writes=[B_kf])
                    p.op("dve", lambda e: e.tensor_copy(out=kb[:], in_=kf[:]), reads=[B_kf], writes=[B_kf])
                    for half in range(2):
                        bk, bb = nextbank()
                        bkb = bk[:].bitcast(BF16)
                        for kk in range(8):
                            p.op("pe", lambda e, a=half * 8 + kk, kk=kk, bkb=bkb: e.transpose(out=bkb[:, kk * 128:(kk + 1) * 128], in_=kb[:, a, :], identity=c_idb[:]),
                                 reads=[B_kf, B_const], writes=[bb], pe_accum=True)
                        p.op("act", lambda e, half=half, bkb=bkb: e.copy(out=keysT[:, half * 8:half * 8 + 8, :], in_=bkb.rearrange("p (k t) -> p k t", t=128)),
                             reads=[bb], writes=[B_kT])
                    p.barrier()

                x1t = SB(st, "x1t", [128, 1, D], F32)
                xn = SB(st, "cxn", [128, D], BF16)
                cst_ = SB(st, "cst_", [128, 2, 4], F32)
                cst2 = SB(st, "cst2", [128, 2, 4], F32)
                xnT2 = [SB(st, f"xnT{i}", [128, 16, 256], BF16) for i in range(2)]
                qT = SB(st, "cqT", [128, 16, 256], BF16)
                wq = [SB(st, f"wq{i}", [128, 16, 128], BF16) for i in range(2)]
                P2g = SB(st, "P2g", [128, 32, 128], BF16)
                OH1 = SB(st, "OH1", [128, 32, 128], BF16)
                scr = P2g[:].bitcast(F32).rearrange("p a b -> p (a b)").rearrange("p (h n) -> p h n", n=128)
                eq = OH1[:].bitcast(F32).rearrange("p a b -> p (a b)").rearrange("p (h k j) -> p h k j", k=16, j=16)
                wk = SB(st, "cwk", [128, 256], F32)
                topv2 = [SB(st, f"topv{i}", [128, 16, 16], F32) for i in range(2)]
                idxu2 = [SB(st, f"idxu{i}", [128, 16, 16], U32) for i in range(2)]
                idxf = SB(st, "idxf", [128, 16, 16], F32)
                cand = SB(st, "cand", [128, 8, 16, 16], F32)
                best = SB(st, "best", [128, 8, 16], F32)
                posu = SB(st, "posu", [128, 8, 16], U32)
                ku = SB(st, "ku", [128, 2, 8, 16], U32)
                kf_ = SB(st, "kf_", [128, 2, 8, 16], F32)
                gat = SB(st, "gat", [128, 8, 16], F32)
                gz = SB(st, "gz", [128, 8, 2], F32)
                I12_2 = [SB(st, f"I12_{i}", [128, 3, 128], F32) for i in range(2)]
                I12T = SB(st, "I12T", [128, 3, 128], BF16)
                Gs = SB(st, "Gs", [128, 128, 256], BF16)
                NSL = 4
                strm = [SB(st, f"strm{i}", [128, 2, D], BF16) for i in range(NSL)]
                ge = [SB(st, f"ge{i}", [128, 256], BF16) for i in range(2)]
                (B_x1t, B_xn, B_cst, B_cst2, B_qT, B_wk, B_idxf, B_cand, B_best, B_posu, B_ku, B_kf2, B_gat, B_gz,
                 B_I12T, B_OH1, B_P2g, B_Gs) = [Buf() for _ in range(18)]
                B_xnT2, B_topv2, B_idxu2, B_I12_2 = [Buf(), Buf()], [Buf(), Buf()], [Buf(), Buf()], [Buf(), Buf()]
                B_wq, B_ge = [Buf(), Buf()], [Buf(), Buf()]
                B_strm = [Buf() for _ in range(NSL)]
                B_scr, B_eq = B_P2g, B_OH1
                B_P2ga, B_P2gb = Buf(), Buf()
                HB_OH, HB_P2, HB_Pa, HB_Pb = [Buf(), Buf()], [Buf(), Buf()], [Buf(), Buf()], [Buf(), Buf()]
                fz = SB(st, "fz", [128, 2], F32)
                B_fz = Buf()
                sctr = [0]

                def stage1_hc(ti, hc):
                    topv, idxu, B_topv, B_idxu = topv2[ti], idxu2[ti], B_topv2[ti], B_idxu2[ti]
                    p.op("dve", lambda e: e.max(out=topv[:, hc, 0:8], in_=scr[:, hc, :]), reads=[B_scr], writes=[B_topv])
                    p.op("dve", lambda e: e.match_replace(out=wk[:, 0:128], in_to_replace=topv[:, hc, 0:8], in_values=scr[:, hc, :], imm_value=-1e30),
                         reads=[B_scr, B_topv], writes=[B_wk])
                    p.op("dve", lambda e: e.max(out=topv[:, hc, 8:16], in_=wk[:, 0:128]), reads=[B_wk], writes=[B_topv])
                    p.op("dve", lambda e: e.max_index(out=idxu[:, hc, 0:8], in_max=topv[:, hc, 0:8], in_values=scr[:, hc, :]), reads=[B_scr, B_topv], writes=[B_idxu])
                    p.op("dve", lambda e: e.max_index(out=idxu[:, hc, 8:16], in_max=topv[:, hc, 8:16], in_values=scr[:, hc, :]), reads=[B_scr, B_topv], writes=[B_idxu])

                def scores_q(ti, qd, xsl):
                    bk, bb = nextbank()
                    for j in range(4):
                        hc = qd * 4 + j
                        p.op("pe", lambda e, hc=hc, j=j: e.matmul(bk[:, j * 128:(j + 1) * 128], lhsT=qT[:, hc, ti * 128:(ti + 1) * 128], rhs=keysT[:, hc, :],
                                                                    start=True, stop=True), reads=[B_qT, B_kT], writes=[bb], pe_accum=True)
                    p.op("act", lambda e: e.copy(out=scr[:, qd * 4:qd * 4 + 4, :], in_=bk[:, 0:512].rearrange("p (a n) -> p a n", n=128)),
                         reads=[bb, B_P2ga, B_P2gb], writes=[B_scr])

                def phaseA1(gi):
                    tok0 = gi * 256
                    xnT, B_xnT = xnT2[gi % 2], B_xnT2[gi % 2]
                    p.dma("sp", lambda e: e.dma_start(out=c_gf[:, 0, :], in_=rep_d[:, 1, :]), "cgf", writes=[B_gf])
                    for ti in range(2):
                        t = tok0 + ti * 128
                        p.dma("sp", lambda e, t=t: e.dma_start(out=x1t[:, 0, :], in_=x1_d[t:t + 128, :]), "cx", writes=[B_x1t])
                        p.op("act", lambda e, ti=ti: e.activation(out=xn[:], in_=x1t[:, 0, :], func=AF.Square, accum_out=cst2[:, ti, 0:1]), reads=[B_x1t], writes=[B_xn, B_cst2])
                        p.op("act", lambda e, ti=ti: e.activation(out=cst2[:, ti, 1:2], in_=cst2[:, ti, 0:1], func=AF.Sqrt, scale=1.0 / D, bias=EPS), reads=[B_cst2], writes=[B_cst2])
                        p.op("dve", lambda e, ti=ti: e.reciprocal(out=cst2[:, ti, 2:3], in_=cst2[:, ti, 1:2]), reads=[B_cst2], writes=[B_cst2])
                        p.op("dve", lambda e, ti=ti: e.scalar_tensor_tensor(out=xn[:], in0=x1t[:, 0, :], scalar=cst2[:, ti, 2:3], in1=c_gf[:, 0, :], op0=ALU.mult, op1=ALU.mult),
                             reads=[B_x1t, B_cst2, B_gf, B_xn], writes=[B_xn])
                        yield
                        for half in range(2):
                            bk, bb = nextbank()
                            bkb = bk[:].bitcast(BF16)
                            for kk in range(8):
                                k = half * 8 + kk
                                p.op("pe", lambda e, k=k, kk=kk, bkb=bkb: e.transpose(out=bkb[:, kk * 128:(kk + 1) * 128], in_=xn[:, k * 128:(k + 1) * 128], identity=c_idb[:]),
                                     reads=[B_xn, B_const], writes=[bb], pe_accum=True)
                            p.op("act", lambda e, half=half, ti=ti, bkb=bkb: e.copy(out=xnT[:, half * 8:half * 8 + 8, ti * 128:(ti + 1) * 128], in_=bkb.rearrange("p (k t) -> p k t", t=128)),
                                 reads=[bb], writes=[B_xnT])
                            yield
                    def ld_wq(cc):
                        s_ = cc % 2
                        p.dma("sp", lambda e: e.dma_start(out=wq[s_][:], in_=w_q_b[:, cc * 128:(cc + 1) * 128].rearrange("(k p) c -> p k c", p=128)),
                              f"cwq{s_}", reads=[B_w], writes=[B_wq[s_]])
                    ld_wq(0)
                    for cc in range(16):
                        s_ = cc % 2
                        bk, bb = nextbank()
                        for k in range(16):
                            p.op("pe", lambda e, s_=s_, k=k, bk=bk: e.matmul(bk[:, 0:256], lhsT=wq[s_][:, k, :], rhs=xnT[:, k, :], start=(k == 0), stop=(k == 15)),
                                 reads=[B_wq[s_], B_xnT], writes=[bb], pe_accum=True)
                        p.op("act", lambda e, cc=cc, bk=bk: e.copy(out=qT[:, cc, :], in_=bk[:, 0:256]), reads=[bb], writes=[B_qT])
                        if cc + 1 < 16:
                            ld_wq(cc + 1)
                        yield
                        if cc % 4 != 3:
                            yield
                    for qd in range(4):
                        scores_q(0, qd, None)
                        yield
                    for hc in range(16):
                        stage1_hc(0, hc)
                        yield
                    for qd in range(4):
                        scores_q(1, qd, None)
                        yield

                def phaseA2(gi):
                    for hc in range(16):
                        stage1_hc(1, hc)
                        yield
                    for ti in range(2):
                        topv, idxu, B_topv, B_idxu = topv2[ti], idxu2[ti], B_topv2[ti], B_idxu2[ti]
                        I12, B_I12 = I12_2[ti], B_I12_2[ti]
                        p.op("dve", lambda e, idxu=idxu: e.tensor_copy(out=idxf[:], in_=idxu[:]), reads=[B_idxu], writes=[B_idxf])
                        tv = topv[:].rearrange("p (h c) k -> p h c k", c=2)
                        p.op("dve", lambda e, tv=tv: e.tensor_tensor(out=cand[:], in0=bc(tv[:, :, 0, :].unsqueeze(3), [128, 8, 16, 16]), in1=bc(tv[:, :, 1, :].unsqueeze(2), [128, 8, 16, 16]), op=ALU.add),
                             reads=[B_topv], writes=[B_cand])
                        yield
                        for h in range(8):
                            cv = cand[:, h, :, :].rearrange("p a b -> p (a b)")
                            p.op("dve", lambda e, h=h, cv=cv: e.max(out=best[:, h, 0:8], in_=cv), reads=[B_cand], writes=[B_best])
                            p.op("dve", lambda e, h=h, cv=cv: e.match_replace(out=wk[:], in_to_replace=best[:, h, 0:8], in_values=cv, imm_value=-1e30), reads=[B_cand, B_best], writes=[B_wk])
                            p.op("dve", lambda e, h=h: e.max(out=best[:, h, 8:16], in_=wk[:]), reads=[B_wk], writes=[B_best])
                            p.op("dve", lambda e, h=h, cv=cv: e.max_index(out=posu[:, h, 0:8], in_max=best[:, h, 0:8], in_values=cv), reads=[B_cand, B_best], writes=[B_posu])
                            p.op("dve", lambda e, h=h, cv=cv: e.max_index(out=posu[:, h, 8:16], in_max=best[:, h, 8:16], in_values=cv), reads=[B_cand, B_best], writes=[B_posu])
                            yield
                        p.op("dve", lambda e: e.tensor_tensor(out=gat[:], in0=best[:], in1=bc(best[:, :, 0:1], [128, 8, 16]), op=ALU.subtract), reads=[B_best], writes=[B_gat])
                        p.op("act", lambda e: e.activation(out=gat[:], in_=gat[:], func=AF.Exp), reads=[B_gat], writes=[B_gat])
                        p.op("dve", lambda e: e.tensor_reduce(out=gz[:, :, 0], in_=gat[:], axis=AX.X, op=ALU.add), reads=[B_gat], writes=[B_gz])
                        p.op("dve", lambda e: e.reciprocal(out=gz[:, :, 1], in_=gz[:, :, 0]), reads=[B_gz], writes=[B_gz])
                        p.op("dve", lambda e, I12=I12: e.tensor_tensor(out=I12[:, 2, :].rearrange("p (h k) -> p h k", k=16), in0=gat[:], in1=bc(gz[:, :, 1:2], [128, 8, 16]), op=ALU.mult),
                             reads=[B_gat, B_gz], writes=[B_I12])
                        p.op("dve", lambda e: e.tensor_single_scalar(out=ku[:, 0, :, :], in_=posu[:], scalar=4, op=ALU.logical_shift_right), reads=[B_posu], writes=[B_ku])
                        p.op("dve", lambda e: e.tensor_single_scalar(out=ku[:, 1, :, :], in_=posu[:], scalar=15, op=ALU.bitwise_and), reads=[B_posu], writes=[B_ku])
                        p.op("dve", lambda e: e.tensor_copy(out=kf_[:], in_=ku[:]), reads=[B_ku], writes=[B_kf2])
                        yield
                        iv = idxf[:].rearrange("p (h c) k -> p h c k", c=2)
                        for c_ in range(2):
                            p.op("dve", lambda e, c_=c_: e.tensor_tensor(out=eq, in0=bc(kf_[:, c_, :, :].unsqueeze(3), [128, 8, 16, 16]),
                                                                          in1=bc(IOTA[:, 0:16].unsqueeze(1).unsqueeze(1), [128, 8, 16, 16]), op=ALU.is_equal),
                                 reads=[B_kf2, B_const], writes=[B_eq])
                            p.op("dve", lambda e, c_=c_, iv=iv: e.tensor_tensor(out=eq, in0=eq, in1=bc(iv[:, :, c_, :].unsqueeze(2), [128, 8, 16, 16]), op=ALU.mult),
                                 reads=[B_idxf, B_eq], writes=[B_eq])
                            p.op("dve", lambda e, c_=c_, I12=I12: e.tensor_reduce(out=I12[:, c_, :], in_=eq.rearrange("p h k j -> p (h k) j"), axis=AX.X, op=ALU.add),
                                 reads=[B_eq], writes=[B_I12])
                            yield

                def gbuild(gi):
                    for ti in range(2):
                        I12, B_I12 = I12_2[ti], B_I12_2[ti]
                        bk, bb = nextbank()
                        for w_ in range(3):
                            p.op("pe", lambda e, w_=w_, bk=bk, I12=I12: e.transpose(out=bk[:, w_ * 128:(w_ + 1) * 128], in_=I12[:, w_, :], identity=IDF), reads=[B_I12, B_const], writes=[bb], pe_accum=True)
                        p.op("act", lambda e, bk=bk: e.copy(out=I12T[:], in_=bk[:, 0:384].rearrange("p (w t) -> p w t", t=128)), reads=[bb], writes=[B_I12T])
                        if ti == 0:
                            p.op("dve", lambda e: e.memset(fz[:], 0.0), reads=[], writes=[HB_OH[0], HB_OH[1], HB_P2[0], HB_P2[1], HB_Pa[0], HB_Pa[1], HB_Pb[0], HB_Pb[1],
                                                                                   B_P2g, B_OH1, B_P2ga, B_P2gb, B_fz])
                        for e8 in range(8):
                            x = e8 % 2
                            xs_ = slice(x * 16, (x + 1) * 16)
                            t0_ = e8 * 16
                            p.op("dve", lambda e, xs_=xs_, t0_=t0_: e.tensor_tensor(out=OH1[:, xs_, :], in0=bc(iob[:].unsqueeze(1), [128, 16, 128]),
                                                                                 in1=bc(I12T[:, 0, t0_:t0_ + 16].unsqueeze(2), [128, 16, 128]), op=ALU.is_equal),
                                 reads=[B_I12T, B_cc], writes=[HB_OH[x]])
                            p.op("dve", lambda e, xs_=xs_, t0_=t0_: e.tensor_tensor(out=P2g[:, xs_, :], in0=bc(iob[:].unsqueeze(1), [128, 16, 128]),
                                                                                 in1=bc(I12T[:, 1, t0_:t0_ + 16].unsqueeze(2), [128, 16, 128]), op=ALU.is_equal),
                                 reads=[B_I12T, B_cc], writes=[HB_P2[x], HB_Pa[x], HB_Pb[x]])
                            p.op("dve", lambda e, x=x, t0_=t0_: e.tensor_tensor(out=P2g[:, x * 16:x * 16 + 10, :], in0=P2g[:, x * 16:x * 16 + 10, :],
                                                                             in1=bc(I12T[:, 2, t0_:t0_ + 10].unsqueeze(2), [128, 10, 128]), op=ALU.mult),
                                 reads=[B_I12T, HB_P2[x]], writes=[HB_Pa[x]])
                            p.op("pool", lambda e, x=x, t0_=t0_: e.tensor_tensor(out=P2g[:, x * 16 + 10:x * 16 + 16, :], in0=P2g[:, x * 16 + 10:x * 16 + 16, :],
                                                                              in1=bc(I12T[:, 2, t0_ + 10:t0_ + 16].unsqueeze(2), [128, 6, 128]), op=ALU.mult),
                                 reads=[B_I12T, HB_P2[x]], writes=[HB_Pb[x]])
                            for q4 in range(4):
                                bk, bb = nextbank()
                                for j in range(4):
                                    tl = x * 16 + q4 * 4 + j
                                    p.op("pe", lambda e, tl=tl, j=j, bk=bk: e.matmul(bk[:, j * 128:(j + 1) * 128], lhsT=P2g[:, tl, :], rhs=OH1[:, tl, :], start=True, stop=True),
                                         reads=[HB_P2[x], HB_Pa[x], HB_Pb[x], HB_OH[x]], writes=[bb], pe_accum=True)
                                tg0 = ti * 128 + t0_ + q4 * 4
                                src = bk[:, 0:512].rearrange("p (t i) -> p i t", i=128)
                                p.op("act", lambda e, tg0=tg0, src=src: e.copy(out=Gs[:, :, tg0:tg0 + 4], in_=src), reads=[bb], writes=[B_Gs])
                        if ti == 1:
                            p.op("dve", lambda e: e.memset(fz[:], 0.0), reads=[], writes=[HB_OH[0], HB_OH[1], HB_P2[0], HB_P2[1], HB_Pa[0], HB_Pa[1], HB_Pb[0], HB_Pb[1],
                                                                                   B_P2g, B_OH1, B_P2ga, B_P2gb, B_fz])

                def drain(gen, n):
                    if gen is None:
                        return None
                    for _ in range(n):
                        try:
                            next(gen)
                        except StopIteration:
                            return None
                    return gen

                def passes_and_epilogue(gi, ga, gb):
                    tok0 = gi * 256
                    xnT, B_xnT = xnT2[gi % 2], B_xnT2[gi % 2]
                    for c2 in range(64):
                        sl = sctr[0] % NSL
                        sctr[0] += 1
                        p.dma("sp", lambda e, sl=sl, c2=c2: e.dma_start(out=strm[sl][:], in_=ut_b[2 * c2:2 * c2 + 2].rearrange("c p f -> p c f")), f"cs{sl}", writes=[B_strm[sl]])
                        for cj in range(2):
                            c = 2 * c2 + cj
                            s_ = c % 2
                            bk, bb = nextbank()
                            for k in range(16):
                                p.op("pe", lambda e, sl=sl, cj=cj, k=k, bk=bk: e.matmul(bk[:, 0:256], lhsT=strm[sl][:, cj, k * 128:(k + 1) * 128], rhs=xnT[:, k, :], start=(k == 0), stop=(k == 15)),
                                     reads=[B_strm[sl], B_xnT], writes=[bb], pe_accum=True)
                            p.op("act", lambda e, s_=s_, bk=bk: e.activation(out=ge[s_][:], in_=bk[:, 0:256], func=AF.Gelu), reads=[bb], writes=[B_ge[s_]])
                            p.op("dve", lambda e, s_=s_, c=c: e.tensor_tensor(out=Gs[:, c, :], in0=Gs[:, c, :], in1=ge[s_][:], op=ALU.mult),
                                 reads=[B_ge[s_], B_Gs], writes=[B_Gs])
                        if ga is not None:
                            ga = drain(ga, 1)
                        else:
                            gb = drain(gb, 1)
                    ga = drain(ga, 10000)
                    for c2 in range(64):
                        sl = sctr[0] % NSL
                        sctr[0] += 1
                        p.dma("sp", lambda e, sl=sl, c2=c2: e.dma_start(out=strm[sl][:], in_=v_b[c2 * 256:(c2 + 1) * 256, :].rearrange("(c p) f -> p c f", p=128)), f"cs{sl}",
                              reads=[B_wuv], writes=[B_strm[sl]])
                        for cj in range(2):
                            c = 2 * c2 + cj
                            for ti in range(2):
                                for db in range(4):
                                    bi = ti * 4 + db
                                    p.op("pe", lambda e, sl=sl, cj=cj, c=c, ti=ti, db=db, bi=bi: e.matmul(banks[bi][:, 0:512], lhsT=Gs[:, c, ti * 128:(ti + 1) * 128], rhs=strm[sl][:, cj, db * 512:(db + 1) * 512],
                                                                                                   start=(c == 0), stop=(c == 127)), reads=[B_strm[sl], B_Gs], writes=[bank_buf[bi]], pe_accum=True)
                        gb = drain(gb, 1)
                    gb = drain(gb, 10000)
                    p.dma("sp", lambda e: e.dma_start(out=c_gf[:, 0, :], in_=rep_d[:, 2, :]), "cgf", writes=[B_gf])
                    for ti in range(2):
                        t = tok0 + ti * 128
                        p.dma("sp", lambda e, t=t: e.dma_start(out=x1t[:, 0, :], in_=x1_d[t:t + 128, :]), "cx", writes=[B_x1t])
                        for db in range(4):
                            bi = ti * 4 + db
                            p.op("dve", lambda e, db=db, bi=bi: e.tensor_tensor(out=x1t[:, 0, db * 512:(db + 1) * 512], in0=banks[bi][:, 0:512], in1=x1t[:, 0, db * 512:(db + 1) * 512], op=ALU.add),
                                 reads=[bank_buf[bi], B_x1t], writes=[B_x1t])
                        p.op("act", lambda e, ti=ti: e.activation(out=xn[:], in_=x1t[:, 0, :], func=AF.Square, accum_out=cst_[:, ti, 0:1]), reads=[B_x1t, B_xn], writes=[B_xn, B_cst])
                        p.op("act", lambda e, ti=ti: e.activation(out=cst_[:, ti, 1:2], in_=cst_[:, ti, 0:1], func=AF.Sqrt, scale=1.0 / D, bias=EPS), reads=[B_cst], writes=[B_cst])
                        p.op("dve", lambda e, ti=ti: e.reciprocal(out=cst_[:, ti, 2:3], in_=cst_[:, ti, 1:2]), reads=[B_cst], writes=[B_cst])
                        p.op("dve", lambda e, ti=ti: e.scalar_tensor_tensor(out=x1t[:, 0, :], in0=x1t[:, 0, :], scalar=cst_[:, ti, 2:3], in1=c_gf[:, 0, :], op0=ALU.mult, op1=ALU.mult),
                             reads=[B_cst, B_gf, B_x1t], writes=[B_x1t])
                        p.dma("pool", lambda e, t=t: e.dma_start(out=y_out[t:t + 128, :], in_=x1t[:, 0, :]), "cy", reads=[B_x1t])

                NGRP = T_OWN // 256
                drain(phaseA1(0), 10000)
                drain(phaseA2(0), 10000)
                for gi in range(NGRP):
                    gbuild(gi)
                    if gi + 1 < NGRP:
                        passes_and_epilogue(gi, phaseA1(gi + 1), phaseA2(gi + 1))
                    else:
                        passes_and_epilogue(gi, None, None)
                p.barrier()

        if "C" in stages:
            stage_c()
        p.barrier(skip=())
        p.emit()
    return nc


def host_inputs(inp, c):
    b, q = c // 4, c % 4
    f32 = np.float32
    xp = inp["x_prompt"][b]
    xs = inp["x_sample"][b]

    def win(x, lo, hi):
        n = x.shape[0]
        out = np.zeros((hi - lo, x.shape[1]), f32)
        a, bnd = max(lo, 0), min(hi, n)
        out[a - lo:bnd - lo] = x[a:bnd]
        return out

    own = []
    emask = np.zeros((128, NG_OWN, 2), f32)
    gi = 0
    for (x, L) in ((xp, 2048), (xs, 1024)):
        for g in range(L // 512):
            lo = q * L + g * 512 - 128
            own.append(win(x, lo, lo + 768))
            if lo < 0:
                emask[:, gi, 0] = -1e30
            if lo + 768 > x.shape[0]:
                emask[:, gi, 1] = -1e30
            gi += 1
    oth = []
    omask = np.zeros((128, NG_OTH, 4), f32)
    gi = 0
    for (x, L) in ((xp, 2048), (xs, 1024)):
        for j in [jj for jj in range(4) if jj != q]:
            for g in range(L // 512):
                lo = j * L + g * 512 - 2
                oth.append(win(x, lo, lo + 516))
                mf = 1.0 if j < q else 0.0
                omask[:, gi, :] = [mf, 1 - mf, 1 - mf, mf]
                gi += 1
    x_res = np.concatenate([xp[q * 2048:(q + 1) * 2048], xs[q * 1024:(q + 1) * 1024]], axis=0)
    rep = lambda v: np.broadcast_to(np.asarray(v, f32).reshape(1, -1), (128, np.asarray(v).size)).copy()
    rep_d = np.stack([rep(inp["g_mix"][0]), rep(inp["g_ffn"][0]), rep(inp["g_final"]), rep(inp["g_ssm_norm"][0])], axis=1)
    rep_s = np.zeros((128, 160), f32)
    rep_s[:, 0:32] = inp["a_log_f"][0]
    rep_s[:, 32:64] = inp["a_log_b"][0]
    rep_s[:, 64:96] = inp["d_skip"][0]
    rep_s[:, 96:112] = inp["attn_sink"][0]
    slopes = np.exp2(-8.0 * np.arange(1, 17, dtype=np.float64) / 16)
    qi = np.arange(128)[:, None]
    km = np.arange(384)[None, :]
    rel = qi - km + 128
    ab = np.where(np.abs(rel) <= 128, -np.abs(rel).astype(np.float64), -1e30).astype(f32)
    convw = np.ascontiguousarray(inp["conv_w"][0].T.reshape(24, 128, 5).transpose(1, 0, 2))
    convb = np.ascontiguousarray(inp["conv_b"][0].reshape(24, 128).T)
    dtb = np.concatenate([inp["dt_bias_f"][0], inp["dt_bias_b"][0]]).reshape(64, 1).astype(f32)
    cst = np.zeros((128, 9, 128), f32)
    s_ = np.arange(128)[:, None]
    l_ = np.arange(128)[None, :]
    cst[:, 0] = np.eye(128)
    cst[:, 1] = (s_ <= l_)
    cst[:, 2] = (s_ >= l_)
    cst[:, 3] = 1.0
    cst[:, 4] = l_
    import ml_dtypes
    negm = np.zeros((128, 2, 512), f32)
    negm[:, 0] = np.tile(np.where(s_ > l_, NEG, 0.0), (1, 4))
    negm[:, 1] = np.tile(np.where(s_ < l_, NEG, 0.0), (1, 4))
    sel = np.zeros((32, 32, 128), f32)
    for h in range(32):
        sel[h, h, :] = 1.0
    return {
        "x_own": np.stack(own), "x_oth": np.stack(oth), "x_res": x_res,
        "w_in": inp["w_in"][0], "w_ao": inp["w_attn_o"][0], "w_so": inp["w_ssm_o"][0], "w_out": inp["w_out"][0],
        "w_q": inp["w_query"][0], "keys": inp["sub_keys"][0].reshape(16, 128, 128),
        "exp_u": inp["expert_u"][0], "exp_v": inp["expert_v"][0],
        "rep_d": rep_d, "rep_s": rep_s, "attn_bias": ab, "emask": emask, "omask": omask,
        "convw": convw, "convb": convb, "dtb": dtb, "cst": cst, "negm": negm.astype(ml_dtypes.bfloat16), "sel": sel,
    }


def kernel(**inputs):
    inp = {k: np.asarray(v) for k, v in inputs.items()}
    nc = build()
    in_maps = [host_inputs(inp, c) for c in range(NCORES)]
    res = run_bass_kernel_spmd(nc, in_maps, core_ids=list(range(NCORES)))
    yp = np.zeros((2, 8192, D), np.float32)
    ys = np.zeros((2, 4096, D), np.float32)
    for c in range(NCORES):
        b, q = c // 4, c % 4
        y = res.results[c]["y_out"]
        yp[b, q * 2048:(q + 1) * 2048] = y[0:2048]
        ys[b, q * 1024:(q + 1) * 1024] = y[2048:3072]
    return (yp, ys)
```
